# Optimizing a Trainium2 kernel written in Bass

```python
import math
import jax, jax.numpy as jnp
from jax import lax
import numpy as np

D_MODEL = 2048
BATCH = 4
SEQ = 8192
DEPTH = 4

CHUNK = 64
Q_BLOCK = 128
EPS = 1e-6
ROPE_THETA = 10000.0

A_HEAD = 64
A_DIM = 3 * D_MODEL // 8
A_HEADS = A_DIM // A_HEAD
DECAY_LORA = max(32, int(round(1.8 * D_MODEL ** 0.5 / 32)) * 32)
AAA_LORA = max(32, int(round(1.8 * D_MODEL ** 0.5 / 32)) * 32)
GATE_LORA = max(32, int(round(0.6 * D_MODEL ** 0.8 / 32)) * 32)
RW_IN = 3 * A_DIM + DECAY_LORA + AAA_LORA + GATE_LORA
GN_EPS = 64e-5

B_QK = 64
B_V = 2 * B_QK
B_DIM = 3 * D_MODEL // 8
B_HEADS = B_DIM // B_V
B_IN = 2 * (B_HEADS * 2 * B_QK) + B_DIM

C_GROUP = 16
C_DIM = D_MODEL // 4
C_GROUPS = C_DIM // C_GROUP
C_STATE = 64

N_BRANCH = 3
N_IN = RW_IN + B_IN + C_DIM + N_BRANCH * D_MODEL

D_FF = ((8 * D_MODEL // 3) + 255) // 256 * 256
CONV_W = 3

kernel_name = 'hybrid_rwkv7_diffattn_s5_convglu_trunk'


def _rms_norm(x, g):
    xf = x.astype(jnp.float32)
    xf = xf * lax.rsqrt(jnp.mean(xf * xf, axis=-1, keepdims=True) + EPS)
    return (xf * g.astype(jnp.float32)).astype(x.dtype)


def _rope_tables(positions):
    inv_freq = ROPE_THETA ** (-jnp.arange(0, B_QK, 2, dtype=jnp.float32) / B_QK)
    ang = positions.astype(jnp.float32)[..., None] * inv_freq
    return jnp.cos(ang), jnp.sin(ang)


def _apply_rope(t, cos, sin):
    tf = t.astype(jnp.float32)
    t1, t2 = jnp.split(tf, 2, axis=-1)
    c = cos[:, :, None, None, :]
    s = sin[:, :, None, None, :]
    return jnp.concatenate([t1 * c - t2 * s, t1 * s + t2 * c], axis=-1)


def _rwkv7_mixer(z, mix, w0, w2, a0, a2, g2, k_k, k_a, r_k, ln_w, ln_b):
    bsz, seq, _ = z.shape
    zf = z.astype(jnp.float32)
    z_prev = jnp.pad(zf, ((0, 0), (1, 0), (0, 0)))[:, :-1]
    zf = zf + (z_prev - zf) * mix
    splits = [A_DIM, 2 * A_DIM, 3 * A_DIM, 3 * A_DIM + DECAY_LORA, 3 * A_DIM + DECAY_LORA + AAA_LORA]
    r, k, v, zw, za, zg = jnp.split(zf, splits, axis=-1)
    w_log = -jax.nn.softplus(-(w0 + jnp.tanh(zw) @ w2)) - 0.5
    decay = jnp.exp(-jnp.exp(w_log))
    a = jax.nn.sigmoid(a0 + za @ a2)
    g = jax.nn.sigmoid(zg) @ g2
    heads = lambda t: t.reshape(bsz, seq, A_HEADS, A_HEAD)
    kk = heads(k * k_k)
    kk = kk / jnp.maximum(jnp.sqrt(jnp.sum(kk * kk, axis=-1, keepdims=True)), 1e-12)
    k = k * (1.0 + (a - 1.0) * k_a)
    r, k, v, a, decay = heads(r), heads(k), heads(v), heads(a), heads(decay)

    def step(state, inp):
        r_t, w_t, k_t, v_t, kk_t, a_t = inp
        sa = jnp.einsum('bhvk,bhk->bhv', state, -kk_t)
        state = (state * w_t[:, :, None, :]
                 + sa[..., None] * (kk_t * a_t)[:, :, None, :]
                 + v_t[..., None] * k_t[:, :, None, :])
        return state, jnp.einsum('bhvk,bhk->bhv', state, r_t)

    xs = tuple(jnp.moveaxis(t, 1, 0) for t in (r, decay, k, v, kk, a))
    state0 = jnp.zeros((bsz, A_HEADS, A_HEAD, A_HEAD), jnp.float32)
    _, y = lax.scan(step, state0, xs)
    y = jnp.moveaxis(y, 0, 1)
    mu = jnp.mean(y, axis=-1, keepdims=True)
    var = jnp.mean(jnp.square(y - mu), axis=-1, keepdims=True)
    y = ((y - mu) * lax.rsqrt(var + GN_EPS)).reshape(bsz, seq, A_DIM) * ln_w + ln_b
    bonus = jnp.sum(r * k * r_k, axis=-1, keepdims=True) * v
    y = (y + bonus.reshape(bsz, seq, A_DIM)) * g
    return y.astype(z.dtype)


def _diff_attention(z, cos, sin, lq1, lk1, lq2, lk2, subln, lam_init):
    bsz, seq, _ = z.shape
    q, k, v = jnp.split(z, [2 * B_HEADS * B_QK, 4 * B_HEADS * B_QK], axis=-1)
    q = _apply_rope(q.reshape(bsz, seq, B_HEADS, 2, B_QK), cos, sin)
    k = _apply_rope(k.reshape(bsz, seq, B_HEADS, 2, B_QK), cos, sin)
    v = v.reshape(bsz, seq, B_HEADS, B_V).astype(jnp.float32)
    lam = (jnp.exp(jnp.sum(lq1.astype(jnp.float32) * lk1.astype(jnp.float32)))
           - jnp.exp(jnp.sum(lq2.astype(jnp.float32) * lk2.astype(jnp.float32))) + lam_init)
    scale = B_QK ** -0.5
    key_chunk = jnp.arange(seq) // CHUNK

    def block(i):
        start = i * Q_BLOCK
        qb = lax.dynamic_slice_in_dim(q, start, Q_BLOCK, axis=1)
        s = jnp.einsum('bqhce,bkhce->bhcqk', qb, k) * scale
        q_chunk = (start + jnp.arange(Q_BLOCK)) // CHUNK
        mask = key_chunk[None, :] <= q_chunk[:, None]
        s = jnp.where(mask, s, -jnp.inf)
        p = jax.nn.softmax(s, axis=-1)
        attn = p[:, :, 0] - lam * p[:, :, 1]
        return jnp.einsum('bhqk,bkhv->bqhv', attn, v)

    o = lax.map(block, jnp.arange(seq // Q_BLOCK))
    o = jnp.moveaxis(o, 0, 1).reshape(bsz, seq, B_HEADS, B_V)
    o = o * lax.rsqrt(jnp.mean(o * o, axis=-1, keepdims=True) + EPS) * subln
    o = o * (1.0 - lam_init)
    return o.reshape(bsz, seq, B_DIM).astype(z.dtype)


def _complex_affine_combine(e1, e2):
    a1r, a1i, b1r, b1i = e1
    a2r, a2i, b2r, b2i = e2
    ar = a2r * a1r - a2i * a1i
    ai = a2r * a1i + a2i * a1r
    br = a2r * b1r - a2i * b1i + b2r
    bi = a2r * b1i + a2i * b1r + b2i
    return (ar, ai, br, bi)


def _s5_mixer(u, lam_re, lam_im, log_dt, b_re, b_im, c_re, c_im, d, glu_w, glu_b):
    bsz, seq, _ = u.shape
    uf = u.astype(jnp.float32).reshape(bsz, seq, C_GROUPS, C_GROUP)
    dt = jnp.exp(log_dt.astype(jnp.float32))[:, None]
    lr = lam_re.astype(jnp.float32)
    li = lam_im.astype(jnp.float32)
    er = jnp.exp(lr * dt)
    ab_re = er * jnp.cos(li * dt)
    ab_im = er * jnp.sin(li * dt)
    den = lr * lr + li * li
    nr = ab_re - 1.0
    f_re = (nr * lr + ab_im * li) / den
    f_im = (ab_im * lr - nr * li) / den
    br = b_re.astype(jnp.float32)
    bi = b_im.astype(jnp.float32)
    bb_re = f_re[..., None] * br - f_im[..., None] * bi
    bb_im = f_re[..., None] * bi + f_im[..., None] * br
    bu_re = jnp.einsum('gpc,bsgc->bsgp', bb_re, uf)
    bu_im = jnp.einsum('gpc,bsgc->bsgp', bb_im, uf)
    a_re = jnp.broadcast_to(ab_re[None, None], (1, seq, C_GROUPS, C_STATE))
    a_im = jnp.broadcast_to(ab_im[None, None], (1, seq, C_GROUPS, C_STATE))
    _, _, x_re, x_im = lax.associative_scan(_complex_affine_combine, (a_re, a_im, bu_re, bu_im), axis=1)
    y = (jnp.einsum('gcp,bsgp->bsgc', c_re.astype(jnp.float32), x_re)
         - jnp.einsum('gcp,bsgp->bsgc', c_im.astype(jnp.float32), x_im)
         + d.reshape(C_GROUPS, C_GROUP) * uf)
    y = jax.nn.gelu(y.reshape(bsz, seq, C_DIM))
    y = y * jax.nn.sigmoid(y @ glu_w + glu_b)
    return y.astype(u.dtype)


def _conv_glu_ffn(h, w_up, conv_w, w_down):
    seq = h.shape[1]
    val, gate = jnp.split(h @ w_up, 2, axis=-1)
    gp = jnp.pad(gate, ((0, 0), (CONV_W - 1, 0), (0, 0)))
    gate = sum(conv_w[j] * gp[:, j:j + seq] for j in range(CONV_W))
    return (jax.nn.gelu(gate) * val) @ w_down


def setup_inputs(seed: int = 0) -> dict:
    key = jax.random.key(seed)
    keys = jax.random.split(key, 48)
    counter = [0]

    def nk():
        counter[0] += 1
        return keys[counter[0] - 1]

    def nrm(shape, scale):
        return scale * jax.random.normal(nk(), shape, jnp.float32)

    L = DEPTH
    f32 = jnp.float32
    inp = {}
    inp['x'] = nrm((BATCH, SEQ, D_MODEL), 1.0)
    offset = jax.random.randint(nk(), (BATCH, 1), 0, 64, dtype=jnp.int32) * CHUNK
    inp['positions'] = (offset + jnp.arange(SEQ, dtype=jnp.int32)[None, :]).astype(jnp.int32)
    inp['norm_mix'] = 1.0 + nrm((L, D_MODEL), 0.02)
    inp['norm_ffn'] = 1.0 + nrm((L, D_MODEL), 0.02)
    inp['w_in'] = nrm((L, D_MODEL, N_IN), D_MODEL ** -0.5)
    inp['b_gate'] = nrm((L, N_BRANCH * D_MODEL), 0.01)
    inp['rw_mix'] = jax.random.uniform(nk(), (L, RW_IN), f32)
    inp['rw_w0'] = jnp.linspace(-6.0, -1.0, A_DIM, dtype=f32)[None, :] + nrm((L, A_DIM), 0.1)
    inp['rw_w2'] = nrm((L, DECAY_LORA, A_DIM), DECAY_LORA ** -0.5)
    inp['rw_a0'] = nrm((L, A_DIM), 0.1)
    inp['rw_a2'] = nrm((L, AAA_LORA, A_DIM), AAA_LORA ** -0.5)
    inp['rw_g2'] = nrm((L, GATE_LORA, A_DIM), GATE_LORA ** -0.5)
    inp['rw_k_k'] = 0.85 + nrm((L, A_DIM), 0.05)
    inp['rw_k_a'] = 1.0 + nrm((L, A_DIM), 0.05)
    inp['rw_r_k'] = nrm((L, A_HEADS, A_HEAD), 0.1)
    inp['rw_ln_w'] = 1.0 + nrm((L, A_DIM), 0.02)
    inp['rw_ln_b'] = nrm((L, A_DIM), 0.01)
    inp['da_lq1'] = nrm((L, B_QK), 0.1)
    inp['da_lk1'] = nrm((L, B_QK), 0.1)
    inp['da_lq2'] = nrm((L, B_QK), 0.1)
    inp['da_lk2'] = nrm((L, B_QK), 0.1)
    inp['da_subln'] = 1.0 + nrm((L, B_V), 0.02)
    inp['s5_lam_re'] = -0.5 + nrm((L, C_GROUPS, C_STATE), 0.01)
    inp['s5_lam_im'] = jnp.pi * jnp.arange(C_STATE, dtype=f32) + nrm((L, C_GROUPS, C_STATE), 0.01)
    inp['s5_log_dt'] = jax.random.uniform(nk(), (L, C_GROUPS), f32, math.log(1e-3), math.log(1e-1))
    inp['s5_b_re'] = nrm((L, C_GROUPS, C_STATE, C_GROUP), (2 * C_GROUP) ** -0.5)
    inp['s5_b_im'] = nrm((L, C_GROUPS, C_STATE, C_GROUP), (2 * C_GROUP) ** -0.5)
    inp['s5_c_re'] = nrm((L, C_GROUPS, C_GROUP, C_STATE), (2 * C_STATE) ** -0.5)
    inp['s5_c_im'] = nrm((L, C_GROUPS, C_GROUP, C_STATE), (2 * C_STATE) ** -0.5)
    inp['s5_d'] = nrm((L, C_DIM), 1.0)
    inp['s5_glu_w'] = nrm((L, C_DIM, C_DIM), C_DIM ** -0.5)
    inp['s5_glu_b'] = nrm((L, C_DIM), 0.01)
    inp['proj_a'] = nrm((L, A_DIM, D_MODEL), A_DIM ** -0.5)
    inp['proj_b'] = nrm((L, B_DIM, D_MODEL), B_DIM ** -0.5)
    inp['proj_c'] = nrm((L, C_DIM, D_MODEL), C_DIM ** -0.5)
    inp['w_out'] = nrm((L, D_MODEL, D_MODEL), D_MODEL ** -0.5)
    inp['ffn_up'] = nrm((L, D_MODEL, 2 * D_FF), D_MODEL ** -0.5)
    inp['ffn_conv'] = nrm((L, CONV_W, D_FF), CONV_W ** -0.5)
    inp['ffn_down'] = nrm((L, D_FF, D_MODEL), D_FF ** -0.5)
    inp['norm_final'] = 1.0 + nrm((D_MODEL,), 0.02)
    return inp


def reference(x, positions, norm_mix, norm_ffn, w_in, b_gate,
              rw_mix, rw_w0, rw_w2, rw_a0, rw_a2, rw_g2, rw_k_k, rw_k_a, rw_r_k, rw_ln_w, rw_ln_b,
              da_lq1, da_lk1, da_lq2, da_lk2, da_subln,
              s5_lam_re, s5_lam_im, s5_log_dt, s5_b_re, s5_b_im, s5_c_re, s5_c_im, s5_d, s5_glu_w, s5_glu_b,
              proj_a, proj_b, proj_c, w_out, ffn_up, ffn_conv, ffn_down, norm_final):
    bsz, seq, _ = x.shape
    cos, sin = _rope_tables(positions)
    for l in range(DEPTH):
        h = _rms_norm(x, norm_mix[l])
        z = h @ w_in[l]
        z_a, z_b, z_c, z_g = jnp.split(z, [RW_IN, RW_IN + B_IN, RW_IN + B_IN + C_DIM], axis=-1)
        y_a = _rwkv7_mixer(z_a, rw_mix[l], rw_w0[l], rw_w2[l], rw_a0[l], rw_a2[l], rw_g2[l],
                           rw_k_k[l], rw_k_a[l], rw_r_k[l], rw_ln_w[l], rw_ln_b[l])
        lam_init = 0.8 - 0.6 * math.exp(-0.3 * l)
        y_b = _diff_attention(z_b, cos, sin, da_lq1[l], da_lk1[l], da_lq2[l], da_lk2[l], da_subln[l], lam_init)
        y_c = _s5_mixer(z_c, s5_lam_re[l], s5_lam_im[l], s5_log_dt[l], s5_b_re[l], s5_b_im[l],
                        s5_c_re[l], s5_c_im[l], s5_d[l], s5_glu_w[l], s5_glu_b[l])
        gates = jax.nn.sigmoid(z_g + b_gate[l]).reshape(bsz, seq, N_BRANCH, D_MODEL)
        merged = (gates[:, :, 0] * (y_a @ proj_a[l])
                  + gates[:, :, 1] * (y_b @ proj_b[l])
                  + gates[:, :, 2] * (y_c @ proj_c[l]))
        x = x + (merged @ w_out[l]).astype(x.dtype)
        x = x + _conv_glu_ffn(_rms_norm(x, norm_ffn[l]), ffn_up[l], ffn_conv[l], ffn_down[l]).astype(x.dtype)
    return _rms_norm(x, norm_final)
```

```python
import math
from contextlib import ExitStack
import numpy as np
import ml_dtypes
import concourse.bass as bass
import concourse.mybir as mybir
from concourse.bass_utils import run_bass_kernel_spmd

F32 = mybir.dt.float32
BF16 = mybir.dt.bfloat16
I32 = mybir.dt.int32
AF = mybir.ActivationFunctionType
ALU = mybir.AluOpType
AX = mybir.AxisListType

D = 2048
DEPTH = 4
A_DIM = 768
RW_IN = 2752
B_IN = 2304
C_DIM = 512
N_IN = 11712
D_FF = 5632
EPS = 1e-6
GN_EPS = 64e-5
ZROWS = 4288
TS = 512


class Res:
    __slots__ = ("name", "w", "r", "ds")

    def __init__(self, name):
        self.name = name
        self.w = []
        self.r = []
        self.ds = None


class Sem:
    __slots__ = ("h", "cnt", "dma")

    def __init__(self, h, dma):
        self.h = h
        self.cnt = 0
        self.dma = dma


class KB:
    def __init__(self, nc):
        self.nc = nc
        self.E = {"pe": nc.tensor, "act": nc.scalar, "dve": nc.vector, "pool": nc.gpsimd, "sp": nc.sync}
        self.sems = []
        self.eidx = {}
        for e in self.E:
            self.eidx[e] = len(self.sems)
            self.sems.append(Sem(nc.alloc_semaphore("q_" + e), False))
        self.seen = {e: {} for e in self.E}
        self.rr = 0
        self.nops = 0

    def _wait(self, e, evs):
        need = {}
        for k, v in evs:
            S = self.sems[k]
            if S.dma:
                v = S.cnt
            if v > need.get(k, 0):
                need[k] = v
        own = self.eidx[e]
        for k, v in need.items():
            if k == own and e == "pe":
                continue
            if self.seen[e].get(k, 0) < v:
                self.E[e].wait_ge(self.sems[k].h, v)
                self.seen[e][k] = v

    def _deps(self, r, w):
        evs = []
        for x in r:
            evs += x.w
        for x in w:
            evs += x.w
            evs += x.r
        return evs

    def _mark(self, ev, r, w):
        for x in r:
            x.r = [p for p in x.r if p[0] != ev[0]] + [ev]
        for x in w:
            x.w = [ev]
            x.r = []

    def op(self, e, fn, r=(), w=()):
        self._wait(e, self._deps(r, w))
        ins = fn(self.E[e])
        k = self.eidx[e]
        self.sems[k].cnt += 1
        ins.then_inc(self.sems[k].h, 1)
        self._mark((k, self.sems[k].cnt), r, w)
        self.nops += 1

    NDMA = 64

    def dsem(self, res):
        if res.ds is None:
            if not hasattr(self, "dpool"):
                self.dpool = []
                self.dnext = 0
            if len(self.dpool) < self.NDMA:
                self.dpool.append(len(self.sems))
                self.sems.append(Sem(self.nc.alloc_semaphore("d%d" % len(self.sems)), True))
                res.ds = self.dpool[-1]
            else:
                res.ds = self.dpool[self.dnext % self.NDMA]
                self.dnext += 1
        return res.ds

    def dma(self, e, out, in_, r=(), w=(), sres=None, slow=False):
        self._wait(e, self._deps(r, w))
        if sres is None:
            sres = w[0] if w else r[0]
        k = self.dsem(sres)
        if slow:
            ins = self.E[e].dma_start(out=out, in_=in_, allow_slow_non_contiguous=True)
        else:
            ins = self.E[e].dma_start(out=out, in_=in_)
        self.sems[k].cnt += 16
        ins.then_inc(self.sems[k].h, 16)
        self._mark((k, self.sems[k].cnt), r, w)
        self.nops += 1

    def barrier(self):
        evs = [(k, S.cnt) for k, S in enumerate(self.sems) if S.cnt > 0]
        for e in self.E:
            self._wait(e, evs)

    def ev(self):
        self.rr ^= 1
        return "act" if self.rr else "dve"


class Tl:
    def __init__(self, t, name):
        self.t = t
        self.res = Res(name)

    def __getitem__(self, idx):
        return self.t[idx]


class Ctx:
    def __init__(self, nc, kb):
        self.nc = nc
        self.kb = kb
        self.n = 0
        self.stack = None

    def sb(self, shape, dt, name=None):
        self.n += 1
        name = (name or "t") + "_%d" % self.n
        if self.stack is not None:
            t = self.stack.enter_context(self.nc.sbuf_tensor(name, list(shape), dt))
        else:
            t = self.nc.alloc_sbuf_tensor(name, list(shape), dt)
        return Tl(t, name)

    def ps(self, shape, dt, name=None):
        self.n += 1
        name = (name or "p") + "_%d" % self.n
        if self.stack is not None:
            t = self.stack.enter_context(self.nc.psum_tensor(name, list(shape), dt))
        else:
            t = self.nc.alloc_psum_tensor(name, list(shape), dt)
        return Tl(t, name)


def copy_evac(kb, eng, out_ap, in_ap, r, w, scale=None, bias=None, func=None):
    if eng == "act" or func is not None or scale is not None or bias is not None:
        kw = {}
        if scale is not None:
            kw["scale"] = scale
        if bias is not None:
            kw["bias"] = bias
        f = func if func is not None else AF.Identity
        kb.op("act", lambda h: h.activation(out=out_ap, in_=in_ap, func=f, **kw), r=r, w=w)
    else:
        kb.op(eng, lambda h: h.tensor_copy(out=out_ap, in_=in_ap), r=r, w=w)


class Prog:
    def __init__(self, S, depth, debug=False):
        self.S = S
        self.depth = depth
        self.debug = debug
        self.nc = bass.Bass("TRN2", target_bir_lowering=False)
        self.kb = KB(self.nc)
        self.cx = Ctx(self.nc, self.kb)
        self.dram = {}

    def din(self, name, shape, dt=F32):
        t = self.nc.dram_tensor(name, list(shape), dt, kind="ExternalInput")
        self.dram[name] = t
        return t.ap()

    def dscr(self, name, shape, dt, out=False):
        kind = "ExternalOutput" if (out or name in getattr(self, "dbg_out", ())) else "Internal"
        if name in getattr(self, "ext_in", ()):
            kind = "ExternalInput"
        t = self.nc.dram_tensor(name, list(shape), dt, kind=kind)
        self.dram[name] = t
        return t.ap()


PARAM_SHAPES = {
    "norm_mix": (DEPTH, D), "norm_ffn": (DEPTH, D), "w_in": (DEPTH, D, N_IN), "b_gate": (DEPTH, 3 * D),
    "rw_mix": (DEPTH, RW_IN), "rw_w0": (DEPTH, A_DIM), "rw_w2": (DEPTH, 96, A_DIM), "rw_a0": (DEPTH, A_DIM),
    "rw_a2": (DEPTH, 96, A_DIM), "rw_g2": (DEPTH, 256, A_DIM), "rw_k_k": (DEPTH, A_DIM), "rw_k_a": (DEPTH, A_DIM),
    "rw_r_k": (DEPTH, A_DIM), "rw_ln_w": (DEPTH, A_DIM), "rw_ln_b": (DEPTH, A_DIM),
    "da_lq1": (DEPTH, 64), "da_lk1": (DEPTH, 64), "da_lq2": (DEPTH, 64), "da_lk2": (DEPTH, 64), "da_subln": (DEPTH, 128),
    "s5_lam_re": (DEPTH, 32, 64), "s5_lam_im": (DEPTH, 32, 64), "s5_log_dt": (DEPTH, 32),
    "s5_b_re": (DEPTH, 32, 64, 16), "s5_b_im": (DEPTH, 32, 64, 16), "s5_c_re": (DEPTH, 32, 16, 64), "s5_c_im": (DEPTH, 32, 16, 64),
    "s5_d": (DEPTH, 512), "s5_glu_w": (DEPTH, 512, 512), "s5_glu_b": (DEPTH, 512),
    "proj_a": (DEPTH, A_DIM, D), "proj_b": (DEPTH, A_DIM, D), "proj_c": (DEPTH, C_DIM, D), "w_out": (DEPTH, D, D),
    "ffn_up": (DEPTH, D, 2 * D_FF), "ffn_conv": (DEPTH, 3, D_FF), "ffn_down": (DEPTH, D_FF, D), "norm_final": (D,),
}


def units_p1():
    u = []
    for c in range(0, 2304, 128):
        u.append((c, 128))
    u += [(2304, 96), (2400, 96), (2496, 128), (2624, 128)]
    for c in range(2752, 4288, 64):
        u.append((c, 64))
    return u


def group_blocks(units, maxc=512):
    blocks = []
    cur = []
    for (c, m) in units:
        if cur and (c + m - cur[0][0] > maxc or c != cur[-1][0] + cur[-1][1]):
            blocks.append(cur)
            cur = []
        cur.append((c, m))
    if cur:
        blocks.append(cur)
    return blocks


class Dense:
    def __init__(self, P):
        self.P = P
        cx, kb = P.cx, P.kb
        self.psf = [cx.ps([128, 512], F32, "psf") for _ in range(6)]
        self.psb = [cx.ps([128, 1024], BF16, "psb") for _ in range(2)]
        self.ipf = 0
        self.ipb = 0

    def nps(self, n=None):
        n = n or len(self.psf)
        self.ipf = (self.ipf + 1) % n
        return self.psf[self.ipf]

    def npb(self):
        self.ipb = (self.ipb + 1) % len(self.psb)
        return self.psb[self.ipb]


def rmsnorm_T(P, dn, xt, gcol, hT, junk, ss, rs, xn, ident):
    kb = P.kb
    for j in range(4):
        kb.op("act", lambda h: h.activation(out=junk[:, :], in_=xt[:, j, :], func=AF.Square,
                                            accum_out=ss[:, j:j + 1]), r=[xt.res], w=[junk.res, ss.res])
    kb.op("act", lambda h: h.activation(out=rs[:, 0:4], in_=ss[:, 0:4], func=AF.Sqrt, scale=1.0 / D, bias=EPS),
          r=[ss.res], w=[rs.res])
    kb.op("dve", lambda h: h.reciprocal(out=rs[:, 0:4], in_=rs[:, 0:4]), r=[rs.res], w=[rs.res])
    for j in range(4):
        kb.op("dve", lambda h: h.tensor_scalar(out=xn[:, j, :], in0=xt[:, j, :], scalar1=rs[:, j:j + 1],
                                               scalar2=None, op0=ALU.mult), r=[xt.res, rs.res], w=[xn.res])
    for c in range(16):
        pb = dn.npb()
        for j in range(4):
            kb.op("pe", lambda h: h.transpose(out=pb[:, j * 128:(j + 1) * 128], in_=xn[:, j, c * 128:(c + 1) * 128],
                                              identity=ident[:, :]), r=[xn.res, ident.res], w=[pb.res])
        kb.op("act", lambda h: h.activation(out=hT[:, c, :], in_=pb[:, 0:512], func=AF.Identity,
                                            scale=gcol[:, c:c + 1]), r=[pb.res, gcol.res], w=[hT.res])


def phase1(P, dn, l, C):
    kb, cx, S = P.kb, P.cx, P.S
    x_src = C["x"] if l == 0 else C["xres"]
    gcol, bgcol, ident = C["gmix"], C["bgate"], C["ident"]
    xt = C["xt"]
    hT = C["hT"]
    wb = C["wblk"]
    st = C["stage"]
    stm = C["stage_tm"]
    ublocks = group_blocks(units_p1())
    wi = 0
    si = 0
    for it in range(S // TS):
        t0 = it * TS
        kb.dma("sp", xt[:, :, :], x_src[t0:t0 + TS, :].rearrange("(j p) d -> p j d", p=128), w=[xt.res])
        rmsnorm_T(P, dn, xt, gcol[l], hT, C["junk"], C["ss"], C["rs"], C["xn"], ident)
        for blk in ublocks:
            c0 = blk[0][0]
            ncols = blk[-1][0] + blk[-1][1] - c0
            w = wb[wi % len(wb)]
            wi += 1
            kb.dma("sp", w[:, :, 0:ncols], C["w_in_b"][l, :, c0:c0 + ncols].rearrange("(k p) c -> p k c", p=128),
                   w=[w.res])
            for (c, m) in blk:
                ps = dn.nps()
                for k in range(16):
                    kb.op("pe", lambda h: h.matmul(ps[0:m, :], w[:, k, c - c0:c - c0 + m], hT[:, k, :],
                                                   start=(k == 0), stop=(k == 15)), r=[w.res, hT.res], w=[ps.res])
                sg = st[si % len(st)]
                si += 1
                copy_evac(kb, kb.ev(), sg[0:m, :], ps[0:m, :], r=[ps.res], w=[sg.res])
                kb.dma("act", C["zT"][c:c + m, t0:t0 + TS], sg[0:m, :], r=[sg.res])
        for (c0, ncols, dst, dc) in [(4288, 512, "vatt", 0), (4800, 256, "vatt", 512), (5056, 512, "u5", 0)]:
            w = wb[wi % len(wb)]
            wi += 1
            kb.dma("sp", w[:, :, 0:ncols], C["w_in_b"][l, :, c0:c0 + ncols].rearrange("(k p) c -> p k c", p=128),
                   w=[w.res])
            sg = stm[si % len(stm)]
            si += 1
            for j in range(4):
                ps = dn.nps()
                for k in range(16):
                    kb.op("pe", lambda h: h.matmul(ps[:, 0:ncols], hT[:, k, j * 128:(j + 1) * 128], w[:, k, 0:ncols],
                                                   start=(k == 0), stop=(k == 15)), r=[w.res, hT.res], w=[ps.res])
                copy_evac(kb, kb.ev(), sg[:, j, 0:ncols], ps[:, 0:ncols], r=[ps.res], w=[sg.res])
            kb.dma("act", C[dst][t0:t0 + TS, dc:dc + ncols].rearrange("(j p) c -> p j c", p=128), sg[:, :, 0:ncols],
                   r=[sg.res])
        for gb in range(12):
            c0 = 5568 + gb * 512
            w = wb[wi % len(wb)]
            wi += 1
            kb.dma("sp", w[:, :, :], C["w_in_b"][l, :, c0:c0 + 512].rearrange("(k p) c -> p k c", p=128), w=[w.res])
            for u in range(4):
                ps = dn.nps()
                for k in range(16):
                    kb.op("pe", lambda h: h.matmul(ps[:, :], w[:, k, u * 128:(u + 1) * 128], hT[:, k, :],
                                                   start=(k == 0), stop=(k == 15)), r=[w.res, hT.res], w=[ps.res])
                sg = st[si % len(st)]
                si += 1
                gi = gb * 4 + u
                kb.op("act", lambda h: h.activation(out=sg[:, :], in_=ps[:, :], func=AF.Sigmoid,
                                                    bias=bgcol[l][:, gi:gi + 1]), r=[ps.res, bgcol[l].res], w=[sg.res])
                kb.dma("act", C["gT"][gi * 128:(gi + 1) * 128, t0:t0 + TS], sg[:, :], r=[sg.res])


def gelu_tanh(kb, out_ap, x_ap, tmp_ap, r, w_out, w_tmp):
    kb.op("dve", lambda h: h.tensor_tensor(out=tmp_ap, in0=x_ap, in1=x_ap, op=ALU.mult), r=r, w=[w_tmp])
    kb.op("dve", lambda h: h.tensor_scalar(out=tmp_ap, in0=tmp_ap, scalar1=0.044715 * 1.5957691216, scalar2=1.5957691216,
                                           op0=ALU.mult, op1=ALU.add), r=[w_tmp], w=[w_tmp])
    kb.op("dve", lambda h: h.tensor_tensor(out=tmp_ap, in0=tmp_ap, in1=x_ap, op=ALU.mult), r=r + [w_tmp], w=[w_tmp])
    kb.op("act", lambda h: h.activation(out=tmp_ap, in_=tmp_ap, func=AF.Sigmoid), r=[w_tmp], w=[w_tmp])
    kb.op("dve", lambda h: h.tensor_tensor(out=out_ap, in0=tmp_ap, in1=x_ap, op=ALU.mult), r=r + [w_tmp], w=[w_out])


def phase3(P, dn, l, C, last):
    kb, cx, S = P.kb, P.cx, P.S
    x_src = C["x"] if l == 0 else C["xres"]
    ident = C["ident"]
    xt, hT, wb = C["xt"], C["hT"], C["wblk"]
    yT, mT, aT, gt = C["yTt"], C["mT"], C["aT"], C["gt"]
    wi = 0
    gi_ = 0
    for it in range(S // TS):
        t0 = it * TS
        kb.dma("sp", xt[:, :, :], x_src[t0:t0 + TS, :].rearrange("(j p) d -> p j d", p=128), w=[xt.res])
        kb.dma("sp", yT[:, :, :], C["yT"][:, t0:t0 + TS].rearrange("(k p) t -> p k t", p=128), w=[yT.res])
        for fb in range(4):
            w = wb[wi % len(wb)]
            wi += 1
            kb.dma("sp", w[:, :, :], C["proj_b"][l, :, fb * 512:(fb + 1) * 512].rearrange("(k p) c -> p k c", p=128),
                   w=[w.res])
            for u in range(4):
                fc = fb * 4 + u
                g = gt[gi_ % len(gt)]
                gi_ += 1
                kb.dma("sp", g[:, :, :], C["gT"][:, t0:t0 + TS].rearrange("(i f p) t -> p i f t", i=3, p=128)[:, :, fc, :],
                       w=[g.res])
                acc = C["macc"]
                for i, (k0, k1) in enumerate([(0, 6), (6, 12), (12, 16)]):
                    ps = dn.nps()
                    for k in range(k0, k1):
                        kb.op("pe", lambda h: h.matmul(ps[:, :], w[:, k, u * 128:(u + 1) * 128], yT[:, k, :],
                                                       start=(k == k0), stop=(k == k1 - 1)), r=[w.res, yT.res], w=[ps.res])
                    if i == 0:
                        kb.op("dve", lambda h: h.tensor_tensor(out=acc[:, :], in0=ps[:, :], in1=g[:, 0, :], op=ALU.mult),
                              r=[ps.res, g.res], w=[acc.res])
                    else:
                        tmp = C["mtmp"]
                        kb.op("dve", lambda h: h.tensor_tensor(out=tmp[:, :], in0=ps[:, :], in1=g[:, i, :], op=ALU.mult),
                              r=[ps.res, g.res], w=[tmp.res])
                        if i == 1:
                            kb.op("pool", lambda h: h.tensor_tensor(out=acc[:, :], in0=acc[:, :], in1=tmp[:, :], op=ALU.add),
                                  r=[tmp.res, acc.res], w=[acc.res])
                        else:
                            kb.op("pool", lambda h: h.tensor_tensor(out=mT[:, fc, :], in0=acc[:, :], in1=tmp[:, :], op=ALU.add),
                                  r=[tmp.res, acc.res], w=[mT.res])
        for cb in range(4):
            w = wb[wi % len(wb)]
            wi += 1
            kb.dma("sp", w[:, :, :], C["w_out_b"][l, :, cb * 512:(cb + 1) * 512].rearrange("(k p) c -> p k c", p=128),
                   w=[w.res])
            for j in range(4):
                ps = dn.nps()
                for k in range(16):
                    kb.op("pe", lambda h: h.matmul(ps[:, :], mT[:, k, j * 128:(j + 1) * 128], w[:, k, :],
                                                   start=(k == 0), stop=(k == 15)), r=[w.res, mT.res], w=[ps.res])
                kb.op("dve", lambda h: h.tensor_tensor(out=xt[:, j, cb * 512:(cb + 1) * 512], in0=ps[:, :],
                                                       in1=xt[:, j, cb * 512:(cb + 1) * 512], op=ALU.add),
                      r=[ps.res, xt.res], w=[xt.res])
        rmsnorm_T(P, dn, xt, C["gffn"][l], hT, C["junk"], C["ss"], C["rs"], C["xn"], ident)
        cw = C["convw"][l]
        carry = C["carry"]
        for fb in range(11):
            wv = wb[wi % len(wb)]
            wi += 1
            kb.dma("sp", wv[:, :, :], C["up_b"][l, :, fb * 512:(fb + 1) * 512].rearrange("(k p) c -> p k c", p=128),
                   w=[wv.res])
            wg = wb[wi % len(wb)]
            wi += 1
            kb.dma("sp", wg[:, :, :], C["up_b"][l, :, D_FF + fb * 512:D_FF + (fb + 1) * 512].rearrange("(k p) c -> p k c", p=128),
                   w=[wg.res])
            for u in range(4):
                f = fb * 4 + u
                psv = dn.nps()
                for k in range(16):
                    kb.op("pe", lambda h: h.matmul(psv[:, :], wv[:, k, u * 128:(u + 1) * 128], hT[:, k, :],
                                                   start=(k == 0), stop=(k == 15)), r=[wv.res, hT.res], w=[psv.res])
                psg = dn.nps()
                for k in range(16):
                    kb.op("pe", lambda h: h.matmul(psg[:, :], wg[:, k, u * 128:(u + 1) * 128], hT[:, k, :],
                                                   start=(k == 0), stop=(k == 15)), r=[wg.res, hT.res], w=[psg.res])
                G = C["G"]
                cv = C["cv"]
                ct = C["ct"]
                kb.op("pool", lambda h: h.tensor_copy(out=G[:, 0:2], in_=carry[:, f, :]), r=[carry.res], w=[G.res])
                kb.op("act", lambda h: h.activation(out=G[:, 2:2 + TS], in_=psg[:, :], func=AF.Identity),
                      r=[psg.res], w=[G.res])
                kb.op("pool", lambda h: h.tensor_copy(out=carry[:, f, :], in_=G[:, TS:TS + 2]), r=[G.res], w=[carry.res])
                kb.op("dve", lambda h: h.tensor_scalar(out=cv[:, :], in0=G[:, 2:2 + TS], scalar1=cw[:, 2, f:f + 1],
                                                       scalar2=None, op0=ALU.mult), r=[G.res, cw.res], w=[cv.res])
                kb.op("dve", lambda h: h.scalar_tensor_tensor(out=cv[:, :], in0=G[:, 1:1 + TS], scalar=cw[:, 1, f:f + 1],
                                                              in1=cv[:, :], op0=ALU.mult, op1=ALU.add),
                      r=[G.res, cw.res, cv.res], w=[cv.res])
                kb.op("dve", lambda h: h.scalar_tensor_tensor(out=cv[:, :], in0=G[:, 0:TS], scalar=cw[:, 0, f:f + 1],
                                                              in1=cv[:, :], op0=ALU.mult, op1=ALU.add),
                      r=[G.res, cw.res, cv.res], w=[cv.res])
                gelu_tanh(kb, cv[:, :], cv[:, :], ct[:, :], [cv.res], cv.res, ct.res)
                kb.op("dve", lambda h: h.tensor_tensor(out=aT[:, f, :], in0=psv[:, :], in1=cv[:, :], op=ALU.mult),
                      r=[psv.res, cv.res], w=[aT.res])
        wd = C["wdn"]
        di = 0
        for cb in range(4):
            pss = [dn.nps() for _ in range(4)]
            for pc in range(4):
                w = wd[di % len(wd)]
                di += 1
                kb.dma("sp", w[:, :, :], C["down_b"][l, pc * 1408:(pc + 1) * 1408, cb * 512:(cb + 1) * 512]
                       .rearrange("(k p) c -> p k c", p=128), w=[w.res])
                for j in range(4):
                    for k in range(11):
                        f = pc * 11 + k
                        kb.op("pe", lambda h: h.matmul(pss[j][:, :], aT[:, f, j * 128:(j + 1) * 128], w[:, k, :],
                                                       start=(f == 0), stop=(f == 43)), r=[w.res, aT.res], w=[pss[j].res])
            for j in range(4):
                kb.op("dve", lambda h: h.tensor_tensor(out=xt[:, j, cb * 512:(cb + 1) * 512], in0=pss[j][:, :],
                                                       in1=xt[:, j, cb * 512:(cb + 1) * 512], op=ALU.add),
                      r=[pss[j].res, xt.res], w=[xt.res])
        if not last:
            kb.dma("act", C["xres"][t0:t0 + TS, :].rearrange("(j p) d -> p j d", p=128), xt[:, :, :], r=[xt.res])
        else:
            junk, ss, rs = C["junk"], C["ss"], C["rs"]
            gf = C["gfin"]
            for j in range(4):
                kb.op("act", lambda h: h.activation(out=junk[:, :], in_=xt[:, j, :], func=AF.Square,
                                                    accum_out=ss[:, j:j + 1]), r=[xt.res], w=[junk.res, ss.res])
            kb.op("act", lambda h: h.activation(out=rs[:, 0:4], in_=ss[:, 0:4], func=AF.Sqrt, scale=1.0 / D, bias=EPS),
                  r=[ss.res], w=[rs.res])
            kb.op("dve", lambda h: h.reciprocal(out=rs[:, 0:4], in_=rs[:, 0:4]), r=[rs.res], w=[rs.res])
            for j in range(4):
                kb.op("dve", lambda h: h.scalar_tensor_tensor(out=xt[:, j, :], in0=xt[:, j, :], scalar=rs[:, j:j + 1],
                                                              in1=gf[:, :], op0=ALU.mult, op1=ALU.mult),
                      r=[xt.res, rs.res, gf.res], w=[xt.res])
            kb.dma("act", C["out"][t0:t0 + TS, :].rearrange("(j p) d -> p j d", p=128), xt[:, :, :], r=[xt.res])


def build(S, depth, phases=("p1", "mix", "p3"), debug=False, ext_in=(), dbg_out=()):
    P = Prog(S, depth, debug)
    P.ext_in = ext_in
    P.dbg_out = dbg_out
    nc, kb, cx = P.nc, P.kb, P.cx
    C = {}
    C["x"] = P.din("x", (S, D))
    C["positions"] = P.din("positions", (1, S), I32)
    C["ident_in"] = P.din("ident", (128, 128), BF16)
    for n, shp in PARAM_SHAPES.items():
        C["in_" + n] = P.din(n, shp)
    C["out"] = P.dscr("out", (S, D), F32, out=True)
    C["xres"] = P.dscr("xres", (S, D), F32)
    C["w_in_b"] = P.dscr("w_in_bf", (depth, D, N_IN), BF16)
    C["proj_b"] = P.dscr("projs_b", (depth, D, D), BF16)
    C["w_out_b"] = P.dscr("w_out_b", (depth, D, D), BF16)
    C["up_b"] = P.dscr("up_b", (depth, D, 2 * D_FF), BF16)
    C["down_b"] = P.dscr("down_b", (depth, D_FF, D), BF16)
    C["zT"] = P.dscr("zT", (ZROWS, S), BF16)
    C["vatt"] = P.dscr("vatt", (S, 768), BF16)
    C["u5"] = P.dscr("u5", (S, 512), BF16)
    C["gT"] = P.dscr("gT", (3 * D, S), BF16)
    C["yT"] = P.dscr("yT", (D, S), BF16)

    wres = [Res("wconv%d" % l) for l in range(depth)]
    C["wres"] = wres

    def conv(l, dst, src, rows, step=128):
        for r0 in range(0, rows, step):
            r1 = min(rows, r0 + step)
            kb.dma("pool", dst(r0, r1), src(r0, r1), sres=wres[l])

    for l in range(depth):
        conv(l, lambda a, b: C["w_in_b"][l, a:b, :], lambda a, b: C["in_w_in"][l, a:b, :], D)
        conv(l, lambda a, b: C["proj_b"][l, a:b, :], lambda a, b: C["in_proj_a"][l, a:b, :], 768)
        conv(l, lambda a, b: C["proj_b"][l, 768 + a:768 + b, :], lambda a, b: C["in_proj_b"][l, a:b, :], 768)
        conv(l, lambda a, b: C["proj_b"][l, 1536 + a:1536 + b, :], lambda a, b: C["in_proj_c"][l, a:b, :], 512)
        conv(l, lambda a, b: C["w_out_b"][l, a:b, :], lambda a, b: C["in_w_out"][l, a:b, :], D)
        conv(l, lambda a, b: C["up_b"][l, a:b, :], lambda a, b: C["in_ffn_up"][l, a:b, :], D)
        conv(l, lambda a, b: C["down_b"][l, a:b, :], lambda a, b: C["in_ffn_down"][l, a:b, :], D_FF)
        wres[l].w = [(wres[l].ds, kb.sems[wres[l].ds].cnt)]

    C["ident"] = cx.sb([128, 128], BF16, "ident")
    kb.dma("sp", C["ident"][:, :], C["ident_in"][:, :], w=[C["ident"].res])
    L = depth
    gm = cx.sb([128, L, 16], F32, "gmix")
    gf = cx.sb([128, L, 16], F32, "gffn")
    bg = cx.sb([128, L, 48], F32, "bgate")
    cw = cx.sb([128, L, 3, 44], F32, "convw")
    kb.dma("sp", gm[:, :, :], C["in_norm_mix"][0:L, :].rearrange("l (c p) -> p l c", p=128), w=[gm.res], slow=True)
    kb.dma("sp", gf[:, :, :], C["in_norm_ffn"][0:L, :].rearrange("l (c p) -> p l c", p=128), w=[gf.res], slow=True)
    kb.dma("sp", bg[:, :, :], C["in_b_gate"][0:L, :].rearrange("l (c p) -> p l c", p=128), w=[bg.res], slow=True)
    for l in range(L):
        kb.dma("sp", cw[:, l, :, :], C["in_ffn_conv"][l, :, :].rearrange("j (c p) -> p j c", p=128), w=[cw.res], slow=True)

    class V:
        def __init__(self, tl, f):
            self.tl, self.f, self.res = tl, f, tl.res

        def __getitem__(self, idx):
            return self.f(self.tl)[idx]

    C["gmix"] = [V(gm, lambda t, l=l: t[:, l, :]) for l in range(L)]
    C["gffn"] = [V(gf, lambda t, l=l: t[:, l, :]) for l in range(L)]
    C["bgate"] = [V(bg, lambda t, l=l: t[:, l, :]) for l in range(L)]
    C["convw"] = [V(cw, lambda t, l=l: t[:, l, :, :]) for l in range(L)]
    def alloc_dense():
        C["xt"] = cx.sb([128, 4, D], F32, "xt")
        C["hT"] = cx.sb([128, 16, TS], BF16, "hT")
        C["xn"] = cx.sb([128, 4, D], BF16, "xn")
        C["junk"] = cx.sb([128, D], BF16, "junk")
        C["ss"] = cx.sb([128, 4], F32, "ss")
        C["rs"] = cx.sb([128, 4], F32, "rs")
        C["wblk"] = [cx.sb([128, 16, 512], BF16, "wblk") for _ in range(2)]

    C["rw_masks"] = P.din("rw_masks", (3, 128, 128))
    C["bones"] = P.din("bones", (128, 128), BF16)
    C["chunkmask"] = P.din("chunkmask", (128, 256))
    C["s5mask"] = P.din("s5mask", (128, 2, 256))
    C["identf"] = P.din("identf", (128, 128))
    C["invf"] = P.din("invf", (64, 1))
    C["sgn"] = P.din("sgn", (64, 1))
    C["ropeC"] = P.dscr("ropeC", (64, S), BF16)
    C["ropeS"] = P.dscr("ropeS", (64, S), BF16)
    dn = Dense(P)
    C["dn"] = dn

    if "zero" in phases:
        zt = cx.sb([128, 2048], BF16, "zeros")
        kb.op("pool", lambda h: h.memset(zt[:, :], 0.0), w=[zt.res])
        for r0 in list(range(0, 768, 128)) + list(range(1536, 2048, 128)):
            for t0 in range(0, S, 2048):
                n = min(2048, S - t0)
                kb.dma("sp", C["yT"][r0:r0 + 128, t0:t0 + n], zt[:, 0:n], r=[zt.res])
        kb.barrier()
    for l in range(depth):
        last = (l == depth - 1)
        kb._wait("sp", wres[l].w)
        if "p1" in phases:
            with ExitStack() as stk:
                cx.stack = stk
                alloc_dense()
                C["stage"] = [cx.sb([128, TS], BF16, "stage") for _ in range(4)]
                C["stage_tm"] = [cx.sb([128, 4, 512], BF16, "stage_tm") for _ in range(2)]
                phase1(P, dn, l, C)
                kb.barrier()
                cx.stack = None
        if "rope" in phases and l == 0:
            with ExitStack() as stk:
                cx.stack = stk
                rope_tables(P, C)
                kb.barrier()
                cx.stack = None
        if "rwkv" in phases:
            with ExitStack() as stk:
                cx.stack = stk
                rwkv_mixer(P, dn, l, C)
                kb.barrier()
                cx.stack = None
        if "s5" in phases:
            with ExitStack() as stk:
                cx.stack = stk
                s5_mixer(P, dn, l, C)
                kb.barrier()
                cx.stack = None
        if "att" in phases:
            with ExitStack() as stk:
                cx.stack = stk
                attention(P, dn, l, C)
                kb.barrier()
                cx.stack = None
        if "p3" in phases:
            with ExitStack() as stk:
                cx.stack = stk
                alloc_dense()
                big = cx.sb([128, 44 * TS], BF16, "big")
                C["yTt"] = V(big, lambda t: t[:, 0:16 * TS].rearrange("p (k t) -> p k t", k=16))
                C["mT"] = V(big, lambda t: t[:, 16 * TS:32 * TS].rearrange("p (k t) -> p k t", k=16))
                C["aT"] = V(big, lambda t: t[:, :].rearrange("p (k t) -> p k t", k=44))
                C["gt"] = [cx.sb([128, 3, TS], BF16, "gt") for _ in range(2)]
                C["macc"] = cx.sb([128, TS], F32, "macc")
                C["mtmp"] = cx.sb([128, TS], F32, "mtmp")
                C["G"] = cx.sb([128, TS + 2], F32, "G")
                C["cv"] = cx.sb([128, TS], F32, "cv")
                C["ct"] = cx.sb([128, TS], F32, "ct")
                C["carry"] = cx.sb([128, 44, 2], F32, "carry")
                C["wdn"] = [cx.sb([128, 11, 512], BF16, "wdn") for _ in range(2)]
                kb.op("pool", lambda h: h.memset(C["carry"][:, :, :], 0.0), w=[C["carry"].res])
                if last:
                    C["gfin"] = cx.sb([128, D], F32, "gfin")
                    kb.dma("sp", C["gfin"][:, :], C["in_norm_final"].partition_broadcast(128), w=[C["gfin"].res])
                phase3(P, dn, l, C, last)
                kb.barrier()
                cx.stack = None
    kb.barrier()
    return P, C


def rope_tables(P, C):
    kb, cx, S = P.kb, P.cx, P.S
    pi = cx.sb([64, S], I32, "pos_i")
    ang = cx.sb([64, S], F32, "ang")
    a2 = cx.sb([64, S], F32, "ang2")
    ni = cx.sb([64, S], I32, "n_i")
    nf = cx.sb([64, S], F32, "n_f")
    ob = cx.sb([64, S], BF16, "rope_o")
    invf = cx.sb([64, 1], F32, "invf")
    sgn = cx.sb([64, 1], F32, "sgn")
    kb.dma("sp", invf[:, :], C["invf"][:, :], w=[invf.res])
    kb.dma("sp", sgn[:, :], C["sgn"][:, :], w=[sgn.res])
    kb.dma("sp", pi[:, :], C["positions"][0].partition_broadcast(64), w=[pi.res])
    kb.op("dve", lambda h: h.tensor_copy(out=ang[:, :], in_=pi[:, :]), r=[pi.res], w=[ang.res])
    kb.op("dve", lambda h: h.tensor_scalar(out=ang[:, :], in0=ang[:, :], scalar1=invf[:, 0:1], scalar2=None, op0=ALU.mult),
          r=[ang.res, invf.res], w=[ang.res])
    TWO_PI = 2.0 * math.pi
    for which in range(2):
        src = ang
        if which == 0:
            kb.op("dve", lambda h: h.tensor_scalar(out=a2[:, :], in0=ang[:, :], scalar1=math.pi / 2, scalar2=None, op0=ALU.add),
                  r=[ang.res], w=[a2.res])
            src = a2
        kb.op("dve", lambda h: h.tensor_scalar(out=ni[:, :], in0=src[:, :], scalar1=1.0 / TWO_PI, scalar2=None, op0=ALU.mult),
              r=[src.res], w=[ni.res])
        kb.op("dve", lambda h: h.tensor_copy(out=nf[:, :], in_=ni[:, :]), r=[ni.res], w=[nf.res])
        kb.op("dve", lambda h: h.scalar_tensor_tensor(out=nf[:, :], in0=nf[:, :], scalar=-TWO_PI, in1=src[:, :],
                                                      op0=ALU.mult, op1=ALU.add), r=[nf.res, src.res], w=[nf.res])
        kb.op("dve", lambda h: h.tensor_scalar(out=nf[:, :], in0=nf[:, :], scalar1=-3.14159, scalar2=3.14159,
                                               op0=ALU.max, op1=ALU.min), r=[nf.res], w=[nf.res])
        kb.op("act", lambda h: h.activation(out=nf[:, :], in_=nf[:, :], func=AF.Sin), r=[nf.res], w=[nf.res])
        if which == 0:
            kb.op("dve", lambda h: h.tensor_copy(out=ob[:, :], in_=nf[:, :]), r=[nf.res], w=[ob.res])
            kb.dma("sp", C["ropeC"][:, :], ob[:, :], r=[ob.res])
        else:
            kb.op("dve", lambda h: h.tensor_scalar(out=ob[:, :], in0=nf[:, :], scalar1=sgn[:, 0:1], scalar2=None, op0=ALU.mult),
                  r=[nf.res, sgn.res], w=[ob.res])
            kb.dma("sp", C["ropeS"][:, :], ob[:, :], r=[ob.res])


def attention(P, dn, l, C):
    kb, cx, S = P.kb, P.cx, P.S
    NB = S // 128
    lam_init = 0.8 - 0.6 * math.exp(-0.3 * l)
    ident = C["ident"]
    lq = cx.sb([128, 4, 64], F32, "lq")
    for i, n in enumerate(["da_lq1", "da_lk1", "da_lq2", "da_lk2"]):
        kb.dma("sp", lq[:, i, :], C["in_" + n][l].partition_broadcast(128), w=[lq.res])
    pr = cx.sb([128, 2, 64], F32, "lpr")
    e12 = cx.sb([128, 2], F32, "e12")
    nlam = cx.sb([128, 1], F32, "nlam")
    kb.op("dve", lambda h: h.tensor_tensor(out=pr[:, 0, :], in0=lq[:, 0, :], in1=lq[:, 1, :], op=ALU.mult), r=[lq.res], w=[pr.res])
    kb.op("dve", lambda h: h.tensor_tensor(out=pr[:, 1, :], in0=lq[:, 2, :], in1=lq[:, 3, :], op=ALU.mult), r=[lq.res], w=[pr.res])
    kb.op("dve", lambda h: h.tensor_reduce(out=e12[:, :], in_=pr[:, :, :], axis=AX.X, op=ALU.add), r=[pr.res], w=[e12.res])
    kb.op("act", lambda h: h.activation(out=e12[:, :], in_=e12[:, :], func=AF.Exp), r=[e12.res], w=[e12.res])
    kb.op("dve", lambda h: h.tensor_tensor(out=nlam[:, :], in0=e12[:, 1:2], in1=e12[:, 0:1], op=ALU.subtract), r=[e12.res], w=[nlam.res])
    kb.op("dve", lambda h: h.tensor_scalar(out=nlam[:, :], in0=nlam[:, :], scalar1=-lam_init, scalar2=None, op0=ALU.add),
          r=[nlam.res], w=[nlam.res])
    sl = cx.sb([128, 128], F32, "subln")
    kb.dma("sp", sl[:, :], C["in_da_subln"][l].partition_broadcast(128), w=[sl.res])
    kb.op("dve", lambda h: h.tensor_scalar(out=sl[:, :], in0=sl[:, :], scalar1=1.0 - lam_init, scalar2=None, op0=ALU.mult),
          r=[sl.res], w=[sl.res])
    C2 = cx.sb([64, S], BF16, "C2")
    S2 = cx.sb([64, S], BF16, "S2")
    kb.dma("sp", C2[:, :], C["ropeC"][:, :], w=[C2.res])
    kb.dma("sp", S2[:, :], C["ropeS"][:, :], w=[S2.res])
    kr = [cx.sb([64, S], BF16, "kr") for _ in range(2)]
    V1 = cx.sb([128, NB, 129], BF16, "V1")
    kb.op("pool", lambda h: h.memset(V1[:, :, 128:129], 1.0), w=[V1.res])
    CH = min(S, 2048)
    raw = [cx.sb([64, CH], BF16, "raw") for _ in range(2)]
    swp = [cx.sb([64, CH], BF16, "swp") for _ in range(2)]
    tmp = [cx.sb([64, CH], BF16, "rtmp") for _ in range(2)]
    qr = [cx.sb([64, 512], BF16, "qr") for _ in range(2)]
    PT = [cx.sb([128, 512], BF16, "PT") for _ in range(3)]
    o0 = cx.sb([128, 4, 128], F32, "o0")
    att = cx.sb([128, 4, 128], F32, "att")
    rec = cx.sb([128, 4], F32, "rec")
    ssq = cx.sb([128, 4], F32, "ssq")
    jk = cx.sb([128, 128], BF16, "ajunk")
    yb = cx.sb([128, 4, 128], BF16, "yb")
    stg = [cx.sb([128, 512], BF16, "astg") for _ in range(2)]
    OA, OB = dn.psf[4], dn.psf[5]
    ri = 0
    pti = 0
    sti = 0

    def rope(dst_ap, row0, t0, n, ri):
        a, b, t = raw[ri % 2], swp[ri % 2], tmp[ri % 2]
        kb.dma("sp", a[:, 0:n], C["zT"][row0:row0 + 64, t0:t0 + n], w=[a.res])
        kb.dma("sp", b[0:32, 0:n], C["zT"][row0 + 32:row0 + 64, t0:t0 + n], w=[b.res])
        kb.dma("sp", b[32:64, 0:n], C["zT"][row0:row0 + 32, t0:t0 + n], w=[b.res])
        kb.op("dve", lambda h: h.tensor_tensor(out=t[:, 0:n], in0=a[:, 0:n], in1=C2[:, t0:t0 + n], op=ALU.mult),
              r=[a.res, C2.res], w=[t.res])
        kb.op("pool", lambda h: h.tensor_tensor(out=b[:, 0:n], in0=b[:, 0:n], in1=S2[:, t0:t0 + n], op=ALU.mult),
              r=[b.res, S2.res], w=[b.res])
        return t, b

    for hd in range(6):
        kb.dma("sp", V1[:, :, 0:128], C["vatt"][:, hd * 128:(hd + 1) * 128].rearrange("(n p) c -> p n c", p=128), w=[V1.res])
        for c in range(2):
            row0 = 3520 + (hd * 2 + c) * 64
            for t0 in range(0, S, CH):
                t, b = rope(None, row0, t0, CH, ri)
                ri += 1
                kb.op("dve", lambda h: h.tensor_tensor(out=kr[c][:, t0:t0 + CH], in0=t[:, 0:CH], in1=b[:, 0:CH], op=ALU.add),
                      r=[t.res, b.res], w=[kr[c].res])
        for Q in range(S // 512):
            q0 = Q * 512
            for c in range(2):
                row0 = 2752 + (hd * 2 + c) * 64
                t, b = rope(None, row0, q0, 512, ri)
                ri += 1
                q = qr[c]
                kb.op("dve", lambda h: h.tensor_tensor(out=q[:, :], in0=t[:, 0:512], in1=b[:, 0:512], op=ALU.add),
                      r=[t.res, b.res], w=[q.res])
                nkb = Q * 4 + 4
                for kbk in range(nkb):
                    i = kbk - Q * 4
                    j0 = max(i, 0)
                    ps = dn.nps(4)
                    kb.op("pe", lambda h: h.matmul(ps[:, j0 * 128:512], kr[c][:, kbk * 128:(kbk + 1) * 128], q[:, j0 * 128:512],
                                                   start=True, stop=True), r=[kr[c].res, q.res], w=[ps.res])
                    pt = PT[pti % 3]
                    pti += 1
                    kb.op("act", lambda h: h.activation(out=pt[:, j0 * 128:512], in_=ps[:, j0 * 128:512], func=AF.Exp, scale=0.125),
                          r=[ps.res], w=[pt.res])
                    if i >= 0:
                        kb.op("pool", lambda h: h.memset(pt[64:128, i * 128:i * 128 + 64], 0.0), w=[pt.res])
                    for j in range(j0, 4):
                        O = OA if j < 2 else OB
                        first = (kbk == 0 and (j == 0 or j == 2))
                        kb.op("pe", lambda h: h.matmul(O[:, (j % 2) * 129:(j % 2) * 129 + 129], pt[:, j * 128:(j + 1) * 128],
                                                       V1[:, kbk, :], start=first, stop=(kbk == Q * 4 + j),
                                                       skip_group_check=True), r=[pt.res, V1.res], w=[O.res])
                for j in range(4):
                    O = OA if j < 2 else OB
                    b0 = (j % 2) * 129
                    kb.op("dve", lambda h: h.reciprocal(out=rec[:, j:j + 1], in_=O[:, b0 + 128:b0 + 129]), r=[O.res], w=[rec.res])
                    if c == 0:
                        kb.op("dve", lambda h: h.tensor_scalar(out=o0[:, j, :], in0=O[:, b0:b0 + 128], scalar1=rec[:, j:j + 1],
                                                               scalar2=None, op0=ALU.mult), r=[O.res, rec.res], w=[o0.res])
                    else:
                        kb.op("dve", lambda h: h.tensor_tensor(out=rec[:, j:j + 1], in0=rec[:, j:j + 1], in1=nlam[:, 0:1], op=ALU.mult),
                              r=[rec.res, nlam.res], w=[rec.res])
                        kb.op("dve", lambda h: h.scalar_tensor_tensor(out=att[:, j, :], in0=O[:, b0:b0 + 128], scalar=rec[:, j:j + 1],
                                                                      in1=o0[:, j, :], op0=ALU.mult, op1=ALU.add),
                              r=[O.res, rec.res, o0.res], w=[att.res])
            for j in range(4):
                kb.op("act", lambda h: h.activation(out=jk[:, :], in_=att[:, j, :], func=AF.Square, accum_out=ssq[:, j:j + 1]),
                      r=[att.res], w=[jk.res, ssq.res])
            kb.op("act", lambda h: h.activation(out=ssq[:, :], in_=ssq[:, :], func=AF.Sqrt, scale=1.0 / 128, bias=EPS),
                  r=[ssq.res], w=[ssq.res])
            kb.op("dve", lambda h: h.reciprocal(out=ssq[:, :], in_=ssq[:, :]), r=[ssq.res], w=[ssq.res])
            pb = dn.npb()
            for j in range(4):
                kb.op("dve", lambda h: h.scalar_tensor_tensor(out=yb[:, j, :], in0=att[:, j, :], scalar=ssq[:, j:j + 1], in1=sl[:, :],
                                                              op0=ALU.mult, op1=ALU.mult), r=[att.res, ssq.res, sl.res], w=[yb.res])
                kb.op("pe", lambda h: h.transpose(out=pb[:, j * 128:(j + 1) * 128], in_=yb[:, j, :], identity=ident[:, :]),
                      r=[yb.res, ident.res], w=[pb.res])
            sg = stg[sti % 2]
            sti += 1
            copy_evac(kb, kb.ev(), sg[:, :], pb[:, 0:512], r=[pb.res], w=[sg.res])
            kb.dma("act", C["yT"][768 + hd * 128:768 + (hd + 1) * 128, q0:q0 + 512], sg[:, :], r=[sg.res])


def host_consts():
    invf = (10000.0 ** (-np.arange(0, 64, 2, dtype=np.float32) / 64)).astype(np.float32)
    p = np.arange(128)[:, None, None]
    m = np.arange(2)[None, :, None]
    col = np.arange(256)[None, None, :]
    s5mask = ((col // 16) >= ((m * 128 + p) // 16)).astype(np.float32)
    ii = np.arange(128)
    mus = (ii[:, None] < ii[None, :]).astype(np.float32)
    mui = (ii[:, None] <= ii[None, :]).astype(np.float32)
    mls = (ii[None, :] < ii[:, None]).astype(np.float32)
    bones = ((ii[:, None] // 64) == (ii[None, :] // 64)).astype(ml_dtypes.bfloat16)
    cmk = np.tile((np.arange(256) % 64 != 0).astype(np.float32)[None, :], (128, 1))
    return {"rw_masks": np.stack([mus, mui, mls]), "bones": bones, "chunkmask": cmk, "s5mask": s5mask, "identf": np.eye(128, dtype=np.float32), "invf": np.concatenate([invf, invf])[:, None].astype(np.float32).copy(),
            "sgn": np.concatenate([-np.ones(32), np.ones(32)])[:, None].astype(np.float32).copy()}


_CACHE = {}


def kernel(**inputs):
    S = 8192
    nb = 4
    if "prog" not in _CACHE:
        _CACHE["prog"] = build(S, DEPTH, phases=("p1", "rwkv", "s5", "rope", "att", "p3"))
    P, C = _CACHE["prog"]
    consts = host_consts()
    consts["ident"] = np.eye(128, dtype=ml_dtypes.bfloat16)
    in_maps = []
    for b in range(nb):
        m = {"x": np.ascontiguousarray(np.asarray(inputs["x"])[b], dtype=np.float32),
             "positions": np.ascontiguousarray(np.asarray(inputs["positions"])[b][None, :], dtype=np.int32)}
        for n, shp in PARAM_SHAPES.items():
            m[n] = np.ascontiguousarray(np.asarray(inputs[n], dtype=np.float32).reshape(shp))
        m.update(consts)
        in_maps.append(m)
    res = run_bass_kernel_spmd(P.nc, in_maps, core_ids=list(range(nb)))
    return np.stack([np.asarray(r["out"], dtype=np.float32) for r in res.results], axis=0)


def s5_mixer(P, dn, l, C):
    kb, cx, S = P.kb, P.cx, P.S
    ident = C["ident"]
    NCH = S // 16
    M = min(128, NCH)
    NMT = NCH // M
    V = lambda t, f: VW(t, f)
    TT = cx.sb([128, 32, 2, 256], BF16, "TT")
    GG = cx.sb([128, 32, 2, 2, 64], BF16, "GG")
    FFr = cx.sb([64, 32, 256], BF16, "FFr")
    nFFi = cx.sb([64, 32, 256], BF16, "nFFi")
    MU1 = cx.sb([64, 32, 2], F32, "MU1")
    MU2 = cx.sb([64, 32, 2], F32, "MU2")
    gluW = cx.sb([128, 4, 512], BF16, "gluW")
    glub = cx.sb([128, 4], F32, "glub")
    dB = cx.sb([128, 512], F32, "dB")
    kb.dma("pool", gluW[:, :, :], C["in_s5_glu_w"][l].rearrange("(k p) c -> p k c", p=128), w=[gluW.res])
    kb.dma("sp", glub[:, :], C["in_s5_glu_b"][l].rearrange("(c p) -> p c", p=128), w=[glub.res], slow=True)
    kb.dma("sp", dB[:, :], C["in_s5_d"][l].partition_broadcast(128), w=[dB.res])
    with ExitStack() as stk:
        old = cx.stack
        cx.stack = stk
        f32 = lambda shp, n: cx.sb(shp, F32, n)
        lr, li, dt = f32([64, 32], "lr"), f32([64, 32], "li"), f32([64, 32], "dt")
        kb.dma("sp", lr[:, :], C["in_s5_lam_re"][l].rearrange("g n -> n g"), w=[lr.res], slow=True)
        kb.dma("sp", li[:, :], C["in_s5_lam_im"][l].rearrange("g n -> n g"), w=[li.res], slow=True)
        kb.dma("sp", dt[:, :], C["in_s5_log_dt"][l].partition_broadcast(64), w=[dt.res])
        bre, bim = f32([64, 32, 16], "bre"), f32([64, 32, 16], "bim")
        cre, cim = f32([64, 32, 16], "cre"), f32([64, 32, 16], "cim")
        kb.dma("sp", bre[:, :, :], C["in_s5_b_re"][l].rearrange("g n c -> n g c"), w=[bre.res])
        kb.dma("sp", bim[:, :, :], C["in_s5_b_im"][l].rearrange("g n c -> n g c"), w=[bim.res])
        kb.dma("sp", cre[:, :, :], C["in_s5_c_re"][l].rearrange("g c n -> n g c"), w=[cre.res], slow=True)
        kb.dma("sp", cim[:, :, :], C["in_s5_c_im"][l].rearrange("g c n -> n g c"), w=[cim.res], slow=True)
        msk = cx.sb([128, 2, 256], F32, "s5mask")
        kb.dma("sp", msk[:, :, :], C["s5mask"][:, :, :], w=[msk.res])
        idf = cx.sb([64, 64], F32, "identf")
        kb.dma("sp", idf[:, :], C["identf"][0:64, 0:64], w=[idf.res])

        def tt(out, a, b, op, r, w):
            kb.op("dve", lambda h: h.tensor_tensor(out=out, in0=a, in1=b, op=op), r=r, w=w)

        def ts(out, a, s1, s2, op0, op1, r, w):
            if s2 is None:
                kb.op("dve", lambda h: h.tensor_scalar(out=out, in0=a, scalar1=s1, scalar2=None, op0=op0), r=r, w=w)
            else:
                kb.op("dve", lambda h: h.tensor_scalar(out=out, in0=a, scalar1=s1, scalar2=s2, op0=op0, op1=op1), r=r, w=w)

        def cmul(o_r, o_i, ar, ai, br, bi, t1, res_in, res_out):
            r = res_in + [T1.res]
            tt(o_r, ar, br, ALU.mult, res_in, res_out)
            tt(t1, ai, bi, ALU.mult, res_in, [T1.res])
            tt(o_r, o_r, t1, ALU.subtract, res_out + [T1.res], res_out)
            tt(o_i, ar, bi, ALU.mult, res_in, res_out)
            tt(t1, ai, br, ALU.mult, res_in, [T1.res])
            tt(o_i, o_i, t1, ALU.add, res_out + [T1.res], res_out)

        T1 = f32([64, 32 * 16], "T1")
        kb.op("act", lambda h: h.activation(out=dt[:, :], in_=dt[:, :], func=AF.Exp), r=[dt.res], w=[dt.res])
        aa, th, er = f32([64, 32], "aa"), f32([64, 32], "th"), f32([64, 32], "er")
        tt(aa[:, :], lr[:, :], dt[:, :], ALU.mult, [lr.res, dt.res], [aa.res])
        tt(th[:, :], li[:, :], dt[:, :], ALU.mult, [li.res, dt.res], [th.res])
        kb.op("act", lambda h: h.activation(out=er[:, :], in_=aa[:, :], func=AF.Exp, scale=1.0 / 16), r=[aa.res], w=[er.res])
        zr, zi = f32([64, 32], "zr"), f32([64, 32], "zi")
        ts(zr[:, :], th[:, :], 1.0 / 16, math.pi / 2, ALU.mult, ALU.add, [th.res], [zr.res])
        kb.op("act", lambda h: h.activation(out=zr[:, :], in_=zr[:, :], func=AF.Sin), r=[zr.res], w=[zr.res])
        kb.op("act", lambda h: h.activation(out=zi[:, :], in_=th[:, :], func=AF.Sin, scale=1.0 / 16), r=[th.res], w=[zi.res])
        tt(zr[:, :], zr[:, :], er[:, :], ALU.mult, [zr.res, er.res], [zr.res])
        tt(zi[:, :], zi[:, :], er[:, :], ALU.mult, [zi.res, er.res], [zi.res])
        z2r, z2i = f32([64, 32], "z2r"), f32([64, 32], "z2i")
        cur = (zr, zi)
        nxt = (z2r, z2i)
        for _ in range(4):
            cmul(nxt[0][:, :], nxt[1][:, :], cur[0][:, :], cur[1][:, :], cur[0][:, :], cur[1][:, :], T1[:, 0:32],
                 [cur[0].res, cur[1].res], [nxt[0].res, nxt[1].res])
            cur, nxt = nxt, cur
        abr, abi = cur
        den, nr, fre, fim = f32([64, 32], "den"), f32([64, 32], "nr"), f32([64, 32], "fre"), f32([64, 32], "fim")
        tt(den[:, :], lr[:, :], lr[:, :], ALU.mult, [lr.res], [den.res])
        tt(T1[:, 0:32], li[:, :], li[:, :], ALU.mult, [li.res], [T1.res])
        tt(den[:, :], den[:, :], T1[:, 0:32], ALU.add, [den.res, T1.res], [den.res])
        kb.op("dve", lambda h: h.reciprocal(out=den[:, :], in_=den[:, :]), r=[den.res], w=[den.res])
        ts(nr[:, :], abr[:, :], -1.0, None, ALU.add, None, [abr.res], [nr.res])
        tt(fre[:, :], nr[:, :], lr[:, :], ALU.mult, [nr.res, lr.res], [fre.res])
        tt(T1[:, 0:32], abi[:, :], li[:, :], ALU.mult, [abi.res, li.res], [T1.res])
        tt(fre[:, :], fre[:, :], T1[:, 0:32], ALU.add, [fre.res, T1.res], [fre.res])
        tt(fre[:, :], fre[:, :], den[:, :], ALU.mult, [fre.res, den.res], [fre.res])
        tt(fim[:, :], abi[:, :], lr[:, :], ALU.mult, [abi.res, lr.res], [fim.res])
        tt(T1[:, 0:32], nr[:, :], li[:, :], ALU.mult, [nr.res, li.res], [T1.res])
        tt(fim[:, :], fim[:, :], T1[:, 0:32], ALU.subtract, [fim.res, T1.res], [fim.res])
        tt(fim[:, :], fim[:, :], den[:, :], ALU.mult, [fim.res, den.res], [fim.res])
        bbr, bbi = f32([64, 32, 16], "bbr"), f32([64, 32, 16], "bbi")
        bc = lambda t: t[:, :].unsqueeze(2).to_broadcast([64, 32, 16])
        T3 = T1[:, :].rearrange("p (g c) -> p g c", c=16)
        cmul(bbr[:, :, :], bbi[:, :, :], bc(fre), bc(fim), bre[:, :, :], bim[:, :, :], T3,
             [fre.res, fim.res, bre.res, bim.res], [bbr.res, bbi.res])
        pwr, pwi = f32([64, 17, 32], "pwr"), f32([64, 17, 32], "pwi")
        npr, npi = f32([64, 16, 32], "npr"), f32([64, 16, 32], "npi")
        kb.op("dve", lambda h: h.memset(pwr[:, 0, :], 1.0), w=[pwr.res])
        kb.op("dve", lambda h: h.memset(pwi[:, 0, :], 0.0), w=[pwi.res])
        kb.op("dve", lambda h: h.memset(npr[:, 0, :], 1.0), w=[npr.res])
        kb.op("dve", lambda h: h.memset(npi[:, 0, :], 0.0), w=[npi.res])
        for t in range(16):
            cmul(pwr[:, t + 1, :], pwi[:, t + 1, :], pwr[:, t, :], pwi[:, t, :], abr[:, :], abi[:, :], T1[:, 0:32],
                 [pwr.res, pwi.res, abr.res, abi.res], [pwr.res, pwi.res])
        ivr, ivi = f32([64, 32], "ivr"), f32([64, 32], "ivi")
        tt(ivr[:, :], abr[:, :], abr[:, :], ALU.mult, [abr.res], [ivr.res])
        tt(T1[:, 0:32], abi[:, :], abi[:, :], ALU.mult, [abi.res], [T1.res])
        tt(ivr[:, :], ivr[:, :], T1[:, 0:32], ALU.add, [ivr.res, T1.res], [ivr.res])
        kb.op("dve", lambda h: h.reciprocal(out=ivr[:, :], in_=ivr[:, :]), r=[ivr.res], w=[ivr.res])
        tt(ivi[:, :], abi[:, :], ivr[:, :], ALU.mult, [abi.res, ivr.res], [ivi.res])
        ts(ivi[:, :], ivi[:, :], -1.0, None, ALU.mult, None, [ivi.res], [ivi.res])
        tt(ivr[:, :], abr[:, :], ivr[:, :], ALU.mult, [abr.res, ivr.res], [ivr.res])
        for t in range(15):
            cmul(npr[:, t + 1, :], npi[:, t + 1, :], npr[:, t, :], npi[:, t, :], ivr[:, :], ivi[:, :], T1[:, 0:32],
                 [npr.res, npi.res, ivr.res, ivi.res], [npr.res, npi.res])
        kb.op("dve", lambda h: h.tensor_copy(out=MU1[:, :, 0], in_=pwr[:, 16, :]), r=[pwr.res], w=[MU1.res])
        kb.op("dve", lambda h: h.tensor_copy(out=MU1[:, :, 1], in_=pwr[:, 16, :]), r=[pwr.res], w=[MU1.res])
        kb.op("dve", lambda h: h.tensor_copy(out=MU2[:, :, 0], in_=pwi[:, 16, :]), r=[pwi.res], w=[MU2.res])
        kb.op("dve", lambda h: h.tensor_copy(out=MU2[:, :, 1], in_=pwi[:, 16, :]), r=[pwi.res], w=[MU2.res])
        GBr, GBi = f32([64, 16, 16], "GBr"), f32([64, 16, 16], "GBi")
        CFr, CFi = f32([64, 16, 16], "CFr"), f32([64, 16, 16], "CFi")
        GSr, GSi = f32([64, 16, 16], "GSr"), f32([64, 16, 16], "GSi")
        Fr, Fi = f32([64, 16, 16], "Fr"), f32([64, 16, 16], "Fi")
        T4 = T1[:, 0:256].rearrange("p (s c) -> p s c", c=16)
        rpr, rpi = f32([64, 16, 32], "rpr"), f32([64, 16, 32], "rpi")
        for s_ in range(16):
            kb.op("dve", lambda h: h.tensor_copy(out=rpr[:, s_, :], in_=pwr[:, 15 - s_, :]), r=[pwr.res], w=[rpr.res])
            kb.op("dve", lambda h: h.tensor_copy(out=rpi[:, s_, :], in_=pwi[:, 15 - s_, :]), r=[pwi.res], w=[rpi.res])
        for g in range(32):
            pw_b = lambda t, lo=0: t[:, lo:lo + 16, g].unsqueeze(2).to_broadcast([64, 16, 16])
            v_b = lambda t: t[:, g, :].unsqueeze(1).to_broadcast([64, 16, 16])
            cmul(GBr[:, :, :], GBi[:, :, :], pw_b(npr), pw_b(npi), v_b(bbr), v_b(bbi), T4,
                 [npr.res, npi.res, bbr.res, bbi.res], [GBr.res, GBi.res])
            cmul(CFr[:, :, :], CFi[:, :, :], pw_b(pwr), pw_b(pwi), v_b(cre), v_b(cim), T4,
                 [pwr.res, pwi.res, cre.res, cim.res], [CFr.res, CFi.res])
            ts(CFi[:, :, :], CFi[:, :, :], -1.0, None, ALU.mult, None, [CFi.res], [CFi.res])
            cmul(GSr[:, :, :], GSi[:, :, :], pw_b(rpr), pw_b(rpi), v_b(bbr), v_b(bbi), T4,
                 [rpr.res, rpi.res, bbr.res, bbi.res], [GSr.res, GSi.res])
            cmul(Fr[:, :, :], Fi[:, :, :], pw_b(pwr, 1), pw_b(pwi, 1), v_b(cre), v_b(cim), T4,
                 [pwr.res, pwi.res, cre.res, cim.res], [Fr.res, Fi.res])
            kb.op("act", lambda h: h.activation(out=FFr[:, g, :], in_=Fr[:, :, :].rearrange("p t c -> p (t c)"), func=AF.Identity),
                  r=[Fr.res], w=[FFr.res])
            kb.op("act", lambda h: h.activation(out=nFFi[:, g, :], in_=Fi[:, :, :].rearrange("p t c -> p (t c)"), func=AF.Identity,
                                                scale=-1.0), r=[Fi.res], w=[nFFi.res])
            flat = lambda t: t[:, :, :].rearrange("p s c -> p (s c)")
            for m in range(2):
                ps = dn.nps()
                kb.op("pe", lambda h: h.matmul(ps[:, 0:256], flat(GBr)[:, m * 128:(m + 1) * 128], flat(CFr), start=True, stop=False),
                      r=[GBr.res, CFr.res], w=[ps.res])
                kb.op("pe", lambda h: h.matmul(ps[:, 0:256], flat(GBi)[:, m * 128:(m + 1) * 128], flat(CFi), start=False, stop=True),
                      r=[GBi.res, CFi.res], w=[ps.res])
                kb.op("dve", lambda h: h.tensor_tensor(out=TT[:, g, m, :], in0=ps[:, 0:256], in1=msk[:, m, :], op=ALU.mult),
                      r=[ps.res, msk.res], w=[TT.res])
            ps = dn.nps()
            for m in range(2):
                for ri_, src in enumerate((GSr, GSi)):
                    k_ = m * 2 + ri_
                    kb.op("pe", lambda h: h.transpose(out=ps[:, k_ * 64:(k_ + 1) * 64], in_=flat(src)[:, m * 128:(m + 1) * 128],
                                                      identity=idf[:, :]), r=[src.res, idf.res], w=[ps.res])
            kb.op("act", lambda h: h.activation(out=GG[:, g, :, :, :].rearrange("p m r n -> p (m r n)"), in_=ps[:, 0:256],
                                                func=AF.Identity), r=[ps.res], w=[GG.res])
        kb.barrier()
        cx.stack = old
    UY = cx.sb([128, 16 * 512], BF16, "UY")
    Ucm = VW(UY, lambda t: t[:, :].rearrange("p (s c) -> p s c", c=512))
    YT = VW(UY, lambda t: t[:, :].rearrange("p (k m s) -> p k m s", k=4, s=16))
    UT = cx.sb([128, 32, 2, 128], BF16, "UT")
    Eall = cx.sb([64, 32, 2, 129], F32, "Eall")
    Xb = cx.sb([64, 32, 2, 128], BF16, "Xb")
    Ycm = cx.sb([128, 16, 512], BF16, "Ycm")
    P1 = cx.sb([64, 32, 2], F32, "P1")
    P2 = cx.sb([64, 32, 2], F32, "P2")
    gtmp = cx.sb([128, 2048], F32, "gtmp")
    gx = cx.sb([128, 2048], F32, "gx")
    sg5 = [cx.sb([128, 512], BF16, "s5stg") for _ in range(2)]
    sgt = cx.sb([128, 512], F32, "s5sig")
    kb.op("dve", lambda h: h.memset(Eall[:, :, :, 0:1], 0.0), w=[Eall.res])
    si = 0
    for mt in range(NMT):
        tok0 = mt * M * 16
        kb.dma("sp", Ucm[0:M, :, :], C["u5"][tok0:tok0 + M * 16, :].rearrange("(m s) c -> m s c", s=16), w=[UY.res])
        Ugm = VW(Ycm, lambda t: t[:, :, :].rearrange("p s c -> p (s c)").rearrange("p (g s c) -> p g s c", g=32, s=16))
        kb.op("pool", lambda h: h.tensor_copy(out=Ugm[0:M, :, :, :], in_=Ucm[0:M, :, :].rearrange("m s (g c) -> m g s c", c=16)),
              r=[UY.res], w=[Ycm.res])
        for g0 in range(0, 32, 4):
            pb = dn.npb()
            for gg in range(4):
                for hh in range(2):
                    k_ = gg * 2 + hh
                    kb.op("pe", lambda h: h.transpose(out=pb[:, k_ * 128:k_ * 128 + M],
                                                      in_=Ugm[0:M, g0 + gg, hh * 8:(hh + 1) * 8, :].rearrange("m s c -> m (s c)"),
                                                      identity=ident[0:M, 0:M]), r=[Ycm.res, ident.res], w=[pb.res])
            copy_evac(kb, kb.ev(), UT[:, g0:g0 + 4, :, 0:M],
                      pb[:, :].rearrange("p (g h m) -> p g h m", g=4, h=2)[:, :, :, 0:M], r=[pb.res], w=[UT.res])
        for g0 in range(0, 32, 2):
            ps = dn.nps()
            for gg in range(2):
                g = g0 + gg
                for ri_ in range(2):
                    k_ = gg * 2 + ri_
                    for m in range(2):
                        kb.op("pe", lambda h: h.matmul(ps[0:64, k_ * 128:k_ * 128 + M], GG[:, g, m, ri_, :], UT[:, g, m, 0:M],
                                                       start=(k_ == 0 and m == 0), stop=(m == 1), skip_group_check=True),
                              r=[GG.res, UT.res], w=[ps.res])
            copy_evac(kb, kb.ev(), Eall[:, g0:g0 + 2, :, 1:1 + M],
                      ps[0:64, :].rearrange("p (g r m) -> p g r m", g=2, r=2)[:, :, :, 0:M], r=[ps.res], w=[Eall.res])
        for m in range(M):
            Xm = Eall[:, :, :, m]
            Xn = Eall[:, :, :, m + 1]
            kb.op("dve", lambda h: h.tensor_tensor(out=P1[:, :, :], in0=Xm, in1=MU1[:, :, :], op=ALU.mult),
                  r=[Eall.res, MU1.res], w=[P1.res])
            kb.op("pool", lambda h: h.tensor_tensor(out=P2[:, :, :], in0=Xm, in1=MU2[:, :, :], op=ALU.mult),
                  r=[Eall.res, MU2.res], w=[P2.res])
            kb.op("dve", lambda h: h.tensor_tensor(out=Xn, in0=Xn, in1=P1[:, :, :], op=ALU.add), r=[Eall.res, P1.res], w=[Eall.res])
            kb.op("dve", lambda h: h.tensor_tensor(out=Eall[:, :, 0, m + 1], in0=Eall[:, :, 0, m + 1], in1=P2[:, :, 1], op=ALU.subtract),
                  r=[Eall.res, P2.res], w=[Eall.res])
            kb.op("dve", lambda h: h.tensor_tensor(out=Eall[:, :, 1, m + 1], in0=Eall[:, :, 1, m + 1], in1=P2[:, :, 0], op=ALU.add),
                  r=[Eall.res, P2.res], w=[Eall.res])
        kb.op("act", lambda h: h.activation(out=Xb[:, :, :, 0:M], in_=Eall[:, :, :, 0:M], func=AF.Identity), r=[Eall.res], w=[Xb.res])
        for g in range(32):
            ps = dn.nps()
            kb.op("pe", lambda h: h.matmul(ps[0:M, 0:256], UT[:, g, 0, 0:M], TT[:, g, 0, :], start=True, stop=False),
                  r=[UT.res, TT.res], w=[ps.res])
            kb.op("pe", lambda h: h.matmul(ps[0:M, 0:256], UT[:, g, 1, 0:M], TT[:, g, 1, :], start=False, stop=False),
                  r=[UT.res, TT.res], w=[ps.res])
            kb.op("pe", lambda h: h.matmul(ps[0:M, 0:256], Xb[:, g, 0, 0:M], FFr[:, g, :], start=False, stop=False),
                  r=[Xb.res, FFr.res], w=[ps.res])
            kb.op("pe", lambda h: h.matmul(ps[0:M, 0:256], Xb[:, g, 1, 0:M], nFFi[:, g, :], start=False, stop=True),
                  r=[Xb.res, nFFi.res], w=[ps.res])
            copy_evac(kb, kb.ev(), Ycm[0:M, :, g * 16:(g + 1) * 16], ps[0:M, 0:256].rearrange("m (t c) -> m t c", c=16),
                      r=[ps.res], w=[Ycm.res])
        kb.op("dve", lambda h: h.tensor_copy(out=Eall[:, :, :, 0], in_=Eall[:, :, :, M]), r=[Eall.res], w=[Eall.res])
        for s0 in range(0, 16, 4):
            gxv = gx[:, :].rearrange("p (s c) -> p s c", c=512)
            kb.op("dve", lambda h: h.tensor_tensor(out=gxv[0:M], in0=Ucm[0:M, s0:s0 + 4, :],
                                                   in1=dB[0:M, :].unsqueeze(1).to_broadcast([M, 4, 512]), op=ALU.mult),
                  r=[UY.res, dB.res], w=[gx.res])
            kb.op("dve", lambda h: h.tensor_tensor(out=gxv[0:M], in0=gxv[0:M], in1=Ycm[0:M, s0:s0 + 4, :], op=ALU.add),
                  r=[gx.res, Ycm.res], w=[gx.res])
            gelu_tanh(kb, Ycm[0:M, s0:s0 + 4, :].rearrange("m s c -> m (s c)"), gx[0:M, :], gtmp[0:M, :], [gx.res], Ycm.res, gtmp.res)
        for s0 in range(0, 16, 2):
            pb = dn.npb()
            for ss_ in range(2):
                for cc in range(4):
                    k_ = ss_ * 4 + cc
                    kb.op("pe", lambda h: h.transpose(out=pb[:, k_ * 128:k_ * 128 + M], in_=Ycm[0:M, s0 + ss_, cc * 128:(cc + 1) * 128],
                                                      identity=ident[0:M, 0:M]), r=[Ycm.res, ident.res], w=[pb.res])
            copy_evac(kb, kb.ev(), YT[:, :, 0:M, s0:s0 + 2].rearrange("p k m s -> p s k m"),
                      pb[:, :].rearrange("p (s k m) -> p s k m", s=2, k=4)[:, :, :, 0:M], r=[pb.res], w=[UY.res])
        ntok = M * 16
        for tb in range(0, ntok, 512):
            nn = min(512, ntok - tb)
            for co in range(4):
                ps = dn.nps()
                for k in range(4):
                    kb.op("pe", lambda h: h.matmul(ps[:, 0:nn], gluW[:, k, co * 128:(co + 1) * 128],
                                                   YT[:, k, tb // 16:(tb + nn) // 16, :].rearrange("p m s -> p (m s)"),
                                                   start=(k == 0), stop=(k == 3)), r=[gluW.res, UY.res], w=[ps.res])
                kb.op("act", lambda h: h.activation(out=sgt[:, 0:nn], in_=ps[:, 0:nn], func=AF.Sigmoid, bias=glub[:, co:co + 1]),
                      r=[ps.res, glub.res], w=[sgt.res])
                sg = sg5[si % 2]
                si += 1
                kb.op("dve", lambda h: h.tensor_tensor(out=sg[:, 0:nn], in0=sgt[:, 0:nn],
                                                       in1=YT[:, co, tb // 16:(tb + nn) // 16, :].rearrange("p m s -> p (m s)"), op=ALU.mult),
                      r=[sgt.res, UY.res], w=[sg.res])
                kb.dma("act", C["yT"][1536 + co * 128:1536 + (co + 1) * 128, tok0 + tb:tok0 + tb + nn], sg[:, 0:nn], r=[sg.res])


class VW:
    def __init__(self, tl, f):
        self.tl, self.f, self.res = tl, f, tl.res

    def __getitem__(self, idx):
        return self.f(self.tl)[idx]


TB = 256


def rwkv_mixer(P, dn, l, C):
    kb, cx, S = P.kb, P.cx, P.S
    ident = C["ident"]
    NTB = S // TB
    f32 = lambda shp, n: cx.sb(shp, F32, n)
    b16 = lambda shp, n: cx.sb(shp, BF16, n)

    def tt(e, out, a, b, op, r, w):
        kb.op(e, lambda h: h.tensor_tensor(out=out, in0=a, in1=b, op=op), r=r, w=w)

    def pcol(name, lo, n, nm):
        t = f32([128, n // 128], nm)
        kb.dma("sp", t[:, :], C["in_" + name][l, lo:lo + n].rearrange("(c q) -> q c", q=128), w=[t.res], slow=True)
        return t

    mix_r, mix_k, mix_v = pcol("rw_mix", 0, 768, "mixr"), pcol("rw_mix", 768, 768, "mixk"), pcol("rw_mix", 1536, 768, "mixv")
    mix_g = pcol("rw_mix", 2496, 256, "mixg")
    mix_wa = f32([96, 2], "mixwa")
    kb.dma("sp", mix_wa[:, :], C["in_rw_mix"][l, 2304:2496].rearrange("(c q) -> q c", q=96), w=[mix_wa.res], slow=True)
    w0, a0 = pcol("rw_w0", 0, 768, "w0"), pcol("rw_a0", 0, 768, "a0")
    k_k, k_a = pcol("rw_k_k", 0, 768, "k_k"), pcol("rw_k_a", 0, 768, "k_a")
    r_k, ln_w, ln_b = pcol("rw_r_k", 0, 768, "r_k"), pcol("rw_ln_w", 0, 768, "ln_w"), pcol("rw_ln_b", 0, 768, "ln_b")
    omka = f32([128, 6], "omka")
    kb.op("dve", lambda h: h.tensor_scalar(out=omka[:, :], in0=k_a[:, :], scalar1=-1.0, scalar2=1.0, op0=ALU.mult, op1=ALU.add),
          r=[k_a.res], w=[omka.res])
    w2b, a2b, g2b = b16([96, 768], "w2b"), b16([96, 768], "a2b"), b16([128, 2, 768], "g2b")
    kb.dma("pool", w2b[:, :], C["in_rw_w2"][l], w=[w2b.res])
    kb.dma("pool", a2b[:, :], C["in_rw_a2"][l], w=[a2b.res])
    kb.dma("pool", g2b[:, :, :], C["in_rw_g2"][l].rearrange("(k q) c -> q k c", q=128), w=[g2b.res])
    MUs, MUi, MLs = f32([128, 128], "MUs"), f32([128, 128], "MUi"), f32([128, 128], "MLs")
    kb.dma("sp", MUs[:, :], C["rw_masks"][0], w=[MUs.res])
    kb.dma("sp", MUi[:, :], C["rw_masks"][1], w=[MUi.res])
    kb.dma("sp", MLs[:, :], C["rw_masks"][2], w=[MLs.res])
    MUsi = f32([128, 256], "MUsi")
    kb.dma("sp", MUsi[:, 0:128], C["rw_masks"][0], w=[MUsi.res])
    kb.dma("sp", MUsi[:, 128:256], C["rw_masks"][1], w=[MUsi.res])
    bones = b16([128, 128], "bones")
    kb.dma("sp", bones[:, :], C["bones"][:, :], w=[bones.res])
    cmk = f32([128, TB], "cmk")
    kb.dma("sp", cmk[:, :], C["chunkmask"][:, 0:TB], w=[cmk.res])
    zw = b16([96, 2, TB + 1], "zwa")
    zg = b16([128, 2, TB + 1], "zg")
    th, zab = b16([96, TB], "th"), b16([96, TB], "zab")
    sgg = b16([128, 2, TB], "sgg")
    ltmp = f32([128, 2, TB], "ltmp")
    zin = [b16([128, 3, TB + 1], "zin") for _ in range(2)]
    rs, ks, vs = f32([128, TB], "rs"), f32([128, TB], "ks"), f32([128, TB], "vs")
    dtm = f32([128, TB], "dtm")
    logw, alpha, gg_ = f32([128, TB], "logw"), f32([128, TB], "alpha"), f32([128, TB], "gfm")
    kk, kk2, rinv = f32([128, TB], "kk"), b16([128, TB], "kk2"), f32([128, TB], "rinv")
    kp, bq = f32([128, TB], "kp"), f32([128, TB], "bq")
    cc, ec, enc, ecm, ecc = f32([128, TB], "cc"), f32([128, TB], "ec"), f32([128, TB], "enc"), f32([128, TB], "ecm"), f32([128, TB], "ecc")
    gC = f32([128, 4], "gC")
    rkb = b16([128, TB], "rkb")
    bonus = f32([128, TB], "bonus")
    AQ = b16([128, 4, 256], "AQ")
    bblk, kblk, b2blk, k2blk, vblk = (b16([128, 4, 128], n) for n in ("bblk", "kblk", "b2blk", "k2blk", "vblk"))
    for t in (AQ, bblk, kblk, b2blk, k2blk, vblk):
        kb.op("pool", lambda h: h.memset(t[:, :, :], 0.0), w=[t.res])
    TM = b16([128, 4, 4, 128], "TM")
    Xn = [b16([128, 4, 128], "Xn") for _ in range(2)]
    Nn = [b16([128, 4, 128], "Nn") for _ in range(2)]
    Pm = b16([128, 4, 128], "Pm")
    ArbT, AakT, ArkT, AkV, WT = (b16([128, 4, 128], n) for n in ("ArbT", "AakT", "ArkT", "AkV", "WT"))
    U = f32([128, 4, 128], "U")
    Ytm = f32([128, 4, 128], "Ytm")
    ysq = f32([128, 4, 128], "ysq")
    ynb = b16([128, 4, 128], "ynb")
    st1, st2, st3 = f32([128, 4], "st1"), f32([128, 4], "st2"), f32([128, 4], "st3")
    yfm = f32([128, TB], "yfm")
    SAb = b16([128, 128], "SAb")
    Sst = [f32([128, 128], "Sst") for _ in range(6)]
    Sb = [b16([128, 128], "Sb") for _ in range(6)]
    for p in range(6):
        kb.op("pool", lambda h: h.memset(Sst[p][:, :], 0.0), w=[Sst[p].res])
        kb.op("pool", lambda h: h.memset(Sb[p][:, :], 0.0), w=[Sb[p].res])
    ystg = [b16([128, TB], "ystg") for _ in range(2)]
    identb = ident

    def c3(ap):
        return ap.rearrange("p (c t) -> p c t", t=64)

    def shift(eng, out_ap, zt_ap, mixcol, npart, r, w):
        tt(eng, dtm[0:npart, :], zt_ap[:, 0:TB], zt_ap[:, 1:TB + 1], ALU.subtract, r, [dtm.res])
        kb.op("dve", lambda h: h.scalar_tensor_tensor(out=out_ap, in0=dtm[0:npart, :], scalar=mixcol, in1=zt_ap[:, 1:TB + 1],
                                                      op0=ALU.mult, op1=ALU.add), r=r + [dtm.res], w=w)

    def load_shifted(tile_ap, res, row0, nrows, t0):
        if t0 == 0:
            kb.op("pool", lambda h: h.memset(tile_ap[:, 0:1], 0.0), w=[res])
            kb.dma("sp", tile_ap[:, 1:TB + 1], C["zT"][row0:row0 + nrows, 0:TB], w=[res])
        else:
            kb.dma("sp", tile_ap[:, :], C["zT"][row0:row0 + nrows, t0 - 1:t0 + TB], w=[res])

    si = 0
    zi = 0
    for tb in range(NTB):
        t0 = tb * TB
        load_shifted(zw[:, 0, :], zw.res, 2304, 96, t0)
        load_shifted(zw[:, 1, :], zw.res, 2400, 96, t0)
        load_shifted(zg[:, 0, :], zg.res, 2496, 128, t0)
        load_shifted(zg[:, 1, :], zg.res, 2624, 128, t0)
        shift("dve", ltmp[0:96, 0, :], zw[:, 0, :], mix_wa[:, 0:1], 96, [zw.res, mix_wa.res], [ltmp.res])
        kb.op("act", lambda h: h.activation(out=th[:, :], in_=ltmp[0:96, 0, :], func=AF.Tanh), r=[ltmp.res], w=[th.res])
        shift("dve", ltmp[0:96, 1, :], zw[:, 1, :], mix_wa[:, 1:2], 96, [zw.res, mix_wa.res], [ltmp.res])
        kb.op("act", lambda h: h.activation(out=zab[:, :], in_=ltmp[0:96, 1, :], func=AF.Identity), r=[ltmp.res], w=[zab.res])
        for k in range(2):
            shift("dve", ltmp[:, k, :], zg[:, k, :], mix_g[:, k:k + 1], 128, [zg.res, mix_g.res], [ltmp.res])
            kb.op("act", lambda h: h.activation(out=sgg[:, k, :], in_=ltmp[:, k, :], func=AF.Sigmoid), r=[ltmp.res], w=[sgg.res])
        for p in range(6):
            pc = slice(p * 128, (p + 1) * 128)
            z3 = zin[zi % 2]
            zi += 1
            for i, row0 in enumerate((p * 128, 768 + p * 128, 1536 + p * 128)):
                load_shifted(z3[:, i, :], z3.res, row0, 128, t0)
            shift("pool", rs[:, :], z3[:, 0, :], mix_r[:, p:p + 1], 128, [z3.res, mix_r.res], [rs.res])
            shift("pool", ks[:, :], z3[:, 1, :], mix_k[:, p:p + 1], 128, [z3.res, mix_k.res], [ks.res])
            shift("pool", vs[:, :], z3[:, 2, :], mix_v[:, p:p + 1], 128, [z3.res, mix_v.res], [vs.res])
            ps = dn.nps()
            kb.op("pe", lambda h: h.matmul(ps[:, 0:TB], w2b[:, pc], th[:, :], start=True, stop=True), r=[w2b.res, th.res], w=[ps.res])
            kb.op("act", lambda h: h.activation(out=logw[:, :], in_=ps[:, 0:TB], func=AF.Sigmoid, bias=w0[:, p:p + 1]),
                  r=[ps.res, w0.res], w=[logw.res])
            kb.op("dve", lambda h: h.tensor_scalar(out=logw[:, :], in0=logw[:, :], scalar1=-math.exp(-0.5), scalar2=None, op0=ALU.mult),
                  r=[logw.res], w=[logw.res])
            ps = dn.nps()
            kb.op("pe", lambda h: h.matmul(ps[:, 0:TB], a2b[:, pc], zab[:, :], start=True, stop=True), r=[a2b.res, zab.res], w=[ps.res])
            kb.op("act", lambda h: h.activation(out=alpha[:, :], in_=ps[:, 0:TB], func=AF.Sigmoid, bias=a0[:, p:p + 1]),
                  r=[ps.res, a0.res], w=[alpha.res])
            ps = dn.nps()
            for k in range(2):
                kb.op("pe", lambda h: h.matmul(ps[:, 0:TB], g2b[:, k, pc], sgg[:, k, :], start=(k == 0), stop=(k == 1)),
                      r=[g2b.res, sgg.res], w=[ps.res])
            kb.op("act", lambda h: h.activation(out=gg_[:, :], in_=ps[:, 0:TB], func=AF.Identity), r=[ps.res], w=[gg_.res])
            kb.op("dve", lambda h: h.tensor_scalar(out=kk[:, :], in0=ks[:, :], scalar1=k_k[:, p:p + 1], scalar2=None, op0=ALU.mult),
                  r=[ks.res, k_k.res], w=[kk.res])
            tt("pool", kk2[:, :], kk[:, :], kk[:, :], ALU.mult, [kk.res], [kk2.res])
            ps = dn.nps()
            kb.op("pe", lambda h: h.matmul(ps[:, 0:TB], bones[:, :], kk2[:, :], start=True, stop=True), r=[bones.res, kk2.res], w=[ps.res])
            kb.op("act", lambda h: h.activation(out=rinv[:, :], in_=ps[:, 0:TB], func=AF.Sqrt), r=[ps.res], w=[rinv.res])
            kb.op("dve", lambda h: h.tensor_scalar(out=rinv[:, :], in0=rinv[:, :], scalar1=1e-12, scalar2=None, op0=ALU.max),
                  r=[rinv.res], w=[rinv.res])
            kb.op("dve", lambda h: h.reciprocal(out=rinv[:, :], in_=rinv[:, :]), r=[rinv.res], w=[rinv.res])
            tt("dve", kk[:, :], kk[:, :], rinv[:, :], ALU.mult, [kk.res, rinv.res], [kk.res])
            kb.op("dve", lambda h: h.tensor_scalar(out=kp[:, :], in0=alpha[:, :], scalar1=k_a[:, p:p + 1], scalar2=omka[:, p:p + 1],
                                                   op0=ALU.mult, op1=ALU.add), r=[alpha.res, k_a.res, omka.res], w=[kp.res])
            tt("dve", kp[:, :], kp[:, :], ks[:, :], ALU.mult, [kp.res, ks.res], [kp.res])
            tt("pool", bq[:, :], kk[:, :], alpha[:, :], ALU.mult, [kk.res, alpha.res], [bq.res])
            kb.op("dve", lambda h: h.tensor_tensor_scan(out=cc[:, :], data0=cmk[:, :], data1=logw[:, :], initial=0.0,
                                                        op0=ALU.mult, op1=ALU.add), r=[cmk.res, logw.res], w=[cc.res])
            kb.op("act", lambda h: h.activation(out=ec[:, :], in_=cc[:, :], func=AF.Exp), r=[cc.res], w=[ec.res])
            kb.op("act", lambda h: h.activation(out=enc[:, :], in_=cc[:, :], func=AF.Exp, scale=-1.0), r=[cc.res], w=[enc.res])
            tt("pool", ecm[:, :], cc[:, :], logw[:, :], ALU.subtract, [cc.res, logw.res], [ecm.res])
            kb.op("act", lambda h: h.activation(out=ecm[:, :], in_=ecm[:, :], func=AF.Exp), r=[ecm.res], w=[ecm.res])
            kb.op("act", lambda h: h.activation(out=gC[:, :], in_=c3(cc[:, :])[:, :, 63], func=AF.Exp), r=[cc.res], w=[gC.res])
            tt("dve", c3(ecc[:, :]), c3(cc[:, :])[:, :, 63:64].to_broadcast([128, 4, 64]), c3(cc[:, :]), ALU.subtract,
               [cc.res], [ecc.res])
            kb.op("act", lambda h: h.activation(out=ecc[:, :], in_=ecc[:, :], func=AF.Exp), r=[ecc.res], w=[ecc.res])
            kb.op("dve", lambda h: h.scalar_tensor_tensor(out=rkb[:, :], in0=rs[:, :], scalar=r_k[:, p:p + 1], in1=kp[:, :],
                                                          op0=ALU.mult, op1=ALU.mult), r=[rs.res, r_k.res, kp.res], w=[rkb.res])
            ps = dn.nps()
            kb.op("pe", lambda h: h.matmul(ps[:, 0:TB], bones[:, :], rkb[:, :], start=True, stop=True), r=[bones.res, rkb.res], w=[ps.res])
            tt("dve", bonus[:, :], ps[:, 0:TB], vs[:, :], ALU.mult, [ps.res, vs.res], [bonus.res])
            engs = ["dve", "pool"]
            ei = 0
            for hh in range(2):
                lo = hh * 64
                sl_ = slice(lo, lo + 64)

                def blk(dst, colofs, a, b, ra, rb, neg=False):
                    nonlocal ei
                    e = engs[ei % 2]
                    ei += 1
                    o = dst[sl_, :, colofs + lo:colofs + lo + 64]
                    if neg:
                        kb.op("dve", lambda h: h.scalar_tensor_tensor(out=o, in0=c3(a[sl_, :]), scalar=-1.0, in1=c3(b[sl_, :]),
                                                                      op0=ALU.mult, op1=ALU.mult), r=[ra, rb], w=[dst.res])
                    elif b is None:
                        kb.op(e, lambda h: h.tensor_copy(out=o, in_=c3(a[sl_, :])), r=[ra], w=[dst.res])
                    else:
                        tt(e, o, c3(a[sl_, :]), c3(b[sl_, :]), ALU.mult, [ra, rb], [dst.res])

                blk(AQ, 0, kk, ecm, kk.res, ecm.res, neg=True)
                blk(AQ, 128, rs, ec, rs.res, ec.res)
                blk(bblk, 0, bq, enc, bq.res, enc.res)
                blk(kblk, 0, kp, enc, kp.res, enc.res)
                blk(b2blk, 0, bq, ecc, bq.res, ecc.res)
                blk(k2blk, 0, kp, ecc, kp.res, ecc.res)
                blk(vblk, 0, vs, None, vs.res, None)
            srcs = [(AQ, 0), (b2blk, 0), (k2blk, 0), (vblk, 0)]
            for half in range(2):
                pb = dn.npb()
                for qi in range(2):
                    src, co = srcs[half * 2 + qi]
                    for ch in range(4):
                        k_ = qi * 4 + ch
                        kb.op("pe", lambda h: h.transpose(out=pb[:, k_ * 128:(k_ + 1) * 128], in_=src[:, ch, co:co + 128],
                                                          identity=identb[:, :]), r=[src.res, identb.res], w=[pb.res])
                copy_evac(kb, kb.ev(), TM[:, half * 2:half * 2 + 2, :, :].rearrange("p q c m -> p (q c m)"), pb[:, :],
                          r=[pb.res], w=[TM.res])
            ps = dn.nps()
            for ch in range(4):
                kb.op("pe", lambda h: h.matmul(ps[:, ch * 128:(ch + 1) * 128], AQ[:, ch, 0:128], bblk[:, ch, :],
                                               start=(ch == 0), stop=True, skip_group_check=True), r=[AQ.res, bblk.res], w=[ps.res])
            X0, N0 = Xn[0], Nn[0]
            tt("dve", N0[:, :, :], ps[:, :].rearrange("p (c m) -> p c m", m=128), MLs[:, :].unsqueeze(1).to_broadcast([128, 4, 128]),
               ALU.mult, [ps.res, MLs.res], [N0.res])
            for (lhs, dA, dB_) in ((bblk, X0, ArbT), (kblk, AakT, ArkT)):
                for half in range(2):
                    ps = dn.nps()
                    for c2 in range(2):
                        ch = half * 2 + c2
                        kb.op("pe", lambda h: h.matmul(ps[:, c2 * 256:(c2 + 1) * 256], lhs[:, ch, :], AQ[:, ch, :],
                                                       start=(c2 == 0), stop=True, skip_group_check=True), r=[lhs.res, AQ.res], w=[ps.res])
                    pv = ps[:, :].rearrange("p (c m) -> p c m", m=256)
                    tt("dve", dA[:, half * 2:half * 2 + 2, :], pv[:, :, 0:128], MUs[:, :].unsqueeze(1).to_broadcast([128, 2, 128]),
                       ALU.mult, [ps.res, MUs.res], [dA.res])
                    tt("dve", dB_[:, half * 2:half * 2 + 2, :], pv[:, :, 128:256], MUi[:, :].unsqueeze(1).to_broadcast([128, 2, 128]),
                       ALU.mult, [ps.res, MUi.res], [dB_.res])
            tt("pool", Pm[:, :, :], X0[:, :, :], identb[:, :].unsqueeze(1).to_broadcast([128, 4, 128]), ALU.add,
               [X0.res, identb.res], [Pm.res])
            Xc, Nc = X0, N0
            for j in range(1, 6):
                Xnx, Nnx = Xn[j % 2], Nn[j % 2]
                if j < 5:
                    ps = dn.nps()
                    for ch in range(4):
                        kb.op("pe", lambda h: h.matmul(ps[:, ch * 128:(ch + 1) * 128], Nc[:, ch, :], Xc[:, ch, :],
                                                       start=(ch == 0), stop=True, skip_group_check=True), r=[Nc.res, Xc.res], w=[ps.res])
                    copy_evac(kb, "act", Xnx[:, :, :].rearrange("p c m -> p (c m)"), ps[:, :], r=[ps.res], w=[Xnx.res])
                ps = dn.nps()
                for ch in range(4):
                    kb.op("pe", lambda h: h.matmul(ps[:, ch * 128:(ch + 1) * 128], Xc[:, ch, :], Nc[:, ch, :],
                                                   start=(ch == 0), stop=True, skip_group_check=True), r=[Nc.res, Xc.res], w=[ps.res])
                copy_evac(kb, "dve", Nnx[:, :, :].rearrange("p c m -> p (c m)"), ps[:, :], r=[ps.res], w=[Nnx.res])
                ps = dn.nps()
                for ch in range(4):
                    kb.op("pe", lambda h: h.matmul(ps[:, ch * 128:(ch + 1) * 128], Nnx[:, ch, :], Pm[:, ch, :],
                                                   start=(ch == 0), stop=True, skip_group_check=True), r=[Nnx.res, Pm.res], w=[ps.res])
                tt("dve", Pm[:, :, :].rearrange("p c m -> p (c m)"), ps[:, :], Pm[:, :, :].rearrange("p c m -> p (c m)"), ALU.add,
                   [ps.res, Pm.res], [Pm.res])
                Xc, Nc = Xnx, Nnx
            ps = dn.nps()
            for ch in range(4):
                kb.op("pe", lambda h: h.matmul(ps[:, ch * 128:(ch + 1) * 128], AakT[:, ch, :], TM[:, 3, ch, :],
                                               start=(ch == 0), stop=True, skip_group_check=True), r=[AakT.res, TM.res], w=[ps.res])
            copy_evac(kb, "act", AkV[:, :, :].rearrange("p c m -> p (c m)"), ps[:, :], r=[ps.res], w=[AkV.res])
            ps = dn.nps()
            for ch in range(4):
                kb.op("pe", lambda h: h.matmul(ps[:, ch * 128:(ch + 1) * 128], Pm[:, ch, :], AkV[:, ch, :],
                                               start=(ch == 0), stop=True, skip_group_check=True), r=[Pm.res, AkV.res], w=[ps.res])
            copy_evac(kb, "dve", U[:, :, :].rearrange("p c m -> p (c m)"), ps[:, :], r=[ps.res], w=[U.res])
            ps = dn.nps()
            for ch in range(4):
                kb.op("pe", lambda h: h.matmul(ps[:, ch * 128:(ch + 1) * 128], TM[:, 0, ch, :], Pm[:, ch, :],
                                               start=(ch == 0), stop=True, skip_group_check=True), r=[TM.res, Pm.res], w=[ps.res])
            copy_evac(kb, "act", WT[:, :, :].rearrange("p c m -> p (c m)"), ps[:, :], r=[ps.res], w=[WT.res])
            S_, Sb_ = Sst[p], Sb[p]
            for ch in range(4):
                ps = dn.nps()
                kb.op("pe", lambda h: h.matmul(ps[:, 0:128], WT[:, ch, :], Sb_[:, :], start=True, stop=True), r=[WT.res, Sb_.res], w=[ps.res])
                tt("dve", SAb[:, :], ps[:, 0:128], U[:, ch, :], ALU.add, [ps.res, U.res], [SAb.res])
                psy = dn.nps()
                kb.op("pe", lambda h: h.matmul(psy[:, 0:128], AQ[:, ch, 128:256], Sb_[:, :], start=True, stop=False),
                      r=[AQ.res, Sb_.res], w=[psy.res])
                kb.op("pe", lambda h: h.matmul(psy[:, 0:128], ArbT[:, ch, :], SAb[:, :], start=False, stop=False),
                      r=[ArbT.res, SAb.res], w=[psy.res])
                kb.op("pe", lambda h: h.matmul(psy[:, 0:128], ArkT[:, ch, :], TM[:, 3, ch, :], start=False, stop=True),
                      r=[ArkT.res, TM.res], w=[psy.res])
                copy_evac(kb, "act", Ytm[:, ch, :], psy[:, 0:128], r=[psy.res], w=[Ytm.res])
                pss = dn.nps()
                kb.op("pe", lambda h: h.matmul(pss[:, 0:128], TM[:, 1, ch, :], SAb[:, :], start=True, stop=False),
                      r=[TM.res, SAb.res], w=[pss.res])
                kb.op("pe", lambda h: h.matmul(pss[:, 0:128], TM[:, 2, ch, :], TM[:, 3, ch, :], start=False, stop=True),
                      r=[TM.res], w=[pss.res])
                kb.op("dve", lambda h: h.scalar_tensor_tensor(out=S_[:, :], in0=S_[:, :], scalar=gC[:, ch:ch + 1], in1=pss[:, 0:128],
                                                              op0=ALU.mult, op1=ALU.add), r=[S_.res, gC.res, pss.res], w=[S_.res])
                kb.op("act", lambda h: h.activation(out=Sb_[:, :], in_=S_[:, :], func=AF.Identity), r=[S_.res], w=[Sb_.res])
            kb.op("dve", lambda h: h.tensor_reduce(out=st1[:, :], in_=Ytm[:, :, :], axis=AX.X, op=ALU.add), r=[Ytm.res], w=[st1.res])
            kb.op("act", lambda h: h.activation(out=ysq[:, :, :], in_=Ytm[:, :, :], func=AF.Square), r=[Ytm.res], w=[ysq.res])
            kb.op("dve", lambda h: h.tensor_reduce(out=st2[:, :], in_=ysq[:, :, :], axis=AX.X, op=ALU.add), r=[ysq.res], w=[st2.res])
            kb.op("dve", lambda h: h.tensor_scalar(out=st1[:, :], in0=st1[:, :], scalar1=1.0 / 64, scalar2=None, op0=ALU.mult),
                  r=[st1.res], w=[st1.res])
            tt("dve", st3[:, :], st1[:, :], st1[:, :], ALU.mult, [st1.res], [st3.res])
            kb.op("dve", lambda h: h.scalar_tensor_tensor(out=st2[:, :], in0=st2[:, :], scalar=1.0 / 64, in1=st3[:, :],
                                                          op0=ALU.mult, op1=ALU.subtract), r=[st2.res, st3.res], w=[st2.res])
            kb.op("act", lambda h: h.activation(out=st2[:, :], in_=st2[:, :], func=AF.Sqrt, bias=GN_EPS), r=[st2.res], w=[st2.res])
            kb.op("dve", lambda h: h.reciprocal(out=st2[:, :], in_=st2[:, :]), r=[st2.res], w=[st2.res])
            pb = dn.npb()
            for ch in range(4):
                kb.op("dve", lambda h: h.tensor_scalar(out=ynb[:, ch, :], in0=Ytm[:, ch, :], scalar1=st1[:, ch:ch + 1],
                                                       scalar2=st2[:, ch:ch + 1], op0=ALU.subtract, op1=ALU.mult),
                      r=[Ytm.res, st1.res, st2.res], w=[ynb.res])
                kb.op("pe", lambda h: h.transpose(out=pb[:, ch * 128:(ch + 1) * 128], in_=ynb[:, ch, :], identity=identb[:, :]),
                      r=[ynb.res, identb.res], w=[pb.res])
            pbv = pb[:, 0:512].rearrange("p (c m) -> p c m", m=128)
            kb.op("act", lambda h: h.activation(out=c3(yfm[0:64, :]), in_=pbv[0:64, :, 0:64], func=AF.Identity), r=[pb.res], w=[yfm.res])
            kb.op("act", lambda h: h.activation(out=c3(yfm[64:128, :]), in_=pbv[64:128, :, 64:128], func=AF.Identity), r=[pb.res], w=[yfm.res])
            kb.op("dve", lambda h: h.tensor_scalar(out=yfm[:, :], in0=yfm[:, :], scalar1=ln_w[:, p:p + 1], scalar2=ln_b[:, p:p + 1],
                                                   op0=ALU.mult, op1=ALU.add), r=[yfm.res, ln_w.res, ln_b.res], w=[yfm.res])
            tt("pool", yfm[:, :], yfm[:, :], bonus[:, :], ALU.add, [yfm.res, bonus.res], [yfm.res])
            sg = ystg[si % 2]
            si += 1
            tt("dve", sg[:, :], yfm[:, :], gg_[:, :], ALU.mult, [yfm.res, gg_.res], [sg.res])
            kb.dma("act", C["yT"][p * 128:(p + 1) * 128, t0:t0 + TB], sg[:, :], r=[sg.res])
```

```python
import math
from contextlib import ExitStack
import numpy as np
import ml_dtypes
import concourse.bass as bass
import concourse.mybir as mybir
from concourse.bass_utils import run_bass_kernel_spmd

F32 = mybir.dt.float32
BF16 = mybir.dt.bfloat16
I32 = mybir.dt.int32
AF = mybir.ActivationFunctionType
ALU = mybir.AluOpType
AX = mybir.AxisListType

D = 2048
DEPTH = 4
A_DIM = 768
RW_IN = 2752
B_IN = 2304
C_DIM = 512
N_IN = 11712
D_FF = 5632
EPS = 1e-6
GN_EPS = 64e-5
ZROWS = 4288
TS = 512


class Res:
    __slots__ = ("name", "w", "r", "ds")

    def __init__(self, name):
        self.name = name
        self.w = []
        self.r = []
        self.ds = None


class Sem:
    __slots__ = ("h", "cnt", "dma")

    def __init__(self, h, dma):
        self.h = h
        self.cnt = 0
        self.dma = dma


class KB:
    def __init__(self, nc):
        self.nc = nc
        self.E = {"pe": nc.tensor, "act": nc.scalar, "dve": nc.vector, "pool": nc.gpsimd, "sp": nc.sync}
        self.sems = []
        self.eidx = {}
        for e in self.E:
            self.eidx[e] = len(self.sems)
            self.sems.append(Sem(nc.alloc_semaphore("q_" + e), False))
        self.seen = {e: {} for e in self.E}
        self.rr = 0
        self.nops = 0

    def _wait(self, e, evs):
        need = {}
        for k, v in evs:
            S = self.sems[k]
            if S.dma:
                v = S.cnt
            if v > need.get(k, 0):
                need[k] = v
        own = self.eidx[e]
        for k, v in need.items():
            if k == own and e == "pe":
                continue
            if self.seen[e].get(k, 0) < v:
                self.E[e].wait_ge(self.sems[k].h, v)
                self.seen[e][k] = v

    def _deps(self, r, w):
        evs = []
        for x in r:
            evs += x.w
        for x in w:
            evs += x.w
            evs += x.r
        return evs

    def _mark(self, ev, r, w):
        for x in r:
            x.r = [p for p in x.r if p[0] != ev[0]] + [ev]
        for x in w:
            x.w = [ev]
            x.r = []

    def op(self, e, fn, r=(), w=()):
        self._wait(e, self._deps(r, w))
        ins = fn(self.E[e])
        k = self.eidx[e]
        self.sems[k].cnt += 1
        ins.then_inc(self.sems[k].h, 1)
        self._mark((k, self.sems[k].cnt), r, w)
        self.nops += 1

    NDMA = 64

    def dsem(self, res):
        if res.ds is None:
            if not hasattr(self, "dpool"):
                self.dpool = []
                self.dnext = 0
            if len(self.dpool) < self.NDMA:
                self.dpool.append(len(self.sems))
                self.sems.append(Sem(self.nc.alloc_semaphore("d%d" % len(self.sems)), True))
                res.ds = self.dpool[-1]
            else:
                res.ds = self.dpool[self.dnext % self.NDMA]
                self.dnext += 1
        return res.ds

    def dma(self, e, out, in_, r=(), w=(), sres=None, slow=False):
        self._wait(e, self._deps(r, w))
        if sres is None:
            sres = w[0] if w else r[0]
        k = self.dsem(sres)
        if slow:
            ins = self.E[e].dma_start(out=out, in_=in_, allow_slow_non_contiguous=True)
        else:
            ins = self.E[e].dma_start(out=out, in_=in_)
        self.sems[k].cnt += 16
        ins.then_inc(self.sems[k].h, 16)
        self._mark((k, self.sems[k].cnt), r, w)
        self.nops += 1

    def barrier(self):
        evs = [(k, S.cnt) for k, S in enumerate(self.sems) if S.cnt > 0]
        for e in self.E:
            self._wait(e, evs)

    def ev(self):
        self.rr ^= 1
        return "act" if self.rr else "dve"


class Tl:
    def __init__(self, t, name):
        self.t = t
        self.res = Res(name)

    def __getitem__(self, idx):
        return self.t[idx]


class Ctx:
    def __init__(self, nc, kb):
        self.nc = nc
        self.kb = kb
        self.n = 0
        self.stack = None

    def sb(self, shape, dt, name=None):
        self.n += 1
        name = (name or "t") + "_%d" % self.n
        if self.stack is not None:
            t = self.stack.enter_context(self.nc.sbuf_tensor(name, list(shape), dt))
        else:
            t = self.nc.alloc_sbuf_tensor(name, list(shape), dt)
        return Tl(t, name)

    def ps(self, shape, dt, name=None):
        self.n += 1
        name = (name or "p") + "_%d" % self.n
        if self.stack is not None:
            t = self.stack.enter_context(self.nc.psum_tensor(name, list(shape), dt))
        else:
            t = self.nc.alloc_psum_tensor(name, list(shape), dt)
        return Tl(t, name)


def copy_evac(kb, eng, out_ap, in_ap, r, w, scale=None, bias=None, func=None):
    if eng == "act" or func is not None or scale is not None or bias is not None:
        kw = {}
        if scale is not None:
            kw["scale"] = scale
        if bias is not None:
            kw["bias"] = bias
        f = func if func is not None else AF.Identity
        kb.op("act", lambda h: h.activation(out=out_ap, in_=in_ap, func=f, **kw), r=r, w=w)
    else:
        kb.op(eng, lambda h: h.tensor_copy(out=out_ap, in_=in_ap), r=r, w=w)


class Prog:
    def __init__(self, S, depth, debug=False):
        self.S = S
        self.depth = depth
        self.debug = debug
        self.nc = bass.Bass("TRN2", target_bir_lowering=False)
        self.kb = KB(self.nc)
        self.cx = Ctx(self.nc, self.kb)
        self.dram = {}

    def din(self, name, shape, dt=F32):
        t = self.nc.dram_tensor(name, list(shape), dt, kind="ExternalInput")
        self.dram[name] = t
        return t.ap()

    def dscr(self, name, shape, dt, out=False):
        kind = "ExternalOutput" if (out or name in getattr(self, "dbg_out", ())) else "Internal"
        if name in getattr(self, "ext_in", ()):
            kind = "ExternalInput"
        t = self.nc.dram_tensor(name, list(shape), dt, kind=kind)
        self.dram[name] = t
        return t.ap()


PARAM_SHAPES = {
    "norm_mix": (DEPTH, D), "norm_ffn": (DEPTH, D), "w_in": (DEPTH, D, N_IN), "b_gate": (DEPTH, 3 * D),
    "rw_mix": (DEPTH, RW_IN), "rw_w0": (DEPTH, A_DIM), "rw_w2": (DEPTH, 96, A_DIM), "rw_a0": (DEPTH, A_DIM),
    "rw_a2": (DEPTH, 96, A_DIM), "rw_g2": (DEPTH, 256, A_DIM), "rw_k_k": (DEPTH, A_DIM), "rw_k_a": (DEPTH, A_DIM),
    "rw_r_k": (DEPTH, A_DIM), "rw_ln_w": (DEPTH, A_DIM), "rw_ln_b": (DEPTH, A_DIM),
    "da_lq1": (DEPTH, 64), "da_lk1": (DEPTH, 64), "da_lq2": (DEPTH, 64), "da_lk2": (DEPTH, 64), "da_subln": (DEPTH, 128),
    "s5_lam_re": (DEPTH, 32, 64), "s5_lam_im": (DEPTH, 32, 64), "s5_log_dt": (DEPTH, 32),
    "s5_b_re": (DEPTH, 32, 64, 16), "s5_b_im": (DEPTH, 32, 64, 16), "s5_c_re": (DEPTH, 32, 16, 64), "s5_c_im": (DEPTH, 32, 16, 64),
    "s5_d": (DEPTH, 512), "s5_glu_w": (DEPTH, 512, 512), "s5_glu_b": (DEPTH, 512),
    "proj_a": (DEPTH, A_DIM, D), "proj_b": (DEPTH, A_DIM, D), "proj_c": (DEPTH, C_DIM, D), "w_out": (DEPTH, D, D),
    "ffn_up": (DEPTH, D, 2 * D_FF), "ffn_conv": (DEPTH, 3, D_FF), "ffn_down": (DEPTH, D_FF, D), "norm_final": (D,),
}


def units_p1():
    u = []
    for c in range(0, 2304, 128):
        u.append((c, 128))
    u += [(2304, 96), (2400, 96), (2496, 128), (2624, 128)]
    for c in range(2752, 4288, 128):
        u.append((c, 128))
    return u


def group_blocks(units, maxc=512):
    blocks = []
    cur = []
    for (c, m) in units:
        if cur and (c + m - cur[0][0] > maxc or c != cur[-1][0] + cur[-1][1]):
            blocks.append(cur)
            cur = []
        cur.append((c, m))
    if cur:
        blocks.append(cur)
    return blocks


class Dense:
    def __init__(self, P):
        self.P = P
        cx, kb = P.cx, P.kb
        self.psf = [cx.ps([128, 512], F32, "psf") for _ in range(6)]
        self.psb = [cx.ps([128, 1024], BF16, "psb") for _ in range(2)]
        self.ipf = 0
        self.ipb = 0

    def nps(self, n=None):
        n = n or len(self.psf)
        self.ipf = (self.ipf + 1) % n
        return self.psf[self.ipf]

    def npb(self):
        self.ipb = (self.ipb + 1) % len(self.psb)
        return self.psb[self.ipb]


def rmsnorm_T(P, dn, xt, gcol, hT, junk, ss, rs, xn, ident):
    kb = P.kb
    for j in range(4):
        kb.op("act", lambda h: h.activation(out=junk[:, :], in_=xt[:, j, :], func=AF.Square,
                                            accum_out=ss[:, j:j + 1]), r=[xt.res], w=[junk.res, ss.res])
    kb.op("act", lambda h: h.activation(out=rs[:, 0:4], in_=ss[:, 0:4], func=AF.Sqrt, scale=1.0 / D, bias=EPS),
          r=[ss.res], w=[rs.res])
    kb.op("dve", lambda h: h.reciprocal(out=rs[:, 0:4], in_=rs[:, 0:4]), r=[rs.res], w=[rs.res])
    for j in range(4):
        kb.op("dve", lambda h: h.tensor_scalar(out=xn[:, j, :], in0=xt[:, j, :], scalar1=rs[:, j:j + 1],
                                               scalar2=None, op0=ALU.mult), r=[xt.res, rs.res], w=[xn.res])
    for c in range(16):
        pb = dn.npb()
        for j in range(4):
            kb.op("pe", lambda h: h.transpose(out=pb[:, j * 128:(j + 1) * 128], in_=xn[:, j, c * 128:(c + 1) * 128],
                                              identity=ident[:, :]), r=[xn.res, ident.res], w=[pb.res])
        kb.op("act", lambda h: h.activation(out=hT[:, c, :], in_=pb[:, 0:512], func=AF.Identity,
                                            scale=gcol[:, c:c + 1]), r=[pb.res, gcol.res], w=[hT.res])


def phase1(P, dn, l, C):
    kb, cx, S = P.kb, P.cx, P.S
    x_src = C["x"] if l == 0 else C["xres"]
    gcol, bgcol, ident = C["gmix"], C["bgate"], C["ident"]
    xt = C["xt"]
    hT = C["hT"]
    wb = C["wblk"]
    st = C["stage"]
    stm = C["stage_tm"]
    ublocks = group_blocks(units_p1())
    wi = 0
    si = 0
    for it in range(S // TS):
        t0 = it * TS
        kb.dma("sp", xt[:, :, :], x_src[t0:t0 + TS, :].rearrange("(j p) d -> p j d", p=128), w=[xt.res])
        rmsnorm_T(P, dn, xt, gcol[l], hT, C["junk"], C["ss"], C["rs"], C["xn"], ident)
        for blk in ublocks:
            c0 = blk[0][0]
            ncols = blk[-1][0] + blk[-1][1] - c0
            w = wb[wi % len(wb)]
            wi += 1
            kb.dma("sp", w[:, :, 0:ncols], C["w_in_b"][l, :, c0:c0 + ncols].rearrange("(k p) c -> p k c", p=128),
                   w=[w.res])
            for (c, m) in blk:
                ps = dn.nps()
                for k in range(16):
                    kb.op("pe", lambda h: h.matmul(ps[0:m, :], w[:, k, c - c0:c - c0 + m], hT[:, k, :],
                                                   start=(k == 0), stop=(k == 15)), r=[w.res, hT.res], w=[ps.res])
                sg = st[si % len(st)]
                si += 1
                copy_evac(kb, kb.ev(), sg[0:m, :], ps[0:m, :], r=[ps.res], w=[sg.res])
                kb.dma("act", C["zT"][c:c + m, t0:t0 + TS], sg[0:m, :], r=[sg.res])
        for (c0, ncols, dst, dc) in [(4288, 512, "vatt", 0), (4800, 256, "vatt", 512), (5056, 512, "u5", 0)]:
            w = wb[wi % len(wb)]
            wi += 1
            kb.dma("sp", w[:, :, 0:ncols], C["w_in_b"][l, :, c0:c0 + ncols].rearrange("(k p) c -> p k c", p=128),
                   w=[w.res])
            sg = stm[si % len(stm)]
            si += 1
            for j in range(4):
                ps = dn.nps()
                for k in range(16):
                    kb.op("pe", lambda h: h.matmul(ps[:, 0:ncols], hT[:, k, j * 128:(j + 1) * 128], w[:, k, 0:ncols],
                                                   start=(k == 0), stop=(k == 15)), r=[w.res, hT.res], w=[ps.res])
                copy_evac(kb, kb.ev(), sg[:, j, 0:ncols], ps[:, 0:ncols], r=[ps.res], w=[sg.res])
            kb.dma("act", C[dst][t0:t0 + TS, dc:dc + ncols].rearrange("(j p) c -> p j c", p=128), sg[:, :, 0:ncols],
                   r=[sg.res])
        for gb in range(12):
            c0 = 5568 + gb * 512
            w = wb[wi % len(wb)]
            wi += 1
            kb.dma("sp", w[:, :, :], C["w_in_b"][l, :, c0:c0 + 512].rearrange("(k p) c -> p k c", p=128), w=[w.res])
            for u in range(4):
                ps = dn.nps()
                for k in range(16):
                    kb.op("pe", lambda h: h.matmul(ps[:, :], w[:, k, u * 128:(u + 1) * 128], hT[:, k, :],
                                                   start=(k == 0), stop=(k == 15)), r=[w.res, hT.res], w=[ps.res])
                sg = st[si % len(st)]
                si += 1
                gi = gb * 4 + u
                kb.op("act", lambda h: h.activation(out=sg[:, :], in_=ps[:, :], func=AF.Sigmoid,
                                                    bias=bgcol[l][:, gi:gi + 1]), r=[ps.res, bgcol[l].res], w=[sg.res])
                kb.dma("act", C["gT"][gi * 128:(gi + 1) * 128, t0:t0 + TS], sg[:, :], r=[sg.res])


def gelu_tanh(kb, out_ap, x_ap, tmp_ap, r, w_out, w_tmp):
    kb.op("pool", lambda h: h.tensor_tensor(out=tmp_ap, in0=x_ap, in1=x_ap, op=ALU.mult), r=r, w=[w_tmp])
    kb.op("dve", lambda h: h.tensor_scalar(out=tmp_ap, in0=tmp_ap, scalar1=0.044715 * 1.5957691216, scalar2=1.5957691216,
                                           op0=ALU.mult, op1=ALU.add), r=[w_tmp], w=[w_tmp])
    kb.op("dve", lambda h: h.tensor_tensor(out=tmp_ap, in0=tmp_ap, in1=x_ap, op=ALU.mult), r=r + [w_tmp], w=[w_tmp])
    kb.op("act", lambda h: h.activation(out=tmp_ap, in_=tmp_ap, func=AF.Sigmoid), r=[w_tmp], w=[w_tmp])
    kb.op("dve", lambda h: h.tensor_tensor(out=out_ap, in0=tmp_ap, in1=x_ap, op=ALU.mult), r=r + [w_tmp], w=[w_out])


def phase3(P, dn, l, C, last):
    kb, cx, S = P.kb, P.cx, P.S
    x_src = C["x"] if l == 0 else C["xres"]
    ident = C["ident"]
    xt, hT, wb = C["xt"], C["hT"], C["wblk"]
    yT, mT, aT, gt = C["yTt"], C["mT"], C["aT"], C["gt"]
    wi = 0
    gi_ = 0
    for it in range(S // TS):
        t0 = it * TS
        kb.dma("sp", xt[:, :, :], x_src[t0:t0 + TS, :].rearrange("(j p) d -> p j d", p=128), w=[xt.res])
        kb.dma("sp", yT[:, :, :], C["yT"][:, t0:t0 + TS].rearrange("(k p) t -> p k t", p=128), w=[yT.res])
        for fb in range(4):
            w = wb[wi % len(wb)]
            wi += 1
            kb.dma("sp", w[:, :, :], C["proj_b"][l, :, fb * 512:(fb + 1) * 512].rearrange("(k p) c -> p k c", p=128),
                   w=[w.res])
            for u in range(4):
                fc = fb * 4 + u
                g = gt[gi_ % len(gt)]
                gi_ += 1
                kb.dma("sp", g[:, :, :], C["gT"][:, t0:t0 + TS].rearrange("(i f p) t -> p i f t", i=3, p=128)[:, :, fc, :],
                       w=[g.res])
                acc = C["macc"]
                for i, (k0, k1) in enumerate([(0, 6), (6, 12), (12, 16)]):
                    ps = dn.nps()
                    for k in range(k0, k1):
                        kb.op("pe", lambda h: h.matmul(ps[:, :], w[:, k, u * 128:(u + 1) * 128], yT[:, k, :],
                                                       start=(k == k0), stop=(k == k1 - 1)), r=[w.res, yT.res], w=[ps.res])
                    if i == 0:
                        kb.op("dve", lambda h: h.tensor_tensor(out=acc[:, :], in0=ps[:, :], in1=g[:, 0, :], op=ALU.mult),
                              r=[ps.res, g.res], w=[acc.res])
                    else:
                        tmp = C["mtmp"]
                        kb.op("dve", lambda h: h.tensor_tensor(out=tmp[:, :], in0=ps[:, :], in1=g[:, i, :], op=ALU.mult),
                              r=[ps.res, g.res], w=[tmp.res])
                        if i == 1:
                            kb.op("pool", lambda h: h.tensor_tensor(out=acc[:, :], in0=acc[:, :], in1=tmp[:, :], op=ALU.add),
                                  r=[tmp.res, acc.res], w=[acc.res])
                        else:
                            kb.op("pool", lambda h: h.tensor_tensor(out=mT[:, fc, :], in0=acc[:, :], in1=tmp[:, :], op=ALU.add),
                                  r=[tmp.res, acc.res], w=[mT.res])
        for cb in range(4):
            w = wb[wi % len(wb)]
            wi += 1
            kb.dma("sp", w[:, :, :], C["w_out_b"][l, :, cb * 512:(cb + 1) * 512].rearrange("(k p) c -> p k c", p=128),
                   w=[w.res])
            for j in range(4):
                ps = dn.nps()
                for k in range(16):
                    kb.op("pe", lambda h: h.matmul(ps[:, :], mT[:, k, j * 128:(j + 1) * 128], w[:, k, :],
                                                   start=(k == 0), stop=(k == 15)), r=[w.res, mT.res], w=[ps.res])
                kb.op("dve", lambda h: h.tensor_tensor(out=xt[:, j, cb * 512:(cb + 1) * 512], in0=ps[:, :],
                                                       in1=xt[:, j, cb * 512:(cb + 1) * 512], op=ALU.add),
                      r=[ps.res, xt.res], w=[xt.res])
        rmsnorm_T(P, dn, xt, C["gffn"][l], hT, C["junk"], C["ss"], C["rs"], C["xn"], ident)
        cw = C["convw"][l]
        carry = C["carry"]
        for fb in range(11):
            wv = wb[wi % len(wb)]
            wi += 1
            kb.dma("sp", wv[:, :, :], C["up_b"][l, :, fb * 512:(fb + 1) * 512].rearrange("(k p) c -> p k c", p=128),
                   w=[wv.res])
            wg = wb[wi % len(wb)]
            wi += 1
            kb.dma("sp", wg[:, :, :], C["up_b"][l, :, D_FF + fb * 512:D_FF + (fb + 1) * 512].rearrange("(k p) c -> p k c", p=128),
                   w=[wg.res])
            for u in range(4):
                f = fb * 4 + u
                psv = dn.nps()
                for k in range(16):
                    kb.op("pe", lambda h: h.matmul(psv[:, :], wv[:, k, u * 128:(u + 1) * 128], hT[:, k, :],
                                                   start=(k == 0), stop=(k == 15)), r=[wv.res, hT.res], w=[psv.res])
                psg = dn.nps()
                for k in range(16):
                    kb.op("pe", lambda h: h.matmul(psg[:, :], wg[:, k, u * 128:(u + 1) * 128], hT[:, k, :],
                                                   start=(k == 0), stop=(k == 15)), r=[wg.res, hT.res], w=[psg.res])
                G = C["G"]
                cv = C["cv"]
                ct = C["ct"]
                kb.op("pool", lambda h: h.tensor_copy(out=G[:, 0:2], in_=carry[:, f, :]), r=[carry.res], w=[G.res])
                kb.op("act", lambda h: h.activation(out=G[:, 2:2 + TS], in_=psg[:, :], func=AF.Identity),
                      r=[psg.res], w=[G.res])
                kb.op("pool", lambda h: h.tensor_copy(out=carry[:, f, :], in_=G[:, TS:TS + 2]), r=[G.res], w=[carry.res])
                kb.op("dve", lambda h: h.tensor_scalar(out=cv[:, :], in0=G[:, 2:2 + TS], scalar1=cw[:, 2, f:f + 1],
                                                       scalar2=None, op0=ALU.mult), r=[G.res, cw.res], w=[cv.res])
                kb.op("dve", lambda h: h.scalar_tensor_tensor(out=cv[:, :], in0=G[:, 1:1 + TS], scalar=cw[:, 1, f:f + 1],
                                                              in1=cv[:, :], op0=ALU.mult, op1=ALU.add),
                      r=[G.res, cw.res, cv.res], w=[cv.res])
                kb.op("dve", lambda h: h.scalar_tensor_tensor(out=cv[:, :], in0=G[:, 0:TS], scalar=cw[:, 0, f:f + 1],
                                                              in1=cv[:, :], op0=ALU.mult, op1=ALU.add),
                      r=[G.res, cw.res, cv.res], w=[cv.res])
                gelu_tanh(kb, cv[:, :], cv[:, :], ct[:, :], [cv.res], cv.res, ct.res)
                kb.op("dve", lambda h: h.tensor_tensor(out=aT[:, f, :], in0=psv[:, :], in1=cv[:, :], op=ALU.mult),
                      r=[psv.res, cv.res], w=[aT.res])
        wd = C["wdn"]
        di = 0
        for cb in range(4):
            pss = [dn.nps() for _ in range(4)]
            for pc in range(4):
                w = wd[di % len(wd)]
                di += 1
                kb.dma("sp", w[:, :, :], C["down_b"][l, pc * 1408:(pc + 1) * 1408, cb * 512:(cb + 1) * 512]
                       .rearrange("(k p) c -> p k c", p=128), w=[w.res])
                for j in range(4):
                    for k in range(11):
                        f = pc * 11 + k
                        kb.op("pe", lambda h: h.matmul(pss[j][:, :], aT[:, f, j * 128:(j + 1) * 128], w[:, k, :],
                                                       start=(f == 0), stop=(f == 43)), r=[w.res, aT.res], w=[pss[j].res])
            for j in range(4):
                kb.op("dve", lambda h: h.tensor_tensor(out=xt[:, j, cb * 512:(cb + 1) * 512], in0=pss[j][:, :],
                                                       in1=xt[:, j, cb * 512:(cb + 1) * 512], op=ALU.add),
                      r=[pss[j].res, xt.res], w=[xt.res])
        if not last:
            kb.dma("act", C["xres"][t0:t0 + TS, :].rearrange("(j p) d -> p j d", p=128), xt[:, :, :], r=[xt.res])
        else:
            junk, ss, rs = C["junk"], C["ss"], C["rs"]
            gf = C["gfin"]
            for j in range(4):
                kb.op("act", lambda h: h.activation(out=junk[:, :], in_=xt[:, j, :], func=AF.Square,
                                                    accum_out=ss[:, j:j + 1]), r=[xt.res], w=[junk.res, ss.res])
            kb.op("act", lambda h: h.activation(out=rs[:, 0:4], in_=ss[:, 0:4], func=AF.Sqrt, scale=1.0 / D, bias=EPS),
                  r=[ss.res], w=[rs.res])
            kb.op("dve", lambda h: h.reciprocal(out=rs[:, 0:4], in_=rs[:, 0:4]), r=[rs.res], w=[rs.res])
            for j in range(4):
                kb.op("dve", lambda h: h.scalar_tensor_tensor(out=xt[:, j, :], in0=xt[:, j, :], scalar=rs[:, j:j + 1],
                                                              in1=gf[:, :], op0=ALU.mult, op1=ALU.mult),
                      r=[xt.res, rs.res, gf.res], w=[xt.res])
            kb.dma("act", C["out"][t0:t0 + TS, :].rearrange("(j p) d -> p j d", p=128), xt[:, :, :], r=[xt.res])


def build(S, depth, phases=("p1", "mix", "p3"), debug=False, ext_in=(), dbg_out=()):
    P = Prog(S, depth, debug)
    P.ext_in = ext_in
    P.dbg_out = dbg_out
    nc, kb, cx = P.nc, P.kb, P.cx
    C = {}
    C["x"] = P.din("x", (S, D))
    C["positions"] = P.din("positions", (1, S), I32)
    C["ident_in"] = P.din("ident", (128, 128), BF16)
    for n, shp in PARAM_SHAPES.items():
        C["in_" + n] = P.din(n, shp)
    C["out"] = P.dscr("out", (S, D), F32, out=True)
    C["xres"] = P.dscr("xres", (S, D), F32)
    C["w_in_b"] = P.dscr("w_in_bf", (depth, D, N_IN), BF16)
    C["proj_b"] = P.dscr("projs_b", (depth, D, D), BF16)
    C["w_out_b"] = P.dscr("w_out_b", (depth, D, D), BF16)
    C["up_b"] = P.dscr("up_b", (depth, D, 2 * D_FF), BF16)
    C["down_b"] = P.dscr("down_b", (depth, D_FF, D), BF16)
    C["zT"] = P.dscr("zT", (ZROWS, S), BF16)
    C["vatt"] = P.dscr("vatt", (S, 768), BF16)
    C["u5"] = P.dscr("u5", (S, 512), BF16)
    C["gT"] = P.dscr("gT", (3 * D, S), BF16)
    C["yT"] = P.dscr("yT", (D, S), BF16)

    wres = [Res("wconv%d" % l) for l in range(depth)]
    C["wres"] = wres

    def conv(l, dst, src, rows, step=128):
        for r0 in range(0, rows, step):
            r1 = min(rows, r0 + step)
            kb.dma("pool", dst(r0, r1), src(r0, r1), sres=wres[l])

    for l in range(depth):
        conv(l, lambda a, b: C["w_in_b"][l, a:b, :], lambda a, b: C["in_w_in"][l, a:b, :], D)
        conv(l, lambda a, b: C["proj_b"][l, a:b, :], lambda a, b: C["in_proj_a"][l, a:b, :], 768)
        conv(l, lambda a, b: C["proj_b"][l, 768 + a:768 + b, :], lambda a, b: C["in_proj_b"][l, a:b, :], 768)
        conv(l, lambda a, b: C["proj_b"][l, 1536 + a:1536 + b, :], lambda a, b: C["in_proj_c"][l, a:b, :], 512)
        conv(l, lambda a, b: C["w_out_b"][l, a:b, :], lambda a, b: C["in_w_out"][l, a:b, :], D)
        conv(l, lambda a, b: C["up_b"][l, a:b, :], lambda a, b: C["in_ffn_up"][l, a:b, :], D)
        conv(l, lambda a, b: C["down_b"][l, a:b, :], lambda a, b: C["in_ffn_down"][l, a:b, :], D_FF)
        wres[l].w = [(wres[l].ds, kb.sems[wres[l].ds].cnt)]

    C["ident"] = cx.sb([128, 128], BF16, "ident")
    kb.dma("sp", C["ident"][:, :], C["ident_in"][:, :], w=[C["ident"].res])
    L = depth
    gm = cx.sb([128, L, 16], F32, "gmix")
    gf = cx.sb([128, L, 16], F32, "gffn")
    bg = cx.sb([128, L, 48], F32, "bgate")
    cw = cx.sb([128, L, 3, 44], F32, "convw")
    kb.dma("sp", gm[:, :, :], C["in_norm_mix"][0:L, :].rearrange("l (c p) -> p l c", p=128), w=[gm.res], slow=True)
    kb.dma("sp", gf[:, :, :], C["in_norm_ffn"][0:L, :].rearrange("l (c p) -> p l c", p=128), w=[gf.res], slow=True)
    kb.dma("sp", bg[:, :, :], C["in_b_gate"][0:L, :].rearrange("l (c p) -> p l c", p=128), w=[bg.res], slow=True)
    for l in range(L):
        kb.dma("sp", cw[:, l, :, :], C["in_ffn_conv"][l, :, :].rearrange("j (c p) -> p j c", p=128), w=[cw.res], slow=True)

    class V:
        def __init__(self, tl, f):
            self.tl, self.f, self.res = tl, f, tl.res

        def __getitem__(self, idx):
            return self.f(self.tl)[idx]

    C["gmix"] = [V(gm, lambda t, l=l: t[:, l, :]) for l in range(L)]
    C["gffn"] = [V(gf, lambda t, l=l: t[:, l, :]) for l in range(L)]
    C["bgate"] = [V(bg, lambda t, l=l: t[:, l, :]) for l in range(L)]
    C["convw"] = [V(cw, lambda t, l=l: t[:, l, :, :]) for l in range(L)]
    def alloc_dense(nw=2):
        C["xt"] = cx.sb([128, 4, D], F32, "xt")
        C["hT"] = cx.sb([128, 16, TS], BF16, "hT")
        C["xn"] = cx.sb([128, 4, D], BF16, "xn")
        C["junk"] = cx.sb([128, D], BF16, "junk")
        C["ss"] = cx.sb([128, 4], F32, "ss")
        C["rs"] = cx.sb([128, 4], F32, "rs")
        C["wblk"] = [cx.sb([128, 16, 512], BF16, "wblk") for _ in range(nw)]

    C["rw_masks"] = P.din("rw_masks", (3, 128, 128))
    C["bones"] = P.din("bones", (128, 128), BF16)
    C["chunkmask"] = P.din("chunkmask", (128, 256))
    C["s5mask"] = P.din("s5mask", (128, 2, 256))
    C["identf"] = P.din("identf", (128, 128))
    C["invf"] = P.din("invf", (64, 1))
    C["sgn"] = P.din("sgn", (64, 1))
    C["ropeC"] = P.dscr("ropeC", (64, S), BF16)
    C["ropeS"] = P.dscr("ropeS", (64, S), BF16)
    dn = Dense(P)
    C["dn"] = dn

    if "zero" in phases:
        zt = cx.sb([128, 2048], BF16, "zeros")
        kb.op("pool", lambda h: h.memset(zt[:, :], 0.0), w=[zt.res])
        for r0 in list(range(0, 768, 128)) + list(range(1536, 2048, 128)):
            for t0 in range(0, S, 2048):
                n = min(2048, S - t0)
                kb.dma("sp", C["yT"][r0:r0 + 128, t0:t0 + n], zt[:, 0:n], r=[zt.res])
        kb.barrier()
    for l in range(depth):
        last = (l == depth - 1)
        kb._wait("sp", wres[l].w)
        if "p1" in phases:
            with ExitStack() as stk:
                cx.stack = stk
                alloc_dense(4)
                C["stage"] = [cx.sb([128, TS], BF16, "stage") for _ in range(4)]
                C["stage_tm"] = [cx.sb([128, 4, 512], BF16, "stage_tm") for _ in range(2)]
                phase1(P, dn, l, C)
                kb.barrier()
                cx.stack = None
        if "rope" in phases and l == 0:
            with ExitStack() as stk:
                cx.stack = stk
                rope_tables(P, C)
                kb.barrier()
                cx.stack = None
        if "rwkv" in phases:
            with ExitStack() as stk:
                cx.stack = stk
                rwkv_mixer(P, dn, l, C)
                kb.barrier()
                cx.stack = None
        if "s5" in phases:
            with ExitStack() as stk:
                cx.stack = stk
                s5_mixer(P, dn, l, C)
                kb.barrier()
                cx.stack = None
        if "att" in phases:
            with ExitStack() as stk:
                cx.stack = stk
                attention(P, dn, l, C)
                kb.barrier()
                cx.stack = None
        if "p3" in phases:
            with ExitStack() as stk:
                cx.stack = stk
                alloc_dense()
                big = cx.sb([128, 44 * TS], BF16, "big")
                C["yTt"] = V(big, lambda t: t[:, 0:16 * TS].rearrange("p (k t) -> p k t", k=16))
                C["mT"] = V(big, lambda t: t[:, 16 * TS:32 * TS].rearrange("p (k t) -> p k t", k=16))
                C["aT"] = V(big, lambda t: t[:, :].rearrange("p (k t) -> p k t", k=44))
                C["gt"] = [cx.sb([128, 3, TS], BF16, "gt") for _ in range(2)]
                C["macc"] = cx.sb([128, TS], F32, "macc")
                C["mtmp"] = cx.sb([128, TS], F32, "mtmp")
                C["G"] = cx.sb([128, TS + 2], F32, "G")
                C["cv"] = cx.sb([128, TS], F32, "cv")
                C["ct"] = cx.sb([128, TS], F32, "ct")
                C["carry"] = cx.sb([128, 44, 2], F32, "carry")
                C["wdn"] = [cx.sb([128, 11, 512], BF16, "wdn") for _ in range(2)]
                kb.op("pool", lambda h: h.memset(C["carry"][:, :, :], 0.0), w=[C["carry"].res])
                if last:
                    C["gfin"] = cx.sb([128, D], F32, "gfin")
                    kb.dma("sp", C["gfin"][:, :], C["in_norm_final"].partition_broadcast(128), w=[C["gfin"].res])
                phase3(P, dn, l, C, last)
                kb.barrier()
                cx.stack = None
    kb.barrier()
    return P, C


def rope_tables(P, C):
    kb, cx, S = P.kb, P.cx, P.S
    pi = cx.sb([64, S], I32, "pos_i")
    ang = cx.sb([64, S], F32, "ang")
    a2 = cx.sb([64, S], F32, "ang2")
    ni = cx.sb([64, S], I32, "n_i")
    nf = cx.sb([64, S], F32, "n_f")
    ob = cx.sb([64, S], BF16, "rope_o")
    invf = cx.sb([64, 1], F32, "invf")
    sgn = cx.sb([64, 1], F32, "sgn")
    kb.dma("sp", invf[:, :], C["invf"][:, :], w=[invf.res])
    kb.dma("sp", sgn[:, :], C["sgn"][:, :], w=[sgn.res])
    kb.dma("sp", pi[:, :], C["positions"][0].partition_broadcast(64), w=[pi.res])
    kb.op("dve", lambda h: h.tensor_copy(out=ang[:, :], in_=pi[:, :]), r=[pi.res], w=[ang.res])
    kb.op("dve", lambda h: h.tensor_scalar(out=ang[:, :], in0=ang[:, :], scalar1=invf[:, 0:1], scalar2=None, op0=ALU.mult),
          r=[ang.res, invf.res], w=[ang.res])
    TWO_PI = 2.0 * math.pi
    for which in range(2):
        src = ang
        if which == 0:
            kb.op("dve", lambda h: h.tensor_scalar(out=a2[:, :], in0=ang[:, :], scalar1=math.pi / 2, scalar2=None, op0=ALU.add),
                  r=[ang.res], w=[a2.res])
            src = a2
        kb.op("dve", lambda h: h.tensor_scalar(out=ni[:, :], in0=src[:, :], scalar1=1.0 / TWO_PI, scalar2=None, op0=ALU.mult),
              r=[src.res], w=[ni.res])
        kb.op("dve", lambda h: h.tensor_copy(out=nf[:, :], in_=ni[:, :]), r=[ni.res], w=[nf.res])
        kb.op("dve", lambda h: h.scalar_tensor_tensor(out=nf[:, :], in0=nf[:, :], scalar=-TWO_PI, in1=src[:, :],
                                                      op0=ALU.mult, op1=ALU.add), r=[nf.res, src.res], w=[nf.res])
        kb.op("dve", lambda h: h.tensor_scalar(out=nf[:, :], in0=nf[:, :], scalar1=-3.14159, scalar2=3.14159,
                                               op0=ALU.max, op1=ALU.min), r=[nf.res], w=[nf.res])
        kb.op("act", lambda h: h.activation(out=nf[:, :], in_=nf[:, :], func=AF.Sin), r=[nf.res], w=[nf.res])
        if which == 0:
            kb.op("dve", lambda h: h.tensor_copy(out=ob[:, :], in_=nf[:, :]), r=[nf.res], w=[ob.res])
            kb.dma("sp", C["ropeC"][:, :], ob[:, :], r=[ob.res])
        else:
            kb.op("dve", lambda h: h.tensor_scalar(out=ob[:, :], in0=nf[:, :], scalar1=sgn[:, 0:1], scalar2=None, op0=ALU.mult),
                  r=[nf.res, sgn.res], w=[ob.res])
            kb.dma("sp", C["ropeS"][:, :], ob[:, :], r=[ob.res])


def attention(P, dn, l, C):
    kb, cx, S = P.kb, P.cx, P.S
    NB = S // 128
    lam_init = 0.8 - 0.6 * math.exp(-0.3 * l)
    ident = C["ident"]
    lq = cx.sb([128, 4, 64], F32, "lq")
    for i, n in enumerate(["da_lq1", "da_lk1", "da_lq2", "da_lk2"]):
        kb.dma("sp", lq[:, i, :], C["in_" + n][l].partition_broadcast(128), w=[lq.res])
    pr = cx.sb([128, 2, 64], F32, "lpr")
    e12 = cx.sb([128, 2], F32, "e12")
    nlam = cx.sb([128, 1], F32, "nlam")
    kb.op("dve", lambda h: h.tensor_tensor(out=pr[:, 0, :], in0=lq[:, 0, :], in1=lq[:, 1, :], op=ALU.mult), r=[lq.res], w=[pr.res])
    kb.op("dve", lambda h: h.tensor_tensor(out=pr[:, 1, :], in0=lq[:, 2, :], in1=lq[:, 3, :], op=ALU.mult), r=[lq.res], w=[pr.res])
    kb.op("dve", lambda h: h.tensor_reduce(out=e12[:, :], in_=pr[:, :, :], axis=AX.X, op=ALU.add), r=[pr.res], w=[e12.res])
    kb.op("act", lambda h: h.activation(out=e12[:, :], in_=e12[:, :], func=AF.Exp), r=[e12.res], w=[e12.res])
    kb.op("dve", lambda h: h.tensor_tensor(out=nlam[:, :], in0=e12[:, 1:2], in1=e12[:, 0:1], op=ALU.subtract), r=[e12.res], w=[nlam.res])
    kb.op("dve", lambda h: h.tensor_scalar(out=nlam[:, :], in0=nlam[:, :], scalar1=-lam_init, scalar2=None, op0=ALU.add),
          r=[nlam.res], w=[nlam.res])
    sl = cx.sb([128, 128], F32, "subln")
    kb.dma("sp", sl[:, :], C["in_da_subln"][l].partition_broadcast(128), w=[sl.res])
    kb.op("dve", lambda h: h.tensor_scalar(out=sl[:, :], in0=sl[:, :], scalar1=1.0 - lam_init, scalar2=None, op0=ALU.mult),
          r=[sl.res], w=[sl.res])
    C2 = cx.sb([64, S], BF16, "C2")
    S2 = cx.sb([64, S], BF16, "S2")
    kb.dma("sp", C2[:, :], C["ropeC"][:, :], w=[C2.res])
    kb.dma("sp", S2[:, :], C["ropeS"][:, :], w=[S2.res])
    kr = [cx.sb([64, S], BF16, "kr") for _ in range(2)]
    V1 = cx.sb([128, NB, 129], BF16, "V1")
    kb.op("pool", lambda h: h.memset(V1[:, :, 128:129], 1.0), w=[V1.res])
    CH = min(S, 2048)
    raw = [cx.sb([64, CH], BF16, "raw") for _ in range(2)]
    swp = [cx.sb([64, CH], BF16, "swp") for _ in range(2)]
    tmp = [cx.sb([64, CH], BF16, "rtmp") for _ in range(2)]
    qr = [cx.sb([64, 512], BF16, "qr") for _ in range(2)]
    PT = [cx.sb([128, 512], BF16, "PT") for _ in range(4)]
    o0 = cx.sb([128, 4, 128], F32, "o0")
    att = cx.sb([128, 4, 128], F32, "att")
    rec = cx.sb([128, 4], F32, "rec")
    ssq = cx.sb([128, 4], F32, "ssq")
    jk = cx.sb([128, 128], BF16, "ajunk")
    yb = cx.sb([128, 4, 128], BF16, "yb")
    stg = [cx.sb([128, 512], BF16, "astg") for _ in range(2)]
    OA, OB = dn.psf[4], dn.psf[5]
    ri = 0
    pti = 0
    sti = 0

    def rope(dst_ap, row0, t0, n, ri):
        a, b, t = raw[ri % 2], swp[ri % 2], tmp[ri % 2]
        kb.dma("sp", a[:, 0:n], C["zT"][row0:row0 + 64, t0:t0 + n], w=[a.res])
        kb.dma("sp", b[0:32, 0:n], C["zT"][row0 + 32:row0 + 64, t0:t0 + n], w=[b.res])
        kb.dma("sp", b[32:64, 0:n], C["zT"][row0:row0 + 32, t0:t0 + n], w=[b.res])
        kb.op("dve", lambda h: h.tensor_tensor(out=t[:, 0:n], in0=a[:, 0:n], in1=C2[:, t0:t0 + n], op=ALU.mult),
              r=[a.res, C2.res], w=[t.res])
        kb.op("pool", lambda h: h.tensor_tensor(out=b[:, 0:n], in0=b[:, 0:n], in1=S2[:, t0:t0 + n], op=ALU.mult),
              r=[b.res, S2.res], w=[b.res])
        return t, b

    for hd in range(6):
        kb.dma("sp", V1[:, :, 0:128], C["vatt"][:, hd * 128:(hd + 1) * 128].rearrange("(n p) c -> p n c", p=128), w=[V1.res])
        for c in range(2):
            row0 = 3520 + (hd * 2 + c) * 64
            for t0 in range(0, S, CH):
                t, b = rope(None, row0, t0, CH, ri)
                ri += 1
                kb.op("dve", lambda h: h.tensor_tensor(out=kr[c][:, t0:t0 + CH], in0=t[:, 0:CH], in1=b[:, 0:CH], op=ALU.add),
                      r=[t.res, b.res], w=[kr[c].res])
        for Q in range(S // 512):
            q0 = Q * 512
            for c in range(2):
                row0 = 2752 + (hd * 2 + c) * 64
                t, b = rope(None, row0, q0, 512, ri)
                ri += 1
                q = qr[c]
                kb.op("dve", lambda h: h.tensor_tensor(out=q[:, :], in0=t[:, 0:512], in1=b[:, 0:512], op=ALU.add),
                      r=[t.res, b.res], w=[q.res])
                nkb = Q * 4 + 4

                def stA(kbk):
                    nonlocal pti
                    i = kbk - Q * 4
                    j0 = max(i, 0)
                    ps = dn.nps(4)
                    kb.op("pe", lambda h: h.matmul(ps[:, j0 * 128:512], kr[c][:, kbk * 128:(kbk + 1) * 128], q[:, j0 * 128:512],
                                                   start=True, stop=True), r=[kr[c].res, q.res], w=[ps.res])
                    pt = PT[pti % len(PT)]
                    pti += 1
                    kb.op("act", lambda h: h.activation(out=pt[:, j0 * 128:512], in_=ps[:, j0 * 128:512], func=AF.Exp, scale=0.125),
                          r=[ps.res], w=[pt.res])
                    if i >= 0:
                        kb.op("pool", lambda h: h.memset(pt[64:128, i * 128:i * 128 + 64], 0.0), w=[pt.res])
                    return (kbk, j0, pt)

                def stB(a):
                    kbk, j0, pt = a
                    for j in range(j0, 4):
                        O = OA if j < 2 else OB
                        first = (kbk == 0 and (j == 0 or j == 2))
                        kb.op("pe", lambda h: h.matmul(O[:, (j % 2) * 129:(j % 2) * 129 + 129], pt[:, j * 128:(j + 1) * 128],
                                                       V1[:, kbk, :], start=first, stop=(kbk == Q * 4 + j),
                                                       skip_group_check=True), r=[pt.res, V1.res], w=[O.res])

                pend = []
                for kbk in range(nkb):
                    pend.append(stA(kbk))
                    if len(pend) > 2:
                        stB(pend.pop(0))
                while pend:
                    stB(pend.pop(0))
                for j in range(4):
                    O = OA if j < 2 else OB
                    b0 = (j % 2) * 129
                    kb.op("dve", lambda h: h.reciprocal(out=rec[:, j:j + 1], in_=O[:, b0 + 128:b0 + 129]), r=[O.res], w=[rec.res])
                    if c == 0:
                        kb.op("dve", lambda h: h.tensor_scalar(out=o0[:, j, :], in0=O[:, b0:b0 + 128], scalar1=rec[:, j:j + 1],
                                                               scalar2=None, op0=ALU.mult), r=[O.res, rec.res], w=[o0.res])
                    else:
                        kb.op("dve", lambda h: h.tensor_tensor(out=rec[:, j:j + 1], in0=rec[:, j:j + 1], in1=nlam[:, 0:1], op=ALU.mult),
                              r=[rec.res, nlam.res], w=[rec.res])
                        kb.op("dve", lambda h: h.scalar_tensor_tensor(out=att[:, j, :], in0=O[:, b0:b0 + 128], scalar=rec[:, j:j + 1],
                                                                      in1=o0[:, j, :], op0=ALU.mult, op1=ALU.add),
                              r=[O.res, rec.res, o0.res], w=[att.res])
            for j in range(4):
                kb.op("act", lambda h: h.activation(out=jk[:, :], in_=att[:, j, :], func=AF.Square, accum_out=ssq[:, j:j + 1]),
                      r=[att.res], w=[jk.res, ssq.res])
            kb.op("act", lambda h: h.activation(out=ssq[:, :], in_=ssq[:, :], func=AF.Sqrt, scale=1.0 / 128, bias=EPS),
                  r=[ssq.res], w=[ssq.res])
            kb.op("dve", lambda h: h.reciprocal(out=ssq[:, :], in_=ssq[:, :]), r=[ssq.res], w=[ssq.res])
            pb = dn.npb()
            for j in range(4):
                kb.op("dve", lambda h: h.scalar_tensor_tensor(out=yb[:, j, :], in0=att[:, j, :], scalar=ssq[:, j:j + 1], in1=sl[:, :],
                                                              op0=ALU.mult, op1=ALU.mult), r=[att.res, ssq.res, sl.res], w=[yb.res])
                kb.op("pe", lambda h: h.transpose(out=pb[:, j * 128:(j + 1) * 128], in_=yb[:, j, :], identity=ident[:, :]),
                      r=[yb.res, ident.res], w=[pb.res])
            sg = stg[sti % 2]
            sti += 1
            copy_evac(kb, kb.ev(), sg[:, :], pb[:, 0:512], r=[pb.res], w=[sg.res])
            kb.dma("act", C["yT"][768 + hd * 128:768 + (hd + 1) * 128, q0:q0 + 512], sg[:, :], r=[sg.res])


def host_consts():
    invf = (10000.0 ** (-np.arange(0, 64, 2, dtype=np.float32) / 64)).astype(np.float32)
    p = np.arange(128)[:, None, None]
    m = np.arange(2)[None, :, None]
    col = np.arange(256)[None, None, :]
    s5mask = ((col // 16) >= ((m * 128 + p) // 16)).astype(np.float32)
    ii = np.arange(128)
    mus = (ii[:, None] < ii[None, :]).astype(np.float32)
    mui = (ii[:, None] <= ii[None, :]).astype(np.float32)
    mls = (ii[None, :] < ii[:, None]).astype(np.float32)
    bones = ((ii[:, None] // 64) == (ii[None, :] // 64)).astype(ml_dtypes.bfloat16)
    cmk = np.tile((np.arange(256) % 64 != 0).astype(np.float32)[None, :], (128, 1))
    return {"rw_masks": np.stack([mus, mui, mls]), "bones": bones, "chunkmask": cmk, "s5mask": s5mask, "identf": np.eye(128, dtype=np.float32), "invf": np.concatenate([invf, invf])[:, None].astype(np.float32).copy(),
            "sgn": np.concatenate([-np.ones(32), np.ones(32)])[:, None].astype(np.float32).copy()}


_CACHE = {}


def kernel(**inputs):
    S = 8192
    nb = 4
    if "prog" not in _CACHE:
        _CACHE["prog"] = build(S, DEPTH, phases=("p1", "rwkv", "s5", "rope", "att", "p3"))
    P, C = _CACHE["prog"]
    consts = host_consts()
    consts["ident"] = np.eye(128, dtype=ml_dtypes.bfloat16)
    in_maps = []
    for b in range(nb):
        m = {"x": np.ascontiguousarray(np.asarray(inputs["x"])[b], dtype=np.float32),
             "positions": np.ascontiguousarray(np.asarray(inputs["positions"])[b][None, :], dtype=np.int32)}
        for n, shp in PARAM_SHAPES.items():
            m[n] = np.ascontiguousarray(np.asarray(inputs[n], dtype=np.float32).reshape(shp))
        m.update(consts)
        in_maps.append(m)
    res = run_bass_kernel_spmd(P.nc, in_maps, core_ids=list(range(nb)))
    return np.stack([np.asarray(r["out"], dtype=np.float32) for r in res.results], axis=0)


def s5_mixer(P, dn, l, C):
    kb, cx, S = P.kb, P.cx, P.S
    ident = C["ident"]
    NCH = S // 16
    M = min(128, NCH)
    NMT = NCH // M
    V = lambda t, f: VW(t, f)
    TT = cx.sb([128, 32, 2, 256], BF16, "TT")
    GG = cx.sb([128, 32, 2, 2, 64], BF16, "GG")
    FFr = cx.sb([64, 32, 256], BF16, "FFr")
    nFFi = cx.sb([64, 32, 256], BF16, "nFFi")
    MU1 = cx.sb([64, 32, 2], F32, "MU1")
    MU2 = cx.sb([64, 32, 2], F32, "MU2")
    gluW = cx.sb([128, 4, 512], BF16, "gluW")
    glub = cx.sb([128, 4], F32, "glub")
    dB = cx.sb([128, 512], F32, "dB")
    kb.dma("pool", gluW[:, :, :], C["in_s5_glu_w"][l].rearrange("(k p) c -> p k c", p=128), w=[gluW.res])
    kb.dma("sp", glub[:, :], C["in_s5_glu_b"][l].rearrange("(c p) -> p c", p=128), w=[glub.res], slow=True)
    kb.dma("sp", dB[:, :], C["in_s5_d"][l].partition_broadcast(128), w=[dB.res])
    with ExitStack() as stk:
        old = cx.stack
        cx.stack = stk
        f32 = lambda shp, n: cx.sb(shp, F32, n)
        lr, li, dt = f32([64, 32], "lr"), f32([64, 32], "li"), f32([64, 32], "dt")
        kb.dma("sp", lr[:, :], C["in_s5_lam_re"][l].rearrange("g n -> n g"), w=[lr.res], slow=True)
        kb.dma("sp", li[:, :], C["in_s5_lam_im"][l].rearrange("g n -> n g"), w=[li.res], slow=True)
        kb.dma("sp", dt[:, :], C["in_s5_log_dt"][l].partition_broadcast(64), w=[dt.res])
        bre, bim = f32([64, 32, 16], "bre"), f32([64, 32, 16], "bim")
        cre, cim = f32([64, 32, 16], "cre"), f32([64, 32, 16], "cim")
        kb.dma("sp", bre[:, :, :], C["in_s5_b_re"][l].rearrange("g n c -> n g c"), w=[bre.res])
        kb.dma("sp", bim[:, :, :], C["in_s5_b_im"][l].rearrange("g n c -> n g c"), w=[bim.res])
        kb.dma("sp", cre[:, :, :], C["in_s5_c_re"][l].rearrange("g c n -> n g c"), w=[cre.res], slow=True)
        kb.dma("sp", cim[:, :, :], C["in_s5_c_im"][l].rearrange("g c n -> n g c"), w=[cim.res], slow=True)
        msk = cx.sb([128, 2, 256], F32, "s5mask")
        kb.dma("sp", msk[:, :, :], C["s5mask"][:, :, :], w=[msk.res])
        idf = cx.sb([64, 64], F32, "identf")
        kb.dma("sp", idf[:, :], C["identf"][0:64, 0:64], w=[idf.res])

        def tt(out, a, b, op, r, w):
            kb.op("dve", lambda h: h.tensor_tensor(out=out, in0=a, in1=b, op=op), r=r, w=w)

        def ts(out, a, s1, s2, op0, op1, r, w):
            if s2 is None:
                kb.op("dve", lambda h: h.tensor_scalar(out=out, in0=a, scalar1=s1, scalar2=None, op0=op0), r=r, w=w)
            else:
                kb.op("dve", lambda h: h.tensor_scalar(out=out, in0=a, scalar1=s1, scalar2=s2, op0=op0, op1=op1), r=r, w=w)

        def cmul(o_r, o_i, ar, ai, br, bi, t1, res_in, res_out):
            r = res_in + [T1.res]
            tt(o_r, ar, br, ALU.mult, res_in, res_out)
            tt(t1, ai, bi, ALU.mult, res_in, [T1.res])
            tt(o_r, o_r, t1, ALU.subtract, res_out + [T1.res], res_out)
            tt(o_i, ar, bi, ALU.mult, res_in, res_out)
            tt(t1, ai, br, ALU.mult, res_in, [T1.res])
            tt(o_i, o_i, t1, ALU.add, res_out + [T1.res], res_out)

        T1 = f32([64, 32 * 16], "T1")
        kb.op("act", lambda h: h.activation(out=dt[:, :], in_=dt[:, :], func=AF.Exp), r=[dt.res], w=[dt.res])
        aa, th, er = f32([64, 32], "aa"), f32([64, 32], "th"), f32([64, 32], "er")
        tt(aa[:, :], lr[:, :], dt[:, :], ALU.mult, [lr.res, dt.res], [aa.res])
        tt(th[:, :], li[:, :], dt[:, :], ALU.mult, [li.res, dt.res], [th.res])
        kb.op("act", lambda h: h.activation(out=er[:, :], in_=aa[:, :], func=AF.Exp, scale=1.0 / 16), r=[aa.res], w=[er.res])
        zr, zi = f32([64, 32], "zr"), f32([64, 32], "zi")
        ts(zr[:, :], th[:, :], 1.0 / 16, math.pi / 2, ALU.mult, ALU.add, [th.res], [zr.res])
        kb.op("act", lambda h: h.activation(out=zr[:, :], in_=zr[:, :], func=AF.Sin), r=[zr.res], w=[zr.res])
        kb.op("act", lambda h: h.activation(out=zi[:, :], in_=th[:, :], func=AF.Sin, scale=1.0 / 16), r=[th.res], w=[zi.res])
        tt(zr[:, :], zr[:, :], er[:, :], ALU.mult, [zr.res, er.res], [zr.res])
        tt(zi[:, :], zi[:, :], er[:, :], ALU.mult, [zi.res, er.res], [zi.res])
        z2r, z2i = f32([64, 32], "z2r"), f32([64, 32], "z2i")
        cur = (zr, zi)
        nxt = (z2r, z2i)
        for _ in range(4):
            cmul(nxt[0][:, :], nxt[1][:, :], cur[0][:, :], cur[1][:, :], cur[0][:, :], cur[1][:, :], T1[:, 0:32],
                 [cur[0].res, cur[1].res], [nxt[0].res, nxt[1].res])
            cur, nxt = nxt, cur
        abr, abi = cur
        den, nr, fre, fim = f32([64, 32], "den"), f32([64, 32], "nr"), f32([64, 32], "fre"), f32([64, 32], "fim")
        tt(den[:, :], lr[:, :], lr[:, :], ALU.mult, [lr.res], [den.res])
        tt(T1[:, 0:32], li[:, :], li[:, :], ALU.mult, [li.res], [T1.res])
        tt(den[:, :], den[:, :], T1[:, 0:32], ALU.add, [den.res, T1.res], [den.res])
        kb.op("dve", lambda h: h.reciprocal(out=den[:, :], in_=den[:, :]), r=[den.res], w=[den.res])
        ts(nr[:, :], abr[:, :], -1.0, None, ALU.add, None, [abr.res], [nr.res])
        tt(fre[:, :], nr[:, :], lr[:, :], ALU.mult, [nr.res, lr.res], [fre.res])
        tt(T1[:, 0:32], abi[:, :], li[:, :], ALU.mult, [abi.res, li.res], [T1.res])
        tt(fre[:, :], fre[:, :], T1[:, 0:32], ALU.add, [fre.res, T1.res], [fre.res])
        tt(fre[:, :], fre[:, :], den[:, :], ALU.mult, [fre.res, den.res], [fre.res])
        tt(fim[:, :], abi[:, :], lr[:, :], ALU.mult, [abi.res, lr.res], [fim.res])
        tt(T1[:, 0:32], nr[:, :], li[:, :], ALU.mult, [nr.res, li.res], [T1.res])
        tt(fim[:, :], fim[:, :], T1[:, 0:32], ALU.subtract, [fim.res, T1.res], [fim.res])
        tt(fim[:, :], fim[:, :], den[:, :], ALU.mult, [fim.res, den.res], [fim.res])
        bbr, bbi = f32([64, 32, 16], "bbr"), f32([64, 32, 16], "bbi")
        bc = lambda t: t[:, :].unsqueeze(2).to_broadcast([64, 32, 16])
        T3 = T1[:, :].rearrange("p (g c) -> p g c", c=16)
        cmul(bbr[:, :, :], bbi[:, :, :], bc(fre), bc(fim), bre[:, :, :], bim[:, :, :], T3,
             [fre.res, fim.res, bre.res, bim.res], [bbr.res, bbi.res])
        pwr, pwi = f32([64, 17, 32], "pwr"), f32([64, 17, 32], "pwi")
        npr, npi = f32([64, 16, 32], "npr"), f32([64, 16, 32], "npi")
        kb.op("dve", lambda h: h.memset(pwr[:, 0, :], 1.0), w=[pwr.res])
        kb.op("dve", lambda h: h.memset(pwi[:, 0, :], 0.0), w=[pwi.res])
        kb.op("dve", lambda h: h.memset(npr[:, 0, :], 1.0), w=[npr.res])
        kb.op("dve", lambda h: h.memset(npi[:, 0, :], 0.0), w=[npi.res])
        for t in range(16):
            cmul(pwr[:, t + 1, :], pwi[:, t + 1, :], pwr[:, t, :], pwi[:, t, :], abr[:, :], abi[:, :], T1[:, 0:32],
                 [pwr.res, pwi.res, abr.res, abi.res], [pwr.res, pwi.res])
        ivr, ivi = f32([64, 32], "ivr"), f32([64, 32], "ivi")
        tt(ivr[:, :], abr[:, :], abr[:, :], ALU.mult, [abr.res], [ivr.res])
        tt(T1[:, 0:32], abi[:, :], abi[:, :], ALU.mult, [abi.res], [T1.res])
        tt(ivr[:, :], ivr[:, :], T1[:, 0:32], ALU.add, [ivr.res, T1.res], [ivr.res])
        kb.op("dve", lambda h: h.reciprocal(out=ivr[:, :], in_=ivr[:, :]), r=[ivr.res], w=[ivr.res])
        tt(ivi[:, :], abi[:, :], ivr[:, :], ALU.mult, [abi.res, ivr.res], [ivi.res])
        ts(ivi[:, :], ivi[:, :], -1.0, None, ALU.mult, None, [ivi.res], [ivi.res])
        tt(ivr[:, :], abr[:, :], ivr[:, :], ALU.mult, [abr.res, ivr.res], [ivr.res])
        for t in range(15):
            cmul(npr[:, t + 1, :], npi[:, t + 1, :], npr[:, t, :], npi[:, t, :], ivr[:, :], ivi[:, :], T1[:, 0:32],
                 [npr.res, npi.res, ivr.res, ivi.res], [npr.res, npi.res])
        kb.op("dve", lambda h: h.tensor_copy(out=MU1[:, :, 0], in_=pwr[:, 16, :]), r=[pwr.res], w=[MU1.res])
        kb.op("dve", lambda h: h.tensor_copy(out=MU1[:, :, 1], in_=pwr[:, 16, :]), r=[pwr.res], w=[MU1.res])
        kb.op("dve", lambda h: h.tensor_copy(out=MU2[:, :, 0], in_=pwi[:, 16, :]), r=[pwi.res], w=[MU2.res])
        kb.op("dve", lambda h: h.tensor_copy(out=MU2[:, :, 1], in_=pwi[:, 16, :]), r=[pwi.res], w=[MU2.res])
        GBr, GBi = f32([64, 16, 16], "GBr"), f32([64, 16, 16], "GBi")
        CFr, CFi = f32([64, 16, 16], "CFr"), f32([64, 16, 16], "CFi")
        GSr, GSi = f32([64, 16, 16], "GSr"), f32([64, 16, 16], "GSi")
        Fr, Fi = f32([64, 16, 16], "Fr"), f32([64, 16, 16], "Fi")
        T4 = T1[:, 0:256].rearrange("p (s c) -> p s c", c=16)
        rpr, rpi = f32([64, 16, 32], "rpr"), f32([64, 16, 32], "rpi")
        for s_ in range(16):
            kb.op("dve", lambda h: h.tensor_copy(out=rpr[:, s_, :], in_=pwr[:, 15 - s_, :]), r=[pwr.res], w=[rpr.res])
            kb.op("dve", lambda h: h.tensor_copy(out=rpi[:, s_, :], in_=pwi[:, 15 - s_, :]), r=[pwi.res], w=[rpi.res])
        for g in range(32):
            pw_b = lambda t, lo=0: t[:, lo:lo + 16, g].unsqueeze(2).to_broadcast([64, 16, 16])
            v_b = lambda t: t[:, g, :].unsqueeze(1).to_broadcast([64, 16, 16])
            cmul(GBr[:, :, :], GBi[:, :, :], pw_b(npr), pw_b(npi), v_b(bbr), v_b(bbi), T4,
                 [npr.res, npi.res, bbr.res, bbi.res], [GBr.res, GBi.res])
            cmul(CFr[:, :, :], CFi[:, :, :], pw_b(pwr), pw_b(pwi), v_b(cre), v_b(cim), T4,
                 [pwr.res, pwi.res, cre.res, cim.res], [CFr.res, CFi.res])
            ts(CFi[:, :, :], CFi[:, :, :], -1.0, None, ALU.mult, None, [CFi.res], [CFi.res])
            cmul(GSr[:, :, :], GSi[:, :, :], pw_b(rpr), pw_b(rpi), v_b(bbr), v_b(bbi), T4,
                 [rpr.res, rpi.res, bbr.res, bbi.res], [GSr.res, GSi.res])
            cmul(Fr[:, :, :], Fi[:, :, :], pw_b(pwr, 1), pw_b(pwi, 1), v_b(cre), v_b(cim), T4,
                 [pwr.res, pwi.res, cre.res, cim.res], [Fr.res, Fi.res])
            kb.op("act", lambda h: h.activation(out=FFr[:, g, :], in_=Fr[:, :, :].rearrange("p t c -> p (t c)"), func=AF.Identity),
                  r=[Fr.res], w=[FFr.res])
            kb.op("act", lambda h: h.activation(out=nFFi[:, g, :], in_=Fi[:, :, :].rearrange("p t c -> p (t c)"), func=AF.Identity,
                                                scale=-1.0), r=[Fi.res], w=[nFFi.res])
            flat = lambda t: t[:, :, :].rearrange("p s c -> p (s c)")
            for m in range(2):
                ps = dn.nps()
                kb.op("pe", lambda h: h.matmul(ps[:, 0:256], flat(GBr)[:, m * 128:(m + 1) * 128], flat(CFr), start=True, stop=False),
                      r=[GBr.res, CFr.res], w=[ps.res])
                kb.op("pe", lambda h: h.matmul(ps[:, 0:256], flat(GBi)[:, m * 128:(m + 1) * 128], flat(CFi), start=False, stop=True),
                      r=[GBi.res, CFi.res], w=[ps.res])
                kb.op("dve", lambda h: h.tensor_tensor(out=TT[:, g, m, :], in0=ps[:, 0:256], in1=msk[:, m, :], op=ALU.mult),
                      r=[ps.res, msk.res], w=[TT.res])
            ps = dn.nps()
            for m in range(2):
                for ri_, src in enumerate((GSr, GSi)):
                    k_ = m * 2 + ri_
                    kb.op("pe", lambda h: h.transpose(out=ps[:, k_ * 64:(k_ + 1) * 64], in_=flat(src)[:, m * 128:(m + 1) * 128],
                                                      identity=idf[:, :]), r=[src.res, idf.res], w=[ps.res])
            kb.op("act", lambda h: h.activation(out=GG[:, g, :, :, :].rearrange("p m r n -> p (m r n)"), in_=ps[:, 0:256],
                                                func=AF.Identity), r=[ps.res], w=[GG.res])
        kb.barrier()
        cx.stack = old
    UY = cx.sb([128, 16 * 512], BF16, "UY")
    Ucm = VW(UY, lambda t: t[:, :].rearrange("p (s c) -> p s c", c=512))
    YT = VW(UY, lambda t: t[:, :].rearrange("p (k m s) -> p k m s", k=4, s=16))
    UT = cx.sb([128, 32, 2, 128], BF16, "UT")
    Eall = cx.sb([64, 32, 2, 129], F32, "Eall")
    Xb = cx.sb([64, 32, 2, 128], BF16, "Xb")
    Ycm = cx.sb([128, 16, 512], BF16, "Ycm")
    P1 = cx.sb([64, 32, 2], F32, "P1")
    P2 = cx.sb([64, 32, 2], F32, "P2")
    gtmp = cx.sb([128, 2048], F32, "gtmp")
    gx = cx.sb([128, 2048], F32, "gx")
    sg5 = [cx.sb([128, 512], BF16, "s5stg") for _ in range(2)]
    sgt = cx.sb([128, 512], F32, "s5sig")
    kb.op("dve", lambda h: h.memset(Eall[:, :, :, 0:1], 0.0), w=[Eall.res])
    si = 0
    for mt in range(NMT):
        tok0 = mt * M * 16
        kb.dma("sp", Ucm[0:M, :, :], C["u5"][tok0:tok0 + M * 16, :].rearrange("(m s) c -> m s c", s=16), w=[UY.res])
        Ugm = VW(Ycm, lambda t: t[:, :, :].rearrange("p s c -> p (s c)").rearrange("p (g s c) -> p g s c", g=32, s=16))
        kb.op("pool", lambda h: h.tensor_copy(out=Ugm[0:M, :, :, :], in_=Ucm[0:M, :, :].rearrange("m s (g c) -> m g s c", c=16)),
              r=[UY.res], w=[Ycm.res])
        for g0 in range(0, 32, 4):
            pb = dn.npb()
            for gg in range(4):
                for hh in range(2):
                    k_ = gg * 2 + hh
                    kb.op("pe", lambda h: h.transpose(out=pb[:, k_ * 128:k_ * 128 + M],
                                                      in_=Ugm[0:M, g0 + gg, hh * 8:(hh + 1) * 8, :].rearrange("m s c -> m (s c)"),
                                                      identity=ident[0:M, 0:M]), r=[Ycm.res, ident.res], w=[pb.res])
            copy_evac(kb, kb.ev(), UT[:, g0:g0 + 4, :, 0:M],
                      pb[:, :].rearrange("p (g h m) -> p g h m", g=4, h=2)[:, :, :, 0:M], r=[pb.res], w=[UT.res])
        for g0 in range(0, 32, 2):
            ps = dn.nps()
            for gg in range(2):
                g = g0 + gg
                for ri_ in range(2):
                    k_ = gg * 2 + ri_
                    for m in range(2):
                        kb.op("pe", lambda h: h.matmul(ps[0:64, k_ * 128:k_ * 128 + M], GG[:, g, m, ri_, :], UT[:, g, m, 0:M],
                                                       start=(k_ == 0 and m == 0), stop=(m == 1), skip_group_check=True),
                              r=[GG.res, UT.res], w=[ps.res])
            copy_evac(kb, kb.ev(), Eall[:, g0:g0 + 2, :, 1:1 + M],
                      ps[0:64, :].rearrange("p (g r m) -> p g r m", g=2, r=2)[:, :, :, 0:M], r=[ps.res], w=[Eall.res])
        for m in range(M):
            Xm = Eall[:, :, :, m]
            Xn = Eall[:, :, :, m + 1]
            kb.op("dve", lambda h: h.tensor_tensor(out=P1[:, :, :], in0=Xm, in1=MU1[:, :, :], op=ALU.mult),
                  r=[Eall.res, MU1.res], w=[P1.res])
            kb.op("pool", lambda h: h.tensor_tensor(out=P2[:, :, :], in0=Xm, in1=MU2[:, :, :], op=ALU.mult),
                  r=[Eall.res, MU2.res], w=[P2.res])
            kb.op("dve", lambda h: h.tensor_tensor(out=Xn, in0=Xn, in1=P1[:, :, :], op=ALU.add), r=[Eall.res, P1.res], w=[Eall.res])
            kb.op("dve", lambda h: h.tensor_tensor(out=Eall[:, :, 0, m + 1], in0=Eall[:, :, 0, m + 1], in1=P2[:, :, 1], op=ALU.subtract),
                  r=[Eall.res, P2.res], w=[Eall.res])
            kb.op("dve", lambda h: h.tensor_tensor(out=Eall[:, :, 1, m + 1], in0=Eall[:, :, 1, m + 1], in1=P2[:, :, 0], op=ALU.add),
                  r=[Eall.res, P2.res], w=[Eall.res])
        kb.op("act", lambda h: h.activation(out=Xb[:, :, :, 0:M], in_=Eall[:, :, :, 0:M], func=AF.Identity), r=[Eall.res], w=[Xb.res])
        for g in range(32):
            ps = dn.nps()
            kb.op("pe", lambda h: h.matmul(ps[0:M, 0:256], UT[:, g, 0, 0:M], TT[:, g, 0, :], start=True, stop=False),
                  r=[UT.res, TT.res], w=[ps.res])
            kb.op("pe", lambda h: h.matmul(ps[0:M, 0:256], UT[:, g, 1, 0:M], TT[:, g, 1, :], start=False, stop=False),
                  r=[UT.res, TT.res], w=[ps.res])
            kb.op("pe", lambda h: h.matmul(ps[0:M, 0:256], Xb[:, g, 0, 0:M], FFr[:, g, :], start=False, stop=False),
                  r=[Xb.res, FFr.res], w=[ps.res])
            kb.op("pe", lambda h: h.matmul(ps[0:M, 0:256], Xb[:, g, 1, 0:M], nFFi[:, g, :], start=False, stop=True),
                  r=[Xb.res, nFFi.res], w=[ps.res])
            copy_evac(kb, kb.ev(), Ycm[0:M, :, g * 16:(g + 1) * 16], ps[0:M, 0:256].rearrange("m (t c) -> m t c", c=16),
                      r=[ps.res], w=[Ycm.res])
        kb.op("dve", lambda h: h.tensor_copy(out=Eall[:, :, :, 0], in_=Eall[:, :, :, M]), r=[Eall.res], w=[Eall.res])
        for s0 in range(0, 16, 4):
            gxv = gx[:, :].rearrange("p (s c) -> p s c", c=512)
            kb.op("dve", lambda h: h.tensor_tensor(out=gxv[0:M], in0=Ucm[0:M, s0:s0 + 4, :],
                                                   in1=dB[0:M, :].unsqueeze(1).to_broadcast([M, 4, 512]), op=ALU.mult),
                  r=[UY.res, dB.res], w=[gx.res])
            kb.op("dve", lambda h: h.tensor_tensor(out=gxv[0:M], in0=gxv[0:M], in1=Ycm[0:M, s0:s0 + 4, :], op=ALU.add),
                  r=[gx.res, Ycm.res], w=[gx.res])
            gelu_tanh(kb, Ycm[0:M, s0:s0 + 4, :].rearrange("m s c -> m (s c)"), gx[0:M, :], gtmp[0:M, :], [gx.res], Ycm.res, gtmp.res)
        for s0 in range(0, 16, 2):
            pb = dn.npb()
            for ss_ in range(2):
                for cc in range(4):
                    k_ = ss_ * 4 + cc
                    kb.op("pe", lambda h: h.transpose(out=pb[:, k_ * 128:k_ * 128 + M], in_=Ycm[0:M, s0 + ss_, cc * 128:(cc + 1) * 128],
                                                      identity=ident[0:M, 0:M]), r=[Ycm.res, ident.res], w=[pb.res])
            copy_evac(kb, kb.ev(), YT[:, :, 0:M, s0:s0 + 2].rearrange("p k m s -> p s k m"),
                      pb[:, :].rearrange("p (s k m) -> p s k m", s=2, k=4)[:, :, :, 0:M], r=[pb.res], w=[UY.res])
        ntok = M * 16
        for tb in range(0, ntok, 512):
            nn = min(512, ntok - tb)
            for co in range(4):
                ps = dn.nps()
                for k in range(4):
                    kb.op("pe", lambda h: h.matmul(ps[:, 0:nn], gluW[:, k, co * 128:(co + 1) * 128],
                                                   YT[:, k, tb // 16:(tb + nn) // 16, :].rearrange("p m s -> p (m s)"),
                                                   start=(k == 0), stop=(k == 3)), r=[gluW.res, UY.res], w=[ps.res])
                kb.op("act", lambda h: h.activation(out=sgt[:, 0:nn], in_=ps[:, 0:nn], func=AF.Sigmoid, bias=glub[:, co:co + 1]),
                      r=[ps.res, glub.res], w=[sgt.res])
                sg = sg5[si % 2]
                si += 1
                kb.op("dve", lambda h: h.tensor_tensor(out=sg[:, 0:nn], in0=sgt[:, 0:nn],
                                                       in1=YT[:, co, tb // 16:(tb + nn) // 16, :].rearrange("p m s -> p (m s)"), op=ALU.mult),
                      r=[sgt.res, UY.res], w=[sg.res])
                kb.dma("act", C["yT"][1536 + co * 128:1536 + (co + 1) * 128, tok0 + tb:tok0 + tb + nn], sg[:, 0:nn], r=[sg.res])


class VW:
    def __init__(self, tl, f):
        self.tl, self.f, self.res = tl, f, tl.res

    def __getitem__(self, idx):
        return self.f(self.tl)[idx]


TB = 256


def rwkv_mixer(P, dn, l, C):
    kb, cx, S = P.kb, P.cx, P.S
    ident = C["ident"]
    NTB = S // TB
    f32 = lambda shp, n: cx.sb(shp, F32, n)
    b16 = lambda shp, n: cx.sb(shp, BF16, n)

    def tt(e, out, a, b, op, r, w):
        kb.op(e, lambda h: h.tensor_tensor(out=out, in0=a, in1=b, op=op), r=r, w=w)

    def pcol(name, lo, n, nm):
        t = f32([128, n // 128], nm)
        kb.dma("sp", t[:, :], C["in_" + name][l, lo:lo + n].rearrange("(c q) -> q c", q=128), w=[t.res], slow=True)
        return t

    mix_r, mix_k, mix_v = pcol("rw_mix", 0, 768, "mixr"), pcol("rw_mix", 768, 768, "mixk"), pcol("rw_mix", 1536, 768, "mixv")
    mix_g = pcol("rw_mix", 2496, 256, "mixg")
    mix_wa = f32([96, 2], "mixwa")
    kb.dma("sp", mix_wa[:, :], C["in_rw_mix"][l, 2304:2496].rearrange("(c q) -> q c", q=96), w=[mix_wa.res], slow=True)
    w0, a0 = pcol("rw_w0", 0, 768, "w0"), pcol("rw_a0", 0, 768, "a0")
    k_k, k_a = pcol("rw_k_k", 0, 768, "k_k"), pcol("rw_k_a", 0, 768, "k_a")
    r_k, ln_w, ln_b = pcol("rw_r_k", 0, 768, "r_k"), pcol("rw_ln_w", 0, 768, "ln_w"), pcol("rw_ln_b", 0, 768, "ln_b")
    omka = f32([128, 6], "omka")
    kb.op("dve", lambda h: h.tensor_scalar(out=omka[:, :], in0=k_a[:, :], scalar1=-1.0, scalar2=1.0, op0=ALU.mult, op1=ALU.add),
          r=[k_a.res], w=[omka.res])
    w2b, a2b, g2b = b16([96, 768], "w2b"), b16([96, 768], "a2b"), b16([128, 2, 768], "g2b")
    kb.dma("pool", w2b[:, :], C["in_rw_w2"][l], w=[w2b.res])
    kb.dma("pool", a2b[:, :], C["in_rw_a2"][l], w=[a2b.res])
    kb.dma("pool", g2b[:, :, :], C["in_rw_g2"][l].rearrange("(k q) c -> q k c", q=128), w=[g2b.res])
    MUs, MUi, MLs = f32([128, 128], "MUs"), f32([128, 128], "MUi"), f32([128, 128], "MLs")
    kb.dma("sp", MUs[:, :], C["rw_masks"][0], w=[MUs.res])
    kb.dma("sp", MUi[:, :], C["rw_masks"][1], w=[MUi.res])
    kb.dma("sp", MLs[:, :], C["rw_masks"][2], w=[MLs.res])
    MUsi = f32([128, 256], "MUsi")
    kb.dma("sp", MUsi[:, 0:128], C["rw_masks"][0], w=[MUsi.res])
    kb.dma("sp", MUsi[:, 128:256], C["rw_masks"][1], w=[MUsi.res])
    bones = b16([128, 128], "bones")
    kb.dma("sp", bones[:, :], C["bones"][:, :], w=[bones.res])
    cmk = f32([128, TB], "cmk")
    kb.dma("sp", cmk[:, :], C["chunkmask"][:, 0:TB], w=[cmk.res])
    zw = b16([96, 2, TB + 1], "zwa")
    zg = b16([128, 2, TB + 1], "zg")
    th, zab = b16([96, TB], "th"), b16([96, TB], "zab")
    sgg = b16([128, 2, TB], "sgg")
    ltmp = f32([128, 2, TB], "ltmp")
    Sst = [f32([128, 128], "Sst") for _ in range(6)]
    Sb = [b16([128, 128], "Sb") for _ in range(6)]
    for p in range(6):
        kb.op("pool", lambda h: h.memset(Sst[p][:, :], 0.0), w=[Sst[p].res])
        kb.op("pool", lambda h: h.memset(Sb[p][:, :], 0.0), w=[Sb[p].res])
    identb = ident


    def c3(ap):
        return ap.rearrange("p (c t) -> p c t", t=64)

    def shift(eng, out_ap, zt_ap, mixcol, npart, r, w, dtm):
        tt(eng, dtm[0:npart, :], zt_ap[:, 0:TB], zt_ap[:, 1:TB + 1], ALU.subtract, r, [dtm.res])
        kb.op("dve", lambda h: h.scalar_tensor_tensor(out=out_ap, in0=dtm[0:npart, :], scalar=mixcol, in1=zt_ap[:, 1:TB + 1],
                                                      op0=ALU.mult, op1=ALU.add), r=r + [dtm.res], w=w)

    def load_shifted(tile_ap, res, row0, nrows, t0):
        if t0 == 0:
            kb.op("pool", lambda h: h.memset(tile_ap[:, 0:1], 0.0), w=[res])
            kb.dma("sp", tile_ap[:, 1:TB + 1], C["zT"][row0:row0 + nrows, 0:TB], w=[res])
        else:
            kb.dma("sp", tile_ap[:, :], C["zT"][row0:row0 + nrows, t0 - 1:t0 + TB], w=[res])

    dtm0 = f32([128, TB], "dtm0")

    def make_unit():
        z3 = b16([128, 3, TB + 1], "zin")
        rs, ks, vs = f32([128, TB], "rs"), f32([128, TB], "ks"), f32([128, TB], "vs")
        dtm = f32([128, TB], "dtm")
        logw, alpha, gg_ = f32([128, TB], "logw"), f32([128, TB], "alpha"), f32([128, TB], "gfm")
        kk, kk2, rinv = f32([128, TB], "kk"), b16([128, TB], "kk2"), f32([128, TB], "rinv")
        kp, bq = f32([128, TB], "kp"), f32([128, TB], "bq")
        cc, ec, enc, ecm, ecc = f32([128, TB], "cc"), f32([128, TB], "ec"), f32([128, TB], "enc"), f32([128, TB], "ecm"), f32([128, TB], "ecc")
        gC = f32([128, 4], "gC")
        rkb = b16([128, TB], "rkb")
        bonus = f32([128, TB], "bonus")
        AQ = b16([128, 4, 256], "AQ")
        bblk, kblk, b2blk, k2blk, vblk = (b16([128, 4, 128], n) for n in ("bblk", "kblk", "b2blk", "k2blk", "vblk"))
        for t in (AQ, bblk, kblk, b2blk, k2blk, vblk):
            kb.op("pool", lambda h: h.memset(t[:, :, :], 0.0), w=[t.res])
        TM = b16([128, 4, 4, 128], "TM")
        Xn = [b16([128, 4, 128], "Xn") for _ in range(2)]
        Nn = [b16([128, 4, 128], "Nn") for _ in range(2)]
        Pm = b16([128, 4, 128], "Pm")
        ArbT, AakT, ArkT, AkV, WT = (b16([128, 4, 128], n) for n in ("ArbT", "AakT", "ArkT", "AkV", "WT"))
        U = f32([128, 4, 128], "U")
        Ytm = f32([128, 4, 128], "Ytm")
        ysq = f32([128, 4, 128], "ysq")
        ynb = b16([128, 4, 128], "ynb")
        st1, st2, st3 = f32([128, 4], "st1"), f32([128, 4], "st2"), f32([128, 4], "st3")
        yfm = f32([128, TB], "yfm")
        SAb = b16([128, 128], "SAb")
        ystg = [b16([128, TB], "ystg") for _ in range(2)]
        sic = [0]

        def unit(tb, p):
            t0 = tb * TB
            pc = slice(p * 128, (p + 1) * 128)
            for i, row0 in enumerate((p * 128, 768 + p * 128, 1536 + p * 128)):
                load_shifted(z3[:, i, :], z3.res, row0, 128, t0)
            shift("pool", rs[:, :], z3[:, 0, :], mix_r[:, p:p + 1], 128, [z3.res, mix_r.res], [rs.res], dtm)
            shift("pool", ks[:, :], z3[:, 1, :], mix_k[:, p:p + 1], 128, [z3.res, mix_k.res], [ks.res], dtm)
            shift("pool", vs[:, :], z3[:, 2, :], mix_v[:, p:p + 1], 128, [z3.res, mix_v.res], [vs.res], dtm)
            yield
            ps = dn.nps()
            kb.op("pe", lambda h: h.matmul(ps[:, 0:TB], w2b[:, pc], th[:, :], start=True, stop=True), r=[w2b.res, th.res], w=[ps.res])
            kb.op("act", lambda h: h.activation(out=logw[:, :], in_=ps[:, 0:TB], func=AF.Sigmoid, bias=w0[:, p:p + 1]),
                  r=[ps.res, w0.res], w=[logw.res])
            kb.op("dve", lambda h: h.tensor_scalar(out=logw[:, :], in0=logw[:, :], scalar1=-math.exp(-0.5), scalar2=None, op0=ALU.mult),
                  r=[logw.res], w=[logw.res])
            ps = dn.nps()
            kb.op("pe", lambda h: h.matmul(ps[:, 0:TB], a2b[:, pc], zab[:, :], start=True, stop=True), r=[a2b.res, zab.res], w=[ps.res])
            kb.op("act", lambda h: h.activation(out=alpha[:, :], in_=ps[:, 0:TB], func=AF.Sigmoid, bias=a0[:, p:p + 1]),
                  r=[ps.res, a0.res], w=[alpha.res])
            ps = dn.nps()
            for k in range(2):
                kb.op("pe", lambda h: h.matmul(ps[:, 0:TB], g2b[:, k, pc], sgg[:, k, :], start=(k == 0), stop=(k == 1)),
                      r=[g2b.res, sgg.res], w=[ps.res])
            kb.op("act", lambda h: h.activation(out=gg_[:, :], in_=ps[:, 0:TB], func=AF.Identity), r=[ps.res], w=[gg_.res])
            yield
            kb.op("dve", lambda h: h.tensor_scalar(out=kk[:, :], in0=ks[:, :], scalar1=k_k[:, p:p + 1], scalar2=None, op0=ALU.mult),
                  r=[ks.res, k_k.res], w=[kk.res])
            tt("pool", kk2[:, :], kk[:, :], kk[:, :], ALU.mult, [kk.res], [kk2.res])
            yield
            ps = dn.nps()
            kb.op("pe", lambda h: h.matmul(ps[:, 0:TB], bones[:, :], kk2[:, :], start=True, stop=True), r=[bones.res, kk2.res], w=[ps.res])
            kb.op("act", lambda h: h.activation(out=rinv[:, :], in_=ps[:, 0:TB], func=AF.Sqrt), r=[ps.res], w=[rinv.res])
            kb.op("dve", lambda h: h.tensor_scalar(out=rinv[:, :], in0=rinv[:, :], scalar1=1e-12, scalar2=None, op0=ALU.max),
                  r=[rinv.res], w=[rinv.res])
            kb.op("dve", lambda h: h.reciprocal(out=rinv[:, :], in_=rinv[:, :]), r=[rinv.res], w=[rinv.res])
            tt("dve", kk[:, :], kk[:, :], rinv[:, :], ALU.mult, [kk.res, rinv.res], [kk.res])
            yield
            yield
            kb.op("dve", lambda h: h.tensor_scalar(out=kp[:, :], in0=alpha[:, :], scalar1=k_a[:, p:p + 1], scalar2=omka[:, p:p + 1],
                                                   op0=ALU.mult, op1=ALU.add), r=[alpha.res, k_a.res, omka.res], w=[kp.res])
            tt("dve", kp[:, :], kp[:, :], ks[:, :], ALU.mult, [kp.res, ks.res], [kp.res])
            yield
            tt("pool", bq[:, :], kk[:, :], alpha[:, :], ALU.mult, [kk.res, alpha.res], [bq.res])
            yield
            yield
            kb.op("dve", lambda h: h.tensor_tensor_scan(out=cc[:, :], data0=cmk[:, :], data1=logw[:, :], initial=0.0,
                                                        op0=ALU.mult, op1=ALU.add), r=[cmk.res, logw.res], w=[cc.res])
            kb.op("act", lambda h: h.activation(out=ec[:, :], in_=cc[:, :], func=AF.Exp), r=[cc.res], w=[ec.res])
            kb.op("act", lambda h: h.activation(out=enc[:, :], in_=cc[:, :], func=AF.Exp, scale=-1.0), r=[cc.res], w=[enc.res])
            tt("pool", ecm[:, :], cc[:, :], logw[:, :], ALU.subtract, [cc.res, logw.res], [ecm.res])
            yield
            kb.op("act", lambda h: h.activation(out=ecm[:, :], in_=ecm[:, :], func=AF.Exp), r=[ecm.res], w=[ecm.res])
            kb.op("act", lambda h: h.activation(out=gC[:, :], in_=c3(cc[:, :])[:, :, 63], func=AF.Exp), r=[cc.res], w=[gC.res])
            tt("dve", c3(ecc[:, :]), c3(cc[:, :])[:, :, 63:64].to_broadcast([128, 4, 64]), c3(cc[:, :]), ALU.subtract,
               [cc.res], [ecc.res])
            kb.op("act", lambda h: h.activation(out=ecc[:, :], in_=ecc[:, :], func=AF.Exp), r=[ecc.res], w=[ecc.res])
            yield
            kb.op("dve", lambda h: h.scalar_tensor_tensor(out=rkb[:, :], in0=rs[:, :], scalar=r_k[:, p:p + 1], in1=kp[:, :],
                                                          op0=ALU.mult, op1=ALU.mult), r=[rs.res, r_k.res, kp.res], w=[rkb.res])
            ps = dn.nps()
            kb.op("pe", lambda h: h.matmul(ps[:, 0:TB], bones[:, :], rkb[:, :], start=True, stop=True), r=[bones.res, rkb.res], w=[ps.res])
            tt("dve", bonus[:, :], ps[:, 0:TB], vs[:, :], ALU.mult, [ps.res, vs.res], [bonus.res])
            yield
            yield
            engs = ["dve", "pool"]
            ei = 0
            for hh in range(2):
                lo = hh * 64
                sl_ = slice(lo, lo + 64)

                def blk(dst, colofs, a, b, ra, rb, neg=False):
                    nonlocal ei
                    e = engs[ei % 2]
                    ei += 1
                    o = dst[sl_, :, colofs + lo:colofs + lo + 64]
                    if neg:
                        kb.op("dve", lambda h: h.scalar_tensor_tensor(out=o, in0=c3(a[sl_, :]), scalar=-1.0, in1=c3(b[sl_, :]),
                                                                      op0=ALU.mult, op1=ALU.mult), r=[ra, rb], w=[dst.res])
                    elif b is None:
                        kb.op(e, lambda h: h.tensor_copy(out=o, in_=c3(a[sl_, :])), r=[ra], w=[dst.res])
                    else:
                        tt(e, o, c3(a[sl_, :]), c3(b[sl_, :]), ALU.mult, [ra, rb], [dst.res])

                blk(AQ, 0, kk, ecm, kk.res, ecm.res, neg=True)
                blk(AQ, 128, rs, ec, rs.res, ec.res)
                blk(bblk, 0, bq, enc, bq.res, enc.res)
                blk(kblk, 0, kp, enc, kp.res, enc.res)
                blk(b2blk, 0, bq, ecc, bq.res, ecc.res)
                blk(k2blk, 0, kp, ecc, kp.res, ecc.res)
                blk(vblk, 0, vs, None, vs.res, None)
            yield
            srcs = [(AQ, 0), (b2blk, 0), (k2blk, 0), (vblk, 0)]
            for half in range(2):
                pb = dn.npb()
                for qi in range(2):
                    src, co = srcs[half * 2 + qi]
                    for ch in range(4):
                        k_ = qi * 4 + ch
                        kb.op("pe", lambda h: h.transpose(out=pb[:, k_ * 128:(k_ + 1) * 128], in_=src[:, ch, co:co + 128],
                                                          identity=identb[:, :]), r=[src.res, identb.res], w=[pb.res])
                copy_evac(kb, kb.ev(), TM[:, half * 2:half * 2 + 2, :, :].rearrange("p q c m -> p (q c m)"), pb[:, :],
                          r=[pb.res], w=[TM.res])
            yield
            ps = dn.nps()
            for ch in range(4):
                kb.op("pe", lambda h: h.matmul(ps[:, ch * 128:(ch + 1) * 128], AQ[:, ch, 0:128], bblk[:, ch, :],
                                               start=(ch == 0), stop=True, skip_group_check=True), r=[AQ.res, bblk.res], w=[ps.res])
            X0, N0 = Xn[0], Nn[0]
            tt("dve", N0[:, :, :], ps[:, :].rearrange("p (c m) -> p c m", m=128), MLs[:, :].unsqueeze(1).to_broadcast([128, 4, 128]),
               ALU.mult, [ps.res, MLs.res], [N0.res])
            for (lhs, dA, dB_) in ((bblk, X0, ArbT), (kblk, AakT, ArkT)):
                for half in range(2):
                    ps = dn.nps()
                    for c2 in range(2):
                        ch = half * 2 + c2
                        kb.op("pe", lambda h: h.matmul(ps[:, c2 * 256:(c2 + 1) * 256], lhs[:, ch, :], AQ[:, ch, :],
                                                       start=(c2 == 0), stop=True, skip_group_check=True), r=[lhs.res, AQ.res], w=[ps.res])
                    pv = ps[:, :].rearrange("p (c m) -> p c m", m=256)
                    tt("dve", dA[:, half * 2:half * 2 + 2, :], pv[:, :, 0:128], MUs[:, :].unsqueeze(1).to_broadcast([128, 2, 128]),
                       ALU.mult, [ps.res, MUs.res], [dA.res])
                    tt("dve", dB_[:, half * 2:half * 2 + 2, :], pv[:, :, 128:256], MUi[:, :].unsqueeze(1).to_broadcast([128, 2, 128]),
                       ALU.mult, [ps.res, MUi.res], [dB_.res])
            yield
            tt("pool", Pm[:, :, :], X0[:, :, :], identb[:, :].unsqueeze(1).to_broadcast([128, 4, 128]), ALU.add,
               [X0.res, identb.res], [Pm.res])
            Xc, Nc = X0, N0
            for j in range(1, 6):
                Xnx, Nnx = Xn[j % 2], Nn[j % 2]
                if j < 5:
                    ps = dn.nps()
                    for ch in range(4):
                        kb.op("pe", lambda h: h.matmul(ps[:, ch * 128:(ch + 1) * 128], Nc[:, ch, :], Xc[:, ch, :],
                                                       start=(ch == 0), stop=True, skip_group_check=True), r=[Nc.res, Xc.res], w=[ps.res])
                    copy_evac(kb, "act", Xnx[:, :, :].rearrange("p c m -> p (c m)"), ps[:, :], r=[ps.res], w=[Xnx.res])
                ps = dn.nps()
                for ch in range(4):
                    kb.op("pe", lambda h: h.matmul(ps[:, ch * 128:(ch + 1) * 128], Xc[:, ch, :], Nc[:, ch, :],
                                                   start=(ch == 0), stop=True, skip_group_check=True), r=[Nc.res, Xc.res], w=[ps.res])
                copy_evac(kb, "dve", Nnx[:, :, :].rearrange("p c m -> p (c m)"), ps[:, :], r=[ps.res], w=[Nnx.res])
                ps = dn.nps()
                for ch in range(4):
                    kb.op("pe", lambda h: h.matmul(ps[:, ch * 128:(ch + 1) * 128], Nnx[:, ch, :], Pm[:, ch, :],
                                                   start=(ch == 0), stop=True, skip_group_check=True), r=[Nnx.res, Pm.res], w=[ps.res])
                tt("dve", Pm[:, :, :].rearrange("p c m -> p (c m)"), ps[:, :], Pm[:, :, :].rearrange("p c m -> p (c m)"), ALU.add,
                   [ps.res, Pm.res], [Pm.res])
                Xc, Nc = Xnx, Nnx
                yield
            yield
            ps = dn.nps()
            for ch in range(4):
                kb.op("pe", lambda h: h.matmul(ps[:, ch * 128:(ch + 1) * 128], AakT[:, ch, :], TM[:, 3, ch, :],
                                               start=(ch == 0), stop=True, skip_group_check=True), r=[AakT.res, TM.res], w=[ps.res])
            copy_evac(kb, "act", AkV[:, :, :].rearrange("p c m -> p (c m)"), ps[:, :], r=[ps.res], w=[AkV.res])
            yield
            ps = dn.nps()
            for ch in range(4):
                kb.op("pe", lambda h: h.matmul(ps[:, ch * 128:(ch + 1) * 128], Pm[:, ch, :], AkV[:, ch, :],
                                               start=(ch == 0), stop=True, skip_group_check=True), r=[Pm.res, AkV.res], w=[ps.res])
            copy_evac(kb, "dve", U[:, :, :].rearrange("p c m -> p (c m)"), ps[:, :], r=[ps.res], w=[U.res])
            yield
            ps = dn.nps()
            for ch in range(4):
                kb.op("pe", lambda h: h.matmul(ps[:, ch * 128:(ch + 1) * 128], TM[:, 0, ch, :], Pm[:, ch, :],
                                               start=(ch == 0), stop=True, skip_group_check=True), r=[TM.res, Pm.res], w=[ps.res])
            copy_evac(kb, "act", WT[:, :, :].rearrange("p c m -> p (c m)"), ps[:, :], r=[ps.res], w=[WT.res])
            yield
            yield
            S_, Sb_ = Sst[p], Sb[p]
            for ch in range(4):
                ps = dn.nps()
                kb.op("pe", lambda h: h.matmul(ps[:, 0:128], WT[:, ch, :], Sb_[:, :], start=True, stop=True), r=[WT.res, Sb_.res], w=[ps.res])
                tt("dve", SAb[:, :], ps[:, 0:128], U[:, ch, :], ALU.add, [ps.res, U.res], [SAb.res])
                yield
                psy = dn.nps()
                kb.op("pe", lambda h: h.matmul(psy[:, 0:128], AQ[:, ch, 128:256], Sb_[:, :], start=True, stop=False),
                      r=[AQ.res, Sb_.res], w=[psy.res])
                kb.op("pe", lambda h: h.matmul(psy[:, 0:128], ArbT[:, ch, :], SAb[:, :], start=False, stop=False),
                      r=[ArbT.res, SAb.res], w=[psy.res])
                kb.op("pe", lambda h: h.matmul(psy[:, 0:128], ArkT[:, ch, :], TM[:, 3, ch, :], start=False, stop=True),
                      r=[ArkT.res, TM.res], w=[psy.res])
                copy_evac(kb, "act", Ytm[:, ch, :], psy[:, 0:128], r=[psy.res], w=[Ytm.res])
                pss = dn.nps()
                kb.op("pe", lambda h: h.matmul(pss[:, 0:128], TM[:, 1, ch, :], SAb[:, :], start=True, stop=False),
                      r=[TM.res, SAb.res], w=[pss.res])
                kb.op("pe", lambda h: h.matmul(pss[:, 0:128], TM[:, 2, ch, :], TM[:, 3, ch, :], start=False, stop=True),
                      r=[TM.res], w=[pss.res])
                kb.op("dve", lambda h: h.scalar_tensor_tensor(out=S_[:, :], in0=S_[:, :], scalar=gC[:, ch:ch + 1], in1=pss[:, 0:128],
                                                              op0=ALU.mult, op1=ALU.add), r=[S_.res, gC.res, pss.res], w=[S_.res])
                kb.op("act", lambda h: h.activation(out=Sb_[:, :], in_=S_[:, :], func=AF.Identity), r=[S_.res], w=[Sb_.res])
                yield
            yield
            kb.op("dve", lambda h: h.tensor_reduce(out=st1[:, :], in_=Ytm[:, :, :], axis=AX.X, op=ALU.add), r=[Ytm.res], w=[st1.res])
            kb.op("act", lambda h: h.activation(out=ysq[:, :, :], in_=Ytm[:, :, :], func=AF.Square), r=[Ytm.res], w=[ysq.res])
            kb.op("dve", lambda h: h.tensor_reduce(out=st2[:, :], in_=ysq[:, :, :], axis=AX.X, op=ALU.add), r=[ysq.res], w=[st2.res])
            kb.op("dve", lambda h: h.tensor_scalar(out=st1[:, :], in0=st1[:, :], scalar1=1.0 / 64, scalar2=None, op0=ALU.mult),
                  r=[st1.res], w=[st1.res])
            tt("dve", st3[:, :], st1[:, :], st1[:, :], ALU.mult, [st1.res], [st3.res])
            yield
            kb.op("dve", lambda h: h.scalar_tensor_tensor(out=st2[:, :], in0=st2[:, :], scalar=1.0 / 64, in1=st3[:, :],
                                                          op0=ALU.mult, op1=ALU.subtract), r=[st2.res, st3.res], w=[st2.res])
            kb.op("act", lambda h: h.activation(out=st2[:, :], in_=st2[:, :], func=AF.Sqrt, bias=GN_EPS), r=[st2.res], w=[st2.res])
            kb.op("dve", lambda h: h.reciprocal(out=st2[:, :], in_=st2[:, :]), r=[st2.res], w=[st2.res])
            pb = dn.npb()
            for ch in range(4):
                kb.op("dve", lambda h: h.tensor_scalar(out=ynb[:, ch, :], in0=Ytm[:, ch, :], scalar1=st1[:, ch:ch + 1],
                                                       scalar2=st2[:, ch:ch + 1], op0=ALU.subtract, op1=ALU.mult),
                      r=[Ytm.res, st1.res, st2.res], w=[ynb.res])
                kb.op("pe", lambda h: h.transpose(out=pb[:, ch * 128:(ch + 1) * 128], in_=ynb[:, ch, :], identity=identb[:, :]),
                      r=[ynb.res, identb.res], w=[pb.res])
            pbv = pb[:, 0:512].rearrange("p (c m) -> p c m", m=128)
            kb.op("act", lambda h: h.activation(out=c3(yfm[0:64, :]), in_=pbv[0:64, :, 0:64], func=AF.Identity), r=[pb.res], w=[yfm.res])
            kb.op("act", lambda h: h.activation(out=c3(yfm[64:128, :]), in_=pbv[64:128, :, 64:128], func=AF.Identity), r=[pb.res], w=[yfm.res])
            kb.op("dve", lambda h: h.tensor_scalar(out=yfm[:, :], in0=yfm[:, :], scalar1=ln_w[:, p:p + 1], scalar2=ln_b[:, p:p + 1],
                                                   op0=ALU.mult, op1=ALU.add), r=[yfm.res, ln_w.res, ln_b.res], w=[yfm.res])
            tt("pool", yfm[:, :], yfm[:, :], bonus[:, :], ALU.add, [yfm.res, bonus.res], [yfm.res])
            yield
            sg = ystg[sic[0] % 2]
            sic[0] += 1
            tt("dve", sg[:, :], yfm[:, :], gg_[:, :], ALU.mult, [yfm.res, gg_.res], [sg.res])
            yield
            kb.dma("act", C["yT"][p * 128:(p + 1) * 128, t0:t0 + TB], sg[:, :], r=[sg.res])

            yield

        return unit

    units = [make_unit(), make_unit()]
    for tb in range(NTB):
        t0 = tb * TB
        load_shifted(zw[:, 0, :], zw.res, 2304, 96, t0)
        load_shifted(zw[:, 1, :], zw.res, 2400, 96, t0)
        load_shifted(zg[:, 0, :], zg.res, 2496, 128, t0)
        load_shifted(zg[:, 1, :], zg.res, 2624, 128, t0)
        shift("dve", ltmp[0:96, 0, :], zw[:, 0, :], mix_wa[:, 0:1], 96, [zw.res, mix_wa.res], [ltmp.res], dtm0)
        kb.op("act", lambda h: h.activation(out=th[:, :], in_=ltmp[0:96, 0, :], func=AF.Tanh), r=[ltmp.res], w=[th.res])
        shift("dve", ltmp[0:96, 1, :], zw[:, 1, :], mix_wa[:, 1:2], 96, [zw.res, mix_wa.res], [ltmp.res], dtm0)
        kb.op("act", lambda h: h.activation(out=zab[:, :], in_=ltmp[0:96, 1, :], func=AF.Identity), r=[ltmp.res], w=[zab.res])
        for k in range(2):
            shift("dve", ltmp[:, k, :], zg[:, k, :], mix_g[:, k:k + 1], 128, [zg.res, mix_g.res], [ltmp.res], dtm0)
            kb.op("act", lambda h: h.activation(out=sgg[:, k, :], in_=ltmp[:, k, :], func=AF.Sigmoid), r=[ltmp.res], w=[sgg.res])
        for pp in range(0, 6, 2):
            gens = [units[0](tb, pp), units[1](tb, pp + 1)]
            while gens:
                for g_ in list(gens):
                    try:
                        next(g_)
                    except StopIteration:
                        gens.remove(g_)
```

```python
import math
from contextlib import ExitStack
import numpy as np
import ml_dtypes
import concourse.bass as bass
import concourse.mybir as mybir
from concourse.bass_utils import run_bass_kernel_spmd

F32 = mybir.dt.float32
BF16 = mybir.dt.bfloat16
I32 = mybir.dt.int32
AF = mybir.ActivationFunctionType
ALU = mybir.AluOpType
AX = mybir.AxisListType

D = 2048
DEPTH = 4
A_DIM = 768
RW_IN = 2752
B_IN = 2304
C_DIM = 512
N_IN = 11712
D_FF = 5632
EPS = 1e-6
GN_EPS = 64e-5
ZROWS = 4288
TS = 512


class Res:
    __slots__ = ("name", "w", "r", "ds")

    def __init__(self, name):
        self.name = name
        self.w = []
        self.r = []
        self.ds = None


class Sem:
    __slots__ = ("h", "cnt", "dma")

    def __init__(self, h, dma):
        self.h = h
        self.cnt = 0
        self.dma = dma


class KB:
    def __init__(self, nc):
        self.nc = nc
        self.E = {"pe": nc.tensor, "act": nc.scalar, "dve": nc.vector, "pool": nc.gpsimd, "sp": nc.sync}
        self.sems = []
        self.eidx = {}
        for e in self.E:
            self.eidx[e] = len(self.sems)
            self.sems.append(Sem(nc.alloc_semaphore("q_" + e), False))
        self.seen = {e: {} for e in self.E}
        self.rr = 0
        self.nops = 0

    def _wait(self, e, evs):
        need = {}
        for k, v in evs:
            S = self.sems[k]
            if S.dma:
                v = S.cnt
            if v > need.get(k, 0):
                need[k] = v
        own = self.eidx[e]
        for k, v in need.items():
            if k == own and e == "pe":
                continue
            if self.seen[e].get(k, 0) < v:
                self.E[e].wait_ge(self.sems[k].h, v)
                self.seen[e][k] = v

    def _deps(self, r, w):
        evs = []
        for x in r:
            evs += x.w
        for x in w:
            evs += x.w
            evs += x.r
        return evs

    def _mark(self, ev, r, w):
        for x in r:
            x.r = [p for p in x.r if p[0] != ev[0]] + [ev]
        for x in w:
            x.w = [ev]
            x.r = []

    def op(self, e, fn, r=(), w=(), inc=True):
        self._wait(e, self._deps(r, w))
        ins = fn(self.E[e])
        k = self.eidx[e]
        if inc:
            self.sems[k].cnt += 1
            ins.then_inc(self.sems[k].h, 1)
            self._mark((k, self.sems[k].cnt), r, w)
        else:
            self._mark((k, self.sems[k].cnt + 1), r, w)
        self.nops += 1

    NDMA = 64

    def dsem(self, res):
        if res.ds is None:
            if not hasattr(self, "dpool"):
                self.dpool = []
                self.dnext = 0
            if len(self.dpool) < self.NDMA:
                self.dpool.append(len(self.sems))
                self.sems.append(Sem(self.nc.alloc_semaphore("d%d" % len(self.sems)), True))
                res.ds = self.dpool[-1]
            else:
                res.ds = self.dpool[self.dnext % self.NDMA]
                self.dnext += 1
        return res.ds

    def dma(self, e, out, in_, r=(), w=(), sres=None, slow=False):
        self._wait(e, self._deps(r, w))
        if sres is None:
            sres = w[0] if w else r[0]
        k = self.dsem(sres)
        if slow:
            ins = self.E[e].dma_start(out=out, in_=in_, allow_slow_non_contiguous=True)
        else:
            ins = self.E[e].dma_start(out=out, in_=in_)
        self.sems[k].cnt += 16
        ins.then_inc(self.sems[k].h, 16)
        self._mark((k, self.sems[k].cnt), r, w)
        self.nops += 1

    def barrier(self):
        evs = [(k, S.cnt) for k, S in enumerate(self.sems) if S.cnt > 0]
        for e in self.E:
            self._wait(e, evs)

    def ev(self):
        self.rr ^= 1
        return "act" if self.rr else "dve"


class Tl:
    def __init__(self, t, name):
        self.t = t
        self.res = Res(name)

    def __getitem__(self, idx):
        return self.t[idx]


class Ctx:
    def __init__(self, nc, kb):
        self.nc = nc
        self.kb = kb
        self.n = 0
        self.stack = None

    def sb(self, shape, dt, name=None):
        self.n += 1
        name = (name or "t") + "_%d" % self.n
        if self.stack is not None:
            t = self.stack.enter_context(self.nc.sbuf_tensor(name, list(shape), dt))
        else:
            t = self.nc.alloc_sbuf_tensor(name, list(shape), dt)
        return Tl(t, name)

    def ps(self, shape, dt, name=None):
        self.n += 1
        name = (name or "p") + "_%d" % self.n
        if self.stack is not None:
            t = self.stack.enter_context(self.nc.psum_tensor(name, list(shape), dt))
        else:
            t = self.nc.alloc_psum_tensor(name, list(shape), dt)
        return Tl(t, name)


def copy_evac(kb, eng, out_ap, in_ap, r, w, scale=None, bias=None, func=None):
    if eng == "act" or func is not None or scale is not None or bias is not None:
        kw = {}
        if scale is not None:
            kw["scale"] = scale
        if bias is not None:
            kw["bias"] = bias
        f = func if func is not None else AF.Identity
        kb.op("act", lambda h: h.activation(out=out_ap, in_=in_ap, func=f, **kw), r=r, w=w)
    else:
        kb.op(eng, lambda h: h.tensor_copy(out=out_ap, in_=in_ap), r=r, w=w)


class Prog:
    def __init__(self, S, depth, debug=False):
        self.S = S
        self.depth = depth
        self.debug = debug
        self.nc = bass.Bass("TRN2", target_bir_lowering=False)
        self.kb = KB(self.nc)
        self.cx = Ctx(self.nc, self.kb)
        self.dram = {}

    def din(self, name, shape, dt=F32):
        t = self.nc.dram_tensor(name, list(shape), dt, kind="ExternalInput")
        self.dram[name] = t
        return t.ap()

    def dscr(self, name, shape, dt, out=False):
        kind = "ExternalOutput" if (out or name in getattr(self, "dbg_out", ())) else "Internal"
        if name in getattr(self, "ext_in", ()):
            kind = "ExternalInput"
        t = self.nc.dram_tensor(name, list(shape), dt, kind=kind)
        self.dram[name] = t
        return t.ap()


PARAM_SHAPES = {
    "norm_mix": (DEPTH, D), "norm_ffn": (DEPTH, D), "w_in": (DEPTH, D, N_IN), "b_gate": (DEPTH, 3 * D),
    "rw_mix": (DEPTH, RW_IN), "rw_w0": (DEPTH, A_DIM), "rw_w2": (DEPTH, 96, A_DIM), "rw_a0": (DEPTH, A_DIM),
    "rw_a2": (DEPTH, 96, A_DIM), "rw_g2": (DEPTH, 256, A_DIM), "rw_k_k": (DEPTH, A_DIM), "rw_k_a": (DEPTH, A_DIM),
    "rw_r_k": (DEPTH, A_DIM), "rw_ln_w": (DEPTH, A_DIM), "rw_ln_b": (DEPTH, A_DIM),
    "da_lq1": (DEPTH, 64), "da_lk1": (DEPTH, 64), "da_lq2": (DEPTH, 64), "da_lk2": (DEPTH, 64), "da_subln": (DEPTH, 128),
    "s5_lam_re": (DEPTH, 32, 64), "s5_lam_im": (DEPTH, 32, 64), "s5_log_dt": (DEPTH, 32),
    "s5_b_re": (DEPTH, 32, 64, 16), "s5_b_im": (DEPTH, 32, 64, 16), "s5_c_re": (DEPTH, 32, 16, 64), "s5_c_im": (DEPTH, 32, 16, 64),
    "s5_d": (DEPTH, 512), "s5_glu_w": (DEPTH, 512, 512), "s5_glu_b": (DEPTH, 512),
    "proj_a": (DEPTH, A_DIM, D), "proj_b": (DEPTH, A_DIM, D), "proj_c": (DEPTH, C_DIM, D), "w_out": (DEPTH, D, D),
    "ffn_up": (DEPTH, D, 2 * D_FF), "ffn_conv": (DEPTH, 3, D_FF), "ffn_down": (DEPTH, D_FF, D), "norm_final": (D,),
}


def units_p1():
    u = []
    for c in range(0, 2304, 128):
        u.append((c, 128))
    u += [(2304, 96), (2400, 96), (2496, 128), (2624, 128)]
    for c in range(2752, 4288, 128):
        u.append((c, 128))
    return u


def group_blocks(units, maxc=512):
    blocks = []
    cur = []
    for (c, m) in units:
        if cur and (c + m - cur[0][0] > maxc or c != cur[-1][0] + cur[-1][1]):
            blocks.append(cur)
            cur = []
        cur.append((c, m))
    if cur:
        blocks.append(cur)
    return blocks


class Dense:
    def __init__(self, P):
        self.P = P
        cx, kb = P.cx, P.kb
        self.psf = [cx.ps([128, 512], F32, "psf") for _ in range(6)]
        self.psb = [cx.ps([128, 1024], BF16, "psb") for _ in range(2)]
        self.ipf = 0
        self.ipb = 0

    def nps(self, n=None):
        n = n or len(self.psf)
        self.ipf = (self.ipf + 1) % n
        return self.psf[self.ipf]

    def npb(self):
        self.ipb = (self.ipb + 1) % len(self.psb)
        return self.psb[self.ipb]


def rmsnorm_T(P, dn, xt, gcol, hT, junk, ss, rs, xn, ident):
    kb = P.kb
    for j in range(4):
        kb.op("act", lambda h: h.activation(out=junk[:, :], in_=xt[:, j, :], func=AF.Square,
                                            accum_out=ss[:, j:j + 1]), r=[xt.res], w=[junk.res, ss.res])
    kb.op("act", lambda h: h.activation(out=rs[:, 0:4], in_=ss[:, 0:4], func=AF.Sqrt, scale=1.0 / D, bias=EPS),
          r=[ss.res], w=[rs.res])
    kb.op("dve", lambda h: h.reciprocal(out=rs[:, 0:4], in_=rs[:, 0:4]), r=[rs.res], w=[rs.res])
    for j in range(4):
        kb.op("dve", lambda h: h.tensor_scalar(out=xn[:, j, :], in0=xt[:, j, :], scalar1=rs[:, j:j + 1],
                                               scalar2=None, op0=ALU.mult), r=[xt.res, rs.res], w=[xn.res])
    for c in range(16):
        pb = dn.npb()
        for j in range(4):
            kb.op("pe", lambda h: h.transpose(out=pb[:, j * 128:(j + 1) * 128], in_=xn[:, j, c * 128:(c + 1) * 128],
                                              identity=ident[:, :]), r=[xn.res, ident.res], w=[pb.res])
        kb.op("act", lambda h: h.activation(out=hT[:, c, :], in_=pb[:, 0:512], func=AF.Identity,
                                            scale=gcol[:, c:c + 1]), r=[pb.res, gcol.res], w=[hT.res])


def phase1(P, dn, l, C):
    kb, cx, S = P.kb, P.cx, P.S
    x_src = C["x"] if l == 0 else C["xres"]
    gcol, bgcol, ident = C["gmix"], C["bgate"], C["ident"]
    xt = C["xt"]
    hT = C["hT"]
    wb = C["wblk"]
    st = C["stage"]
    stm = C["stage_tm"]
    ublocks = group_blocks(units_p1())
    wi = 0
    si = 0
    for it in range(S // TS):
        t0 = it * TS
        kb.dma("sp", xt[:, :, :], x_src[t0:t0 + TS, :].rearrange("(j p) d -> p j d", p=128), w=[xt.res])
        rmsnorm_T(P, dn, xt, gcol[l], hT, C["junk"], C["ss"], C["rs"], C["xn"], ident)
        for blk in ublocks:
            c0 = blk[0][0]
            ncols = blk[-1][0] + blk[-1][1] - c0
            w = wb[wi % len(wb)]
            wi += 1
            kb.dma("sp", w[:, :, 0:ncols], C["w_in_b"][l, :, c0:c0 + ncols].rearrange("(k p) c -> p k c", p=128),
                   w=[w.res])
            for (c, m) in blk:
                ps = dn.nps()
                for k in range(16):
                    kb.op("pe", lambda h: h.matmul(ps[0:m, :], w[:, k, c - c0:c - c0 + m], hT[:, k, :],
                                                   start=(k == 0), stop=(k == 15)), r=[w.res, hT.res], w=[ps.res], inc=(k == 15))
                sg = st[si % len(st)]
                si += 1
                copy_evac(kb, kb.ev(), sg[0:m, :], ps[0:m, :], r=[ps.res], w=[sg.res])
                kb.dma("act", C["zT"][c:c + m, t0:t0 + TS], sg[0:m, :], r=[sg.res])
        for (c0, ncols, dst, dc) in [(4288, 512, "vatt", 0), (4800, 256, "vatt", 512), (5056, 512, "u5", 0)]:
            w = wb[wi % len(wb)]
            wi += 1
            kb.dma("sp", w[:, :, 0:ncols], C["w_in_b"][l, :, c0:c0 + ncols].rearrange("(k p) c -> p k c", p=128),
                   w=[w.res])
            sg = stm[si % len(stm)]
            si += 1
            for j in range(4):
                ps = dn.nps()
                for k in range(16):
                    kb.op("pe", lambda h: h.matmul(ps[:, 0:ncols], hT[:, k, j * 128:(j + 1) * 128], w[:, k, 0:ncols],
                                                   start=(k == 0), stop=(k == 15)), r=[w.res, hT.res], w=[ps.res], inc=(k == 15))
                copy_evac(kb, kb.ev(), sg[:, j, 0:ncols], ps[:, 0:ncols], r=[ps.res], w=[sg.res])
            kb.dma("act", C[dst][t0:t0 + TS, dc:dc + ncols].rearrange("(j p) c -> p j c", p=128), sg[:, :, 0:ncols],
                   r=[sg.res])
        for gb in range(12):
            c0 = 5568 + gb * 512
            w = wb[wi % len(wb)]
            wi += 1
            kb.dma("sp", w[:, :, :], C["w_in_b"][l, :, c0:c0 + 512].rearrange("(k p) c -> p k c", p=128), w=[w.res])
            for u in range(4):
                ps = dn.nps()
                for k in range(16):
                    kb.op("pe", lambda h: h.matmul(ps[:, :], w[:, k, u * 128:(u + 1) * 128], hT[:, k, :],
                                                   start=(k == 0), stop=(k == 15)), r=[w.res, hT.res], w=[ps.res], inc=(k == 15))
                sg = st[si % len(st)]
                si += 1
                gi = gb * 4 + u
                kb.op("act", lambda h: h.activation(out=sg[:, :], in_=ps[:, :], func=AF.Sigmoid,
                                                    bias=bgcol[l][:, gi:gi + 1]), r=[ps.res, bgcol[l].res], w=[sg.res])
                kb.dma("act", C["gT"][gi * 128:(gi + 1) * 128, t0:t0 + TS], sg[:, :], r=[sg.res])


def gelu_tanh(kb, out_ap, x_ap, tmp_ap, r, w_out, w_tmp):
    kb.op("pool", lambda h: h.tensor_tensor(out=tmp_ap, in0=x_ap, in1=x_ap, op=ALU.mult), r=r, w=[w_tmp])
    kb.op("dve", lambda h: h.tensor_scalar(out=tmp_ap, in0=tmp_ap, scalar1=0.044715 * 1.5957691216, scalar2=1.5957691216,
                                           op0=ALU.mult, op1=ALU.add), r=[w_tmp], w=[w_tmp])
    kb.op("dve", lambda h: h.tensor_tensor(out=tmp_ap, in0=tmp_ap, in1=x_ap, op=ALU.mult), r=r + [w_tmp], w=[w_tmp])
    kb.op("act", lambda h: h.activation(out=tmp_ap, in_=tmp_ap, func=AF.Sigmoid), r=[w_tmp], w=[w_tmp])
    kb.op("dve", lambda h: h.tensor_tensor(out=out_ap, in0=tmp_ap, in1=x_ap, op=ALU.mult), r=r + [w_tmp], w=[w_out])


def phase3(P, dn, l, C, last):
    kb, cx, S = P.kb, P.cx, P.S
    x_src = C["x"] if l == 0 else C["xres"]
    ident = C["ident"]
    xt, hT, wb = C["xt"], C["hT"], C["wblk"]
    yT, mT, aT, gt = C["yTt"], C["mT"], C["aT"], C["gt"]
    wi = 0
    gi_ = 0
    for it in range(S // TS):
        t0 = it * TS
        kb.dma("sp", xt[:, :, :], x_src[t0:t0 + TS, :].rearrange("(j p) d -> p j d", p=128), w=[xt.res])
        kb.dma("sp", yT[:, :, :], C["yT"][:, t0:t0 + TS].rearrange("(k p) t -> p k t", p=128), w=[yT.res])
        for fb in range(4):
            w = wb[wi % len(wb)]
            wi += 1
            kb.dma("sp", w[:, :, :], C["proj_b"][l, :, fb * 512:(fb + 1) * 512].rearrange("(k p) c -> p k c", p=128),
                   w=[w.res])
            for u in range(4):
                fc = fb * 4 + u
                g = gt[gi_ % len(gt)]
                gi_ += 1
                kb.dma("sp", g[:, :, :], C["gT"][:, t0:t0 + TS].rearrange("(i f p) t -> p i f t", i=3, p=128)[:, :, fc, :],
                       w=[g.res])
                acc = C["macc"]
                for i, (k0, k1) in enumerate([(0, 6), (6, 12), (12, 16)]):
                    ps = dn.nps()
                    for k in range(k0, k1):
                        kb.op("pe", lambda h: h.matmul(ps[:, :], w[:, k, u * 128:(u + 1) * 128], yT[:, k, :],
                                                       start=(k == k0), stop=(k == k1 - 1)), r=[w.res, yT.res], w=[ps.res], inc=(k == k1 - 1))
                    if i == 0:
                        kb.op("dve", lambda h: h.tensor_tensor(out=acc[:, :], in0=ps[:, :], in1=g[:, 0, :], op=ALU.mult),
                              r=[ps.res, g.res], w=[acc.res])
                    else:
                        tmp = C["mtmp"]
                        kb.op("dve", lambda h: h.tensor_tensor(out=tmp[:, :], in0=ps[:, :], in1=g[:, i, :], op=ALU.mult),
                              r=[ps.res, g.res], w=[tmp.res])
                        if i == 1:
                            kb.op("pool", lambda h: h.tensor_tensor(out=acc[:, :], in0=acc[:, :], in1=tmp[:, :], op=ALU.add),
                                  r=[tmp.res, acc.res], w=[acc.res])
                        else:
                            kb.op("pool", lambda h: h.tensor_tensor(out=mT[:, fc, :], in0=acc[:, :], in1=tmp[:, :], op=ALU.add),
                                  r=[tmp.res, acc.res], w=[mT.res])
        for cb in range(4):
            w = wb[wi % len(wb)]
            wi += 1
            kb.dma("sp", w[:, :, :], C["w_out_b"][l, :, cb * 512:(cb + 1) * 512].rearrange("(k p) c -> p k c", p=128),
                   w=[w.res])
            for j in range(4):
                ps = dn.nps()
                for k in range(16):
                    kb.op("pe", lambda h: h.matmul(ps[:, :], mT[:, k, j * 128:(j + 1) * 128], w[:, k, :],
                                                   start=(k == 0), stop=(k == 15)), r=[w.res, mT.res], w=[ps.res], inc=(k == 15))
                kb.op("dve", lambda h: h.tensor_tensor(out=xt[:, j, cb * 512:(cb + 1) * 512], in0=ps[:, :],
                                                       in1=xt[:, j, cb * 512:(cb + 1) * 512], op=ALU.add),
                      r=[ps.res, xt.res], w=[xt.res])
        rmsnorm_T(P, dn, xt, C["gffn"][l], hT, C["junk"], C["ss"], C["rs"], C["xn"], ident)
        cw = C["convw"][l]
        carry = C["carry"]
        for fb in range(11):
            wv = wb[wi % len(wb)]
            wi += 1
            kb.dma("sp", wv[:, :, :], C["up_b"][l, :, fb * 512:(fb + 1) * 512].rearrange("(k p) c -> p k c", p=128),
                   w=[wv.res])
            wg = wb[wi % len(wb)]
            wi += 1
            kb.dma("sp", wg[:, :, :], C["up_b"][l, :, D_FF + fb * 512:D_FF + (fb + 1) * 512].rearrange("(k p) c -> p k c", p=128),
                   w=[wg.res])
            for u in range(4):
                f = fb * 4 + u
                psv = dn.nps()
                for k in range(16):
                    kb.op("pe", lambda h: h.matmul(psv[:, :], wv[:, k, u * 128:(u + 1) * 128], hT[:, k, :],
                                                   start=(k == 0), stop=(k == 15)), r=[wv.res, hT.res], w=[psv.res], inc=(k == 15))
                psg = dn.nps()
                for k in range(16):
                    kb.op("pe", lambda h: h.matmul(psg[:, :], wg[:, k, u * 128:(u + 1) * 128], hT[:, k, :],
                                                   start=(k == 0), stop=(k == 15)), r=[wg.res, hT.res], w=[psg.res], inc=(k == 15))
                G = C["G"]
                cv = C["cv"]
                ct = C["ct"]
                kb.op("pool", lambda h: h.tensor_copy(out=G[:, 0:2], in_=carry[:, f, :]), r=[carry.res], w=[G.res])
                kb.op("act", lambda h: h.activation(out=G[:, 2:2 + TS], in_=psg[:, :], func=AF.Identity),
                      r=[psg.res], w=[G.res])
                kb.op("pool", lambda h: h.tensor_copy(out=carry[:, f, :], in_=G[:, TS:TS + 2]), r=[G.res], w=[carry.res])
                kb.op("act", lambda h: h.activation(out=cv[:, :], in_=psg[:, :], func=AF.Identity, scale=cw[:, 2, f:f + 1]),
                      r=[psg.res, cw.res], w=[cv.res])
                kb.op("dve", lambda h: h.scalar_tensor_tensor(out=cv[:, :], in0=G[:, 1:1 + TS], scalar=cw[:, 1, f:f + 1],
                                                              in1=cv[:, :], op0=ALU.mult, op1=ALU.add),
                      r=[G.res, cw.res, cv.res], w=[cv.res])
                kb.op("dve", lambda h: h.scalar_tensor_tensor(out=cv[:, :], in0=G[:, 0:TS], scalar=cw[:, 0, f:f + 1],
                                                              in1=cv[:, :], op0=ALU.mult, op1=ALU.add),
                      r=[G.res, cw.res, cv.res], w=[cv.res])
                gelu_tanh(kb, cv[:, :], cv[:, :], ct[:, :], [cv.res], cv.res, ct.res)
                kb.op("dve", lambda h: h.tensor_tensor(out=aT[:, f, :], in0=psv[:, :], in1=cv[:, :], op=ALU.mult),
                      r=[psv.res, cv.res], w=[aT.res])
        wd = C["wdn"]
        di = 0
        for cb in range(4):
            pss = [dn.nps() for _ in range(4)]
            for pc in range(4):
                w = wd[di % len(wd)]
                di += 1
                kb.dma("sp", w[:, :, :], C["down_b"][l, pc * 1408:(pc + 1) * 1408, cb * 512:(cb + 1) * 512]
                       .rearrange("(k p) c -> p k c", p=128), w=[w.res])
                for j in range(4):
                    for k in range(11):
                        f = pc * 11 + k
                        kb.op("pe", lambda h: h.matmul(pss[j][:, :], aT[:, f, j * 128:(j + 1) * 128], w[:, k, :],
                                                       start=(f == 0), stop=(f == 43)), r=[w.res, aT.res], w=[pss[j].res], inc=(k == 10))
            for j in range(4):
                kb.op("dve", lambda h: h.tensor_tensor(out=xt[:, j, cb * 512:(cb + 1) * 512], in0=pss[j][:, :],
                                                       in1=xt[:, j, cb * 512:(cb + 1) * 512], op=ALU.add),
                      r=[pss[j].res, xt.res], w=[xt.res])
        if not last:
            kb.dma("act", C["xres"][t0:t0 + TS, :].rearrange("(j p) d -> p j d", p=128), xt[:, :, :], r=[xt.res])
        else:
            junk, ss, rs = C["junk"], C["ss"], C["rs"]
            gf = C["gfin"]
            for j in range(4):
                kb.op("act", lambda h: h.activation(out=junk[:, :], in_=xt[:, j, :], func=AF.Square,
                                                    accum_out=ss[:, j:j + 1]), r=[xt.res], w=[junk.res, ss.res])
            kb.op("act", lambda h: h.activation(out=rs[:, 0:4], in_=ss[:, 0:4], func=AF.Sqrt, scale=1.0 / D, bias=EPS),
                  r=[ss.res], w=[rs.res])
            kb.op("dve", lambda h: h.reciprocal(out=rs[:, 0:4], in_=rs[:, 0:4]), r=[rs.res], w=[rs.res])
            for j in range(4):
                kb.op("dve", lambda h: h.scalar_tensor_tensor(out=xt[:, j, :], in0=xt[:, j, :], scalar=rs[:, j:j + 1],
                                                              in1=gf[:, :], op0=ALU.mult, op1=ALU.mult),
                      r=[xt.res, rs.res, gf.res], w=[xt.res])
            kb.dma("act", C["out"][t0:t0 + TS, :].rearrange("(j p) d -> p j d", p=128), xt[:, :, :], r=[xt.res])


def build(S, depth, phases=("p1", "mix", "p3"), debug=False, ext_in=(), dbg_out=()):
    P = Prog(S, depth, debug)
    P.ext_in = ext_in
    P.dbg_out = dbg_out
    nc, kb, cx = P.nc, P.kb, P.cx
    C = {}
    C["x"] = P.din("x", (S, D))
    C["positions"] = P.din("positions", (1, S), I32)
    C["ident_in"] = P.din("ident", (128, 128), BF16)
    for n, shp in PARAM_SHAPES.items():
        C["in_" + n] = P.din(n, shp)
    C["out"] = P.dscr("out", (S, D), F32, out=True)
    C["xres"] = P.dscr("xres", (S, D), F32)
    C["w_in_b"] = P.dscr("w_in_bf", (depth, D, N_IN), BF16)
    C["proj_b"] = P.dscr("projs_b", (depth, D, D), BF16)
    C["w_out_b"] = P.dscr("w_out_b", (depth, D, D), BF16)
    C["up_b"] = P.dscr("up_b", (depth, D, 2 * D_FF), BF16)
    C["down_b"] = P.dscr("down_b", (depth, D_FF, D), BF16)
    C["zT"] = P.dscr("zT", (ZROWS, S), BF16)
    C["vatt"] = P.dscr("vatt", (S, 768), BF16)
    C["u5"] = P.dscr("u5", (S, 512), BF16)
    C["gT"] = P.dscr("gT", (3 * D, S), BF16)
    C["yT"] = P.dscr("yT", (D, S), BF16)

    wres = [Res("wconv%d" % l) for l in range(depth)]
    C["wres"] = wres

    def conv(l, dst, src, rows, step=128):
        for r0 in range(0, rows, step):
            r1 = min(rows, r0 + step)
            kb.dma("pool", dst(r0, r1), src(r0, r1), sres=wres[l])

    for l in range(depth):
        conv(l, lambda a, b: C["w_in_b"][l, a:b, :], lambda a, b: C["in_w_in"][l, a:b, :], D)
        conv(l, lambda a, b: C["proj_b"][l, a:b, :], lambda a, b: C["in_proj_a"][l, a:b, :], 768)
        conv(l, lambda a, b: C["proj_b"][l, 768 + a:768 + b, :], lambda a, b: C["in_proj_b"][l, a:b, :], 768)
        conv(l, lambda a, b: C["proj_b"][l, 1536 + a:1536 + b, :], lambda a, b: C["in_proj_c"][l, a:b, :], 512)
        conv(l, lambda a, b: C["w_out_b"][l, a:b, :], lambda a, b: C["in_w_out"][l, a:b, :], D)
        conv(l, lambda a, b: C["up_b"][l, a:b, :], lambda a, b: C["in_ffn_up"][l, a:b, :], D)
        conv(l, lambda a, b: C["down_b"][l, a:b, :], lambda a, b: C["in_ffn_down"][l, a:b, :], D_FF)
        wres[l].w = [(wres[l].ds, kb.sems[wres[l].ds].cnt)]

    C["ident"] = cx.sb([128, 128], BF16, "ident")
    kb.dma("sp", C["ident"][:, :], C["ident_in"][:, :], w=[C["ident"].res])
    L = depth
    gm = cx.sb([128, L, 16], F32, "gmix")
    gf = cx.sb([128, L, 16], F32, "gffn")
    bg = cx.sb([128, L, 48], F32, "bgate")
    cw = cx.sb([128, L, 3, 44], F32, "convw")
    kb.dma("sp", gm[:, :, :], C["in_norm_mix"][0:L, :].rearrange("l (c p) -> p l c", p=128), w=[gm.res], slow=True)
    kb.dma("sp", gf[:, :, :], C["in_norm_ffn"][0:L, :].rearrange("l (c p) -> p l c", p=128), w=[gf.res], slow=True)
    kb.dma("sp", bg[:, :, :], C["in_b_gate"][0:L, :].rearrange("l (c p) -> p l c", p=128), w=[bg.res], slow=True)
    for l in range(L):
        kb.dma("sp", cw[:, l, :, :], C["in_ffn_conv"][l, :, :].rearrange("j (c p) -> p j c", p=128), w=[cw.res], slow=True)

    class V:
        def __init__(self, tl, f):
            self.tl, self.f, self.res = tl, f, tl.res

        def __getitem__(self, idx):
            return self.f(self.tl)[idx]

    C["gmix"] = [V(gm, lambda t, l=l: t[:, l, :]) for l in range(L)]
    C["gffn"] = [V(gf, lambda t, l=l: t[:, l, :]) for l in range(L)]
    C["bgate"] = [V(bg, lambda t, l=l: t[:, l, :]) for l in range(L)]
    C["convw"] = [V(cw, lambda t, l=l: t[:, l, :, :]) for l in range(L)]
    def alloc_dense(nw=2):
        C["xt"] = cx.sb([128, 4, D], F32, "xt")
        C["hT"] = cx.sb([128, 16, TS], BF16, "hT")
        C["xn"] = cx.sb([128, 4, D], BF16, "xn")
        C["junk"] = cx.sb([128, D], BF16, "junk")
        C["ss"] = cx.sb([128, 4], F32, "ss")
        C["rs"] = cx.sb([128, 4], F32, "rs")
        C["wblk"] = [cx.sb([128, 16, 512], BF16, "wblk") for _ in range(nw)]

    C["rw_masks"] = P.din("rw_masks", (3, 128, 128))
    C["bones"] = P.din("bones", (128, 128), BF16)
    C["chunkmask"] = P.din("chunkmask", (128, 256))
    C["s5mask"] = P.din("s5mask", (128, 2, 256))
    C["identf"] = P.din("identf", (128, 128))
    C["invf"] = P.din("invf", (64, 1))
    C["sgn"] = P.din("sgn", (64, 1))
    C["ropeC"] = P.dscr("ropeC", (64, S), BF16)
    C["ropeS"] = P.dscr("ropeS", (64, S), BF16)
    dn = Dense(P)
    C["dn"] = dn

    if "zero" in phases:
        zt = cx.sb([128, 2048], BF16, "zeros")
        kb.op("pool", lambda h: h.memset(zt[:, :], 0.0), w=[zt.res])
        for r0 in list(range(0, 768, 128)) + list(range(1536, 2048, 128)):
            for t0 in range(0, S, 2048):
                n = min(2048, S - t0)
                kb.dma("sp", C["yT"][r0:r0 + 128, t0:t0 + n], zt[:, 0:n], r=[zt.res])
        kb.barrier()
    for l in range(depth):
        last = (l == depth - 1)
        kb._wait("sp", wres[l].w)
        if "p1" in phases:
            with ExitStack() as stk:
                cx.stack = stk
                alloc_dense(4)
                C["stage"] = [cx.sb([128, TS], BF16, "stage") for _ in range(4)]
                C["stage_tm"] = [cx.sb([128, 4, 512], BF16, "stage_tm") for _ in range(2)]
                phase1(P, dn, l, C)
                kb.barrier()
                cx.stack = None
        if "rope" in phases and l == 0:
            with ExitStack() as stk:
                cx.stack = stk
                rope_tables(P, C)
                kb.barrier()
                cx.stack = None
        if "rwkv" in phases:
            with ExitStack() as stk:
                cx.stack = stk
                rwkv_mixer(P, dn, l, C)
                kb.barrier()
                cx.stack = None
        if "s5" in phases:
            with ExitStack() as stk:
                cx.stack = stk
                s5_mixer(P, dn, l, C)
                kb.barrier()
                cx.stack = None
        if "att" in phases:
            with ExitStack() as stk:
                cx.stack = stk
                attention(P, dn, l, C)
                kb.barrier()
                cx.stack = None
        if "p3" in phases:
            with ExitStack() as stk:
                cx.stack = stk
                alloc_dense()
                big = cx.sb([128, 44 * TS], BF16, "big")
                C["yTt"] = V(big, lambda t: t[:, 0:16 * TS].rearrange("p (k t) -> p k t", k=16))
                C["mT"] = V(big, lambda t: t[:, 16 * TS:32 * TS].rearrange("p (k t) -> p k t", k=16))
                C["aT"] = V(big, lambda t: t[:, :].rearrange("p (k t) -> p k t", k=44))
                C["gt"] = [cx.sb([128, 3, TS], BF16, "gt") for _ in range(2)]
                C["macc"] = cx.sb([128, TS], F32, "macc")
                C["mtmp"] = cx.sb([128, TS], F32, "mtmp")
                C["G"] = cx.sb([128, TS + 2], F32, "G")
                C["cv"] = cx.sb([128, TS], F32, "cv")
                C["ct"] = cx.sb([128, TS], F32, "ct")
                C["carry"] = cx.sb([128, 44, 2], F32, "carry")
                C["wdn"] = [cx.sb([128, 11, 512], BF16, "wdn") for _ in range(2)]
                kb.op("pool", lambda h: h.memset(C["carry"][:, :, :], 0.0), w=[C["carry"].res])
                if last:
                    C["gfin"] = cx.sb([128, D], F32, "gfin")
                    kb.dma("sp", C["gfin"][:, :], C["in_norm_final"].partition_broadcast(128), w=[C["gfin"].res])
                phase3(P, dn, l, C, last)
                kb.barrier()
                cx.stack = None
    kb.barrier()
    return P, C


def rope_tables(P, C):
    kb, cx, S = P.kb, P.cx, P.S
    pi = cx.sb([64, S], I32, "pos_i")
    ang = cx.sb([64, S], F32, "ang")
    a2 = cx.sb([64, S], F32, "ang2")
    ni = cx.sb([64, S], I32, "n_i")
    nf = cx.sb([64, S], F32, "n_f")
    ob = cx.sb([64, S], BF16, "rope_o")
    invf = cx.sb([64, 1], F32, "invf")
    sgn = cx.sb([64, 1], F32, "sgn")
    kb.dma("sp", invf[:, :], C["invf"][:, :], w=[invf.res])
    kb.dma("sp", sgn[:, :], C["sgn"][:, :], w=[sgn.res])
    kb.dma("sp", pi[:, :], C["positions"][0].partition_broadcast(64), w=[pi.res])
    kb.op("dve", lambda h: h.tensor_copy(out=ang[:, :], in_=pi[:, :]), r=[pi.res], w=[ang.res])
    kb.op("dve", lambda h: h.tensor_scalar(out=ang[:, :], in0=ang[:, :], scalar1=invf[:, 0:1], scalar2=None, op0=ALU.mult),
          r=[ang.res, invf.res], w=[ang.res])
    TWO_PI = 2.0 * math.pi
    for which in range(2):
        src = ang
        if which == 0:
            kb.op("dve", lambda h: h.tensor_scalar(out=a2[:, :], in0=ang[:, :], scalar1=math.pi / 2, scalar2=None, op0=ALU.add),
                  r=[ang.res], w=[a2.res])
            src = a2
        kb.op("dve", lambda h: h.tensor_scalar(out=ni[:, :], in0=src[:, :], scalar1=1.0 / TWO_PI, scalar2=None, op0=ALU.mult),
              r=[src.res], w=[ni.res])
        kb.op("dve", lambda h: h.tensor_copy(out=nf[:, :], in_=ni[:, :]), r=[ni.res], w=[nf.res])
        kb.op("dve", lambda h: h.scalar_tensor_tensor(out=nf[:, :], in0=nf[:, :], scalar=-TWO_PI, in1=src[:, :],
                                                      op0=ALU.mult, op1=ALU.add), r=[nf.res, src.res], w=[nf.res])
        kb.op("dve", lambda h: h.tensor_scalar(out=nf[:, :], in0=nf[:, :], scalar1=-3.14159, scalar2=3.14159,
                                               op0=ALU.max, op1=ALU.min), r=[nf.res], w=[nf.res])
        kb.op("act", lambda h: h.activation(out=nf[:, :], in_=nf[:, :], func=AF.Sin), r=[nf.res], w=[nf.res])
        if which == 0:
            kb.op("dve", lambda h: h.tensor_copy(out=ob[:, :], in_=nf[:, :]), r=[nf.res], w=[ob.res])
            kb.dma("sp", C["ropeC"][:, :], ob[:, :], r=[ob.res])
        else:
            kb.op("dve", lambda h: h.tensor_scalar(out=ob[:, :], in0=nf[:, :], scalar1=sgn[:, 0:1], scalar2=None, op0=ALU.mult),
                  r=[nf.res, sgn.res], w=[ob.res])
            kb.dma("sp", C["ropeS"][:, :], ob[:, :], r=[ob.res])


def attention(P, dn, l, C):
    kb, cx, S = P.kb, P.cx, P.S
    NB = S // 128
    lam_init = 0.8 - 0.6 * math.exp(-0.3 * l)
    ident = C["ident"]
    lq = cx.sb([128, 4, 64], F32, "lq")
    for i, n in enumerate(["da_lq1", "da_lk1", "da_lq2", "da_lk2"]):
        kb.dma("sp", lq[:, i, :], C["in_" + n][l].partition_broadcast(128), w=[lq.res])
    pr = cx.sb([128, 2, 64], F32, "lpr")
    e12 = cx.sb([128, 2], F32, "e12")
    nlam = cx.sb([128, 1], F32, "nlam")
    kb.op("dve", lambda h: h.tensor_tensor(out=pr[:, 0, :], in0=lq[:, 0, :], in1=lq[:, 1, :], op=ALU.mult), r=[lq.res], w=[pr.res])
    kb.op("dve", lambda h: h.tensor_tensor(out=pr[:, 1, :], in0=lq[:, 2, :], in1=lq[:, 3, :], op=ALU.mult), r=[lq.res], w=[pr.res])
    kb.op("dve", lambda h: h.tensor_reduce(out=e12[:, :], in_=pr[:, :, :], axis=AX.X, op=ALU.add), r=[pr.res], w=[e12.res])
    kb.op("act", lambda h: h.activation(out=e12[:, :], in_=e12[:, :], func=AF.Exp), r=[e12.res], w=[e12.res])
    kb.op("dve", lambda h: h.tensor_tensor(out=nlam[:, :], in0=e12[:, 1:2], in1=e12[:, 0:1], op=ALU.subtract), r=[e12.res], w=[nlam.res])
    kb.op("dve", lambda h: h.tensor_scalar(out=nlam[:, :], in0=nlam[:, :], scalar1=-lam_init, scalar2=None, op0=ALU.add),
          r=[nlam.res], w=[nlam.res])
    sl = cx.sb([128, 128], F32, "subln")
    kb.dma("sp", sl[:, :], C["in_da_subln"][l].partition_broadcast(128), w=[sl.res])
    kb.op("dve", lambda h: h.tensor_scalar(out=sl[:, :], in0=sl[:, :], scalar1=1.0 - lam_init, scalar2=None, op0=ALU.mult),
          r=[sl.res], w=[sl.res])
    C2 = cx.sb([64, S], BF16, "C2")
    S2 = cx.sb([64, S], BF16, "S2")
    kb.dma("sp", C2[:, :], C["ropeC"][:, :], w=[C2.res])
    kb.dma("sp", S2[:, :], C["ropeS"][:, :], w=[S2.res])
    kr = [cx.sb([64, S], BF16, "kr") for _ in range(2)]
    V1 = cx.sb([128, NB, 129], BF16, "V1")
    kb.op("pool", lambda h: h.memset(V1[:, :, 128:129], 1.0), w=[V1.res])
    CH = min(S, 2048)
    raw = [cx.sb([64, CH], BF16, "raw") for _ in range(2)]
    swp = [cx.sb([64, CH], BF16, "swp") for _ in range(2)]
    tmp = [cx.sb([64, CH], BF16, "rtmp") for _ in range(2)]
    qr = [cx.sb([64, 512], BF16, "qr") for _ in range(2)]
    PT = [cx.sb([128, 512], BF16, "PT") for _ in range(4)]
    o0 = cx.sb([128, 4, 128], F32, "o0")
    att = cx.sb([128, 4, 128], F32, "att")
    rec = cx.sb([128, 4], F32, "rec")
    ssq = cx.sb([128, 4], F32, "ssq")
    jk = cx.sb([128, 128], BF16, "ajunk")
    yb = cx.sb([128, 4, 128], BF16, "yb")
    stg = [cx.sb([128, 512], BF16, "astg") for _ in range(2)]
    OA, OB = dn.psf[4], dn.psf[5]
    ri = 0
    pti = 0
    sti = 0

    def rope(dst_ap, row0, t0, n, ri):
        a, b, t = raw[ri % 2], swp[ri % 2], tmp[ri % 2]
        kb.dma("sp", a[:, 0:n], C["zT"][row0:row0 + 64, t0:t0 + n], w=[a.res])
        kb.dma("sp", b[0:32, 0:n], C["zT"][row0 + 32:row0 + 64, t0:t0 + n], w=[b.res])
        kb.dma("sp", b[32:64, 0:n], C["zT"][row0:row0 + 32, t0:t0 + n], w=[b.res])
        kb.op("dve", lambda h: h.tensor_tensor(out=t[:, 0:n], in0=a[:, 0:n], in1=C2[:, t0:t0 + n], op=ALU.mult),
              r=[a.res, C2.res], w=[t.res])
        kb.op("pool", lambda h: h.tensor_tensor(out=b[:, 0:n], in0=b[:, 0:n], in1=S2[:, t0:t0 + n], op=ALU.mult),
              r=[b.res, S2.res], w=[b.res])
        return t, b

    for hd in range(6):
        kb.dma("sp", V1[:, :, 0:128], C["vatt"][:, hd * 128:(hd + 1) * 128].rearrange("(n p) c -> p n c", p=128), w=[V1.res])
        for c in range(2):
            row0 = 3520 + (hd * 2 + c) * 64
            for t0 in range(0, S, CH):
                t, b = rope(None, row0, t0, CH, ri)
                ri += 1
                kb.op("dve", lambda h: h.tensor_tensor(out=kr[c][:, t0:t0 + CH], in0=t[:, 0:CH], in1=b[:, 0:CH], op=ALU.add),
                      r=[t.res, b.res], w=[kr[c].res])
        for Q in range(S // 512):
            q0 = Q * 512
            for c in range(2):
                row0 = 2752 + (hd * 2 + c) * 64
                t, b = rope(None, row0, q0, 512, ri)
                ri += 1
                q = qr[c]
                kb.op("dve", lambda h: h.tensor_tensor(out=q[:, :], in0=t[:, 0:512], in1=b[:, 0:512], op=ALU.add),
                      r=[t.res, b.res], w=[q.res])
                nkb = Q * 4 + 4

                def stA(kbk):
                    nonlocal pti
                    i = kbk - Q * 4
                    j0 = max(i, 0)
                    ps = dn.nps(4)
                    kb.op("pe", lambda h: h.matmul(ps[:, j0 * 128:512], kr[c][:, kbk * 128:(kbk + 1) * 128], q[:, j0 * 128:512],
                                                   start=True, stop=True), r=[kr[c].res, q.res], w=[ps.res])
                    pt = PT[pti % len(PT)]
                    pti += 1
                    kb.op("act", lambda h: h.activation(out=pt[:, j0 * 128:512], in_=ps[:, j0 * 128:512], func=AF.Exp, scale=0.125),
                          r=[ps.res], w=[pt.res])
                    if i >= 0:
                        kb.op("pool", lambda h: h.memset(pt[64:128, i * 128:i * 128 + 64], 0.0), w=[pt.res])
                    return (kbk, j0, pt)

                def stB(a):
                    kbk, j0, pt = a
                    for j in range(j0, 4):
                        O = OA if j < 2 else OB
                        first = (kbk == 0 and (j == 0 or j == 2))
                        kb.op("pe", lambda h: h.matmul(O[:, (j % 2) * 129:(j % 2) * 129 + 129], pt[:, j * 128:(j + 1) * 128],
                                                       V1[:, kbk, :], start=first, stop=(kbk == Q * 4 + j),
                                                       skip_group_check=True), r=[pt.res, V1.res], w=[O.res])

                pend = []
                for kbk in range(nkb):
                    pend.append(stA(kbk))
                    if len(pend) > 2:
                        stB(pend.pop(0))
                while pend:
                    stB(pend.pop(0))
                for j in range(4):
                    O = OA if j < 2 else OB
                    b0 = (j % 2) * 129
                    kb.op("dve", lambda h: h.reciprocal(out=rec[:, j:j + 1], in_=O[:, b0 + 128:b0 + 129]), r=[O.res], w=[rec.res])
                    if c == 0:
                        kb.op("dve", lambda h: h.tensor_scalar(out=o0[:, j, :], in0=O[:, b0:b0 + 128], scalar1=rec[:, j:j + 1],
                                                               scalar2=None, op0=ALU.mult), r=[O.res, rec.res], w=[o0.res])
                    else:
                        kb.op("dve", lambda h: h.tensor_tensor(out=rec[:, j:j + 1], in0=rec[:, j:j + 1], in1=nlam[:, 0:1], op=ALU.mult),
                              r=[rec.res, nlam.res], w=[rec.res])
                        kb.op("dve", lambda h: h.scalar_tensor_tensor(out=att[:, j, :], in0=O[:, b0:b0 + 128], scalar=rec[:, j:j + 1],
                                                                      in1=o0[:, j, :], op0=ALU.mult, op1=ALU.add),
                              r=[O.res, rec.res, o0.res], w=[att.res])
            for j in range(4):
                kb.op("act", lambda h: h.activation(out=jk[:, :], in_=att[:, j, :], func=AF.Square, accum_out=ssq[:, j:j + 1]),
                      r=[att.res], w=[jk.res, ssq.res])
            kb.op("act", lambda h: h.activation(out=ssq[:, :], in_=ssq[:, :], func=AF.Sqrt, scale=1.0 / 128, bias=EPS),
                  r=[ssq.res], w=[ssq.res])
            kb.op("dve", lambda h: h.reciprocal(out=ssq[:, :], in_=ssq[:, :]), r=[ssq.res], w=[ssq.res])
            pb = dn.npb()
            for j in range(4):
                kb.op("dve", lambda h: h.scalar_tensor_tensor(out=yb[:, j, :], in0=att[:, j, :], scalar=ssq[:, j:j + 1], in1=sl[:, :],
                                                              op0=ALU.mult, op1=ALU.mult), r=[att.res, ssq.res, sl.res], w=[yb.res])
                kb.op("pe", lambda h: h.transpose(out=pb[:, j * 128:(j + 1) * 128], in_=yb[:, j, :], identity=ident[:, :]),
                      r=[yb.res, ident.res], w=[pb.res])
            sg = stg[sti % 2]
            sti += 1
            copy_evac(kb, kb.ev(), sg[:, :], pb[:, 0:512], r=[pb.res], w=[sg.res])
            kb.dma("act", C["yT"][768 + hd * 128:768 + (hd + 1) * 128, q0:q0 + 512], sg[:, :], r=[sg.res])


def host_consts():
    invf = (10000.0 ** (-np.arange(0, 64, 2, dtype=np.float32) / 64)).astype(np.float32)
    p = np.arange(128)[:, None, None]
    m = np.arange(2)[None, :, None]
    col = np.arange(256)[None, None, :]
    s5mask = ((col // 16) >= ((m * 128 + p) // 16)).astype(np.float32)
    ii = np.arange(128)
    mus = (ii[:, None] < ii[None, :]).astype(np.float32)
    mui = (ii[:, None] <= ii[None, :]).astype(np.float32)
    mls = (ii[None, :] < ii[:, None]).astype(np.float32)
    bones = ((ii[:, None] // 64) == (ii[None, :] // 64)).astype(ml_dtypes.bfloat16)
    cmk = np.tile((np.arange(256) % 64 != 0).astype(np.float32)[None, :], (128, 1))
    return {"rw_masks": np.stack([mus, mui, mls]), "bones": bones, "chunkmask": cmk, "s5mask": s5mask, "identf": np.eye(128, dtype=np.float32), "invf": np.concatenate([invf, invf])[:, None].astype(np.float32).copy(),
            "sgn": np.concatenate([-np.ones(32), np.ones(32)])[:, None].astype(np.float32).copy()}


_CACHE = {}


def kernel(**inputs):
    S = 8192
    nb = 4
    if "prog" not in _CACHE:
        _CACHE["prog"] = build(S, DEPTH, phases=("p1", "rwkv", "s5", "rope", "att", "p3"))
    P, C = _CACHE["prog"]
    consts = host_consts()
    consts["ident"] = np.eye(128, dtype=ml_dtypes.bfloat16)
    in_maps = []
    for b in range(nb):
        m = {"x": np.ascontiguousarray(np.asarray(inputs["x"])[b], dtype=np.float32),
             "positions": np.ascontiguousarray(np.asarray(inputs["positions"])[b][None, :], dtype=np.int32)}
        for n, shp in PARAM_SHAPES.items():
            m[n] = np.ascontiguousarray(np.asarray(inputs[n], dtype=np.float32).reshape(shp))
        m.update(consts)
        in_maps.append(m)
    res = run_bass_kernel_spmd(P.nc, in_maps, core_ids=list(range(nb)))
    return np.stack([np.asarray(r["out"], dtype=np.float32) for r in res.results], axis=0)


def s5_mixer(P, dn, l, C):
    kb, cx, S = P.kb, P.cx, P.S
    ident = C["ident"]
    NCH = S // 16
    M = min(128, NCH)
    NMT = NCH // M
    V = lambda t, f: VW(t, f)
    TT = cx.sb([128, 32, 2, 256], BF16, "TT")
    GG = cx.sb([128, 32, 2, 2, 64], BF16, "GG")
    FFr = cx.sb([64, 32, 256], BF16, "FFr")
    nFFi = cx.sb([64, 32, 256], BF16, "nFFi")
    MU1 = cx.sb([64, 32, 2], F32, "MU1")
    MU2 = cx.sb([64, 32, 2], F32, "MU2")
    gluW = cx.sb([128, 4, 512], BF16, "gluW")
    glub = cx.sb([128, 4], F32, "glub")
    dB = cx.sb([128, 512], F32, "dB")
    kb.dma("pool", gluW[:, :, :], C["in_s5_glu_w"][l].rearrange("(k p) c -> p k c", p=128), w=[gluW.res])
    kb.dma("sp", glub[:, :], C["in_s5_glu_b"][l].rearrange("(c p) -> p c", p=128), w=[glub.res], slow=True)
    kb.dma("sp", dB[:, :], C["in_s5_d"][l].partition_broadcast(128), w=[dB.res])
    with ExitStack() as stk:
        old = cx.stack
        cx.stack = stk
        f32 = lambda shp, n: cx.sb(shp, F32, n)
        lr, li, dt = f32([64, 32], "lr"), f32([64, 32], "li"), f32([64, 32], "dt")
        kb.dma("sp", lr[:, :], C["in_s5_lam_re"][l].rearrange("g n -> n g"), w=[lr.res], slow=True)
        kb.dma("sp", li[:, :], C["in_s5_lam_im"][l].rearrange("g n -> n g"), w=[li.res], slow=True)
        kb.dma("sp", dt[:, :], C["in_s5_log_dt"][l].partition_broadcast(64), w=[dt.res])
        bre, bim = f32([64, 32, 16], "bre"), f32([64, 32, 16], "bim")
        cre, cim = f32([64, 32, 16], "cre"), f32([64, 32, 16], "cim")
        kb.dma("sp", bre[:, :, :], C["in_s5_b_re"][l].rearrange("g n c -> n g c"), w=[bre.res])
        kb.dma("sp", bim[:, :, :], C["in_s5_b_im"][l].rearrange("g n c -> n g c"), w=[bim.res])
        kb.dma("sp", cre[:, :, :], C["in_s5_c_re"][l].rearrange("g c n -> n g c"), w=[cre.res], slow=True)
        kb.dma("sp", cim[:, :, :], C["in_s5_c_im"][l].rearrange("g c n -> n g c"), w=[cim.res], slow=True)
        msk = cx.sb([128, 2, 256], F32, "s5mask")
        kb.dma("sp", msk[:, :, :], C["s5mask"][:, :, :], w=[msk.res])
        idf = cx.sb([64, 64], F32, "identf")
        kb.dma("sp", idf[:, :], C["identf"][0:64, 0:64], w=[idf.res])

        def tt(out, a, b, op, r, w):
            kb.op("dve", lambda h: h.tensor_tensor(out=out, in0=a, in1=b, op=op), r=r, w=w)

        def ts(out, a, s1, s2, op0, op1, r, w):
            if s2 is None:
                kb.op("dve", lambda h: h.tensor_scalar(out=out, in0=a, scalar1=s1, scalar2=None, op0=op0), r=r, w=w)
            else:
                kb.op("dve", lambda h: h.tensor_scalar(out=out, in0=a, scalar1=s1, scalar2=s2, op0=op0, op1=op1), r=r, w=w)

        def cmul(o_r, o_i, ar, ai, br, bi, t1, res_in, res_out):
            r = res_in + [T1.res]
            tt(o_r, ar, br, ALU.mult, res_in, res_out)
            tt(t1, ai, bi, ALU.mult, res_in, [T1.res])
            tt(o_r, o_r, t1, ALU.subtract, res_out + [T1.res], res_out)
            tt(o_i, ar, bi, ALU.mult, res_in, res_out)
            tt(t1, ai, br, ALU.mult, res_in, [T1.res])
            tt(o_i, o_i, t1, ALU.add, res_out + [T1.res], res_out)

        T1 = f32([64, 32 * 16], "T1")
        kb.op("act", lambda h: h.activation(out=dt[:, :], in_=dt[:, :], func=AF.Exp), r=[dt.res], w=[dt.res])
        aa, th, er = f32([64, 32], "aa"), f32([64, 32], "th"), f32([64, 32], "er")
        tt(aa[:, :], lr[:, :], dt[:, :], ALU.mult, [lr.res, dt.res], [aa.res])
        tt(th[:, :], li[:, :], dt[:, :], ALU.mult, [li.res, dt.res], [th.res])
        kb.op("act", lambda h: h.activation(out=er[:, :], in_=aa[:, :], func=AF.Exp, scale=1.0 / 16), r=[aa.res], w=[er.res])
        zr, zi = f32([64, 32], "zr"), f32([64, 32], "zi")
        ts(zr[:, :], th[:, :], 1.0 / 16, math.pi / 2, ALU.mult, ALU.add, [th.res], [zr.res])
        kb.op("act", lambda h: h.activation(out=zr[:, :], in_=zr[:, :], func=AF.Sin), r=[zr.res], w=[zr.res])
        kb.op("act", lambda h: h.activation(out=zi[:, :], in_=th[:, :], func=AF.Sin, scale=1.0 / 16), r=[th.res], w=[zi.res])
        tt(zr[:, :], zr[:, :], er[:, :], ALU.mult, [zr.res, er.res], [zr.res])
        tt(zi[:, :], zi[:, :], er[:, :], ALU.mult, [zi.res, er.res], [zi.res])
        z2r, z2i = f32([64, 32], "z2r"), f32([64, 32], "z2i")
        cur = (zr, zi)
        nxt = (z2r, z2i)
        for _ in range(4):
            cmul(nxt[0][:, :], nxt[1][:, :], cur[0][:, :], cur[1][:, :], cur[0][:, :], cur[1][:, :], T1[:, 0:32],
                 [cur[0].res, cur[1].res], [nxt[0].res, nxt[1].res])
            cur, nxt = nxt, cur
        abr, abi = cur
        den, nr, fre, fim = f32([64, 32], "den"), f32([64, 32], "nr"), f32([64, 32], "fre"), f32([64, 32], "fim")
        tt(den[:, :], lr[:, :], lr[:, :], ALU.mult, [lr.res], [den.res])
        tt(T1[:, 0:32], li[:, :], li[:, :], ALU.mult, [li.res], [T1.res])
        tt(den[:, :], den[:, :], T1[:, 0:32], ALU.add, [den.res, T1.res], [den.res])
        kb.op("dve", lambda h: h.reciprocal(out=den[:, :], in_=den[:, :]), r=[den.res], w=[den.res])
        ts(nr[:, :], abr[:, :], -1.0, None, ALU.add, None, [abr.res], [nr.res])
        tt(fre[:, :], nr[:, :], lr[:, :], ALU.mult, [nr.res, lr.res], [fre.res])
        tt(T1[:, 0:32], abi[:, :], li[:, :], ALU.mult, [abi.res, li.res], [T1.res])
        tt(fre[:, :], fre[:, :], T1[:, 0:32], ALU.add, [fre.res, T1.res], [fre.res])
        tt(fre[:, :], fre[:, :], den[:, :], ALU.mult, [fre.res, den.res], [fre.res])
        tt(fim[:, :], abi[:, :], lr[:, :], ALU.mult, [abi.res, lr.res], [fim.res])
        tt(T1[:, 0:32], nr[:, :], li[:, :], ALU.mult, [nr.res, li.res], [T1.res])
        tt(fim[:, :], fim[:, :], T1[:, 0:32], ALU.subtract, [fim.res, T1.res], [fim.res])
        tt(fim[:, :], fim[:, :], den[:, :], ALU.mult, [fim.res, den.res], [fim.res])
        bbr, bbi = f32([64, 32, 16], "bbr"), f32([64, 32, 16], "bbi")
        bc = lambda t: t[:, :].unsqueeze(2).to_broadcast([64, 32, 16])
        T3 = T1[:, :].rearrange("p (g c) -> p g c", c=16)
        cmul(bbr[:, :, :], bbi[:, :, :], bc(fre), bc(fim), bre[:, :, :], bim[:, :, :], T3,
             [fre.res, fim.res, bre.res, bim.res], [bbr.res, bbi.res])
        pwr, pwi = f32([64, 17, 32], "pwr"), f32([64, 17, 32], "pwi")
        npr, npi = f32([64, 16, 32], "npr"), f32([64, 16, 32], "npi")
        kb.op("dve", lambda h: h.memset(pwr[:, 0, :], 1.0), w=[pwr.res])
        kb.op("dve", lambda h: h.memset(pwi[:, 0, :], 0.0), w=[pwi.res])
        kb.op("dve", lambda h: h.memset(npr[:, 0, :], 1.0), w=[npr.res])
        kb.op("dve", lambda h: h.memset(npi[:, 0, :], 0.0), w=[npi.res])
        for t in range(16):
            cmul(pwr[:, t + 1, :], pwi[:, t + 1, :], pwr[:, t, :], pwi[:, t, :], abr[:, :], abi[:, :], T1[:, 0:32],
                 [pwr.res, pwi.res, abr.res, abi.res], [pwr.res, pwi.res])
        ivr, ivi = f32([64, 32], "ivr"), f32([64, 32], "ivi")
        tt(ivr[:, :], abr[:, :], abr[:, :], ALU.mult, [abr.res], [ivr.res])
        tt(T1[:, 0:32], abi[:, :], abi[:, :], ALU.mult, [abi.res], [T1.res])
        tt(ivr[:, :], ivr[:, :], T1[:, 0:32], ALU.add, [ivr.res, T1.res], [ivr.res])
        kb.op("dve", lambda h: h.reciprocal(out=ivr[:, :], in_=ivr[:, :]), r=[ivr.res], w=[ivr.res])
        tt(ivi[:, :], abi[:, :], ivr[:, :], ALU.mult, [abi.res, ivr.res], [ivi.res])
        ts(ivi[:, :], ivi[:, :], -1.0, None, ALU.mult, None, [ivi.res], [ivi.res])
        tt(ivr[:, :], abr[:, :], ivr[:, :], ALU.mult, [abr.res, ivr.res], [ivr.res])
        for t in range(15):
            cmul(npr[:, t + 1, :], npi[:, t + 1, :], npr[:, t, :], npi[:, t, :], ivr[:, :], ivi[:, :], T1[:, 0:32],
                 [npr.res, npi.res, ivr.res, ivi.res], [npr.res, npi.res])
        kb.op("dve", lambda h: h.tensor_copy(out=MU1[:, :, 0], in_=pwr[:, 16, :]), r=[pwr.res], w=[MU1.res])
        kb.op("dve", lambda h: h.tensor_copy(out=MU1[:, :, 1], in_=pwr[:, 16, :]), r=[pwr.res], w=[MU1.res])
        kb.op("dve", lambda h: h.tensor_copy(out=MU2[:, :, 0], in_=pwi[:, 16, :]), r=[pwi.res], w=[MU2.res])
        kb.op("dve", lambda h: h.tensor_copy(out=MU2[:, :, 1], in_=pwi[:, 16, :]), r=[pwi.res], w=[MU2.res])
        GBr, GBi = f32([64, 16, 16], "GBr"), f32([64, 16, 16], "GBi")
        CFr, CFi = f32([64, 16, 16], "CFr"), f32([64, 16, 16], "CFi")
        GSr, GSi = f32([64, 16, 16], "GSr"), f32([64, 16, 16], "GSi")
        Fr, Fi = f32([64, 16, 16], "Fr"), f32([64, 16, 16], "Fi")
        T4 = T1[:, 0:256].rearrange("p (s c) -> p s c", c=16)
        rpr, rpi = f32([64, 16, 32], "rpr"), f32([64, 16, 32], "rpi")
        for s_ in range(16):
            kb.op("dve", lambda h: h.tensor_copy(out=rpr[:, s_, :], in_=pwr[:, 15 - s_, :]), r=[pwr.res], w=[rpr.res])
            kb.op("dve", lambda h: h.tensor_copy(out=rpi[:, s_, :], in_=pwi[:, 15 - s_, :]), r=[pwi.res], w=[rpi.res])
        for g in range(32):
            pw_b = lambda t, lo=0: t[:, lo:lo + 16, g].unsqueeze(2).to_broadcast([64, 16, 16])
            v_b = lambda t: t[:, g, :].unsqueeze(1).to_broadcast([64, 16, 16])
            cmul(GBr[:, :, :], GBi[:, :, :], pw_b(npr), pw_b(npi), v_b(bbr), v_b(bbi), T4,
                 [npr.res, npi.res, bbr.res, bbi.res], [GBr.res, GBi.res])
            cmul(CFr[:, :, :], CFi[:, :, :], pw_b(pwr), pw_b(pwi), v_b(cre), v_b(cim), T4,
                 [pwr.res, pwi.res, cre.res, cim.res], [CFr.res, CFi.res])
            ts(CFi[:, :, :], CFi[:, :, :], -1.0, None, ALU.mult, None, [CFi.res], [CFi.res])
            cmul(GSr[:, :, :], GSi[:, :, :], pw_b(rpr), pw_b(rpi), v_b(bbr), v_b(bbi), T4,
                 [rpr.res, rpi.res, bbr.res, bbi.res], [GSr.res, GSi.res])
            cmul(Fr[:, :, :], Fi[:, :, :], pw_b(pwr, 1), pw_b(pwi, 1), v_b(cre), v_b(cim), T4,
                 [pwr.res, pwi.res, cre.res, cim.res], [Fr.res, Fi.res])
            kb.op("act", lambda h: h.activation(out=FFr[:, g, :], in_=Fr[:, :, :].rearrange("p t c -> p (t c)"), func=AF.Identity),
                  r=[Fr.res], w=[FFr.res])
            kb.op("act", lambda h: h.activation(out=nFFi[:, g, :], in_=Fi[:, :, :].rearrange("p t c -> p (t c)"), func=AF.Identity,
                                                scale=-1.0), r=[Fi.res], w=[nFFi.res])
            flat = lambda t: t[:, :, :].rearrange("p s c -> p (s c)")
            for m in range(2):
                ps = dn.nps()
                kb.op("pe", lambda h: h.matmul(ps[:, 0:256], flat(GBr)[:, m * 128:(m + 1) * 128], flat(CFr), start=True, stop=False),
                      r=[GBr.res, CFr.res], w=[ps.res])
                kb.op("pe", lambda h: h.matmul(ps[:, 0:256], flat(GBi)[:, m * 128:(m + 1) * 128], flat(CFi), start=False, stop=True),
                      r=[GBi.res, CFi.res], w=[ps.res])
                kb.op("dve", lambda h: h.tensor_tensor(out=TT[:, g, m, :], in0=ps[:, 0:256], in1=msk[:, m, :], op=ALU.mult),
                      r=[ps.res, msk.res], w=[TT.res])
            ps = dn.nps()
            for m in range(2):
                for ri_, src in enumerate((GSr, GSi)):
                    k_ = m * 2 + ri_
                    kb.op("pe", lambda h: h.transpose(out=ps[:, k_ * 64:(k_ + 1) * 64], in_=flat(src)[:, m * 128:(m + 1) * 128],
                                                      identity=idf[:, :]), r=[src.res, idf.res], w=[ps.res])
            kb.op("act", lambda h: h.activation(out=GG[:, g, :, :, :].rearrange("p m r n -> p (m r n)"), in_=ps[:, 0:256],
                                                func=AF.Identity), r=[ps.res], w=[GG.res])
        kb.barrier()
        cx.stack = old
    UY = cx.sb([128, 16 * 512], BF16, "UY")
    Ucm = VW(UY, lambda t: t[:, :].rearrange("p (s c) -> p s c", c=512))
    YT = VW(UY, lambda t: t[:, :].rearrange("p (k m s) -> p k m s", k=4, s=16))
    UT = cx.sb([128, 32, 2, 128], BF16, "UT")
    Eall = cx.sb([64, 32, 2, 129], F32, "Eall")
    Xb = cx.sb([64, 32, 2, 128], BF16, "Xb")
    Ycm = cx.sb([128, 16, 512], BF16, "Ycm")
    P1 = cx.sb([64, 32, 2], F32, "P1")
    P2 = cx.sb([64, 32, 2], F32, "P2")
    gtmp = cx.sb([128, 2048], F32, "gtmp")
    gx = cx.sb([128, 2048], F32, "gx")
    sg5 = [cx.sb([128, 512], BF16, "s5stg") for _ in range(2)]
    sgt = cx.sb([128, 512], F32, "s5sig")
    kb.op("dve", lambda h: h.memset(Eall[:, :, :, 0:1], 0.0), w=[Eall.res])
    si = 0
    for mt in range(NMT):
        tok0 = mt * M * 16
        kb.dma("sp", Ucm[0:M, :, :], C["u5"][tok0:tok0 + M * 16, :].rearrange("(m s) c -> m s c", s=16), w=[UY.res])
        Ugm = VW(Ycm, lambda t: t[:, :, :].rearrange("p s c -> p (s c)").rearrange("p (g s c) -> p g s c", g=32, s=16))
        kb.op("pool", lambda h: h.tensor_copy(out=Ugm[0:M, :, :, :], in_=Ucm[0:M, :, :].rearrange("m s (g c) -> m g s c", c=16)),
              r=[UY.res], w=[Ycm.res])
        for g0 in range(0, 32, 4):
            pb = dn.npb()
            for gg in range(4):
                for hh in range(2):
                    k_ = gg * 2 + hh
                    kb.op("pe", lambda h: h.transpose(out=pb[:, k_ * 128:k_ * 128 + M],
                                                      in_=Ugm[0:M, g0 + gg, hh * 8:(hh + 1) * 8, :].rearrange("m s c -> m (s c)"),
                                                      identity=ident[0:M, 0:M]), r=[Ycm.res, ident.res], w=[pb.res])
            copy_evac(kb, kb.ev(), UT[:, g0:g0 + 4, :, 0:M],
                      pb[:, :].rearrange("p (g h m) -> p g h m", g=4, h=2)[:, :, :, 0:M], r=[pb.res], w=[UT.res])
        for g0 in range(0, 32, 2):
            ps = dn.nps()
            for gg in range(2):
                g = g0 + gg
                for ri_ in range(2):
                    k_ = gg * 2 + ri_
                    for m in range(2):
                        kb.op("pe", lambda h: h.matmul(ps[0:64, k_ * 128:k_ * 128 + M], GG[:, g, m, ri_, :], UT[:, g, m, 0:M],
                                                       start=(k_ == 0 and m == 0), stop=(m == 1), skip_group_check=True),
                              r=[GG.res, UT.res], w=[ps.res])
            copy_evac(kb, kb.ev(), Eall[:, g0:g0 + 2, :, 1:1 + M],
                      ps[0:64, :].rearrange("p (g r m) -> p g r m", g=2, r=2)[:, :, :, 0:M], r=[ps.res], w=[Eall.res])
        for m in range(M):
            Xm = Eall[:, :, :, m]
            Xn = Eall[:, :, :, m + 1]
            kb.op("dve", lambda h: h.tensor_tensor(out=P1[:, :, :], in0=Xm, in1=MU1[:, :, :], op=ALU.mult),
                  r=[Eall.res, MU1.res], w=[P1.res])
            kb.op("pool", lambda h: h.tensor_tensor(out=P2[:, :, :], in0=Xm, in1=MU2[:, :, :], op=ALU.mult),
                  r=[Eall.res, MU2.res], w=[P2.res])
            kb.op("dve", lambda h: h.tensor_tensor(out=Xn, in0=Xn, in1=P1[:, :, :], op=ALU.add), r=[Eall.res, P1.res], w=[Eall.res])
            kb.op("dve", lambda h: h.tensor_tensor(out=Eall[:, :, 0, m + 1], in0=Eall[:, :, 0, m + 1], in1=P2[:, :, 1], op=ALU.subtract),
                  r=[Eall.res, P2.res], w=[Eall.res])
            kb.op("dve", lambda h: h.tensor_tensor(out=Eall[:, :, 1, m + 1], in0=Eall[:, :, 1, m + 1], in1=P2[:, :, 0], op=ALU.add),
                  r=[Eall.res, P2.res], w=[Eall.res])
        kb.op("act", lambda h: h.activation(out=Xb[:, :, :, 0:M], in_=Eall[:, :, :, 0:M], func=AF.Identity), r=[Eall.res], w=[Xb.res])
        for g in range(32):
            ps = dn.nps()
            kb.op("pe", lambda h: h.matmul(ps[0:M, 0:256], UT[:, g, 0, 0:M], TT[:, g, 0, :], start=True, stop=False),
                  r=[UT.res, TT.res], w=[ps.res])
            kb.op("pe", lambda h: h.matmul(ps[0:M, 0:256], UT[:, g, 1, 0:M], TT[:, g, 1, :], start=False, stop=False),
                  r=[UT.res, TT.res], w=[ps.res])
            kb.op("pe", lambda h: h.matmul(ps[0:M, 0:256], Xb[:, g, 0, 0:M], FFr[:, g, :], start=False, stop=False),
                  r=[Xb.res, FFr.res], w=[ps.res])
            kb.op("pe", lambda h: h.matmul(ps[0:M, 0:256], Xb[:, g, 1, 0:M], nFFi[:, g, :], start=False, stop=True),
                  r=[Xb.res, nFFi.res], w=[ps.res])
            copy_evac(kb, kb.ev(), Ycm[0:M, :, g * 16:(g + 1) * 16], ps[0:M, 0:256].rearrange("m (t c) -> m t c", c=16),
                      r=[ps.res], w=[Ycm.res])
        kb.op("dve", lambda h: h.tensor_copy(out=Eall[:, :, :, 0], in_=Eall[:, :, :, M]), r=[Eall.res], w=[Eall.res])
        for s0 in range(0, 16, 4):
            gxv = gx[:, :].rearrange("p (s c) -> p s c", c=512)
            kb.op("dve", lambda h: h.tensor_tensor(out=gxv[0:M], in0=Ucm[0:M, s0:s0 + 4, :],
                                                   in1=dB[0:M, :].unsqueeze(1).to_broadcast([M, 4, 512]), op=ALU.mult),
                  r=[UY.res, dB.res], w=[gx.res])
            kb.op("dve", lambda h: h.tensor_tensor(out=gxv[0:M], in0=gxv[0:M], in1=Ycm[0:M, s0:s0 + 4, :], op=ALU.add),
                  r=[gx.res, Ycm.res], w=[gx.res])
            gelu_tanh(kb, Ycm[0:M, s0:s0 + 4, :].rearrange("m s c -> m (s c)"), gx[0:M, :], gtmp[0:M, :], [gx.res], Ycm.res, gtmp.res)
        for s0 in range(0, 16, 2):
            pb = dn.npb()
            for ss_ in range(2):
                for cc in range(4):
                    k_ = ss_ * 4 + cc
                    kb.op("pe", lambda h: h.transpose(out=pb[:, k_ * 128:k_ * 128 + M], in_=Ycm[0:M, s0 + ss_, cc * 128:(cc + 1) * 128],
                                                      identity=ident[0:M, 0:M]), r=[Ycm.res, ident.res], w=[pb.res])
            copy_evac(kb, kb.ev(), YT[:, :, 0:M, s0:s0 + 2].rearrange("p k m s -> p s k m"),
                      pb[:, :].rearrange("p (s k m) -> p s k m", s=2, k=4)[:, :, :, 0:M], r=[pb.res], w=[UY.res])
        ntok = M * 16
        for tb in range(0, ntok, 512):
            nn = min(512, ntok - tb)
            for co in range(4):
                ps = dn.nps()
                for k in range(4):
                    kb.op("pe", lambda h: h.matmul(ps[:, 0:nn], gluW[:, k, co * 128:(co + 1) * 128],
                                                   YT[:, k, tb // 16:(tb + nn) // 16, :].rearrange("p m s -> p (m s)"),
                                                   start=(k == 0), stop=(k == 3)), r=[gluW.res, UY.res], w=[ps.res])
                kb.op("act", lambda h: h.activation(out=sgt[:, 0:nn], in_=ps[:, 0:nn], func=AF.Sigmoid, bias=glub[:, co:co + 1]),
                      r=[ps.res, glub.res], w=[sgt.res])
                sg = sg5[si % 2]
                si += 1
                kb.op("dve", lambda h: h.tensor_tensor(out=sg[:, 0:nn], in0=sgt[:, 0:nn],
                                                       in1=YT[:, co, tb // 16:(tb + nn) // 16, :].rearrange("p m s -> p (m s)"), op=ALU.mult),
                      r=[sgt.res, UY.res], w=[sg.res])
                kb.dma("act", C["yT"][1536 + co * 128:1536 + (co + 1) * 128, tok0 + tb:tok0 + tb + nn], sg[:, 0:nn], r=[sg.res])


class VW:
    def __init__(self, tl, f):
        self.tl, self.f, self.res = tl, f, tl.res

    def __getitem__(self, idx):
        return self.f(self.tl)[idx]


TB = 256


def rwkv_mixer(P, dn, l, C):
    kb, cx, S = P.kb, P.cx, P.S
    ident = C["ident"]
    NTB = S // TB
    f32 = lambda shp, n: cx.sb(shp, F32, n)
    b16 = lambda shp, n: cx.sb(shp, BF16, n)

    def tt(e, out, a, b, op, r, w):
        kb.op(e, lambda h: h.tensor_tensor(out=out, in0=a, in1=b, op=op), r=r, w=w)

    def pcol(name, lo, n, nm):
        t = f32([128, n // 128], nm)
        kb.dma("sp", t[:, :], C["in_" + name][l, lo:lo + n].rearrange("(c q) -> q c", q=128), w=[t.res], slow=True)
        return t

    mix_r, mix_k, mix_v = pcol("rw_mix", 0, 768, "mixr"), pcol("rw_mix", 768, 768, "mixk"), pcol("rw_mix", 1536, 768, "mixv")
    mix_g = pcol("rw_mix", 2496, 256, "mixg")
    mix_wa = f32([96, 2], "mixwa")
    kb.dma("sp", mix_wa[:, :], C["in_rw_mix"][l, 2304:2496].rearrange("(c q) -> q c", q=96), w=[mix_wa.res], slow=True)
    w0, a0 = pcol("rw_w0", 0, 768, "w0"), pcol("rw_a0", 0, 768, "a0")
    k_k, k_a = pcol("rw_k_k", 0, 768, "k_k"), pcol("rw_k_a", 0, 768, "k_a")
    r_k, ln_w, ln_b = pcol("rw_r_k", 0, 768, "r_k"), pcol("rw_ln_w", 0, 768, "ln_w"), pcol("rw_ln_b", 0, 768, "ln_b")
    omka = f32([128, 6], "omka")
    kb.op("dve", lambda h: h.tensor_scalar(out=omka[:, :], in0=k_a[:, :], scalar1=-1.0, scalar2=1.0, op0=ALU.mult, op1=ALU.add),
          r=[k_a.res], w=[omka.res])
    w2b, a2b, g2b = b16([96, 768], "w2b"), b16([96, 768], "a2b"), b16([128, 2, 768], "g2b")
    kb.dma("pool", w2b[:, :], C["in_rw_w2"][l], w=[w2b.res])
    kb.dma("pool", a2b[:, :], C["in_rw_a2"][l], w=[a2b.res])
    kb.dma("pool", g2b[:, :, :], C["in_rw_g2"][l].rearrange("(k q) c -> q k c", q=128), w=[g2b.res])
    MUs, MUi, MLs = f32([128, 128], "MUs"), f32([128, 128], "MUi"), f32([128, 128], "MLs")
    kb.dma("sp", MUs[:, :], C["rw_masks"][0], w=[MUs.res])
    kb.dma("sp", MUi[:, :], C["rw_masks"][1], w=[MUi.res])
    kb.dma("sp", MLs[:, :], C["rw_masks"][2], w=[MLs.res])
    MUsi = f32([128, 256], "MUsi")
    kb.dma("sp", MUsi[:, 0:128], C["rw_masks"][0], w=[MUsi.res])
    kb.dma("sp", MUsi[:, 128:256], C["rw_masks"][1], w=[MUsi.res])
    bones = b16([128, 128], "bones")
    kb.dma("sp", bones[:, :], C["bones"][:, :], w=[bones.res])
    cmk = f32([128, TB], "cmk")
    kb.dma("sp", cmk[:, :], C["chunkmask"][:, 0:TB], w=[cmk.res])
    zw = b16([96, 2, TB + 1], "zwa")
    zg = b16([128, 2, TB + 1], "zg")
    th, zab = b16([96, TB], "th"), b16([96, TB], "zab")
    sgg = b16([128, 2, TB], "sgg")
    ltmp = f32([128, 2, TB], "ltmp")
    Sst = [f32([128, 128], "Sst") for _ in range(6)]
    Sb = [b16([128, 128], "Sb") for _ in range(6)]
    for p in range(6):
        kb.op("pool", lambda h: h.memset(Sst[p][:, :], 0.0), w=[Sst[p].res])
        kb.op("pool", lambda h: h.memset(Sb[p][:, :], 0.0), w=[Sb[p].res])
    identb = ident


    def c3(ap):
        return ap.rearrange("p (c t) -> p c t", t=64)

    def shift(eng, out_ap, zt_ap, mixcol, npart, r, w, dtm):
        tt(eng, dtm[0:npart, :], zt_ap[:, 0:TB], zt_ap[:, 1:TB + 1], ALU.subtract, r, [dtm.res])
        kb.op("dve", lambda h: h.scalar_tensor_tensor(out=out_ap, in0=dtm[0:npart, :], scalar=mixcol, in1=zt_ap[:, 1:TB + 1],
                                                      op0=ALU.mult, op1=ALU.add), r=r + [dtm.res], w=w)

    def load_shifted(tile_ap, res, row0, nrows, t0):
        if t0 == 0:
            kb.op("pool", lambda h: h.memset(tile_ap[:, 0:1], 0.0), w=[res])
            kb.dma("sp", tile_ap[:, 1:TB + 1], C["zT"][row0:row0 + nrows, 0:TB], w=[res])
        else:
            kb.dma("sp", tile_ap[:, :], C["zT"][row0:row0 + nrows, t0 - 1:t0 + TB], w=[res])

    dtm0 = f32([128, TB], "dtm0")

    def make_unit():
        z3 = b16([128, 3, TB + 1], "zin")
        rs, ks, vs = f32([128, TB], "rs"), f32([128, TB], "ks"), f32([128, TB], "vs")
        dtm = f32([128, TB], "dtm")
        logw, alpha, gg_ = f32([128, TB], "logw"), f32([128, TB], "alpha"), f32([128, TB], "gfm")
        kk, kk2, rinv = f32([128, TB], "kk"), b16([128, TB], "kk2"), f32([128, TB], "rinv")
        kp, bq = f32([128, TB], "kp"), f32([128, TB], "bq")
        cc, ec, enc, ecm, ecc = f32([128, TB], "cc"), f32([128, TB], "ec"), f32([128, TB], "enc"), f32([128, TB], "ecm"), f32([128, TB], "ecc")
        gC = f32([128, 4], "gC")
        rkb = b16([128, TB], "rkb")
        bonus = f32([128, TB], "bonus")
        AQ = b16([128, 4, 256], "AQ")
        bblk, kblk, b2blk, k2blk, vblk = (b16([128, 4, 128], n) for n in ("bblk", "kblk", "b2blk", "k2blk", "vblk"))
        for t in (AQ, bblk, kblk, b2blk, k2blk, vblk):
            kb.op("pool", lambda h: h.memset(t[:, :, :], 0.0), w=[t.res])
        TM = b16([128, 4, 4, 128], "TM")
        Xn = [b16([128, 4, 128], "Xn") for _ in range(2)]
        Nn = [b16([128, 4, 128], "Nn") for _ in range(2)]
        Pm = b16([128, 4, 128], "Pm")
        ArbT, AakT, ArkT, AkV, WT = (b16([128, 4, 128], n) for n in ("ArbT", "AakT", "ArkT", "AkV", "WT"))
        U = f32([128, 4, 128], "U")
        Ytm = f32([128, 4, 128], "Ytm")
        ysq = f32([128, 4, 128], "ysq")
        ynb = b16([128, 4, 128], "ynb")
        st1, st2, st3 = f32([128, 4], "st1"), f32([128, 4], "st2"), f32([128, 4], "st3")
        yfm = f32([128, TB], "yfm")
        SAb = b16([128, 128], "SAb")
        ystg = [b16([128, TB], "ystg") for _ in range(2)]
        sic = [0]

        def unit(tb, p):
            t0 = tb * TB
            pc = slice(p * 128, (p + 1) * 128)
            for i, row0 in enumerate((p * 128, 768 + p * 128, 1536 + p * 128)):
                load_shifted(z3[:, i, :], z3.res, row0, 128, t0)
            shift("pool", rs[:, :], z3[:, 0, :], mix_r[:, p:p + 1], 128, [z3.res, mix_r.res], [rs.res], dtm)
            shift("pool", ks[:, :], z3[:, 1, :], mix_k[:, p:p + 1], 128, [z3.res, mix_k.res], [ks.res], dtm)
            shift("pool", vs[:, :], z3[:, 2, :], mix_v[:, p:p + 1], 128, [z3.res, mix_v.res], [vs.res], dtm)
            yield
            ps = dn.nps()
            kb.op("pe", lambda h: h.matmul(ps[:, 0:TB], w2b[:, pc], th[:, :], start=True, stop=True), r=[w2b.res, th.res], w=[ps.res])
            kb.op("act", lambda h: h.activation(out=logw[:, :], in_=ps[:, 0:TB], func=AF.Sigmoid, bias=w0[:, p:p + 1]),
                  r=[ps.res, w0.res], w=[logw.res])
            kb.op("pool", lambda h: h.tensor_scalar(out=logw[:, :], in0=logw[:, :], scalar1=-math.exp(-0.5), scalar2=None, op0=ALU.mult),
                  r=[logw.res], w=[logw.res])
            ps = dn.nps()
            kb.op("pe", lambda h: h.matmul(ps[:, 0:TB], a2b[:, pc], zab[:, :], start=True, stop=True), r=[a2b.res, zab.res], w=[ps.res])
            kb.op("act", lambda h: h.activation(out=alpha[:, :], in_=ps[:, 0:TB], func=AF.Sigmoid, bias=a0[:, p:p + 1]),
                  r=[ps.res, a0.res], w=[alpha.res])
            ps = dn.nps()
            for k in range(2):
                kb.op("pe", lambda h: h.matmul(ps[:, 0:TB], g2b[:, k, pc], sgg[:, k, :], start=(k == 0), stop=(k == 1)),
                      r=[g2b.res, sgg.res], w=[ps.res])
            kb.op("act", lambda h: h.activation(out=gg_[:, :], in_=ps[:, 0:TB], func=AF.Identity), r=[ps.res], w=[gg_.res])
            yield
            kb.op("act", lambda h: h.activation(out=kk[:, :], in_=ks[:, :], func=AF.Identity, scale=k_k[:, p:p + 1]),
                  r=[ks.res, k_k.res], w=[kk.res])
            tt("pool", kk2[:, :], kk[:, :], kk[:, :], ALU.mult, [kk.res], [kk2.res])
            yield
            ps = dn.nps()
            kb.op("pe", lambda h: h.matmul(ps[:, 0:TB], bones[:, :], kk2[:, :], start=True, stop=True), r=[bones.res, kk2.res], w=[ps.res])
            kb.op("act", lambda h: h.activation(out=rinv[:, :], in_=ps[:, 0:TB], func=AF.Sqrt), r=[ps.res], w=[rinv.res])
            kb.op("dve", lambda h: h.tensor_scalar(out=rinv[:, :], in0=rinv[:, :], scalar1=1e-12, scalar2=None, op0=ALU.max),
                  r=[rinv.res], w=[rinv.res])
            kb.op("dve", lambda h: h.reciprocal(out=rinv[:, :], in_=rinv[:, :]), r=[rinv.res], w=[rinv.res])
            tt("dve", kk[:, :], kk[:, :], rinv[:, :], ALU.mult, [kk.res, rinv.res], [kk.res])
            yield
            yield
            kb.op("dve", lambda h: h.tensor_scalar(out=kp[:, :], in0=alpha[:, :], scalar1=k_a[:, p:p + 1], scalar2=omka[:, p:p + 1],
                                                   op0=ALU.mult, op1=ALU.add), r=[alpha.res, k_a.res, omka.res], w=[kp.res])
            tt("pool", kp[:, :], kp[:, :], ks[:, :], ALU.mult, [kp.res, ks.res], [kp.res])
            yield
            tt("pool", bq[:, :], kk[:, :], alpha[:, :], ALU.mult, [kk.res, alpha.res], [bq.res])
            yield
            yield
            kb.op("dve", lambda h: h.tensor_tensor_scan(out=cc[:, :], data0=cmk[:, :], data1=logw[:, :], initial=0.0,
                                                        op0=ALU.mult, op1=ALU.add), r=[cmk.res, logw.res], w=[cc.res])
            kb.op("act", lambda h: h.activation(out=ec[:, :], in_=cc[:, :], func=AF.Exp), r=[cc.res], w=[ec.res])
            kb.op("act", lambda h: h.activation(out=enc[:, :], in_=cc[:, :], func=AF.Exp, scale=-1.0), r=[cc.res], w=[enc.res])
            tt("pool", ecm[:, :], cc[:, :], logw[:, :], ALU.subtract, [cc.res, logw.res], [ecm.res])
            yield
            kb.op("act", lambda h: h.activation(out=ecm[:, :], in_=ecm[:, :], func=AF.Exp), r=[ecm.res], w=[ecm.res])
            kb.op("act", lambda h: h.activation(out=gC[:, :], in_=c3(cc[:, :])[:, :, 63], func=AF.Exp), r=[cc.res], w=[gC.res])
            tt("dve", c3(ecc[:, :]), c3(cc[:, :])[:, :, 63:64].to_broadcast([128, 4, 64]), c3(cc[:, :]), ALU.subtract,
               [cc.res], [ecc.res])
            kb.op("act", lambda h: h.activation(out=ecc[:, :], in_=ecc[:, :], func=AF.Exp), r=[ecc.res], w=[ecc.res])
            yield
            kb.op("dve", lambda h: h.scalar_tensor_tensor(out=rkb[:, :], in0=rs[:, :], scalar=r_k[:, p:p + 1], in1=kp[:, :],
                                                          op0=ALU.mult, op1=ALU.mult), r=[rs.res, r_k.res, kp.res], w=[rkb.res])
            ps = dn.nps()
            kb.op("pe", lambda h: h.matmul(ps[:, 0:TB], bones[:, :], rkb[:, :], start=True, stop=True), r=[bones.res, rkb.res], w=[ps.res])
            tt("dve", bonus[:, :], ps[:, 0:TB], vs[:, :], ALU.mult, [ps.res, vs.res], [bonus.res])
            yield
            yield
            engs = ["dve", "pool"]
            ei = 0
            for hh in range(2):
                lo = hh * 64
                sl_ = slice(lo, lo + 64)

                def blk(dst, colofs, a, b, ra, rb, neg=False):
                    nonlocal ei
                    e = engs[ei % 2]
                    ei += 1
                    o = dst[sl_, :, colofs + lo:colofs + lo + 64]
                    if neg:
                        kb.op("dve", lambda h: h.scalar_tensor_tensor(out=o, in0=c3(a[sl_, :]), scalar=-1.0, in1=c3(b[sl_, :]),
                                                                      op0=ALU.mult, op1=ALU.mult), r=[ra, rb], w=[dst.res])
                    elif b is None:
                        kb.op(e, lambda h: h.tensor_copy(out=o, in_=c3(a[sl_, :])), r=[ra], w=[dst.res])
                    else:
                        tt(e, o, c3(a[sl_, :]), c3(b[sl_, :]), ALU.mult, [ra, rb], [dst.res])

                blk(AQ, 0, kk, ecm, kk.res, ecm.res, neg=True)
                blk(AQ, 128, rs, ec, rs.res, ec.res)
                blk(bblk, 0, bq, enc, bq.res, enc.res)
                blk(kblk, 0, kp, enc, kp.res, enc.res)
                blk(b2blk, 0, bq, ecc, bq.res, ecc.res)
                blk(k2blk, 0, kp, ecc, kp.res, ecc.res)
                blk(vblk, 0, vs, None, vs.res, None)
            yield
            srcs = [(AQ, 0), (b2blk, 0), (k2blk, 0), (vblk, 0)]
            for half in range(2):
                pb = dn.npb()
                for qi in range(2):
                    src, co = srcs[half * 2 + qi]
                    for ch in range(4):
                        k_ = qi * 4 + ch
                        kb.op("pe", lambda h: h.transpose(out=pb[:, k_ * 128:(k_ + 1) * 128], in_=src[:, ch, co:co + 128],
                                                          identity=identb[:, :]), r=[src.res, identb.res], w=[pb.res])
                copy_evac(kb, kb.ev(), TM[:, half * 2:half * 2 + 2, :, :].rearrange("p q c m -> p (q c m)"), pb[:, :],
                          r=[pb.res], w=[TM.res])
            yield
            ps = dn.nps()
            for ch in range(4):
                kb.op("pe", lambda h: h.matmul(ps[:, ch * 128:(ch + 1) * 128], AQ[:, ch, 0:128], bblk[:, ch, :],
                                               start=(ch == 0), stop=True, skip_group_check=True), r=[AQ.res, bblk.res], w=[ps.res])
            X0, N0 = Xn[0], Nn[0]
            tt("dve", N0[:, :, :], ps[:, :].rearrange("p (c m) -> p c m", m=128), MLs[:, :].unsqueeze(1).to_broadcast([128, 4, 128]),
               ALU.mult, [ps.res, MLs.res], [N0.res])
            for (lhs, dA, dB_) in ((bblk, X0, ArbT), (kblk, AakT, ArkT)):
                for half in range(2):
                    ps = dn.nps()
                    for c2 in range(2):
                        ch = half * 2 + c2
                        kb.op("pe", lambda h: h.matmul(ps[:, c2 * 256:(c2 + 1) * 256], lhs[:, ch, :], AQ[:, ch, :],
                                                       start=(c2 == 0), stop=True, skip_group_check=True), r=[lhs.res, AQ.res], w=[ps.res])
                    pv = ps[:, :].rearrange("p (c m) -> p c m", m=256)
                    tt("dve", dA[:, half * 2:half * 2 + 2, :], pv[:, :, 0:128], MUs[:, :].unsqueeze(1).to_broadcast([128, 2, 128]),
                       ALU.mult, [ps.res, MUs.res], [dA.res])
                    tt("dve", dB_[:, half * 2:half * 2 + 2, :], pv[:, :, 128:256], MUi[:, :].unsqueeze(1).to_broadcast([128, 2, 128]),
                       ALU.mult, [ps.res, MUi.res], [dB_.res])
            yield
            tt("pool", Pm[:, :, :], X0[:, :, :], identb[:, :].unsqueeze(1).to_broadcast([128, 4, 128]), ALU.add,
               [X0.res, identb.res], [Pm.res])
            Xc, Nc = X0, N0
            for j in range(1, 6):
                Xnx, Nnx = Xn[j % 2], Nn[j % 2]
                if j < 5:
                    ps = dn.nps()
                    for ch in range(4):
                        kb.op("pe", lambda h: h.matmul(ps[:, ch * 128:(ch + 1) * 128], Nc[:, ch, :], Xc[:, ch, :],
                                                       start=(ch == 0), stop=True, skip_group_check=True), r=[Nc.res, Xc.res], w=[ps.res])
                    copy_evac(kb, "act", Xnx[:, :, :].rearrange("p c m -> p (c m)"), ps[:, :], r=[ps.res], w=[Xnx.res])
                ps = dn.nps()
                for ch in range(4):
                    kb.op("pe", lambda h: h.matmul(ps[:, ch * 128:(ch + 1) * 128], Xc[:, ch, :], Nc[:, ch, :],
                                                   start=(ch == 0), stop=True, skip_group_check=True), r=[Nc.res, Xc.res], w=[ps.res])
                copy_evac(kb, "act", Nnx[:, :, :].rearrange("p c m -> p (c m)"), ps[:, :], r=[ps.res], w=[Nnx.res])
                ps = dn.nps()
                for ch in range(4):
                    kb.op("pe", lambda h: h.matmul(ps[:, ch * 128:(ch + 1) * 128], Nnx[:, ch, :], Pm[:, ch, :],
                                                   start=(ch == 0), stop=True, skip_group_check=True), r=[Nnx.res, Pm.res], w=[ps.res])
                tt("dve", Pm[:, :, :].rearrange("p c m -> p (c m)"), ps[:, :], Pm[:, :, :].rearrange("p c m -> p (c m)"), ALU.add,
                   [ps.res, Pm.res], [Pm.res])
                Xc, Nc = Xnx, Nnx
                yield
            yield
            ps = dn.nps()
            for ch in range(4):
                kb.op("pe", lambda h: h.matmul(ps[:, ch * 128:(ch + 1) * 128], AakT[:, ch, :], TM[:, 3, ch, :],
                                               start=(ch == 0), stop=True, skip_group_check=True), r=[AakT.res, TM.res], w=[ps.res])
            copy_evac(kb, "act", AkV[:, :, :].rearrange("p c m -> p (c m)"), ps[:, :], r=[ps.res], w=[AkV.res])
            yield
            ps = dn.nps()
            for ch in range(4):
                kb.op("pe", lambda h: h.matmul(ps[:, ch * 128:(ch + 1) * 128], Pm[:, ch, :], AkV[:, ch, :],
                                               start=(ch == 0), stop=True, skip_group_check=True), r=[Pm.res, AkV.res], w=[ps.res])
            copy_evac(kb, "act", U[:, :, :].rearrange("p c m -> p (c m)"), ps[:, :], r=[ps.res], w=[U.res])
            yield
            ps = dn.nps()
            for ch in range(4):
                kb.op("pe", lambda h: h.matmul(ps[:, ch * 128:(ch + 1) * 128], TM[:, 0, ch, :], Pm[:, ch, :],
                                               start=(ch == 0), stop=True, skip_group_check=True), r=[TM.res, Pm.res], w=[ps.res])
            copy_evac(kb, "act", WT[:, :, :].rearrange("p c m -> p (c m)"), ps[:, :], r=[ps.res], w=[WT.res])
            yield
            yield
            S_, Sb_ = Sst[p], Sb[p]
            for ch in range(4):
                ps = dn.nps()
                kb.op("pe", lambda h: h.matmul(ps[:, 0:128], WT[:, ch, :], Sb_[:, :], start=True, stop=True), r=[WT.res, Sb_.res], w=[ps.res])
                tt("dve", SAb[:, :], ps[:, 0:128], U[:, ch, :], ALU.add, [ps.res, U.res], [SAb.res])
                yield
                psy = dn.nps()
                kb.op("pe", lambda h: h.matmul(psy[:, 0:128], AQ[:, ch, 128:256], Sb_[:, :], start=True, stop=False),
                      r=[AQ.res, Sb_.res], w=[psy.res])
                kb.op("pe", lambda h: h.matmul(psy[:, 0:128], ArbT[:, ch, :], SAb[:, :], start=False, stop=False),
                      r=[ArbT.res, SAb.res], w=[psy.res])
                kb.op("pe", lambda h: h.matmul(psy[:, 0:128], ArkT[:, ch, :], TM[:, 3, ch, :], start=False, stop=True),
                      r=[ArkT.res, TM.res], w=[psy.res])
                copy_evac(kb, "act", Ytm[:, ch, :], psy[:, 0:128], r=[psy.res], w=[Ytm.res])
                pss = dn.nps()
                kb.op("pe", lambda h: h.matmul(pss[:, 0:128], TM[:, 1, ch, :], SAb[:, :], start=True, stop=False),
                      r=[TM.res, SAb.res], w=[pss.res])
                kb.op("pe", lambda h: h.matmul(pss[:, 0:128], TM[:, 2, ch, :], TM[:, 3, ch, :], start=False, stop=True),
                      r=[TM.res], w=[pss.res])
                kb.op("dve", lambda h: h.scalar_tensor_tensor(out=S_[:, :], in0=S_[:, :], scalar=gC[:, ch:ch + 1], in1=pss[:, 0:128],
                                                              op0=ALU.mult, op1=ALU.add), r=[S_.res, gC.res, pss.res], w=[S_.res])
                kb.op("act", lambda h: h.activation(out=Sb_[:, :], in_=S_[:, :], func=AF.Identity), r=[S_.res], w=[Sb_.res])
                yield
            yield
            kb.op("dve", lambda h: h.tensor_reduce(out=st1[:, :], in_=Ytm[:, :, :], axis=AX.X, op=ALU.add), r=[Ytm.res], w=[st1.res])
            kb.op("act", lambda h: h.activation(out=ysq[:, :, :], in_=Ytm[:, :, :], func=AF.Square), r=[Ytm.res], w=[ysq.res])
            kb.op("dve", lambda h: h.tensor_reduce(out=st2[:, :], in_=ysq[:, :, :], axis=AX.X, op=ALU.add), r=[ysq.res], w=[st2.res])
            kb.op("dve", lambda h: h.tensor_scalar(out=st1[:, :], in0=st1[:, :], scalar1=1.0 / 64, scalar2=None, op0=ALU.mult),
                  r=[st1.res], w=[st1.res])
            tt("dve", st3[:, :], st1[:, :], st1[:, :], ALU.mult, [st1.res], [st3.res])
            yield
            kb.op("dve", lambda h: h.scalar_tensor_tensor(out=st2[:, :], in0=st2[:, :], scalar=1.0 / 64, in1=st3[:, :],
                                                          op0=ALU.mult, op1=ALU.subtract), r=[st2.res, st3.res], w=[st2.res])
            kb.op("act", lambda h: h.activation(out=st2[:, :], in_=st2[:, :], func=AF.Sqrt, bias=GN_EPS), r=[st2.res], w=[st2.res])
            kb.op("dve", lambda h: h.reciprocal(out=st2[:, :], in_=st2[:, :]), r=[st2.res], w=[st2.res])
            pb = dn.npb()
            for ch in range(4):
                kb.op("dve", lambda h: h.tensor_scalar(out=ynb[:, ch, :], in0=Ytm[:, ch, :], scalar1=st1[:, ch:ch + 1],
                                                       scalar2=st2[:, ch:ch + 1], op0=ALU.subtract, op1=ALU.mult),
                      r=[Ytm.res, st1.res, st2.res], w=[ynb.res])
                kb.op("pe", lambda h: h.transpose(out=pb[:, ch * 128:(ch + 1) * 128], in_=ynb[:, ch, :], identity=identb[:, :]),
                      r=[ynb.res, identb.res], w=[pb.res])
            pbv = pb[:, 0:512].rearrange("p (c m) -> p c m", m=128)
            kb.op("act", lambda h: h.activation(out=c3(yfm[0:64, :]), in_=pbv[0:64, :, 0:64], func=AF.Identity), r=[pb.res], w=[yfm.res])
            kb.op("act", lambda h: h.activation(out=c3(yfm[64:128, :]), in_=pbv[64:128, :, 64:128], func=AF.Identity), r=[pb.res], w=[yfm.res])
            kb.op("dve", lambda h: h.tensor_scalar(out=yfm[:, :], in0=yfm[:, :], scalar1=ln_w[:, p:p + 1], scalar2=ln_b[:, p:p + 1],
                                                   op0=ALU.mult, op1=ALU.add), r=[yfm.res, ln_w.res, ln_b.res], w=[yfm.res])
            tt("pool", yfm[:, :], yfm[:, :], bonus[:, :], ALU.add, [yfm.res, bonus.res], [yfm.res])
            yield
            sg = ystg[sic[0] % 2]
            sic[0] += 1
            tt("pool", sg[:, :], yfm[:, :], gg_[:, :], ALU.mult, [yfm.res, gg_.res], [sg.res])
            yield
            kb.dma("act", C["yT"][p * 128:(p + 1) * 128, t0:t0 + TB], sg[:, :], r=[sg.res])

            yield

        return unit

    units = [make_unit(), make_unit()]
    for tb in range(NTB):
        t0 = tb * TB
        load_shifted(zw[:, 0, :], zw.res, 2304, 96, t0)
        load_shifted(zw[:, 1, :], zw.res, 2400, 96, t0)
        load_shifted(zg[:, 0, :], zg.res, 2496, 128, t0)
        load_shifted(zg[:, 1, :], zg.res, 2624, 128, t0)
        shift("dve", ltmp[0:96, 0, :], zw[:, 0, :], mix_wa[:, 0:1], 96, [zw.res, mix_wa.res], [ltmp.res], dtm0)
        kb.op("act", lambda h: h.activation(out=th[:, :], in_=ltmp[0:96, 0, :], func=AF.Tanh), r=[ltmp.res], w=[th.res])
        shift("dve", ltmp[0:96, 1, :], zw[:, 1, :], mix_wa[:, 1:2], 96, [zw.res, mix_wa.res], [ltmp.res], dtm0)
        kb.op("act", lambda h: h.activation(out=zab[:, :], in_=ltmp[0:96, 1, :], func=AF.Identity), r=[ltmp.res], w=[zab.res])
        for k in range(2):
            shift("dve", ltmp[:, k, :], zg[:, k, :], mix_g[:, k:k + 1], 128, [zg.res, mix_g.res], [ltmp.res], dtm0)
            kb.op("act", lambda h: h.activation(out=sgg[:, k, :], in_=ltmp[:, k, :], func=AF.Sigmoid), r=[ltmp.res], w=[sgg.res])
        for pp in range(0, 6, 2):
            gens = [units[0](tb, pp), units[1](tb, pp + 1)]
            while gens:
                for g_ in list(gens):
                    try:
                        next(g_)
                    except StopIteration:
                        gens.remove(g_)
```

```python
import math
from contextlib import ExitStack
import numpy as np
import ml_dtypes
import concourse.bass as bass
import concourse.mybir as mybir
from concourse.bass_utils import run_bass_kernel_spmd

F32 = mybir.dt.float32
BF16 = mybir.dt.bfloat16
I32 = mybir.dt.int32
AF = mybir.ActivationFunctionType
ALU = mybir.AluOpType
AX = mybir.AxisListType

D = 2048
DEPTH = 4
A_DIM = 768
RW_IN = 2752
B_IN = 2304
C_DIM = 512
N_IN = 11712
D_FF = 5632
EPS = 1e-6
GN_EPS = 64e-5
ZROWS = 4288
TS = 512


class Res:
    __slots__ = ("name", "w", "r", "ds")

    def __init__(self, name):
        self.name = name
        self.w = []
        self.r = []
        self.ds = None


class Sem:
    __slots__ = ("h", "cnt", "dma")

    def __init__(self, h, dma):
        self.h = h
        self.cnt = 0
        self.dma = dma


class KB:
    def __init__(self, nc):
        self.nc = nc
        self.E = {"pe": nc.tensor, "act": nc.scalar, "dve": nc.vector, "pool": nc.gpsimd, "sp": nc.sync}
        self.sems = []
        self.eidx = {}
        for e in self.E:
            self.eidx[e] = len(self.sems)
            self.sems.append(Sem(nc.alloc_semaphore("q_" + e), False))
        self.seen = {e: {} for e in self.E}
        self.rr = 0
        self.nops = 0

    def _wait(self, e, evs):
        need = {}
        for k, v in evs:
            S = self.sems[k]
            if S.dma:
                v = S.cnt
            if v > need.get(k, 0):
                need[k] = v
        own = self.eidx[e]
        for k, v in need.items():
            if k == own and e == "pe":
                continue
            if self.seen[e].get(k, 0) < v:
                self.E[e].wait_ge(self.sems[k].h, v)
                self.seen[e][k] = v

    def _deps(self, r, w):
        evs = []
        for x in r:
            evs += x.w
        for x in w:
            evs += x.w
            evs += x.r
        return evs

    def _mark(self, ev, r, w):
        for x in r:
            x.r = [p for p in x.r if p[0] != ev[0]] + [ev]
        for x in w:
            x.w = [ev]
            x.r = []

    def op(self, e, fn, r=(), w=()):
        self._wait(e, self._deps(r, w))
        ins = fn(self.E[e])
        k = self.eidx[e]
        self.sems[k].cnt += 1
        ins.then_inc(self.sems[k].h, 1)
        self._mark((k, self.sems[k].cnt), r, w)
        self.nops += 1

    NDMA = 64

    def dsem(self, res):
        if res.ds is None:
            if not hasattr(self, "dpool"):
                self.dpool = []
                self.dnext = 0
            if len(self.dpool) < self.NDMA:
                self.dpool.append(len(self.sems))
                self.sems.append(Sem(self.nc.alloc_semaphore("d%d" % len(self.sems)), True))
                res.ds = self.dpool[-1]
            else:
                res.ds = self.dpool[self.dnext % self.NDMA]
                self.dnext += 1
        return res.ds

    def dma(self, e, out, in_, r=(), w=(), sres=None, slow=False):
        self._wait(e, self._deps(r, w))
        if sres is None:
            sres = w[0] if w else r[0]
        k = self.dsem(sres)
        if slow:
            ins = self.E[e].dma_start(out=out, in_=in_, allow_slow_non_contiguous=True)
        else:
            ins = self.E[e].dma_start(out=out, in_=in_)
        self.sems[k].cnt += 16
        ins.then_inc(self.sems[k].h, 16)
        self._mark((k, self.sems[k].cnt), r, w)
        self.nops += 1

    def barrier(self):
        evs = [(k, S.cnt) for k, S in enumerate(self.sems) if S.cnt > 0]
        for e in self.E:
            self._wait(e, evs)

    def ev(self):
        self.rr ^= 1
        return "act" if self.rr else "dve"


class Tl:
    def __init__(self, t, name):
        self.t = t
        self.res = Res(name)

    def __getitem__(self, idx):
        return self.t[idx]


class Ctx:
    def __init__(self, nc, kb):
        self.nc = nc
        self.kb = kb
        self.n = 0
        self.stack = None

    def sb(self, shape, dt, name=None):
        self.n += 1
        name = (name or "t") + "_%d" % self.n
        if self.stack is not None:
            t = self.stack.enter_context(self.nc.sbuf_tensor(name, list(shape), dt))
        else:
            t = self.nc.alloc_sbuf_tensor(name, list(shape), dt)
        return Tl(t, name)

    def ps(self, shape, dt, name=None):
        self.n += 1
        name = (name or "p") + "_%d" % self.n
        if self.stack is not None:
            t = self.stack.enter_context(self.nc.psum_tensor(name, list(shape), dt))
        else:
            t = self.nc.alloc_psum_tensor(name, list(shape), dt)
        return Tl(t, name)


def copy_evac(kb, eng, out_ap, in_ap, r, w, scale=None, bias=None, func=None):
    if eng == "act" or func is not None or scale is not None or bias is not None:
        kw = {}
        if scale is not None:
            kw["scale"] = scale
        if bias is not None:
            kw["bias"] = bias
        f = func if func is not None else AF.Identity
        kb.op("act", lambda h: h.activation(out=out_ap, in_=in_ap, func=f, **kw), r=r, w=w)
    else:
        kb.op(eng, lambda h: h.tensor_copy(out=out_ap, in_=in_ap), r=r, w=w)


class Prog:
    def __init__(self, S, depth, debug=False):
        self.S = S
        self.depth = depth
        self.debug = debug
        self.nc = bass.Bass("TRN2", target_bir_lowering=False)
        self.kb = KB(self.nc)
        self.cx = Ctx(self.nc, self.kb)
        self.dram = {}

    def din(self, name, shape, dt=F32):
        t = self.nc.dram_tensor(name, list(shape), dt, kind="ExternalInput")
        self.dram[name] = t
        return t.ap()

    def dscr(self, name, shape, dt, out=False):
        kind = "ExternalOutput" if (out or name in getattr(self, "dbg_out", ())) else "Internal"
        if name in getattr(self, "ext_in", ()):
            kind = "ExternalInput"
        t = self.nc.dram_tensor(name, list(shape), dt, kind=kind)
        self.dram[name] = t
        return t.ap()


PARAM_SHAPES = {
    "norm_mix": (DEPTH, D), "norm_ffn": (DEPTH, D), "w_in": (DEPTH, D, N_IN), "b_gate": (DEPTH, 3 * D),
    "rw_mix": (DEPTH, RW_IN), "rw_w0": (DEPTH, A_DIM), "rw_w2": (DEPTH, 96, A_DIM), "rw_a0": (DEPTH, A_DIM),
    "rw_a2": (DEPTH, 96, A_DIM), "rw_g2": (DEPTH, 256, A_DIM), "rw_k_k": (DEPTH, A_DIM), "rw_k_a": (DEPTH, A_DIM),
    "rw_r_k": (DEPTH, A_DIM), "rw_ln_w": (DEPTH, A_DIM), "rw_ln_b": (DEPTH, A_DIM),
    "da_lq1": (DEPTH, 64), "da_lk1": (DEPTH, 64), "da_lq2": (DEPTH, 64), "da_lk2": (DEPTH, 64), "da_subln": (DEPTH, 128),
    "s5_lam_re": (DEPTH, 32, 64), "s5_lam_im": (DEPTH, 32, 64), "s5_log_dt": (DEPTH, 32),
    "s5_b_re": (DEPTH, 32, 64, 16), "s5_b_im": (DEPTH, 32, 64, 16), "s5_c_re": (DEPTH, 32, 16, 64), "s5_c_im": (DEPTH, 32, 16, 64),
    "s5_d": (DEPTH, 512), "s5_glu_w": (DEPTH, 512, 512), "s5_glu_b": (DEPTH, 512),
    "proj_a": (DEPTH, A_DIM, D), "proj_b": (DEPTH, A_DIM, D), "proj_c": (DEPTH, C_DIM, D), "w_out": (DEPTH, D, D),
    "ffn_up": (DEPTH, D, 2 * D_FF), "ffn_conv": (DEPTH, 3, D_FF), "ffn_down": (DEPTH, D_FF, D), "norm_final": (D,),
}


def units_p1():
    u = []
    for c in range(0, 2304, 128):
        u.append((c, 128))
    u += [(2304, 96), (2400, 96), (2496, 128), (2624, 128)]
    for c in range(2752, 4288, 128):
        u.append((c, 128))
    return u


def group_blocks(units, maxc=512):
    blocks = []
    cur = []
    for (c, m) in units:
        if cur and (c + m - cur[0][0] > maxc or c != cur[-1][0] + cur[-1][1]):
            blocks.append(cur)
            cur = []
        cur.append((c, m))
    if cur:
        blocks.append(cur)
    return blocks


class Dense:
    def __init__(self, P):
        self.P = P
        cx, kb = P.cx, P.kb
        self.psf = [cx.ps([128, 512], F32, "psf") for _ in range(6)]
        self.psb = [cx.ps([128, 1024], BF16, "psb") for _ in range(2)]
        self.ipf = 0
        self.ipb = 0

    def nps(self, n=None):
        n = n or len(self.psf)
        self.ipf = (self.ipf + 1) % n
        return self.psf[self.ipf]

    def npb(self):
        self.ipb = (self.ipb + 1) % len(self.psb)
        return self.psb[self.ipb]


def rmsnorm_T(P, dn, xt, gcol, hT, junk, ss, rs, xn, ident):
    kb = P.kb
    for j in range(4):
        kb.op("act", lambda h: h.activation(out=junk[:, :], in_=xt[:, j, :], func=AF.Square,
                                            accum_out=ss[:, j:j + 1]), r=[xt.res], w=[junk.res, ss.res])
    kb.op("act", lambda h: h.activation(out=rs[:, 0:4], in_=ss[:, 0:4], func=AF.Sqrt, scale=1.0 / D, bias=EPS),
          r=[ss.res], w=[rs.res])
    kb.op("dve", lambda h: h.reciprocal(out=rs[:, 0:4], in_=rs[:, 0:4]), r=[rs.res], w=[rs.res])
    for j in range(4):
        kb.op("dve", lambda h: h.tensor_scalar(out=xn[:, j, :], in0=xt[:, j, :], scalar1=rs[:, j:j + 1],
                                               scalar2=None, op0=ALU.mult), r=[xt.res, rs.res], w=[xn.res])
    for c in range(16):
        pb = dn.npb()
        for j in range(4):
            kb.op("pe", lambda h: h.transpose(out=pb[:, j * 128:(j + 1) * 128], in_=xn[:, j, c * 128:(c + 1) * 128],
                                              identity=ident[:, :]), r=[xn.res, ident.res], w=[pb.res])
        kb.op("act", lambda h: h.activation(out=hT[:, c, :], in_=pb[:, 0:512], func=AF.Identity,
                                            scale=gcol[:, c:c + 1]), r=[pb.res, gcol.res], w=[hT.res])


def phase1(P, dn, l, C):
    kb, cx, S = P.kb, P.cx, P.S
    x_src = C["x"] if l == 0 else C["xres"]
    gcol, bgcol, ident = C["gmix"], C["bgate"], C["ident"]
    xt = C["xt"]
    hT = C["hT"]
    wb = C["wblk"]
    st = C["stage"]
    stm = C["stage_tm"]
    ublocks = group_blocks(units_p1())
    wi = 0
    si = 0
    for it in range(S // TS):
        t0 = it * TS
        kb.dma("sp", xt[:, :, :], x_src[t0:t0 + TS, :].rearrange("(j p) d -> p j d", p=128), w=[xt.res])
        rmsnorm_T(P, dn, xt, gcol[l], hT, C["junk"], C["ss"], C["rs"], C["xn"], ident)
        for blk in ublocks:
            c0 = blk[0][0]
            ncols = blk[-1][0] + blk[-1][1] - c0
            w = wb[wi % len(wb)]
            wi += 1
            kb.dma("sp", w[:, :, 0:ncols], C["w_in_b"][l, :, c0:c0 + ncols].rearrange("(k p) c -> p k c", p=128),
                   w=[w.res])
            for (c, m) in blk:
                ps = dn.nps()
                for k in range(16):
                    kb.op("pe", lambda h: h.matmul(ps[0:m, :], w[:, k, c - c0:c - c0 + m], hT[:, k, :],
                                                   start=(k == 0), stop=(k == 15)), r=[w.res, hT.res], w=[ps.res])
                sg = st[si % len(st)]
                si += 1
                copy_evac(kb, kb.ev(), sg[0:m, :], ps[0:m, :], r=[ps.res], w=[sg.res])
                kb.dma("act", C["zT"][c:c + m, t0:t0 + TS], sg[0:m, :], r=[sg.res])
        for (c0, ncols, dst, dc) in [(4288, 512, "vatt", 0), (4800, 256, "vatt", 512), (5056, 512, "u5", 0)]:
            w = wb[wi % len(wb)]
            wi += 1
            kb.dma("sp", w[:, :, 0:ncols], C["w_in_b"][l, :, c0:c0 + ncols].rearrange("(k p) c -> p k c", p=128),
                   w=[w.res])
            sg = stm[si % len(stm)]
            si += 1
            for j in range(4):
                ps = dn.nps()
                for k in range(16):
                    kb.op("pe", lambda h: h.matmul(ps[:, 0:ncols], hT[:, k, j * 128:(j + 1) * 128], w[:, k, 0:ncols],
                                                   start=(k == 0), stop=(k == 15)), r=[w.res, hT.res], w=[ps.res])
                copy_evac(kb, kb.ev(), sg[:, j, 0:ncols], ps[:, 0:ncols], r=[ps.res], w=[sg.res])
            kb.dma("act", C[dst][t0:t0 + TS, dc:dc + ncols].rearrange("(j p) c -> p j c", p=128), sg[:, :, 0:ncols],
                   r=[sg.res])
        for gb in range(12):
            c0 = 5568 + gb * 512
            w = wb[wi % len(wb)]
            wi += 1
            kb.dma("sp", w[:, :, :], C["w_in_b"][l, :, c0:c0 + 512].rearrange("(k p) c -> p k c", p=128), w=[w.res])
            for u in range(4):
                ps = dn.nps()
                for k in range(16):
                    kb.op("pe", lambda h: h.matmul(ps[:, :], w[:, k, u * 128:(u + 1) * 128], hT[:, k, :],
                                                   start=(k == 0), stop=(k == 15)), r=[w.res, hT.res], w=[ps.res])
                sg = st[si % len(st)]
                si += 1
                gi = gb * 4 + u
                kb.op("act", lambda h: h.activation(out=sg[:, :], in_=ps[:, :], func=AF.Sigmoid,
                                                    bias=bgcol[l][:, gi:gi + 1]), r=[ps.res, bgcol[l].res], w=[sg.res])
                kb.dma("act", C["gT"][gi * 128:(gi + 1) * 128, t0:t0 + TS], sg[:, :], r=[sg.res])


def gelu_tanh(kb, out_ap, x_ap, tmp_ap, r, w_out, w_tmp):
    kb.op("pool", lambda h: h.tensor_tensor(out=tmp_ap, in0=x_ap, in1=x_ap, op=ALU.mult), r=r, w=[w_tmp])
    kb.op("dve", lambda h: h.tensor_scalar(out=tmp_ap, in0=tmp_ap, scalar1=0.044715 * 1.5957691216, scalar2=1.5957691216,
                                           op0=ALU.mult, op1=ALU.add), r=[w_tmp], w=[w_tmp])
    kb.op("dve", lambda h: h.tensor_tensor(out=tmp_ap, in0=tmp_ap, in1=x_ap, op=ALU.mult), r=r + [w_tmp], w=[w_tmp])
    kb.op("act", lambda h: h.activation(out=tmp_ap, in_=tmp_ap, func=AF.Sigmoid), r=[w_tmp], w=[w_tmp])
    kb.op("dve", lambda h: h.tensor_tensor(out=out_ap, in0=tmp_ap, in1=x_ap, op=ALU.mult), r=r + [w_tmp], w=[w_out])


def phase3(P, dn, l, C, last):
    kb, cx, S = P.kb, P.cx, P.S
    x_src = C["x"] if l == 0 else C["xres"]
    ident = C["ident"]
    xt, hT, wb = C["xt"], C["hT"], C["wblk"]
    yT, mT, aT, gt = C["yTt"], C["mT"], C["aT"], C["gt"]
    wi = 0
    gi_ = 0
    for it in range(S // TS):
        t0 = it * TS
        kb.dma("sp", xt[:, :, :], x_src[t0:t0 + TS, :].rearrange("(j p) d -> p j d", p=128), w=[xt.res])
        kb.dma("sp", yT[:, :, :], C["yT"][:, t0:t0 + TS].rearrange("(k p) t -> p k t", p=128), w=[yT.res])
        for fb in range(4):
            w = wb[wi % len(wb)]
            wi += 1
            kb.dma("sp", w[:, :, :], C["proj_b"][l, :, fb * 512:(fb + 1) * 512].rearrange("(k p) c -> p k c", p=128),
                   w=[w.res])
            for u in range(4):
                fc = fb * 4 + u
                g = gt[gi_ % len(gt)]
                gi_ += 1
                kb.dma("sp", g[:, :, :], C["gT"][:, t0:t0 + TS].rearrange("(i f p) t -> p i f t", i=3, p=128)[:, :, fc, :],
                       w=[g.res])
                acc = C["macc"]
                for i, (k0, k1) in enumerate([(0, 6), (6, 12), (12, 16)]):
                    ps = dn.nps()
                    for k in range(k0, k1):
                        kb.op("pe", lambda h: h.matmul(ps[:, :], w[:, k, u * 128:(u + 1) * 128], yT[:, k, :],
                                                       start=(k == k0), stop=(k == k1 - 1)), r=[w.res, yT.res], w=[ps.res])
                    if i == 0:
                        kb.op("dve", lambda h: h.tensor_tensor(out=acc[:, :], in0=ps[:, :], in1=g[:, 0, :], op=ALU.mult),
                              r=[ps.res, g.res], w=[acc.res])
                    else:
                        tmp = C["mtmp"]
                        kb.op("dve", lambda h: h.tensor_tensor(out=tmp[:, :], in0=ps[:, :], in1=g[:, i, :], op=ALU.mult),
                              r=[ps.res, g.res], w=[tmp.res])
                        if i == 1:
                            kb.op("pool", lambda h: h.tensor_tensor(out=acc[:, :], in0=acc[:, :], in1=tmp[:, :], op=ALU.add),
                                  r=[tmp.res, acc.res], w=[acc.res])
                        else:
                            kb.op("pool", lambda h: h.tensor_tensor(out=mT[:, fc, :], in0=acc[:, :], in1=tmp[:, :], op=ALU.add),
                                  r=[tmp.res, acc.res], w=[mT.res])
        for cb in range(4):
            w = wb[wi % len(wb)]
            wi += 1
            kb.dma("sp", w[:, :, :], C["w_out_b"][l, :, cb * 512:(cb + 1) * 512].rearrange("(k p) c -> p k c", p=128),
                   w=[w.res])
            for j in range(4):
                ps = dn.nps()
                for k in range(16):
                    kb.op("pe", lambda h: h.matmul(ps[:, :], mT[:, k, j * 128:(j + 1) * 128], w[:, k, :],
                                                   start=(k == 0), stop=(k == 15)), r=[w.res, mT.res], w=[ps.res])
                kb.op("dve", lambda h: h.tensor_tensor(out=xt[:, j, cb * 512:(cb + 1) * 512], in0=ps[:, :],
                                                       in1=xt[:, j, cb * 512:(cb + 1) * 512], op=ALU.add),
                      r=[ps.res, xt.res], w=[xt.res])
        rmsnorm_T(P, dn, xt, C["gffn"][l], hT, C["junk"], C["ss"], C["rs"], C["xn"], ident)
        cw = C["convw"][l]
        carry = C["carry"]
        for fb in range(11):
            wv = wb[wi % len(wb)]
            wi += 1
            kb.dma("sp", wv[:, :, :], C["up_b"][l, :, fb * 512:(fb + 1) * 512].rearrange("(k p) c -> p k c", p=128),
                   w=[wv.res])
            wg = wb[wi % len(wb)]
            wi += 1
            kb.dma("sp", wg[:, :, :], C["up_b"][l, :, D_FF + fb * 512:D_FF + (fb + 1) * 512].rearrange("(k p) c -> p k c", p=128),
                   w=[wg.res])
            for u in range(4):
                f = fb * 4 + u
                psv = dn.nps()
                for k in range(16):
                    kb.op("pe", lambda h: h.matmul(psv[:, :], wv[:, k, u * 128:(u + 1) * 128], hT[:, k, :],
                                                   start=(k == 0), stop=(k == 15)), r=[wv.res, hT.res], w=[psv.res])
                psg = dn.nps()
                for k in range(16):
                    kb.op("pe", lambda h: h.matmul(psg[:, :], wg[:, k, u * 128:(u + 1) * 128], hT[:, k, :],
                                                   start=(k == 0), stop=(k == 15)), r=[wg.res, hT.res], w=[psg.res])
                G = C["G"]
                cv = C["cv"]
                ct = C["ct"]
                kb.op("pool", lambda h: h.tensor_copy(out=G[:, 0:2], in_=carry[:, f, :]), r=[carry.res], w=[G.res])
                kb.op("act", lambda h: h.activation(out=G[:, 2:2 + TS], in_=psg[:, :], func=AF.Identity),
                      r=[psg.res], w=[G.res])
                kb.op("pool", lambda h: h.tensor_copy(out=carry[:, f, :], in_=G[:, TS:TS + 2]), r=[G.res], w=[carry.res])
                kb.op("dve", lambda h: h.tensor_scalar(out=cv[:, :], in0=G[:, 2:2 + TS], scalar1=cw[:, 2, f:f + 1],
                                                       scalar2=None, op0=ALU.mult), r=[G.res, cw.res], w=[cv.res])
                kb.op("dve", lambda h: h.scalar_tensor_tensor(out=cv[:, :], in0=G[:, 1:1 + TS], scalar=cw[:, 1, f:f + 1],
                                                              in1=cv[:, :], op0=ALU.mult, op1=ALU.add),
                      r=[G.res, cw.res, cv.res], w=[cv.res])
                kb.op("dve", lambda h: h.scalar_tensor_tensor(out=cv[:, :], in0=G[:, 0:TS], scalar=cw[:, 0, f:f + 1],
                                                              in1=cv[:, :], op0=ALU.mult, op1=ALU.add),
                      r=[G.res, cw.res, cv.res], w=[cv.res])
                gelu_tanh(kb, cv[:, :], cv[:, :], ct[:, :], [cv.res], cv.res, ct.res)
                kb.op("dve", lambda h: h.tensor_tensor(out=aT[:, f, :], in0=psv[:, :], in1=cv[:, :], op=ALU.mult),
                      r=[psv.res, cv.res], w=[aT.res])
        wd = C["wdn"]
        di = 0
        for cb in range(4):
            pss = [dn.nps() for _ in range(4)]
            for pc in range(4):
                w = wd[di % len(wd)]
                di += 1
                kb.dma("sp", w[:, :, :], C["down_b"][l, pc * 1408:(pc + 1) * 1408, cb * 512:(cb + 1) * 512]
                       .rearrange("(k p) c -> p k c", p=128), w=[w.res])
                for j in range(4):
                    for k in range(11):
                        f = pc * 11 + k
                        kb.op("pe", lambda h: h.matmul(pss[j][:, :], aT[:, f, j * 128:(j + 1) * 128], w[:, k, :],
                                                       start=(f == 0), stop=(f == 43)), r=[w.res, aT.res], w=[pss[j].res])
            for j in range(4):
                kb.op("dve", lambda h: h.tensor_tensor(out=xt[:, j, cb * 512:(cb + 1) * 512], in0=pss[j][:, :],
                                                       in1=xt[:, j, cb * 512:(cb + 1) * 512], op=ALU.add),
                      r=[pss[j].res, xt.res], w=[xt.res])
        if not last:
            kb.dma("act", C["xres"][t0:t0 + TS, :].rearrange("(j p) d -> p j d", p=128), xt[:, :, :], r=[xt.res])
        else:
            junk, ss, rs = C["junk"], C["ss"], C["rs"]
            gf = C["gfin"]
            for j in range(4):
                kb.op("act", lambda h: h.activation(out=junk[:, :], in_=xt[:, j, :], func=AF.Square,
                                                    accum_out=ss[:, j:j + 1]), r=[xt.res], w=[junk.res, ss.res])
            kb.op("act", lambda h: h.activation(out=rs[:, 0:4], in_=ss[:, 0:4], func=AF.Sqrt, scale=1.0 / D, bias=EPS),
                  r=[ss.res], w=[rs.res])
            kb.op("dve", lambda h: h.reciprocal(out=rs[:, 0:4], in_=rs[:, 0:4]), r=[rs.res], w=[rs.res])
            for j in range(4):
                kb.op("dve", lambda h: h.scalar_tensor_tensor(out=xt[:, j, :], in0=xt[:, j, :], scalar=rs[:, j:j + 1],
                                                              in1=gf[:, :], op0=ALU.mult, op1=ALU.mult),
                      r=[xt.res, rs.res, gf.res], w=[xt.res])
            kb.dma("act", C["out"][t0:t0 + TS, :].rearrange("(j p) d -> p j d", p=128), xt[:, :, :], r=[xt.res])


def build(S, depth, phases=("p1", "mix", "p3"), debug=False, ext_in=(), dbg_out=()):
    P = Prog(S, depth, debug)
    P.ext_in = ext_in
    P.dbg_out = dbg_out
    nc, kb, cx = P.nc, P.kb, P.cx
    C = {}
    C["x"] = P.din("x", (S, D))
    C["positions"] = P.din("positions", (1, S), I32)
    C["ident_in"] = P.din("ident", (128, 128), BF16)
    for n, shp in PARAM_SHAPES.items():
        C["in_" + n] = P.din(n, shp)
    C["out"] = P.dscr("out", (S, D), F32, out=True)
    C["xres"] = P.dscr("xres", (S, D), F32)
    C["w_in_b"] = P.dscr("w_in_bf", (depth, D, N_IN), BF16)
    C["proj_b"] = P.dscr("projs_b", (depth, D, D), BF16)
    C["w_out_b"] = P.dscr("w_out_b", (depth, D, D), BF16)
    C["up_b"] = P.dscr("up_b", (depth, D, 2 * D_FF), BF16)
    C["down_b"] = P.dscr("down_b", (depth, D_FF, D), BF16)
    C["zT"] = P.dscr("zT", (ZROWS, S), BF16)
    C["vatt"] = P.dscr("vatt", (S, 768), BF16)
    C["u5"] = P.dscr("u5", (S, 512), BF16)
    C["gT"] = P.dscr("gT", (3 * D, S), BF16)
    C["yT"] = P.dscr("yT", (D, S), BF16)

    wres = [Res("wconv%d" % l) for l in range(depth)]
    C["wres"] = wres

    def conv(l, dst, src, rows, step=128):
        for r0 in range(0, rows, step):
            r1 = min(rows, r0 + step)
            kb.dma("pool", dst(r0, r1), src(r0, r1), sres=wres[l])

    for l in range(depth):
        conv(l, lambda a, b: C["w_in_b"][l, a:b, :], lambda a, b: C["in_w_in"][l, a:b, :], D)
        conv(l, lambda a, b: C["proj_b"][l, a:b, :], lambda a, b: C["in_proj_a"][l, a:b, :], 768)
        conv(l, lambda a, b: C["proj_b"][l, 768 + a:768 + b, :], lambda a, b: C["in_proj_b"][l, a:b, :], 768)
        conv(l, lambda a, b: C["proj_b"][l, 1536 + a:1536 + b, :], lambda a, b: C["in_proj_c"][l, a:b, :], 512)
        conv(l, lambda a, b: C["w_out_b"][l, a:b, :], lambda a, b: C["in_w_out"][l, a:b, :], D)
        conv(l, lambda a, b: C["up_b"][l, a:b, :], lambda a, b: C["in_ffn_up"][l, a:b, :], D)
        conv(l, lambda a, b: C["down_b"][l, a:b, :], lambda a, b: C["in_ffn_down"][l, a:b, :], D_FF)
        wres[l].w = [(wres[l].ds, kb.sems[wres[l].ds].cnt)]

    C["ident"] = cx.sb([128, 128], BF16, "ident")
    kb.dma("sp", C["ident"][:, :], C["ident_in"][:, :], w=[C["ident"].res])
    L = depth
    gm = cx.sb([128, L, 16], F32, "gmix")
    gf = cx.sb([128, L, 16], F32, "gffn")
    bg = cx.sb([128, L, 48], F32, "bgate")
    cw = cx.sb([128, L, 3, 44], F32, "convw")
    kb.dma("sp", gm[:, :, :], C["in_norm_mix"][0:L, :].rearrange("l (c p) -> p l c", p=128), w=[gm.res], slow=True)
    kb.dma("sp", gf[:, :, :], C["in_norm_ffn"][0:L, :].rearrange("l (c p) -> p l c", p=128), w=[gf.res], slow=True)
    kb.dma("sp", bg[:, :, :], C["in_b_gate"][0:L, :].rearrange("l (c p) -> p l c", p=128), w=[bg.res], slow=True)
    for l in range(L):
        kb.dma("sp", cw[:, l, :, :], C["in_ffn_conv"][l, :, :].rearrange("j (c p) -> p j c", p=128), w=[cw.res], slow=True)

    class V:
        def __init__(self, tl, f):
            self.tl, self.f, self.res = tl, f, tl.res

        def __getitem__(self, idx):
            return self.f(self.tl)[idx]

    C["gmix"] = [V(gm, lambda t, l=l: t[:, l, :]) for l in range(L)]
    C["gffn"] = [V(gf, lambda t, l=l: t[:, l, :]) for l in range(L)]
    C["bgate"] = [V(bg, lambda t, l=l: t[:, l, :]) for l in range(L)]
    C["convw"] = [V(cw, lambda t, l=l: t[:, l, :, :]) for l in range(L)]
    def alloc_dense(nw=2):
        C["xt"] = cx.sb([128, 4, D], F32, "xt")
        C["hT"] = cx.sb([128, 16, TS], BF16, "hT")
        C["xn"] = cx.sb([128, 4, D], BF16, "xn")
        C["junk"] = cx.sb([128, D], BF16, "junk")
        C["ss"] = cx.sb([128, 4], F32, "ss")
        C["rs"] = cx.sb([128, 4], F32, "rs")
        C["wblk"] = [cx.sb([128, 16, 512], BF16, "wblk") for _ in range(nw)]

    C["rw_masks"] = P.din("rw_masks", (3, 128, 128))
    C["bones"] = P.din("bones", (128, 128), BF16)
    C["chunkmask"] = P.din("chunkmask", (128, 256))
    C["s5mask"] = P.din("s5mask", (128, 2, 256))
    C["identf"] = P.din("identf", (128, 128))
    C["invf"] = P.din("invf", (64, 1))
    C["sgn"] = P.din("sgn", (64, 1))
    C["ropeC"] = P.dscr("ropeC", (64, S), BF16)
    C["ropeS"] = P.dscr("ropeS", (64, S), BF16)
    dn = Dense(P)
    C["dn"] = dn

    if "zero" in phases:
        zt = cx.sb([128, 2048], BF16, "zeros")
        kb.op("pool", lambda h: h.memset(zt[:, :], 0.0), w=[zt.res])
        for r0 in list(range(0, 768, 128)) + list(range(1536, 2048, 128)):
            for t0 in range(0, S, 2048):
                n = min(2048, S - t0)
                kb.dma("sp", C["yT"][r0:r0 + 128, t0:t0 + n], zt[:, 0:n], r=[zt.res])
        kb.barrier()
    for l in range(depth):
        last = (l == depth - 1)
        kb._wait("sp", wres[l].w)
        if "p1" in phases:
            with ExitStack() as stk:
                cx.stack = stk
                alloc_dense(4)
                C["stage"] = [cx.sb([128, TS], BF16, "stage") for _ in range(4)]
                C["stage_tm"] = [cx.sb([128, 4, 512], BF16, "stage_tm") for _ in range(2)]
                phase1(P, dn, l, C)
                kb.barrier()
                cx.stack = None
        if "rope" in phases and l == 0:
            with ExitStack() as stk:
                cx.stack = stk
                rope_tables(P, C)
                kb.barrier()
                cx.stack = None
        if "rwkv" in phases:
            with ExitStack() as stk:
                cx.stack = stk
                rwkv_mixer(P, dn, l, C)
                kb.barrier()
                cx.stack = None
        if "s5" in phases:
            with ExitStack() as stk:
                cx.stack = stk
                s5_mixer(P, dn, l, C)
                kb.barrier()
                cx.stack = None
        if "att" in phases:
            with ExitStack() as stk:
                cx.stack = stk
                attention(P, dn, l, C)
                kb.barrier()
                cx.stack = None
        if "p3" in phases:
            with ExitStack() as stk:
                cx.stack = stk
                alloc_dense()
                big = cx.sb([128, 44 * TS], BF16, "big")
                C["yTt"] = V(big, lambda t: t[:, 0:16 * TS].rearrange("p (k t) -> p k t", k=16))
                C["mT"] = V(big, lambda t: t[:, 16 * TS:32 * TS].rearrange("p (k t) -> p k t", k=16))
                C["aT"] = V(big, lambda t: t[:, :].rearrange("p (k t) -> p k t", k=44))
                C["gt"] = [cx.sb([128, 3, TS], BF16, "gt") for _ in range(2)]
                C["macc"] = cx.sb([128, TS], F32, "macc")
                C["mtmp"] = cx.sb([128, TS], F32, "mtmp")
                C["G"] = cx.sb([128, TS + 2], F32, "G")
                C["cv"] = cx.sb([128, TS], F32, "cv")
                C["ct"] = cx.sb([128, TS], F32, "ct")
                C["carry"] = cx.sb([128, 44, 2], F32, "carry")
                C["wdn"] = [cx.sb([128, 11, 512], BF16, "wdn") for _ in range(2)]
                kb.op("pool", lambda h: h.memset(C["carry"][:, :, :], 0.0), w=[C["carry"].res])
                if last:
                    C["gfin"] = cx.sb([128, D], F32, "gfin")
                    kb.dma("sp", C["gfin"][:, :], C["in_norm_final"].partition_broadcast(128), w=[C["gfin"].res])
                phase3(P, dn, l, C, last)
                kb.barrier()
                cx.stack = None
    kb.barrier()
    return P, C


def rope_tables(P, C):
    kb, cx, S = P.kb, P.cx, P.S
    pi = cx.sb([64, S], I32, "pos_i")
    ang = cx.sb([64, S], F32, "ang")
    a2 = cx.sb([64, S], F32, "ang2")
    ni = cx.sb([64, S], I32, "n_i")
    nf = cx.sb([64, S], F32, "n_f")
    ob = cx.sb([64, S], BF16, "rope_o")
    invf = cx.sb([64, 1], F32, "invf")
    sgn = cx.sb([64, 1], F32, "sgn")
    kb.dma("sp", invf[:, :], C["invf"][:, :], w=[invf.res])
    kb.dma("sp", sgn[:, :], C["sgn"][:, :], w=[sgn.res])
    kb.dma("sp", pi[:, :], C["positions"][0].partition_broadcast(64), w=[pi.res])
    kb.op("dve", lambda h: h.tensor_copy(out=ang[:, :], in_=pi[:, :]), r=[pi.res], w=[ang.res])
    kb.op("dve", lambda h: h.tensor_scalar(out=ang[:, :], in0=ang[:, :], scalar1=invf[:, 0:1], scalar2=None, op0=ALU.mult),
          r=[ang.res, invf.res], w=[ang.res])
    TWO_PI = 2.0 * math.pi
    for which in range(2):
        src = ang
        if which == 0:
            kb.op("dve", lambda h: h.tensor_scalar(out=a2[:, :], in0=ang[:, :], scalar1=math.pi / 2, scalar2=None, op0=ALU.add),
                  r=[ang.res], w=[a2.res])
            src = a2
        kb.op("dve", lambda h: h.tensor_scalar(out=ni[:, :], in0=src[:, :], scalar1=1.0 / TWO_PI, scalar2=None, op0=ALU.mult),
              r=[src.res], w=[ni.res])
        kb.op("dve", lambda h: h.tensor_copy(out=nf[:, :], in_=ni[:, :]), r=[ni.res], w=[nf.res])
        kb.op("dve", lambda h: h.scalar_tensor_tensor(out=nf[:, :], in0=nf[:, :], scalar=-TWO_PI, in1=src[:, :],
                                                      op0=ALU.mult, op1=ALU.add), r=[nf.res, src.res], w=[nf.res])
        kb.op("dve", lambda h: h.tensor_scalar(out=nf[:, :], in0=nf[:, :], scalar1=-3.14159, scalar2=3.14159,
                                               op0=ALU.max, op1=ALU.min), r=[nf.res], w=[nf.res])
        kb.op("act", lambda h: h.activation(out=nf[:, :], in_=nf[:, :], func=AF.Sin), r=[nf.res], w=[nf.res])
        if which == 0:
            kb.op("dve", lambda h: h.tensor_copy(out=ob[:, :], in_=nf[:, :]), r=[nf.res], w=[ob.res])
            kb.dma("sp", C["ropeC"][:, :], ob[:, :], r=[ob.res])
        else:
            kb.op("dve", lambda h: h.tensor_scalar(out=ob[:, :], in0=nf[:, :], scalar1=sgn[:, 0:1], scalar2=None, op0=ALU.mult),
                  r=[nf.res, sgn.res], w=[ob.res])
            kb.dma("sp", C["ropeS"][:, :], ob[:, :], r=[ob.res])


def attention(P, dn, l, C):
    kb, cx, S = P.kb, P.cx, P.S
    NB = S // 128
    lam_init = 0.8 - 0.6 * math.exp(-0.3 * l)
    ident = C["ident"]
    lq = cx.sb([128, 4, 64], F32, "lq")
    for i, n in enumerate(["da_lq1", "da_lk1", "da_lq2", "da_lk2"]):
        kb.dma("sp", lq[:, i, :], C["in_" + n][l].partition_broadcast(128), w=[lq.res])
    pr = cx.sb([128, 2, 64], F32, "lpr")
    e12 = cx.sb([128, 2], F32, "e12")
    nlam = cx.sb([128, 1], F32, "nlam")
    kb.op("dve", lambda h: h.tensor_tensor(out=pr[:, 0, :], in0=lq[:, 0, :], in1=lq[:, 1, :], op=ALU.mult), r=[lq.res], w=[pr.res])
    kb.op("dve", lambda h: h.tensor_tensor(out=pr[:, 1, :], in0=lq[:, 2, :], in1=lq[:, 3, :], op=ALU.mult), r=[lq.res], w=[pr.res])
    kb.op("dve", lambda h: h.tensor_reduce(out=e12[:, :], in_=pr[:, :, :], axis=AX.X, op=ALU.add), r=[pr.res], w=[e12.res])
    kb.op("act", lambda h: h.activation(out=e12[:, :], in_=e12[:, :], func=AF.Exp), r=[e12.res], w=[e12.res])
    kb.op("dve", lambda h: h.tensor_tensor(out=nlam[:, :], in0=e12[:, 1:2], in1=e12[:, 0:1], op=ALU.subtract), r=[e12.res], w=[nlam.res])
    kb.op("dve", lambda h: h.tensor_scalar(out=nlam[:, :], in0=nlam[:, :], scalar1=-lam_init, scalar2=None, op0=ALU.add),
          r=[nlam.res], w=[nlam.res])
    sl = cx.sb([128, 128], F32, "subln")
    kb.dma("sp", sl[:, :], C["in_da_subln"][l].partition_broadcast(128), w=[sl.res])
    kb.op("dve", lambda h: h.tensor_scalar(out=sl[:, :], in0=sl[:, :], scalar1=1.0 - lam_init, scalar2=None, op0=ALU.mult),
          r=[sl.res], w=[sl.res])
    C2 = cx.sb([64, S], BF16, "C2")
    S2 = cx.sb([64, S], BF16, "S2")
    kb.dma("sp", C2[:, :], C["ropeC"][:, :], w=[C2.res])
    kb.dma("sp", S2[:, :], C["ropeS"][:, :], w=[S2.res])
    kr = [cx.sb([64, S], BF16, "kr") for _ in range(2)]
    V1 = cx.sb([128, NB, 129], BF16, "V1")
    kb.op("pool", lambda h: h.memset(V1[:, :, 128:129], 1.0), w=[V1.res])
    CH = min(S, 2048)
    raw = [cx.sb([64, CH], BF16, "raw") for _ in range(2)]
    swp = [cx.sb([64, CH], BF16, "swp") for _ in range(2)]
    tmp = [cx.sb([64, CH], BF16, "rtmp") for _ in range(2)]
    qr = [cx.sb([64, 512], BF16, "qr") for _ in range(2)]
    PT = [cx.sb([128, 512], BF16, "PT") for _ in range(4)]
    o0 = cx.sb([128, 4, 128], F32, "o0")
    att = cx.sb([128, 4, 128], F32, "att")
    rec = cx.sb([128, 4], F32, "rec")
    ssq = cx.sb([128, 4], F32, "ssq")
    jk = cx.sb([128, 128], BF16, "ajunk")
    yb = cx.sb([128, 4, 128], BF16, "yb")
    stg = [cx.sb([128, 512], BF16, "astg") for _ in range(2)]
    OA, OB = dn.psf[4], dn.psf[5]
    ri = 0
    pti = 0
    sti = 0

    def rope(dst_ap, row0, t0, n, ri):
        a, b, t = raw[ri % 2], swp[ri % 2], tmp[ri % 2]
        kb.dma("sp", a[:, 0:n], C["zT"][row0:row0 + 64, t0:t0 + n], w=[a.res])
        kb.dma("sp", b[0:32, 0:n], C["zT"][row0 + 32:row0 + 64, t0:t0 + n], w=[b.res])
        kb.dma("sp", b[32:64, 0:n], C["zT"][row0:row0 + 32, t0:t0 + n], w=[b.res])
        kb.op("dve", lambda h: h.tensor_tensor(out=t[:, 0:n], in0=a[:, 0:n], in1=C2[:, t0:t0 + n], op=ALU.mult),
              r=[a.res, C2.res], w=[t.res])
        kb.op("pool", lambda h: h.tensor_tensor(out=b[:, 0:n], in0=b[:, 0:n], in1=S2[:, t0:t0 + n], op=ALU.mult),
              r=[b.res, S2.res], w=[b.res])
        return t, b

    for hd in range(6):
        kb.dma("sp", V1[:, :, 0:128], C["vatt"][:, hd * 128:(hd + 1) * 128].rearrange("(n p) c -> p n c", p=128), w=[V1.res])
        for c in range(2):
            row0 = 3520 + (hd * 2 + c) * 64
            for t0 in range(0, S, CH):
                t, b = rope(None, row0, t0, CH, ri)
                ri += 1
                kb.op("dve", lambda h: h.tensor_tensor(out=kr[c][:, t0:t0 + CH], in0=t[:, 0:CH], in1=b[:, 0:CH], op=ALU.add),
                      r=[t.res, b.res], w=[kr[c].res])
        for Q in range(S // 512):
            q0 = Q * 512
            for c in range(2):
                row0 = 2752 + (hd * 2 + c) * 64
                t, b = rope(None, row0, q0, 512, ri)
                ri += 1
                q = qr[c]
                kb.op("dve", lambda h: h.tensor_tensor(out=q[:, :], in0=t[:, 0:512], in1=b[:, 0:512], op=ALU.add),
                      r=[t.res, b.res], w=[q.res])
                nkb = Q * 4 + 4

                def stA(kbk):
                    nonlocal pti
                    i = kbk - Q * 4
                    j0 = max(i, 0)
                    ps = dn.nps(4)
                    kb.op("pe", lambda h: h.matmul(ps[:, j0 * 128:512], kr[c][:, kbk * 128:(kbk + 1) * 128], q[:, j0 * 128:512],
                                                   start=True, stop=True), r=[kr[c].res, q.res], w=[ps.res])
                    pt = PT[pti % len(PT)]
                    pti += 1
                    kb.op("act", lambda h: h.activation(out=pt[:, j0 * 128:512], in_=ps[:, j0 * 128:512], func=AF.Exp, scale=0.125),
                          r=[ps.res], w=[pt.res])
                    if i >= 0:
                        kb.op("pool", lambda h: h.memset(pt[64:128, i * 128:i * 128 + 64], 0.0), w=[pt.res])
                    return (kbk, j0, pt)

                def stB(a):
                    kbk, j0, pt = a
                    for j in range(j0, 4):
                        O = OA if j < 2 else OB
                        first = (kbk == 0 and (j == 0 or j == 2))
                        kb.op("pe", lambda h: h.matmul(O[:, (j % 2) * 129:(j % 2) * 129 + 129], pt[:, j * 128:(j + 1) * 128],
                                                       V1[:, kbk, :], start=first, stop=(kbk == Q * 4 + j),
                                                       skip_group_check=True), r=[pt.res, V1.res], w=[O.res])

                pend = []
                for kbk in range(nkb):
                    pend.append(stA(kbk))
                    if len(pend) > 2:
                        stB(pend.pop(0))
                while pend:
                    stB(pend.pop(0))
                for j in range(4):
                    O = OA if j < 2 else OB
                    b0 = (j % 2) * 129
                    kb.op("dve", lambda h: h.reciprocal(out=rec[:, j:j + 1], in_=O[:, b0 + 128:b0 + 129]), r=[O.res], w=[rec.res])
                    if c == 0:
                        kb.op("dve", lambda h: h.tensor_scalar(out=o0[:, j, :], in0=O[:, b0:b0 + 128], scalar1=rec[:, j:j + 1],
                                                               scalar2=None, op0=ALU.mult), r=[O.res, rec.res], w=[o0.res])
                    else:
                        kb.op("dve", lambda h: h.tensor_tensor(out=rec[:, j:j + 1], in0=rec[:, j:j + 1], in1=nlam[:, 0:1], op=ALU.mult),
                              r=[rec.res, nlam.res], w=[rec.res])
                        kb.op("dve", lambda h: h.scalar_tensor_tensor(out=att[:, j, :], in0=O[:, b0:b0 + 128], scalar=rec[:, j:j + 1],
                                                                      in1=o0[:, j, :], op0=ALU.mult, op1=ALU.add),
                              r=[O.res, rec.res, o0.res], w=[att.res])
            for j in range(4):
                kb.op("act", lambda h: h.activation(out=jk[:, :], in_=att[:, j, :], func=AF.Square, accum_out=ssq[:, j:j + 1]),
                      r=[att.res], w=[jk.res, ssq.res])
            kb.op("act", lambda h: h.activation(out=ssq[:, :], in_=ssq[:, :], func=AF.Sqrt, scale=1.0 / 128, bias=EPS),
                  r=[ssq.res], w=[ssq.res])
            kb.op("dve", lambda h: h.reciprocal(out=ssq[:, :], in_=ssq[:, :]), r=[ssq.res], w=[ssq.res])
            pb = dn.npb()
            for j in range(4):
                kb.op("dve", lambda h: h.scalar_tensor_tensor(out=yb[:, j, :], in0=att[:, j, :], scalar=ssq[:, j:j + 1], in1=sl[:, :],
                                                              op0=ALU.mult, op1=ALU.mult), r=[att.res, ssq.res, sl.res], w=[yb.res])
                kb.op("pe", lambda h: h.transpose(out=pb[:, j * 128:(j + 1) * 128], in_=yb[:, j, :], identity=ident[:, :]),
                      r=[yb.res, ident.res], w=[pb.res])
            sg = stg[sti % 2]
            sti += 1
            copy_evac(kb, kb.ev(), sg[:, :], pb[:, 0:512], r=[pb.res], w=[sg.res])
            kb.dma("act", C["yT"][768 + hd * 128:768 + (hd + 1) * 128, q0:q0 + 512], sg[:, :], r=[sg.res])


def host_consts():
    invf = (10000.0 ** (-np.arange(0, 64, 2, dtype=np.float32) / 64)).astype(np.float32)
    p = np.arange(128)[:, None, None]
    m = np.arange(2)[None, :, None]
    col = np.arange(256)[None, None, :]
    s5mask = ((col // 16) >= ((m * 128 + p) // 16)).astype(np.float32)
    ii = np.arange(128)
    mus = (ii[:, None] < ii[None, :]).astype(np.float32)
    mui = (ii[:, None] <= ii[None, :]).astype(np.float32)
    mls = (ii[None, :] < ii[:, None]).astype(np.float32)
    bones = ((ii[:, None] // 64) == (ii[None, :] // 64)).astype(ml_dtypes.bfloat16)
    cmk = np.tile((np.arange(256) % 64 != 0).astype(np.float32)[None, :], (128, 1))
    return {"rw_masks": np.stack([mus, mui, mls]), "bones": bones, "chunkmask": cmk, "s5mask": s5mask, "identf": np.eye(128, dtype=np.float32), "invf": np.concatenate([invf, invf])[:, None].astype(np.float32).copy(),
            "sgn": np.concatenate([-np.ones(32), np.ones(32)])[:, None].astype(np.float32).copy()}


_CACHE = {}


def kernel(**inputs):
    S = 8192
    nb = 4
    if "prog" not in _CACHE:
        _CACHE["prog"] = build(S, DEPTH, phases=("p1", "rwkv", "s5", "rope", "att", "p3"))
    P, C = _CACHE["prog"]
    consts = host_consts()
    consts["ident"] = np.eye(128, dtype=ml_dtypes.bfloat16)
    in_maps = []
    for b in range(nb):
        m = {"x": np.ascontiguousarray(np.asarray(inputs["x"])[b], dtype=np.float32),
             "positions": np.ascontiguousarray(np.asarray(inputs["positions"])[b][None, :], dtype=np.int32)}
        for n, shp in PARAM_SHAPES.items():
            m[n] = np.ascontiguousarray(np.asarray(inputs[n], dtype=np.float32).reshape(shp))
        m.update(consts)
        in_maps.append(m)
    res = run_bass_kernel_spmd(P.nc, in_maps, core_ids=list(range(nb)))
    return np.stack([np.asarray(r["out"], dtype=np.float32) for r in res.results], axis=0)


def s5_mixer(P, dn, l, C):
    kb, cx, S = P.kb, P.cx, P.S
    ident = C["ident"]
    NCH = S // 16
    M = min(128, NCH)
    NMT = NCH // M
    V = lambda t, f: VW(t, f)
    TT = cx.sb([128, 32, 2, 256], BF16, "TT")
    GG = cx.sb([128, 32, 2, 2, 64], BF16, "GG")
    FFr = cx.sb([64, 32, 256], BF16, "FFr")
    nFFi = cx.sb([64, 32, 256], BF16, "nFFi")
    MU1 = cx.sb([64, 32, 2], F32, "MU1")
    MU2 = cx.sb([64, 32, 2], F32, "MU2")
    gluW = cx.sb([128, 4, 512], BF16, "gluW")
    glub = cx.sb([128, 4], F32, "glub")
    dB = cx.sb([128, 512], F32, "dB")
    kb.dma("pool", gluW[:, :, :], C["in_s5_glu_w"][l].rearrange("(k p) c -> p k c", p=128), w=[gluW.res])
    kb.dma("sp", glub[:, :], C["in_s5_glu_b"][l].rearrange("(c p) -> p c", p=128), w=[glub.res], slow=True)
    kb.dma("sp", dB[:, :], C["in_s5_d"][l].partition_broadcast(128), w=[dB.res])
    with ExitStack() as stk:
        old = cx.stack
        cx.stack = stk
        f32 = lambda shp, n: cx.sb(shp, F32, n)
        lr, li, dt = f32([64, 32], "lr"), f32([64, 32], "li"), f32([64, 32], "dt")
        kb.dma("sp", lr[:, :], C["in_s5_lam_re"][l].rearrange("g n -> n g"), w=[lr.res], slow=True)
        kb.dma("sp", li[:, :], C["in_s5_lam_im"][l].rearrange("g n -> n g"), w=[li.res], slow=True)
        kb.dma("sp", dt[:, :], C["in_s5_log_dt"][l].partition_broadcast(64), w=[dt.res])
        bre, bim = f32([64, 32, 16], "bre"), f32([64, 32, 16], "bim")
        cre, cim = f32([64, 32, 16], "cre"), f32([64, 32, 16], "cim")
        kb.dma("sp", bre[:, :, :], C["in_s5_b_re"][l].rearrange("g n c -> n g c"), w=[bre.res])
        kb.dma("sp", bim[:, :, :], C["in_s5_b_im"][l].rearrange("g n c -> n g c"), w=[bim.res])
        kb.dma("sp", cre[:, :, :], C["in_s5_c_re"][l].rearrange("g c n -> n g c"), w=[cre.res], slow=True)
        kb.dma("sp", cim[:, :, :], C["in_s5_c_im"][l].rearrange("g c n -> n g c"), w=[cim.res], slow=True)
        msk = cx.sb([128, 2, 256], F32, "s5mask")
        kb.dma("sp", msk[:, :, :], C["s5mask"][:, :, :], w=[msk.res])
        idf = cx.sb([64, 64], F32, "identf")
        kb.dma("sp", idf[:, :], C["identf"][0:64, 0:64], w=[idf.res])

        def tt(out, a, b, op, r, w):
            kb.op("dve", lambda h: h.tensor_tensor(out=out, in0=a, in1=b, op=op), r=r, w=w)

        def ts(out, a, s1, s2, op0, op1, r, w):
            if s2 is None:
                kb.op("dve", lambda h: h.tensor_scalar(out=out, in0=a, scalar1=s1, scalar2=None, op0=op0), r=r, w=w)
            else:
                kb.op("dve", lambda h: h.tensor_scalar(out=out, in0=a, scalar1=s1, scalar2=s2, op0=op0, op1=op1), r=r, w=w)

        def cmul(o_r, o_i, ar, ai, br, bi, t1, res_in, res_out):
            r = res_in + [T1.res]
            tt(o_r, ar, br, ALU.mult, res_in, res_out)
            tt(t1, ai, bi, ALU.mult, res_in, [T1.res])
            tt(o_r, o_r, t1, ALU.subtract, res_out + [T1.res], res_out)
            tt(o_i, ar, bi, ALU.mult, res_in, res_out)
            tt(t1, ai, br, ALU.mult, res_in, [T1.res])
            tt(o_i, o_i, t1, ALU.add, res_out + [T1.res], res_out)

        T1 = f32([64, 32 * 16], "T1")
        kb.op("act", lambda h: h.activation(out=dt[:, :], in_=dt[:, :], func=AF.Exp), r=[dt.res], w=[dt.res])
        aa, th, er = f32([64, 32], "aa"), f32([64, 32], "th"), f32([64, 32], "er")
        tt(aa[:, :], lr[:, :], dt[:, :], ALU.mult, [lr.res, dt.res], [aa.res])
        tt(th[:, :], li[:, :], dt[:, :], ALU.mult, [li.res, dt.res], [th.res])
        kb.op("act", lambda h: h.activation(out=er[:, :], in_=aa[:, :], func=AF.Exp, scale=1.0 / 16), r=[aa.res], w=[er.res])
        zr, zi = f32([64, 32], "zr"), f32([64, 32], "zi")
        ts(zr[:, :], th[:, :], 1.0 / 16, math.pi / 2, ALU.mult, ALU.add, [th.res], [zr.res])
        kb.op("act", lambda h: h.activation(out=zr[:, :], in_=zr[:, :], func=AF.Sin), r=[zr.res], w=[zr.res])
        kb.op("act", lambda h: h.activation(out=zi[:, :], in_=th[:, :], func=AF.Sin, scale=1.0 / 16), r=[th.res], w=[zi.res])
        tt(zr[:, :], zr[:, :], er[:, :], ALU.mult, [zr.res, er.res], [zr.res])
        tt(zi[:, :], zi[:, :], er[:, :], ALU.mult, [zi.res, er.res], [zi.res])
        z2r, z2i = f32([64, 32], "z2r"), f32([64, 32], "z2i")
        cur = (zr, zi)
        nxt = (z2r, z2i)
        for _ in range(4):
            cmul(nxt[0][:, :], nxt[1][:, :], cur[0][:, :], cur[1][:, :], cur[0][:, :], cur[1][:, :], T1[:, 0:32],
                 [cur[0].res, cur[1].res], [nxt[0].res, nxt[1].res])
            cur, nxt = nxt, cur
        abr, abi = cur
        den, nr, fre, fim = f32([64, 32], "den"), f32([64, 32], "nr"), f32([64, 32], "fre"), f32([64, 32], "fim")
        tt(den[:, :], lr[:, :], lr[:, :], ALU.mult, [lr.res], [den.res])
        tt(T1[:, 0:32], li[:, :], li[:, :], ALU.mult, [li.res], [T1.res])
        tt(den[:, :], den[:, :], T1[:, 0:32], ALU.add, [den.res, T1.res], [den.res])
        kb.op("dve", lambda h: h.reciprocal(out=den[:, :], in_=den[:, :]), r=[den.res], w=[den.res])
        ts(nr[:, :], abr[:, :], -1.0, None, ALU.add, None, [abr.res], [nr.res])
        tt(fre[:, :], nr[:, :], lr[:, :], ALU.mult, [nr.res, lr.res], [fre.res])
        tt(T1[:, 0:32], abi[:, :], li[:, :], ALU.mult, [abi.res, li.res], [T1.res])
        tt(fre[:, :], fre[:, :], T1[:, 0:32], ALU.add, [fre.res, T1.res], [fre.res])
        tt(fre[:, :], fre[:, :], den[:, :], ALU.mult, [fre.res, den.res], [fre.res])
        tt(fim[:, :], abi[:, :], lr[:, :], ALU.mult, [abi.res, lr.res], [fim.res])
        tt(T1[:, 0:32], nr[:, :], li[:, :], ALU.mult, [nr.res, li.res], [T1.res])
        tt(fim[:, :], fim[:, :], T1[:, 0:32], ALU.subtract, [fim.res, T1.res], [fim.res])
        tt(fim[:, :], fim[:, :], den[:, :], ALU.mult, [fim.res, den.res], [fim.res])
        bbr, bbi = f32([64, 32, 16], "bbr"), f32([64, 32, 16], "bbi")
        bc = lambda t: t[:, :].unsqueeze(2).to_broadcast([64, 32, 16])
        T3 = T1[:, :].rearrange("p (g c) -> p g c", c=16)
        cmul(bbr[:, :, :], bbi[:, :, :], bc(fre), bc(fim), bre[:, :, :], bim[:, :, :], T3,
             [fre.res, fim.res, bre.res, bim.res], [bbr.res, bbi.res])
        pwr, pwi = f32([64, 17, 32], "pwr"), f32([64, 17, 32], "pwi")
        npr, npi = f32([64, 16, 32], "npr"), f32([64, 16, 32], "npi")
        kb.op("dve", lambda h: h.memset(pwr[:, 0, :], 1.0), w=[pwr.res])
        kb.op("dve", lambda h: h.memset(pwi[:, 0, :], 0.0), w=[pwi.res])
        kb.op("dve", lambda h: h.memset(npr[:, 0, :], 1.0), w=[npr.res])
        kb.op("dve", lambda h: h.memset(npi[:, 0, :], 0.0), w=[npi.res])
        for t in range(16):
            cmul(pwr[:, t + 1, :], pwi[:, t + 1, :], pwr[:, t, :], pwi[:, t, :], abr[:, :], abi[:, :], T1[:, 0:32],
                 [pwr.res, pwi.res, abr.res, abi.res], [pwr.res, pwi.res])
        ivr, ivi = f32([64, 32], "ivr"), f32([64, 32], "ivi")
        tt(ivr[:, :], abr[:, :], abr[:, :], ALU.mult, [abr.res], [ivr.res])
        tt(T1[:, 0:32], abi[:, :], abi[:, :], ALU.mult, [abi.res], [T1.res])
        tt(ivr[:, :], ivr[:, :], T1[:, 0:32], ALU.add, [ivr.res, T1.res], [ivr.res])
        kb.op("dve", lambda h: h.reciprocal(out=ivr[:, :], in_=ivr[:, :]), r=[ivr.res], w=[ivr.res])
        tt(ivi[:, :], abi[:, :], ivr[:, :], ALU.mult, [abi.res, ivr.res], [ivi.res])
        ts(ivi[:, :], ivi[:, :], -1.0, None, ALU.mult, None, [ivi.res], [ivi.res])
        tt(ivr[:, :], abr[:, :], ivr[:, :], ALU.mult, [abr.res, ivr.res], [ivr.res])
        for t in range(15):
            cmul(npr[:, t + 1, :], npi[:, t + 1, :], npr[:, t, :], npi[:, t, :], ivr[:, :], ivi[:, :], T1[:, 0:32],
                 [npr.res, npi.res, ivr.res, ivi.res], [npr.res, npi.res])
        kb.op("dve", lambda h: h.tensor_copy(out=MU1[:, :, 0], in_=pwr[:, 16, :]), r=[pwr.res], w=[MU1.res])
        kb.op("dve", lambda h: h.tensor_copy(out=MU1[:, :, 1], in_=pwr[:, 16, :]), r=[pwr.res], w=[MU1.res])
        kb.op("dve", lambda h: h.tensor_copy(out=MU2[:, :, 0], in_=pwi[:, 16, :]), r=[pwi.res], w=[MU2.res])
        kb.op("dve", lambda h: h.tensor_copy(out=MU2[:, :, 1], in_=pwi[:, 16, :]), r=[pwi.res], w=[MU2.res])
        GBr, GBi = f32([64, 16, 16], "GBr"), f32([64, 16, 16], "GBi")
        CFr, CFi = f32([64, 16, 16], "CFr"), f32([64, 16, 16], "CFi")
        GSr, GSi = f32([64, 16, 16], "GSr"), f32([64, 16, 16], "GSi")
        Fr, Fi = f32([64, 16, 16], "Fr"), f32([64, 16, 16], "Fi")
        T4 = T1[:, 0:256].rearrange("p (s c) -> p s c", c=16)
        rpr, rpi = f32([64, 16, 32], "rpr"), f32([64, 16, 32], "rpi")
        for s_ in range(16):
            kb.op("dve", lambda h: h.tensor_copy(out=rpr[:, s_, :], in_=pwr[:, 15 - s_, :]), r=[pwr.res], w=[rpr.res])
            kb.op("dve", lambda h: h.tensor_copy(out=rpi[:, s_, :], in_=pwi[:, 15 - s_, :]), r=[pwi.res], w=[rpi.res])
        for g in range(32):
            pw_b = lambda t, lo=0: t[:, lo:lo + 16, g].unsqueeze(2).to_broadcast([64, 16, 16])
            v_b = lambda t: t[:, g, :].unsqueeze(1).to_broadcast([64, 16, 16])
            cmul(GBr[:, :, :], GBi[:, :, :], pw_b(npr), pw_b(npi), v_b(bbr), v_b(bbi), T4,
                 [npr.res, npi.res, bbr.res, bbi.res], [GBr.res, GBi.res])
            cmul(CFr[:, :, :], CFi[:, :, :], pw_b(pwr), pw_b(pwi), v_b(cre), v_b(cim), T4,
                 [pwr.res, pwi.res, cre.res, cim.res], [CFr.res, CFi.res])
            ts(CFi[:, :, :], CFi[:, :, :], -1.0, None, ALU.mult, None, [CFi.res], [CFi.res])
            cmul(GSr[:, :, :], GSi[:, :, :], pw_b(rpr), pw_b(rpi), v_b(bbr), v_b(bbi), T4,
                 [rpr.res, rpi.res, bbr.res, bbi.res], [GSr.res, GSi.res])
            cmul(Fr[:, :, :], Fi[:, :, :], pw_b(pwr, 1), pw_b(pwi, 1), v_b(cre), v_b(cim), T4,
                 [pwr.res, pwi.res, cre.res, cim.res], [Fr.res, Fi.res])
            kb.op("act", lambda h: h.activation(out=FFr[:, g, :], in_=Fr[:, :, :].rearrange("p t c -> p (t c)"), func=AF.Identity),
                  r=[Fr.res], w=[FFr.res])
            kb.op("act", lambda h: h.activation(out=nFFi[:, g, :], in_=Fi[:, :, :].rearrange("p t c -> p (t c)"), func=AF.Identity,
                                                scale=-1.0), r=[Fi.res], w=[nFFi.res])
            flat = lambda t: t[:, :, :].rearrange("p s c -> p (s c)")
            for m in range(2):
                ps = dn.nps()
                kb.op("pe", lambda h: h.matmul(ps[:, 0:256], flat(GBr)[:, m * 128:(m + 1) * 128], flat(CFr), start=True, stop=False),
                      r=[GBr.res, CFr.res], w=[ps.res])
                kb.op("pe", lambda h: h.matmul(ps[:, 0:256], flat(GBi)[:, m * 128:(m + 1) * 128], flat(CFi), start=False, stop=True),
                      r=[GBi.res, CFi.res], w=[ps.res])
                kb.op("dve", lambda h: h.tensor_tensor(out=TT[:, g, m, :], in0=ps[:, 0:256], in1=msk[:, m, :], op=ALU.mult),
                      r=[ps.res, msk.res], w=[TT.res])
            ps = dn.nps()
            for m in range(2):
                for ri_, src in enumerate((GSr, GSi)):
                    k_ = m * 2 + ri_
                    kb.op("pe", lambda h: h.transpose(out=ps[:, k_ * 64:(k_ + 1) * 64], in_=flat(src)[:, m * 128:(m + 1) * 128],
                                                      identity=idf[:, :]), r=[src.res, idf.res], w=[ps.res])
            kb.op("act", lambda h: h.activation(out=GG[:, g, :, :, :].rearrange("p m r n -> p (m r n)"), in_=ps[:, 0:256],
                                                func=AF.Identity), r=[ps.res], w=[GG.res])
        kb.barrier()
        cx.stack = old
    UY = cx.sb([128, 16 * 512], BF16, "UY")
    Ucm = VW(UY, lambda t: t[:, :].rearrange("p (s c) -> p s c", c=512))
    YT = VW(UY, lambda t: t[:, :].rearrange("p (k m s) -> p k m s", k=4, s=16))
    UT = cx.sb([128, 32, 2, 128], BF16, "UT")
    Eall = cx.sb([64, 32, 2, 129], F32, "Eall")
    Xb = cx.sb([64, 32, 2, 128], BF16, "Xb")
    Ycm = cx.sb([128, 16, 512], BF16, "Ycm")
    P1 = cx.sb([64, 32, 2], F32, "P1")
    P2 = cx.sb([64, 32, 2], F32, "P2")
    gtmp = cx.sb([128, 2048], F32, "gtmp")
    gx = cx.sb([128, 2048], F32, "gx")
    sg5 = [cx.sb([128, 512], BF16, "s5stg") for _ in range(2)]
    sgt = cx.sb([128, 512], F32, "s5sig")
    kb.op("dve", lambda h: h.memset(Eall[:, :, :, 0:1], 0.0), w=[Eall.res])
    si = 0
    for mt in range(NMT):
        tok0 = mt * M * 16
        kb.dma("sp", Ucm[0:M, :, :], C["u5"][tok0:tok0 + M * 16, :].rearrange("(m s) c -> m s c", s=16), w=[UY.res])
        Ugm = VW(Ycm, lambda t: t[:, :, :].rearrange("p s c -> p (s c)").rearrange("p (g s c) -> p g s c", g=32, s=16))
        kb.op("pool", lambda h: h.tensor_copy(out=Ugm[0:M, :, :, :], in_=Ucm[0:M, :, :].rearrange("m s (g c) -> m g s c", c=16)),
              r=[UY.res], w=[Ycm.res])
        for g0 in range(0, 32, 4):
            pb = dn.npb()
            for gg in range(4):
                for hh in range(2):
                    k_ = gg * 2 + hh
                    kb.op("pe", lambda h: h.transpose(out=pb[:, k_ * 128:k_ * 128 + M],
                                                      in_=Ugm[0:M, g0 + gg, hh * 8:(hh + 1) * 8, :].rearrange("m s c -> m (s c)"),
                                                      identity=ident[0:M, 0:M]), r=[Ycm.res, ident.res], w=[pb.res])
            copy_evac(kb, kb.ev(), UT[:, g0:g0 + 4, :, 0:M],
                      pb[:, :].rearrange("p (g h m) -> p g h m", g=4, h=2)[:, :, :, 0:M], r=[pb.res], w=[UT.res])
        for g0 in range(0, 32, 2):
            ps = dn.nps()
            for gg in range(2):
                g = g0 + gg
                for ri_ in range(2):
                    k_ = gg * 2 + ri_
                    for m in range(2):
                        kb.op("pe", lambda h: h.matmul(ps[0:64, k_ * 128:k_ * 128 + M], GG[:, g, m, ri_, :], UT[:, g, m, 0:M],
                                                       start=(k_ == 0 and m == 0), stop=(m == 1), skip_group_check=True),
                              r=[GG.res, UT.res], w=[ps.res])
            copy_evac(kb, kb.ev(), Eall[:, g0:g0 + 2, :, 1:1 + M],
                      ps[0:64, :].rearrange("p (g r m) -> p g r m", g=2, r=2)[:, :, :, 0:M], r=[ps.res], w=[Eall.res])
        for m in range(M):
            Xm = Eall[:, :, :, m]
            Xn = Eall[:, :, :, m + 1]
            kb.op("dve", lambda h: h.tensor_tensor(out=P1[:, :, :], in0=Xm, in1=MU1[:, :, :], op=ALU.mult),
                  r=[Eall.res, MU1.res], w=[P1.res])
            kb.op("pool", lambda h: h.tensor_tensor(out=P2[:, :, :], in0=Xm, in1=MU2[:, :, :], op=ALU.mult),
                  r=[Eall.res, MU2.res], w=[P2.res])
            kb.op("dve", lambda h: h.tensor_tensor(out=Xn, in0=Xn, in1=P1[:, :, :], op=ALU.add), r=[Eall.res, P1.res], w=[Eall.res])
            kb.op("dve", lambda h: h.tensor_tensor(out=Eall[:, :, 0, m + 1], in0=Eall[:, :, 0, m + 1], in1=P2[:, :, 1], op=ALU.subtract),
                  r=[Eall.res, P2.res], w=[Eall.res])
            kb.op("dve", lambda h: h.tensor_tensor(out=Eall[:, :, 1, m + 1], in0=Eall[:, :, 1, m + 1], in1=P2[:, :, 0], op=ALU.add),
                  r=[Eall.res, P2.res], w=[Eall.res])
        kb.op("act", lambda h: h.activation(out=Xb[:, :, :, 0:M], in_=Eall[:, :, :, 0:M], func=AF.Identity), r=[Eall.res], w=[Xb.res])
        for g in range(32):
            ps = dn.nps()
            kb.op("pe", lambda h: h.matmul(ps[0:M, 0:256], UT[:, g, 0, 0:M], TT[:, g, 0, :], start=True, stop=False),
                  r=[UT.res, TT.res], w=[ps.res])
            kb.op("pe", lambda h: h.matmul(ps[0:M, 0:256], UT[:, g, 1, 0:M], TT[:, g, 1, :], start=False, stop=False),
                  r=[UT.res, TT.res], w=[ps.res])
            kb.op("pe", lambda h: h.matmul(ps[0:M, 0:256], Xb[:, g, 0, 0:M], FFr[:, g, :], start=False, stop=False),
                  r=[Xb.res, FFr.res], w=[ps.res])
            kb.op("pe", lambda h: h.matmul(ps[0:M, 0:256], Xb[:, g, 1, 0:M], nFFi[:, g, :], start=False, stop=True),
                  r=[Xb.res, nFFi.res], w=[ps.res])
            copy_evac(kb, kb.ev(), Ycm[0:M, :, g * 16:(g + 1) * 16], ps[0:M, 0:256].rearrange("m (t c) -> m t c", c=16),
                      r=[ps.res], w=[Ycm.res])
        kb.op("dve", lambda h: h.tensor_copy(out=Eall[:, :, :, 0], in_=Eall[:, :, :, M]), r=[Eall.res], w=[Eall.res])
        for s0 in range(0, 16, 4):
            gxv = gx[:, :].rearrange("p (s c) -> p s c", c=512)
            kb.op("dve", lambda h: h.tensor_tensor(out=gxv[0:M], in0=Ucm[0:M, s0:s0 + 4, :],
                                                   in1=dB[0:M, :].unsqueeze(1).to_broadcast([M, 4, 512]), op=ALU.mult),
                  r=[UY.res, dB.res], w=[gx.res])
            kb.op("dve", lambda h: h.tensor_tensor(out=gxv[0:M], in0=gxv[0:M], in1=Ycm[0:M, s0:s0 + 4, :], op=ALU.add),
                  r=[gx.res, Ycm.res], w=[gx.res])
            gelu_tanh(kb, Ycm[0:M, s0:s0 + 4, :].rearrange("m s c -> m (s c)"), gx[0:M, :], gtmp[0:M, :], [gx.res], Ycm.res, gtmp.res)
        for s0 in range(0, 16, 2):
            pb = dn.npb()
            for ss_ in range(2):
                for cc in range(4):
                    k_ = ss_ * 4 + cc
                    kb.op("pe", lambda h: h.transpose(out=pb[:, k_ * 128:k_ * 128 + M], in_=Ycm[0:M, s0 + ss_, cc * 128:(cc + 1) * 128],
                                                      identity=ident[0:M, 0:M]), r=[Ycm.res, ident.res], w=[pb.res])
            copy_evac(kb, kb.ev(), YT[:, :, 0:M, s0:s0 + 2].rearrange("p k m s -> p s k m"),
                      pb[:, :].rearrange("p (s k m) -> p s k m", s=2, k=4)[:, :, :, 0:M], r=[pb.res], w=[UY.res])
        ntok = M * 16
        for tb in range(0, ntok, 512):
            nn = min(512, ntok - tb)
            for co in range(4):
                ps = dn.nps()
                for k in range(4):
                    kb.op("pe", lambda h: h.matmul(ps[:, 0:nn], gluW[:, k, co * 128:(co + 1) * 128],
                                                   YT[:, k, tb // 16:(tb + nn) // 16, :].rearrange("p m s -> p (m s)"),
                                                   start=(k == 0), stop=(k == 3)), r=[gluW.res, UY.res], w=[ps.res])
                kb.op("act", lambda h: h.activation(out=sgt[:, 0:nn], in_=ps[:, 0:nn], func=AF.Sigmoid, bias=glub[:, co:co + 1]),
                      r=[ps.res, glub.res], w=[sgt.res])
                sg = sg5[si % 2]
                si += 1
                kb.op("dve", lambda h: h.tensor_tensor(out=sg[:, 0:nn], in0=sgt[:, 0:nn],
                                                       in1=YT[:, co, tb // 16:(tb + nn) // 16, :].rearrange("p m s -> p (m s)"), op=ALU.mult),
                      r=[sgt.res, UY.res], w=[sg.res])
                kb.dma("act", C["yT"][1536 + co * 128:1536 + (co + 1) * 128, tok0 + tb:tok0 + tb + nn], sg[:, 0:nn], r=[sg.res])


class VW:
    def __init__(self, tl, f):
        self.tl, self.f, self.res = tl, f, tl.res

    def __getitem__(self, idx):
        return self.f(self.tl)[idx]


TB = 256


def rwkv_mixer(P, dn, l, C):
    kb, cx, S = P.kb, P.cx, P.S
    ident = C["ident"]
    NTB = S // TB
    f32 = lambda shp, n: cx.sb(shp, F32, n)
    b16 = lambda shp, n: cx.sb(shp, BF16, n)

    def tt(e, out, a, b, op, r, w):
        kb.op(e, lambda h: h.tensor_tensor(out=out, in0=a, in1=b, op=op), r=r, w=w)

    def pcol(name, lo, n, nm):
        t = f32([128, n // 128], nm)
        kb.dma("sp", t[:, :], C["in_" + name][l, lo:lo + n].rearrange("(c q) -> q c", q=128), w=[t.res], slow=True)
        return t

    mix_r, mix_k, mix_v = pcol("rw_mix", 0, 768, "mixr"), pcol("rw_mix", 768, 768, "mixk"), pcol("rw_mix", 1536, 768, "mixv")
    mix_g = pcol("rw_mix", 2496, 256, "mixg")
    mix_wa = f32([96, 2], "mixwa")
    kb.dma("sp", mix_wa[:, :], C["in_rw_mix"][l, 2304:2496].rearrange("(c q) -> q c", q=96), w=[mix_wa.res], slow=True)
    w0, a0 = pcol("rw_w0", 0, 768, "w0"), pcol("rw_a0", 0, 768, "a0")
    k_k, k_a = pcol("rw_k_k", 0, 768, "k_k"), pcol("rw_k_a", 0, 768, "k_a")
    r_k, ln_w, ln_b = pcol("rw_r_k", 0, 768, "r_k"), pcol("rw_ln_w", 0, 768, "ln_w"), pcol("rw_ln_b", 0, 768, "ln_b")
    omka = f32([128, 6], "omka")
    kb.op("dve", lambda h: h.tensor_scalar(out=omka[:, :], in0=k_a[:, :], scalar1=-1.0, scalar2=1.0, op0=ALU.mult, op1=ALU.add),
          r=[k_a.res], w=[omka.res])
    w2b, a2b, g2b = b16([96, 768], "w2b"), b16([96, 768], "a2b"), b16([128, 2, 768], "g2b")
    kb.dma("pool", w2b[:, :], C["in_rw_w2"][l], w=[w2b.res])
    kb.dma("pool", a2b[:, :], C["in_rw_a2"][l], w=[a2b.res])
    kb.dma("pool", g2b[:, :, :], C["in_rw_g2"][l].rearrange("(k q) c -> q k c", q=128), w=[g2b.res])
    MUs, MUi, MLs = f32([128, 128], "MUs"), f32([128, 128], "MUi"), f32([128, 128], "MLs")
    kb.dma("sp", MUs[:, :], C["rw_masks"][0], w=[MUs.res])
    kb.dma("sp", MUi[:, :], C["rw_masks"][1], w=[MUi.res])
    kb.dma("sp", MLs[:, :], C["rw_masks"][2], w=[MLs.res])
    MUsi = f32([128, 256], "MUsi")
    kb.dma("sp", MUsi[:, 0:128], C["rw_masks"][0], w=[MUsi.res])
    kb.dma("sp", MUsi[:, 128:256], C["rw_masks"][1], w=[MUsi.res])
    bones = b16([128, 128], "bones")
    kb.dma("sp", bones[:, :], C["bones"][:, :], w=[bones.res])
    cmk = f32([128, TB], "cmk")
    kb.dma("sp", cmk[:, :], C["chunkmask"][:, 0:TB], w=[cmk.res])
    zw = b16([96, 2, TB + 1], "zwa")
    zg = b16([128, 2, TB + 1], "zg")
    th, zab = b16([96, TB], "th"), b16([96, TB], "zab")
    sgg = b16([128, 2, TB], "sgg")
    ltmp = f32([128, 2, TB], "ltmp")
    Sst = [f32([128, 128], "Sst") for _ in range(6)]
    Sb = [b16([128, 128], "Sb") for _ in range(6)]
    for p in range(6):
        kb.op("pool", lambda h: h.memset(Sst[p][:, :], 0.0), w=[Sst[p].res])
        kb.op("pool", lambda h: h.memset(Sb[p][:, :], 0.0), w=[Sb[p].res])
    identb = ident


    def c3(ap):
        return ap.rearrange("p (c t) -> p c t", t=64)

    def shift(eng, out_ap, zt_ap, mixcol, npart, r, w, dtm):
        tt(eng, dtm[0:npart, :], zt_ap[:, 0:TB], zt_ap[:, 1:TB + 1], ALU.subtract, r, [dtm.res])
        kb.op("dve", lambda h: h.scalar_tensor_tensor(out=out_ap, in0=dtm[0:npart, :], scalar=mixcol, in1=zt_ap[:, 1:TB + 1],
                                                      op0=ALU.mult, op1=ALU.add), r=r + [dtm.res], w=w)

    def load_shifted(tile_ap, res, row0, nrows, t0):
        if t0 == 0:
            kb.op("pool", lambda h: h.memset(tile_ap[:, 0:1], 0.0), w=[res])
            kb.dma("sp", tile_ap[:, 1:TB + 1], C["zT"][row0:row0 + nrows, 0:TB], w=[res])
        else:
            kb.dma("sp", tile_ap[:, :], C["zT"][row0:row0 + nrows, t0 - 1:t0 + TB], w=[res])

    dtm0 = f32([128, TB], "dtm0")

    def make_unit():
        z3 = b16([128, 3, TB + 1], "zin")
        rs, ks, vs = f32([128, TB], "rs"), f32([128, TB], "ks"), f32([128, TB], "vs")
        dtm = f32([128, TB], "dtm")
        logw, alpha, gg_ = f32([128, TB], "logw"), f32([128, TB], "alpha"), f32([128, TB], "gfm")
        kk, kk2, rinv = f32([128, TB], "kk"), b16([128, TB], "kk2"), f32([128, TB], "rinv")
        kp, bq = f32([128, TB], "kp"), f32([128, TB], "bq")
        cc, ec, enc, ecm, ecc = f32([128, TB], "cc"), f32([128, TB], "ec"), f32([128, TB], "enc"), f32([128, TB], "ecm"), f32([128, TB], "ecc")
        gC = f32([128, 4], "gC")
        rkb = b16([128, TB], "rkb")
        bonus = f32([128, TB], "bonus")
        AQ = b16([128, 4, 256], "AQ")
        bblk, kblk, b2blk, k2blk, vblk = (b16([128, 4, 128], n) for n in ("bblk", "kblk", "b2blk", "k2blk", "vblk"))
        for t in (AQ, bblk, kblk, b2blk, k2blk, vblk):
            kb.op("pool", lambda h: h.memset(t[:, :, :], 0.0), w=[t.res])
        TM = b16([128, 4, 4, 128], "TM")
        Xn = [b16([128, 4, 128], "Xn") for _ in range(2)]
        Nn = [b16([128, 4, 128], "Nn") for _ in range(2)]
        Pm = b16([128, 4, 128], "Pm")
        ArbT, AakT, ArkT, AkV, WT = (b16([128, 4, 128], n) for n in ("ArbT", "AakT", "ArkT", "AkV", "WT"))
        U = f32([128, 4, 128], "U")
        Ytm = f32([128, 4, 128], "Ytm")
        ysq = f32([128, 4, 128], "ysq")
        ynb = b16([128, 4, 128], "ynb")
        st1, st2, st3 = f32([128, 4], "st1"), f32([128, 4], "st2"), f32([128, 4], "st3")
        yfm = f32([128, TB], "yfm")
        SAb = b16([128, 128], "SAb")
        ystg = [b16([128, TB], "ystg") for _ in range(2)]
        sic = [0]

        def unit(tb, p):
            t0 = tb * TB
            pc = slice(p * 128, (p + 1) * 128)
            for i, row0 in enumerate((p * 128, 768 + p * 128, 1536 + p * 128)):
                load_shifted(z3[:, i, :], z3.res, row0, 128, t0)
            shift("pool", rs[:, :], z3[:, 0, :], mix_r[:, p:p + 1], 128, [z3.res, mix_r.res], [rs.res], dtm)
            shift("pool", ks[:, :], z3[:, 1, :], mix_k[:, p:p + 1], 128, [z3.res, mix_k.res], [ks.res], dtm)
            shift("pool", vs[:, :], z3[:, 2, :], mix_v[:, p:p + 1], 128, [z3.res, mix_v.res], [vs.res], dtm)
            yield
            ps = dn.nps()
            kb.op("pe", lambda h: h.matmul(ps[:, 0:TB], w2b[:, pc], th[:, :], start=True, stop=True), r=[w2b.res, th.res], w=[ps.res])
            kb.op("act", lambda h: h.activation(out=logw[:, :], in_=ps[:, 0:TB], func=AF.Sigmoid, bias=w0[:, p:p + 1]),
                  r=[ps.res, w0.res], w=[logw.res])
            kb.op("dve", lambda h: h.tensor_scalar(out=logw[:, :], in0=logw[:, :], scalar1=-math.exp(-0.5), scalar2=None, op0=ALU.mult),
                  r=[logw.res], w=[logw.res])
            ps = dn.nps()
            kb.op("pe", lambda h: h.matmul(ps[:, 0:TB], a2b[:, pc], zab[:, :], start=True, stop=True), r=[a2b.res, zab.res], w=[ps.res])
            kb.op("act", lambda h: h.activation(out=alpha[:, :], in_=ps[:, 0:TB], func=AF.Sigmoid, bias=a0[:, p:p + 1]),
                  r=[ps.res, a0.res], w=[alpha.res])
            ps = dn.nps()
            for k in range(2):
                kb.op("pe", lambda h: h.matmul(ps[:, 0:TB], g2b[:, k, pc], sgg[:, k, :], start=(k == 0), stop=(k == 1)),
                      r=[g2b.res, sgg.res], w=[ps.res])
            kb.op("act", lambda h: h.activation(out=gg_[:, :], in_=ps[:, 0:TB], func=AF.Identity), r=[ps.res], w=[gg_.res])
            yield
            kb.op("dve", lambda h: h.tensor_scalar(out=kk[:, :], in0=ks[:, :], scalar1=k_k[:, p:p + 1], scalar2=None, op0=ALU.mult),
                  r=[ks.res, k_k.res], w=[kk.res])
            tt("pool", kk2[:, :], kk[:, :], kk[:, :], ALU.mult, [kk.res], [kk2.res])
            yield
            ps = dn.nps()
            kb.op("pe", lambda h: h.matmul(ps[:, 0:TB], bones[:, :], kk2[:, :], start=True, stop=True), r=[bones.res, kk2.res], w=[ps.res])
            kb.op("act", lambda h: h.activation(out=rinv[:, :], in_=ps[:, 0:TB], func=AF.Sqrt), r=[ps.res], w=[rinv.res])
            kb.op("dve", lambda h: h.tensor_scalar(out=rinv[:, :], in0=rinv[:, :], scalar1=1e-12, scalar2=None, op0=ALU.max),
                  r=[rinv.res], w=[rinv.res])
            kb.op("dve", lambda h: h.reciprocal(out=rinv[:, :], in_=rinv[:, :]), r=[rinv.res], w=[rinv.res])
            tt("dve", kk[:, :], kk[:, :], rinv[:, :], ALU.mult, [kk.res, rinv.res], [kk.res])
            yield
            yield
            kb.op("dve", lambda h: h.tensor_scalar(out=kp[:, :], in0=alpha[:, :], scalar1=k_a[:, p:p + 1], scalar2=omka[:, p:p + 1],
                                                   op0=ALU.mult, op1=ALU.add), r=[alpha.res, k_a.res, omka.res], w=[kp.res])
            tt("dve", kp[:, :], kp[:, :], ks[:, :], ALU.mult, [kp.res, ks.res], [kp.res])
            yield
            tt("pool", bq[:, :], kk[:, :], alpha[:, :], ALU.mult, [kk.res, alpha.res], [bq.res])
            yield
            yield
            kb.op("dve", lambda h: h.tensor_tensor_scan(out=cc[:, :], data0=cmk[:, :], data1=logw[:, :], initial=0.0,
                                                        op0=ALU.mult, op1=ALU.add), r=[cmk.res, logw.res], w=[cc.res])
            kb.op("act", lambda h: h.activation(out=ec[:, :], in_=cc[:, :], func=AF.Exp), r=[cc.res], w=[ec.res])
            kb.op("act", lambda h: h.activation(out=enc[:, :], in_=cc[:, :], func=AF.Exp, scale=-1.0), r=[cc.res], w=[enc.res])
            tt("pool", ecm[:, :], cc[:, :], logw[:, :], ALU.subtract, [cc.res, logw.res], [ecm.res])
            yield
            kb.op("act", lambda h: h.activation(out=ecm[:, :], in_=ecm[:, :], func=AF.Exp), r=[ecm.res], w=[ecm.res])
            kb.op("act", lambda h: h.activation(out=gC[:, :], in_=c3(cc[:, :])[:, :, 63], func=AF.Exp), r=[cc.res], w=[gC.res])
            tt("dve", c3(ecc[:, :]), c3(cc[:, :])[:, :, 63:64].to_broadcast([128, 4, 64]), c3(cc[:, :]), ALU.subtract,
               [cc.res], [ecc.res])
            kb.op("act", lambda h: h.activation(out=ecc[:, :], in_=ecc[:, :], func=AF.Exp), r=[ecc.res], w=[ecc.res])
            yield
            kb.op("dve", lambda h: h.scalar_tensor_tensor(out=rkb[:, :], in0=rs[:, :], scalar=r_k[:, p:p + 1], in1=kp[:, :],
                                                          op0=ALU.mult, op1=ALU.mult), r=[rs.res, r_k.res, kp.res], w=[rkb.res])
            ps = dn.nps()
            kb.op("pe", lambda h: h.matmul(ps[:, 0:TB], bones[:, :], rkb[:, :], start=True, stop=True), r=[bones.res, rkb.res], w=[ps.res])
            tt("dve", bonus[:, :], ps[:, 0:TB], vs[:, :], ALU.mult, [ps.res, vs.res], [bonus.res])
            yield
            yield
            engs = ["dve", "pool"]
            ei = 0
            for hh in range(2):
                lo = hh * 64
                sl_ = slice(lo, lo + 64)

                def blk(dst, colofs, a, b, ra, rb, neg=False):
                    nonlocal ei
                    e = engs[ei % 2]
                    ei += 1
                    o = dst[sl_, :, colofs + lo:colofs + lo + 64]
                    if neg:
                        kb.op("dve", lambda h: h.scalar_tensor_tensor(out=o, in0=c3(a[sl_, :]), scalar=-1.0, in1=c3(b[sl_, :]),
                                                                      op0=ALU.mult, op1=ALU.mult), r=[ra, rb], w=[dst.res])
                    elif b is None:
                        kb.op(e, lambda h: h.tensor_copy(out=o, in_=c3(a[sl_, :])), r=[ra], w=[dst.res])
                    else:
                        tt(e, o, c3(a[sl_, :]), c3(b[sl_, :]), ALU.mult, [ra, rb], [dst.res])

                blk(AQ, 0, kk, ecm, kk.res, ecm.res, neg=True)
                blk(AQ, 128, rs, ec, rs.res, ec.res)
                blk(bblk, 0, bq, enc, bq.res, enc.res)
                blk(kblk, 0, kp, enc, kp.res, enc.res)
                blk(b2blk, 0, bq, ecc, bq.res, ecc.res)
                blk(k2blk, 0, kp, ecc, kp.res, ecc.res)
                blk(vblk, 0, vs, None, vs.res, None)
            yield
            srcs = [(AQ, 0), (b2blk, 0), (k2blk, 0), (vblk, 0)]
            for half in range(2):
                pb = dn.npb()
                for qi in range(2):
                    src, co = srcs[half * 2 + qi]
                    for ch in range(4):
                        k_ = qi * 4 + ch
                        kb.op("pe", lambda h: h.transpose(out=pb[:, k_ * 128:(k_ + 1) * 128], in_=src[:, ch, co:co + 128],
                                                          identity=identb[:, :]), r=[src.res, identb.res], w=[pb.res])
                copy_evac(kb, kb.ev(), TM[:, half * 2:half * 2 + 2, :, :].rearrange("p q c m -> p (q c m)"), pb[:, :],
                          r=[pb.res], w=[TM.res])
            yield
            ps = dn.nps()
            for ch in range(4):
                kb.op("pe", lambda h: h.matmul(ps[:, ch * 128:(ch + 1) * 128], AQ[:, ch, 0:128], bblk[:, ch, :],
                                               start=(ch == 0), stop=True, skip_group_check=True), r=[AQ.res, bblk.res], w=[ps.res])
            X0, N0 = Xn[0], Nn[0]
            tt("dve", N0[:, :, :], ps[:, :].rearrange("p (c m) -> p c m", m=128), MLs[:, :].unsqueeze(1).to_broadcast([128, 4, 128]),
               ALU.mult, [ps.res, MLs.res], [N0.res])
            for (lhs, dA, dB_) in ((bblk, X0, ArbT), (kblk, AakT, ArkT)):
                for half in range(2):
                    ps = dn.nps()
                    for c2 in range(2):
                        ch = half * 2 + c2
                        kb.op("pe", lambda h: h.matmul(ps[:, c2 * 256:(c2 + 1) * 256], lhs[:, ch, :], AQ[:, ch, :],
                                                       start=(c2 == 0), stop=True, skip_group_check=True), r=[lhs.res, AQ.res], w=[ps.res])
                    pv = ps[:, :].rearrange("p (c m) -> p c m", m=256)
                    tt("dve", dA[:, half * 2:half * 2 + 2, :], pv[:, :, 0:128], MUs[:, :].unsqueeze(1).to_broadcast([128, 2, 128]),
                       ALU.mult, [ps.res, MUs.res], [dA.res])
                    tt("dve", dB_[:, half * 2:half * 2 + 2, :], pv[:, :, 128:256], MUi[:, :].unsqueeze(1).to_broadcast([128, 2, 128]),
                       ALU.mult, [ps.res, MUi.res], [dB_.res])
            yield
            tt("pool", Pm[:, :, :], X0[:, :, :], identb[:, :].unsqueeze(1).to_broadcast([128, 4, 128]), ALU.add,
               [X0.res, identb.res], [Pm.res])
            Xc, Nc = X0, N0
            for j in range(1, 6):
                Xnx, Nnx = Xn[j % 2], Nn[j % 2]
                if j < 5:
                    ps = dn.nps()
                    for ch in range(4):
                        kb.op("pe", lambda h: h.matmul(ps[:, ch * 128:(ch + 1) * 128], Nc[:, ch, :], Xc[:, ch, :],
                                                       start=(ch == 0), stop=True, skip_group_check=True), r=[Nc.res, Xc.res], w=[ps.res])
                    copy_evac(kb, "act", Xnx[:, :, :].rearrange("p c m -> p (c m)"), ps[:, :], r=[ps.res], w=[Xnx.res])
                ps = dn.nps()
                for ch in range(4):
                    kb.op("pe", lambda h: h.matmul(ps[:, ch * 128:(ch + 1) * 128], Xc[:, ch, :], Nc[:, ch, :],
                                                   start=(ch == 0), stop=True, skip_group_check=True), r=[Nc.res, Xc.res], w=[ps.res])
                copy_evac(kb, "dve", Nnx[:, :, :].rearrange("p c m -> p (c m)"), ps[:, :], r=[ps.res], w=[Nnx.res])
                ps = dn.nps()
                for ch in range(4):
                    kb.op("pe", lambda h: h.matmul(ps[:, ch * 128:(ch + 1) * 128], Nnx[:, ch, :], Pm[:, ch, :],
                                                   start=(ch == 0), stop=True, skip_group_check=True), r=[Nnx.res, Pm.res], w=[ps.res])
                tt("dve", Pm[:, :, :].rearrange("p c m -> p (c m)"), ps[:, :], Pm[:, :, :].rearrange("p c m -> p (c m)"), ALU.add,
                   [ps.res, Pm.res], [Pm.res])
                Xc, Nc = Xnx, Nnx
                yield
            yield
            ps = dn.nps()
            for ch in range(4):
                kb.op("pe", lambda h: h.matmul(ps[:, ch * 128:(ch + 1) * 128], AakT[:, ch, :], TM[:, 3, ch, :],
                                               start=(ch == 0), stop=True, skip_group_check=True), r=[AakT.res, TM.res], w=[ps.res])
            copy_evac(kb, "act", AkV[:, :, :].rearrange("p c m -> p (c m)"), ps[:, :], r=[ps.res], w=[AkV.res])
            yield
            ps = dn.nps()
            for ch in range(4):
                kb.op("pe", lambda h: h.matmul(ps[:, ch * 128:(ch + 1) * 128], Pm[:, ch, :], AkV[:, ch, :],
                                               start=(ch == 0), stop=True, skip_group_check=True), r=[Pm.res, AkV.res], w=[ps.res])
            copy_evac(kb, "dve", U[:, :, :].rearrange("p c m -> p (c m)"), ps[:, :], r=[ps.res], w=[U.res])
            yield
            ps = dn.nps()
            for ch in range(4):
                kb.op("pe", lambda h: h.matmul(ps[:, ch * 128:(ch + 1) * 128], TM[:, 0, ch, :], Pm[:, ch, :],
                                               start=(ch == 0), stop=True, skip_group_check=True), r=[TM.res, Pm.res], w=[ps.res])
            copy_evac(kb, "act", WT[:, :, :].rearrange("p c m -> p (c m)"), ps[:, :], r=[ps.res], w=[WT.res])
            yield
            yield
            S_, Sb_ = Sst[p], Sb[p]
            for ch in range(4):
                ps = dn.nps()
                kb.op("pe", lambda h: h.matmul(ps[:, 0:128], WT[:, ch, :], Sb_[:, :], start=True, stop=True), r=[WT.res, Sb_.res], w=[ps.res])
                tt("dve", SAb[:, :], ps[:, 0:128], U[:, ch, :], ALU.add, [ps.res, U.res], [SAb.res])
                yield
                psy = dn.nps()
                kb.op("pe", lambda h: h.matmul(psy[:, 0:128], AQ[:, ch, 128:256], Sb_[:, :], start=True, stop=False),
                      r=[AQ.res, Sb_.res], w=[psy.res])
                kb.op("pe", lambda h: h.matmul(psy[:, 0:128], ArbT[:, ch, :], SAb[:, :], start=False, stop=False),
                      r=[ArbT.res, SAb.res], w=[psy.res])
                kb.op("pe", lambda h: h.matmul(psy[:, 0:128], ArkT[:, ch, :], TM[:, 3, ch, :], start=False, stop=True),
                      r=[ArkT.res, TM.res], w=[psy.res])
                copy_evac(kb, "act", Ytm[:, ch, :], psy[:, 0:128], r=[psy.res], w=[Ytm.res])
                pss = dn.nps()
                kb.op("pe", lambda h: h.matmul(pss[:, 0:128], TM[:, 1, ch, :], SAb[:, :], start=True, stop=False),
                      r=[TM.res, SAb.res], w=[pss.res])
                kb.op("pe", lambda h: h.matmul(pss[:, 0:128], TM[:, 2, ch, :], TM[:, 3, ch, :], start=False, stop=True),
                      r=[TM.res], w=[pss.res])
                kb.op("dve", lambda h: h.scalar_tensor_tensor(out=S_[:, :], in0=S_[:, :], scalar=gC[:, ch:ch + 1], in1=pss[:, 0:128],
                                                              op0=ALU.mult, op1=ALU.add), r=[S_.res, gC.res, pss.res], w=[S_.res])
                kb.op("act", lambda h: h.activation(out=Sb_[:, :], in_=S_[:, :], func=AF.Identity), r=[S_.res], w=[Sb_.res])
                yield
            yield
            kb.op("dve", lambda h: h.tensor_reduce(out=st1[:, :], in_=Ytm[:, :, :], axis=AX.X, op=ALU.add), r=[Ytm.res], w=[st1.res])
            kb.op("act", lambda h: h.activation(out=ysq[:, :, :], in_=Ytm[:, :, :], func=AF.Square), r=[Ytm.res], w=[ysq.res])
            kb.op("dve", lambda h: h.tensor_reduce(out=st2[:, :], in_=ysq[:, :, :], axis=AX.X, op=ALU.add), r=[ysq.res], w=[st2.res])
            kb.op("dve", lambda h: h.tensor_scalar(out=st1[:, :], in0=st1[:, :], scalar1=1.0 / 64, scalar2=None, op0=ALU.mult),
                  r=[st1.res], w=[st1.res])
            tt("dve", st3[:, :], st1[:, :], st1[:, :], ALU.mult, [st1.res], [st3.res])
            yield
            kb.op("dve", lambda h: h.scalar_tensor_tensor(out=st2[:, :], in0=st2[:, :], scalar=1.0 / 64, in1=st3[:, :],
                                                          op0=ALU.mult, op1=ALU.subtract), r=[st2.res, st3.res], w=[st2.res])
            kb.op("act", lambda h: h.activation(out=st2[:, :], in_=st2[:, :], func=AF.Sqrt, bias=GN_EPS), r=[st2.res], w=[st2.res])
            kb.op("dve", lambda h: h.reciprocal(out=st2[:, :], in_=st2[:, :]), r=[st2.res], w=[st2.res])
            pb = dn.npb()
            for ch in range(4):
                kb.op("dve", lambda h: h.tensor_scalar(out=ynb[:, ch, :], in0=Ytm[:, ch, :], scalar1=st1[:, ch:ch + 1],
                                                       scalar2=st2[:, ch:ch + 1], op0=ALU.subtract, op1=ALU.mult),
                      r=[Ytm.res, st1.res, st2.res], w=[ynb.res])
                kb.op("pe", lambda h: h.transpose(out=pb[:, ch * 128:(ch + 1) * 128], in_=ynb[:, ch, :], identity=identb[:, :]),
                      r=[ynb.res, identb.res], w=[pb.res])
            pbv = pb[:, 0:512].rearrange("p (c m) -> p c m", m=128)
            kb.op("act", lambda h: h.activation(out=c3(yfm[0:64, :]), in_=pbv[0:64, :, 0:64], func=AF.Identity), r=[pb.res], w=[yfm.res])
            kb.op("act", lambda h: h.activation(out=c3(yfm[64:128, :]), in_=pbv[64:128, :, 64:128], func=AF.Identity), r=[pb.res], w=[yfm.res])
            kb.op("dve", lambda h: h.tensor_scalar(out=yfm[:, :], in0=yfm[:, :], scalar1=ln_w[:, p:p + 1], scalar2=ln_b[:, p:p + 1],
                                                   op0=ALU.mult, op1=ALU.add), r=[yfm.res, ln_w.res, ln_b.res], w=[yfm.res])
            tt("pool", yfm[:, :], yfm[:, :], bonus[:, :], ALU.add, [yfm.res, bonus.res], [yfm.res])
            yield
            sg = ystg[sic[0] % 2]
            sic[0] += 1
            tt("dve", sg[:, :], yfm[:, :], gg_[:, :], ALU.mult, [yfm.res, gg_.res], [sg.res])
            yield
            kb.dma("act", C["yT"][p * 128:(p + 1) * 128, t0:t0 + TB], sg[:, :], r=[sg.res])

            yield

        return unit

    units = [make_unit(), make_unit(), make_unit()]
    for tb in range(NTB):
        t0 = tb * TB
        load_shifted(zw[:, 0, :], zw.res, 2304, 96, t0)
        load_shifted(zw[:, 1, :], zw.res, 2400, 96, t0)
        load_shifted(zg[:, 0, :], zg.res, 2496, 128, t0)
        load_shifted(zg[:, 1, :], zg.res, 2624, 128, t0)
        shift("dve", ltmp[0:96, 0, :], zw[:, 0, :], mix_wa[:, 0:1], 96, [zw.res, mix_wa.res], [ltmp.res], dtm0)
        kb.op("act", lambda h: h.activation(out=th[:, :], in_=ltmp[0:96, 0, :], func=AF.Tanh), r=[ltmp.res], w=[th.res])
        shift("dve", ltmp[0:96, 1, :], zw[:, 1, :], mix_wa[:, 1:2], 96, [zw.res, mix_wa.res], [ltmp.res], dtm0)
        kb.op("act", lambda h: h.activation(out=zab[:, :], in_=ltmp[0:96, 1, :], func=AF.Identity), r=[ltmp.res], w=[zab.res])
        for k in range(2):
            shift("dve", ltmp[:, k, :], zg[:, k, :], mix_g[:, k:k + 1], 128, [zg.res, mix_g.res], [ltmp.res], dtm0)
            kb.op("act", lambda h: h.activation(out=sgg[:, k, :], in_=ltmp[:, k, :], func=AF.Sigmoid), r=[ltmp.res], w=[sgg.res])
        for pp in range(0, 6, 3):
            gens = [units[i_](tb, pp + i_) for i_ in range(3)]
            while gens:
                for g_ in list(gens):
                    try:
                        next(g_)
                    except StopIteration:
                        gens.remove(g_)
```
